# Optimizing a Trainium2 kernel written in Bass

```python
import math
import jax, jax.numpy as jnp
from jax import lax
import numpy as np

D_MODEL = 1024
BATCH = 8
SEQ = 2048
DEPTH = 2

N_EVEN = (DEPTH + 1) // 2
N_ODD = DEPTH // 2
ALPHA = (2 * DEPTH) ** 0.25
BETA = (8 * DEPTH) ** -0.25
LN_EPS = 1e-5

SC_WIDTH = D_MODEL
SC_KERNEL = 3

SSM_INNER = D_MODEL
SSM_HEADDIM = 64
SSM_HEADS = SSM_INNER // SSM_HEADDIM
SSM_GROUPS = 2
SSM_STATE = 128
SSM_CONV = 4
SSM_CHUNK = 128
SSM_XBC = SSM_INNER + 2 * SSM_GROUPS * SSM_STATE
MIX0_IN = 3 * SC_WIDTH + SSM_INNER + SSM_XBC + SSM_HEADS
MIX0_OUT = SC_WIDTH + SSM_INNER

FOX_HEADS = 16
FOX_HEADDIM = D_MODEL // FOX_HEADS
FOX_WIDTH = FOX_HEADS * FOX_HEADDIM
FOX_IN = 3 * FOX_WIDTH + FOX_HEADS
FOX_BLOCK = 128

D_FF = 2816
FFN_KERNEL = 3

kernel_name = 'hybrid_sconv_ssd_fox_deepnorm'


def _split(h, sizes):
    idx = np.cumsum(sizes)[:-1].tolist()
    return jnp.split(h, idx, axis=-1)


def causal_dwconv(x, w, b=None):
    k, c = w.shape
    y = lax.conv_general_dilated(x, w[:, None, :].astype(x.dtype), window_strides=(1,), padding=[(k - 1, 0)], dimension_numbers=('NWC', 'WIO', 'NWC'), feature_group_count=c)
    return y if b is None else y + b.astype(y.dtype)


def layer_norm(x, g, b):
    xf = x.astype(jnp.float32)
    mu = jnp.mean(xf, -1, keepdims=True)
    var = jnp.mean(jnp.square(xf - mu), -1, keepdims=True)
    return ((xf - mu) * lax.rsqrt(var + LN_EPS) * g + b).astype(x.dtype)


def gated_rmsnorm(y, z, g):
    bsz, l, _ = y.shape
    u = (y.astype(jnp.float32) * jax.nn.silu(z.astype(jnp.float32))).reshape(bsz, l, SSM_GROUPS, -1)
    u = u * lax.rsqrt(jnp.mean(jnp.square(u), -1, keepdims=True) + LN_EPS)
    return (u.reshape(bsz, l, SSM_INNER) * g).astype(y.dtype)


def ssd_chunked(xs, dt, a, bm, cm):
    bsz, l = xs.shape[:2]
    nc, q, r = l // SSM_CHUNK, SSM_CHUNK, SSM_HEADS // SSM_GROUPS
    x = xs.astype(jnp.float32).reshape(bsz, nc, q, SSM_GROUPS, r, SSM_HEADDIM)
    dtc = dt.reshape(bsz, nc, q, SSM_GROUPS, r)
    xdt = x * dtc[..., None]
    bc = bm.astype(jnp.float32).reshape(bsz, nc, q, SSM_GROUPS, SSM_STATE)
    cc = cm.astype(jnp.float32).reshape(bsz, nc, q, SSM_GROUPS, SSM_STATE)
    a_cs = jnp.cumsum((dtc * a.reshape(SSM_GROUPS, r)).transpose(0, 1, 3, 4, 2), axis=-1)
    causal = jnp.tril(jnp.ones((q, q), dtype=bool))
    seg = jnp.exp(jnp.where(causal, a_cs[..., :, None] - a_cs[..., None, :], -jnp.inf))
    cb = jnp.einsum('bctgn,bcsgn->bcgts', cc, bc)
    y_diag = jnp.einsum('bcgts,bcgrts,bcsgrp->bctgrp', cb, seg, xdt)
    decay_to_end = jnp.exp(a_cs[..., -1:] - a_cs)
    chunk_states = jnp.einsum('bcsgn,bcgrs,bcsgrp->bcgrpn', bc, decay_to_end, xdt)
    chunk_decay = jnp.exp(a_cs[..., -1])

    def step(h, inp):
        st, dec = inp
        return h * dec[..., None, None] + st, h

    h0 = jnp.zeros((bsz, SSM_GROUPS, r, SSM_HEADDIM, SSM_STATE), jnp.float32)
    _, prev = lax.scan(step, h0, (jnp.moveaxis(chunk_states, 1, 0), jnp.moveaxis(chunk_decay, 1, 0)))
    prev = jnp.moveaxis(prev, 0, 1)
    y_off = jnp.einsum('bctgn,bcgrpn,bcgrt->bctgrp', cc, prev, jnp.exp(a_cs))
    return (y_diag + y_off).reshape(bsz, l, SSM_HEADS, SSM_HEADDIM)


def sconv_ssd_mixer(x, w_in, sc_conv_w, ssm_conv_w, ssm_conv_b, dt_bias, a_log, d_skip, norm_g, w_out):
    bsz, l, _ = x.shape
    h = x @ w_in
    sc_b, sc_c, sc_h, z, xbc, dt_raw = _split(h, [SC_WIDTH, SC_WIDTH, SC_WIDTH, SSM_INNER, SSM_XBC, SSM_HEADS])
    y_a = sc_b * causal_dwconv(sc_c * sc_h, sc_conv_w)
    xbc = jax.nn.silu(causal_dwconv(xbc, ssm_conv_w, ssm_conv_b))
    xs, bm, cm = _split(xbc, [SSM_INNER, SSM_GROUPS * SSM_STATE, SSM_GROUPS * SSM_STATE])
    xs = xs.reshape(bsz, l, SSM_HEADS, SSM_HEADDIM)
    dt = jax.nn.softplus(dt_raw.astype(jnp.float32) + dt_bias.astype(jnp.float32))
    a = -jnp.exp(a_log.astype(jnp.float32))
    y = ssd_chunked(xs, dt, a, bm.reshape(bsz, l, SSM_GROUPS, SSM_STATE), cm.reshape(bsz, l, SSM_GROUPS, SSM_STATE))
    y = y + d_skip.astype(jnp.float32)[:, None] * xs.astype(jnp.float32)
    y_b = gated_rmsnorm(y.reshape(bsz, l, SSM_INNER).astype(x.dtype), z, norm_g)
    return jnp.concatenate([y_a, y_b], axis=-1) @ w_out


def fox_mixer(x, w_in, b_f, w_out):
    bsz, l, _ = x.shape
    nb = l // FOX_BLOCK
    h = x @ w_in
    q, k, v, f_logit = _split(h, [FOX_WIDTH, FOX_WIDTH, FOX_WIDTH, FOX_HEADS])
    heads = lambda t: t.reshape(bsz, l, FOX_HEADS, FOX_HEADDIM).transpose(0, 2, 1, 3)
    q, k, v = heads(q), heads(k), heads(v)
    log_f = jax.nn.log_sigmoid(f_logit.astype(jnp.float32) + b_f.astype(jnp.float32))
    cum = jnp.cumsum(log_f, axis=1).transpose(0, 2, 1)
    scale = FOX_HEADDIM ** -0.5
    key_pos = jnp.arange(l)

    def block(args):
        q_blk, cum_blk, i = args
        qpos = i * FOX_BLOCK + jnp.arange(FOX_BLOCK)
        s = jnp.einsum('bhqd,bhkd->bhqk', q_blk, k).astype(jnp.float32) * scale + cum_blk[..., None] - cum[:, :, None, :]
        s = jnp.where(key_pos[None, :] <= qpos[:, None], s, -jnp.inf)
        p = jax.nn.softmax(s, axis=-1)
        return jnp.einsum('bhqk,bhkd->bhqd', p.astype(v.dtype), v)

    q_blocks = q.reshape(bsz, FOX_HEADS, nb, FOX_BLOCK, FOX_HEADDIM).transpose(2, 0, 1, 3, 4)
    cum_blocks = cum.reshape(bsz, FOX_HEADS, nb, FOX_BLOCK).transpose(2, 0, 1, 3)
    o = lax.map(block, (q_blocks, cum_blocks, jnp.arange(nb)))
    o = o.transpose(1, 0, 3, 2, 4).reshape(bsz, l, FOX_WIDTH)
    return o @ w_out


def conv_ffn(x, w_up, conv_w, conv_b, w_down):
    h = causal_dwconv(x @ w_up, conv_w, conv_b)
    u, g = _split(h, [D_FF, D_FF])
    return (u * jax.nn.silu(g)) @ w_down


def setup_inputs(seed: int = 0) -> dict:
    key = jax.random.key(seed)
    ks = jax.random.split(key, 21)
    f32 = jnp.float32

    def nrm(k, shape, scale):
        return jax.random.normal(k, shape, f32) * scale

    x = nrm(ks[0], (BATCH, SEQ, D_MODEL), 1.0)
    sc_ssm_w_in = nrm(ks[1], (N_EVEN, D_MODEL, MIX0_IN), D_MODEL ** -0.5)
    sc_conv_w = nrm(ks[2], (N_EVEN, SC_KERNEL, SC_WIDTH), SC_KERNEL ** -0.5)
    ssm_conv_w = nrm(ks[3], (N_EVEN, SSM_CONV, SSM_XBC), SSM_CONV ** -0.5)
    ssm_conv_b = nrm(ks[4], (N_EVEN, SSM_XBC), 0.02)
    dt0 = jnp.exp(jax.random.uniform(ks[5], (N_EVEN, SSM_HEADS), f32, minval=math.log(1e-3), maxval=math.log(1e-1)))
    ssm_dt_bias = dt0 + jnp.log(-jnp.expm1(-dt0))
    ssm_a_log = jnp.log(jax.random.uniform(ks[6], (N_EVEN, SSM_HEADS), f32, minval=1.0, maxval=16.0))
    ssm_d = 1.0 + nrm(ks[7], (N_EVEN, SSM_HEADS), 0.1)
    ssm_norm_g = 1.0 + nrm(ks[8], (N_EVEN, SSM_INNER), 0.02)
    sc_ssm_w_out = nrm(ks[9], (N_EVEN, MIX0_OUT, D_MODEL), BETA * MIX0_OUT ** -0.5)
    col_scale = jnp.concatenate([jnp.ones((2 * FOX_WIDTH,), f32), jnp.full((FOX_WIDTH,), BETA, f32), jnp.ones((FOX_HEADS,), f32)])
    fox_w_in = nrm(ks[10], (N_ODD, D_MODEL, FOX_IN), D_MODEL ** -0.5) * col_scale
    fox_b_f = jax.random.uniform(ks[11], (N_ODD, FOX_HEADS), f32, minval=1.0, maxval=4.0)
    fox_w_out = nrm(ks[12], (N_ODD, FOX_WIDTH, D_MODEL), BETA * FOX_WIDTH ** -0.5)
    ffn_w_up = nrm(ks[13], (DEPTH, D_MODEL, 2 * D_FF), BETA * D_MODEL ** -0.5)
    ffn_conv_w = nrm(ks[14], (DEPTH, FFN_KERNEL, 2 * D_FF), FFN_KERNEL ** -0.5)
    ffn_conv_b = nrm(ks[15], (DEPTH, 2 * D_FF), 0.02)
    ffn_w_down = nrm(ks[16], (DEPTH, D_FF, D_MODEL), BETA * D_FF ** -0.5)
    ln_mix_g = 1.0 + nrm(ks[17], (DEPTH, D_MODEL), 0.02)
    ln_mix_b = nrm(ks[18], (DEPTH, D_MODEL), 0.02)
    ln_ffn_g = 1.0 + nrm(ks[19], (DEPTH, D_MODEL), 0.02)
    ln_ffn_b = nrm(ks[20], (DEPTH, D_MODEL), 0.02)
    return {'x': x, 'sc_ssm_w_in': sc_ssm_w_in, 'sc_conv_w': sc_conv_w, 'ssm_conv_w': ssm_conv_w, 'ssm_conv_b': ssm_conv_b, 'ssm_dt_bias': ssm_dt_bias, 'ssm_a_log': ssm_a_log, 'ssm_d': ssm_d, 'ssm_norm_g': ssm_norm_g, 'sc_ssm_w_out': sc_ssm_w_out, 'fox_w_in': fox_w_in, 'fox_b_f': fox_b_f, 'fox_w_out': fox_w_out, 'ffn_w_up': ffn_w_up, 'ffn_conv_w': ffn_conv_w, 'ffn_conv_b': ffn_conv_b, 'ffn_w_down': ffn_w_down, 'ln_mix_g': ln_mix_g, 'ln_mix_b': ln_mix_b, 'ln_ffn_g': ln_ffn_g, 'ln_ffn_b': ln_ffn_b}


def reference(x, sc_ssm_w_in, sc_conv_w, ssm_conv_w, ssm_conv_b, ssm_dt_bias, ssm_a_log, ssm_d, ssm_norm_g, sc_ssm_w_out, fox_w_in, fox_b_f, fox_w_out, ffn_w_up, ffn_conv_w, ffn_conv_b, ffn_w_down, ln_mix_g, ln_mix_b, ln_ffn_g, ln_ffn_b):
    for i in range(DEPTH):
        j = i // 2
        if i % 2 == 0:
            mix = sconv_ssd_mixer(x, sc_ssm_w_in[j], sc_conv_w[j], ssm_conv_w[j], ssm_conv_b[j], ssm_dt_bias[j], ssm_a_log[j], ssm_d[j], ssm_norm_g[j], sc_ssm_w_out[j])
        else:
            mix = fox_mixer(x, fox_w_in[j], fox_b_f[j], fox_w_out[j])
        x = layer_norm(ALPHA * x + mix, ln_mix_g[i], ln_mix_b[i])
        x = layer_norm(ALPHA * x + conv_ffn(x, ffn_w_up[i], ffn_conv_w[i], ffn_conv_b[i], ffn_w_down[i]), ln_ffn_g[i], ln_ffn_b[i])
    return x
```

```python
from contextlib import ExitStack
import numpy as np
import concourse.bass as bass
import concourse.mybir as mybir
from concourse.bass_utils import run_bass_kernel_spmd

F32 = mybir.dt.float32
BF16 = mybir.dt.bfloat16
AF = mybir.ActivationFunctionType
ALU = mybir.AluOpType

COMPUTE = ("pe", "act", "dve", "pool")
QUEUES = ("pe", "act", "dve", "pool", "sp")

ALPHA = 4.0 ** 0.25
LN_EPS = 1e-5
T = 2048
D = 1024
NTB = 16
PAD = 4
DFF = 2816
NJ = 22
import os
SCHEDULE = os.environ.get('MK_SCHED', '1') == '1'
PREFETCH = os.environ.get('MK_PREFETCH', '0') == '1'


class Ins:
    __slots__ = ("eng", "fn", "deps", "idx", "dma_key", "dma_val", "signal", "sigval", "clock", "waits", "is_dma", "pinned")


class Prog:
    def __init__(self, nc):
        self.nc = nc
        self.es = ExitStack()
        self.ins = []
        self.q = {e: [] for e in QUEUES}
        self.last_w = {}
        self.readers = {}
        self.dma_cum = {}
        self.dma_sems = {}
        self.sems = {}
        self.fence_deps = []
        self.scratch_touch = {}
        self.pin = ("dve",)

    def sbuf(self, name, shape, dtype):
        return self.es.enter_context(self.nc.sbuf_tensor(name, list(shape), dtype))

    def psum(self, name, shape, dtype=F32):
        return self.es.enter_context(self.nc.psum_tensor(name, list(shape), dtype))

    def begin_group(self):
        self._grp = []

    def end_group(self):
        g, self._grp = self._grp, None
        fns = [x[0] for x in g]
        reads, writes = [], []
        for _, r, w in g:
            for x in r:
                if x not in reads:
                    reads.append(x)
            for x in w:
                if x not in writes:
                    writes.append(x)

        def run(e, fns=fns):
            h = None
            for f in fns:
                h = f(e)
            return h
        return self.add("pe", run, reads=reads, writes=writes)

    def add(self, eng, fn, reads=(), writes=(), dma_key=None):
        if getattr(self, "_grp", None) is not None:
            assert eng == "pe" and dma_key is None
            self._grp.append((fn, list(reads), list(writes)))
            return None
        i = Ins()
        i.eng = eng
        i.fn = fn
        i.is_dma = dma_key is not None
        i.dma_key = dma_key
        i.signal = False
        i.pinned = eng in self.pin
        deps = set()
        scratch = False
        if any(r.startswith("bk") for r in reads):
            writes = list(writes) + [r for r in reads if r.startswith("bk") and r not in writes]
            reads = [r for r in reads if not r.startswith("bk")]
        for r in reads:
            w = self.last_w.get(r)
            if w is not None:
                deps.add(w)
            if r.startswith("S:"):
                scratch = True
        for w_ in writes:
            w = self.last_w.get(w_)
            if w is not None:
                deps.add(w)
            for rd in self.readers.get(w_, ()):
                deps.add(rd)
            if w_.startswith("S:"):
                scratch = True
        if scratch:
            deps.update(self.fence_deps)
        i.deps = deps
        i.idx = len(self.ins)
        self.ins.append(i)
        self.q[eng].append(i)
        for r in reads:
            self.readers.setdefault(r, []).append(i)
        for w_ in writes:
            self.last_w[w_] = i
            self.readers[w_] = []
        if i.is_dma:
            self.dma_cum[dma_key] = self.dma_cum.get(dma_key, 0) + 16
            i.dma_val = self.dma_cum[dma_key]
        if scratch:
            self.scratch_touch[i.idx] = i
        return i

    def fence(self):
        touched = list(self.scratch_touch.values())
        self.scratch_touch = {}
        if not hasattr(self, "_fdummy"):
            self._fdummy = self.sbuf("fence_dummy", [128, 8], F32)
        fd = self._fdummy
        join = self.add("dve", lambda e: e.memset(fd[:, 0:1], 0.0), writes=["fence_dummy"])
        join.deps.update(touched)
        self.fence_deps = [join]
        for k in [k for k in self.last_w if k.startswith("S:")]:
            del self.last_w[k]
        for k in [k for k in self.readers if k.startswith("S:")]:
            del self.readers[k]


    def schedule(self):
        import heapq

        class _Probe:
            def __init__(self):
                self.recs = []

            def __getattr__(self, name):
                def f(*a, **k):
                    self.recs.append((name, a, k))
                    return None
                return f

        def prod(sh):
            n = 1
            for v in sh:
                n *= int(v)
            return n

        cost, lat = {}, {}
        for i in self.ins:
            p = _Probe()
            i.fn(p)
            name, a, k = p.recs[-1]
            out = k.get("out", a[0] if a else None)
            n = prod(out.shape[1:]) if out is not None and hasattr(out, "shape") else 512
            L = 0.0
            if i.is_dma:
                by = n * out.shape[0] * 4 if out is not None else 0
                c = 0.6 if i.eng == "pool" else 0.15
                L = 2.5 + by / 150e3
            elif i.eng == "pe":
                c = 0.0
                for name, a, k in p.recs:
                    if name == "transpose":
                        c += 0.12
                    else:
                        rhs = k.get("rhs", a[2] if len(a) > 2 else None)
                        nn = prod(rhs.shape[1:]) if rhs is not None else 512
                        lhs = k.get("lhsT", a[1] if len(a) > 1 else None)
                        c1 = 0.035 + max(nn, 64) / 2000.0
                        if lhs is not None and lhs.dtype == F32:
                            c1 *= 4
                        c += c1
            elif i.eng == "act":
                c = 0.22 + n / 1400.0
            elif i.eng == "dve":
                c = 0.12 + n / 960.0
            else:
                c = 0.25 + n / 600.0
            cost[i.idx] = c
            lat[i.idx] = L
        succ = {i.idx: [] for i in self.ins}
        indeg = {}
        import os
        chain = {}
        for e in QUEUES:
            prev = None
            for i in self.q[e]:
                if prev is not None and i.pinned:
                    chain[i.idx] = prev
                prev = i
        for i in self.ins:
            ds = [d for d in i.deps if d is not i]
            if i.idx in chain and chain[i.idx] not in ds:
                ds.append(chain[i.idx])
            indeg[i.idx] = len(ds)
            for d in ds:
                succ[d.idx].append(i)
        byidx = {i.idx: i for i in self.ins}
        pending = {e: [] for e in QUEUES}
        avail = {e: [] for e in QUEUES}
        free = {e: 0.0 for e in QUEUES}
        fin = {}
        ready = {}
        for i in self.ins:
            if indeg[i.idx] == 0:
                ready[i.idx] = 0.0
                heapq.heappush(pending[i.eng], (0.0, i.idx))
        order = []
        newq = {e: [] for e in QUEUES}
        SYNC = 0.12
        n_left = len(self.ins)
        while n_left:
            best = None
            for e in QUEUES:
                pe_, av = pending[e], avail[e]
                while pe_ and pe_[0][0] <= free[e]:
                    r, ix = heapq.heappop(pe_)
                    heapq.heappush(av, ix)
                if av:
                    cand = (free[e], av[0], e, True)
                elif pe_:
                    cand = (pe_[0][0], pe_[0][1], e, False)
                else:
                    continue
                if best is None or cand[:2] < best[:2]:
                    best = cand
            st, ix, e, from_av = best
            if from_av:
                heapq.heappop(avail[e])
            else:
                heapq.heappop(pending[e])
            i = byidx[ix]
            f = st + cost[ix]
            free[e] = f
            fin[ix] = f + lat[ix]
            order.append(i)
            newq[e].append(i)
            n_left -= 1
            for sx in succ[ix]:
                indeg[sx.idx] -= 1
                r = max(ready.get(sx.idx, 0.0), fin[ix] + (0.0 if sx.eng == e and not i.is_dma else SYNC))
                ready[sx.idx] = r
                if indeg[sx.idx] == 0:
                    heapq.heappush(pending[sx.eng], (r, sx.idx))
        self.ins = order
        self.q = newq
        for k, i in enumerate(self.ins):
            i.idx = k
        self.est_us = max(fin.values()) if fin else 0.0

    def finalize(self, tail):
        nc = self.nc
        pos = {}
        for e in QUEUES:
            for k, i in enumerate(self.q[e]):
                pos[i.idx] = k
        prev_clock = {e: ({c: -1 for c in COMPUTE}, frozenset()) for e in QUEUES}
        for i in self.ins:
            clk, dseen = prev_clock[i.eng]
            clk = dict(clk)
            dseen = set(dseen)
            waits = []
            for d in sorted(i.deps, key=lambda d: -d.idx):
                if d is i:
                    continue
                if d.is_dma:
                    if d.idx in dseen:
                        continue
                    waits.append(d)
                    dseen.add(d.idx)
                else:
                    if d.eng == i.eng and d.eng == "pe":
                        continue
                    if clk[d.eng] >= pos[d.idx]:
                        continue
                    waits.append(d)
                    clk[d.eng] = max(clk[d.eng], pos[d.idx])
                dc, dd = d.clock
                for c in COMPUTE:
                    if dc[c] > clk[c]:
                        clk[c] = dc[c]
                dseen |= dd
            final = []
            for d in waits:
                if d.is_dma:
                    final.append(d)
                elif clk[d.eng] == pos[d.idx]:
                    final.append(d)
            i.waits = final
            for d in final:
                d.signal = True
            i.clock = (clk, frozenset(dseen))
            prev_clock[i.eng] = i.clock
        for d in tail:
            d.signal = True
        for e in COMPUTE:
            self.sems[e] = self.es.enter_context(nc.semaphore("s_" + e))
            n = 0
            for i in self.q[e]:
                if i.is_dma:
                    continue
                if i.signal:
                    n += 1
                    i.sigval = n
        for k in self.dma_cum:
            self.dma_sems[k] = self.es.enter_context(nc.semaphore("d_" + str(k).replace(":", "_")))

    def emit(self, tail):
        nc = self.nc
        prog = self

        def wait(eng, d):
            if d.is_dma:
                eng.wait_ge(prog.dma_sems[d.dma_key], d.dma_val)
            else:
                eng.wait_ge(prog.sems[d.eng], d.sigval)

        def run(engname, eng):
            for i in prog.q[engname]:
                for d in i.waits:
                    wait(eng, d)
                h = i.fn(eng)
                if i.is_dma:
                    h.then_inc(prog.dma_sems[i.dma_key], 16)
                elif i.signal:
                    h.then_inc(prog.sems[i.eng], 1)
            if engname == "sp":
                for d in tail:
                    wait(eng, d)

        with nc.Block() as block:
            @block.tensor
            def _(e):
                run("pe", e)

            @block.scalar
            def _(e):
                run("act", e)

            @block.vector
            def _(e):
                run("dve", e)

            @block.gpsimd
            def _(e):
                run("pool", e)

            @block.sync
            def _(e):
                run("sp", e)

    def close(self):
        self.es.close()


class Builder:
    def __init__(self, nc, P, dram):
        self.nc, self.P, self.dram = nc, P, dram
        P_ = P
        self.x_tok = P_.sbuf("x_tok_sb", [128, NTB, D], F32)
        self.xTp = P_.sbuf("xTp", [128, 8, PAD + T], BF16)
        self.ident_f = P_.sbuf("ident_f", [128, 128], F32)
        self.ones_f = P_.sbuf("ones_f", [128, 128], F32)
        self.neghalf = P_.sbuf("neghalf", [128, 1], F32)
        self.lnp = None
        self.stats = P_.sbuf("stats", [128, NTB, 16], F32)
        self.cwb = P_.sbuf("cwb", [128, 44, 4], F32)
        self.fix = P_.sbuf("fix", [128, 44, 2], F32)
        self.ring = [P_.sbuf(f"ring{i}", [128, 4096], BF16) for i in range(3)]
        self.ring_cnt = 0
        self.SW = 20480
        self.S = P_.sbuf("S", [128, self.SW], F32)
        self.sp_ = 0
        self.PS = [P_.psum(f"ps{i}", [128, 1024], F32) for i in range(4)]
        self.out_dmas = []
        self.ident_b = P_.sbuf("ident_b", [128, 128], BF16)
        self.maskneg = P_.sbuf("maskneg", [128, 128], BF16)
        self.small = P_.sbuf("small", [128, 64], F32)
        self.rot = {}
        self.consts()

    def rotbank(self, group, banks):
        k = self.rot.get(group, 0)
        self.rot[group] = k + 1
        return banks[k % len(banks)]

    def reset_scratch(self):
        self.P.fence()
        self.sp_ = 0

    def carve(self, words, dtype=F32):
        a = self.sp_
        self.sp_ += words
        assert self.sp_ <= self.SW, (self.sp_, self.SW)
        v = self.S[:, a:a + words]
        if dtype == BF16:
            v = v.bitcast(BF16)
        return v

    def bank(self, b):
        return self.PS[b // 2][:, (b % 2) * 512:(b % 2) * 512 + 512]

    @staticmethod
    def BK(b):
        return f"bk{b}"

    def ring_next(self):
        i = self.ring_cnt % 3
        self.ring_cnt += 1
        return self.ring[i], f"ring{i}"

    def wload(self, src_ap, words=4096):
        tile, res = self.ring_next()
        self.P.add("pool", lambda e, t=tile, s=src_ap, w=words: e.dma_start(out=t[:, 0:w], in_=s, max_dma_last_dim=8192),
                   writes=[res], dma_key=res)
        return tile, res

    def consts(self):
        P = self.P
        ones_f, ident_f = self.ones_f, self.ident_f
        P.add("pool", lambda e: e.memset(ones_f[:], 1.0), writes=["ones_f"])
        P.add("pool", lambda e: e.affine_select(out=ident_f[:], in_=ones_f[:], pattern=[[-1, 128]], compare_op=ALU.is_equal,
                                                 fill=0.0, base=0, channel_multiplier=1), reads=["ones_f"], writes=["ident_f"])
        nh = self.neghalf
        P.add("pool", lambda e: e.memset(nh[:], -0.5), writes=["neghalf"])
        xTp = self.xTp
        P.add("pool", lambda e: e.memset(xTp[:, :, 0:PAD], 0.0), writes=["xTpad"])
        ident_b, maskneg = self.ident_b, self.maskneg
        P.add("pool", lambda e: e.tensor_copy(out=ident_b[:], in_=ident_f[:]), reads=["ident_f"], writes=["ident_b"])
        P.add("pool", lambda e: e.memset(maskneg[:], -30000.0), writes=["maskneg"])
        P.add("pool", lambda e: e.affine_select(out=maskneg[:], in_=maskneg[:], pattern=[[-1, 128]], compare_op=ALU.is_gt,
                                                 fill=0.0, base=0, channel_multiplier=1), reads=["maskneg"], writes=["maskneg"])

    def load_x(self):
        P, x_tok = self.P, self.x_tok
        xv = self.dram["x"].rearrange("(tb p) d -> p tb d", p=128)
        for g in range(4):
            P.add("sp", lambda e, g=g: e.dma_start(out=x_tok[:, 4 * g:4 * g + 4, :], in_=xv[:, 4 * g:4 * g + 4, :]),
                  writes=[f"xt{tb}_{dh}" for tb in range(4 * g, 4 * g + 4) for dh in range(2)], dma_key=f"xin{g}")

    def store_tb(self, tb):
        P, x_tok = self.P, self.x_tok
        ov = self.dram["out"].rearrange("(tb p) d -> p tb d", p=128)
        i = P.add("sp", lambda e: e.dma_start(out=ov[:, tb, :], in_=x_tok[:, tb, :]), reads=[f"xt{tb}_0", f"xt{tb}_1"], dma_key=f"xout{tb}")
        self.out_dmas.append(i)

    def transpose_tb(self, tb, pst):
        P, x_tok, xTp, ident_f = self.P, self.x_tok, self.xTp, self.ident_f
        ps = self.PS[pst]
        P.begin_group()
        for kc in range(8):
            P.add("pe", lambda e, kc=kc: e.transpose(ps[:, kc * 128:(kc + 1) * 128], x_tok[:, tb, kc * 128:(kc + 1) * 128], ident_f[:]),
                  reads=[f"xt{tb}_{kc // 4}", "ident_f"], writes=[self.BK(2 * pst + kc // 4)])
        P.end_group()
        c0 = PAD + tb * 128
        P.add("act", lambda e: e.activation(out=xTp[:, 0:4, c0:c0 + 128], in_=ps[:, 0:512].rearrange("p (k t) -> p k t", k=4), func=AF.Identity),
              reads=[self.BK(2 * pst)], writes=[f"xT{tb}"])
        P.add("dve", lambda e: e.tensor_copy(out=xTp[:, 4:8, c0:c0 + 128], in_=ps[:, 512:1024].rearrange("p (k t) -> p k t", k=4)),
              reads=[self.BK(2 * pst + 1)], writes=[f"xT{tb}"])

    def load_ln(self, idx):
        self.lnp = self.carve(2 * D).rearrange("p (a d) -> p a d", a=2)
        lnp = self.lnp
        src = self.dram["lnp"][idx].partition_broadcast(128)
        self.P.add("sp", lambda e: e.dma_start(out=lnp[:], in_=src), writes=["S:lnp"], dma_key="lnp")

    def ln_tb(self, tb):
        P, x_tok, st, lnp, nh = self.P, self.x_tok, self.stats, self.lnp, self.neghalf
        R = [f"xt{tb}_0", f"xt{tb}_1"]
        sr = f"st{tb}"
        P.add("dve", lambda e: e.bn_stats(out=st[:, tb, 0:6], in_=x_tok[:, tb, 0:512]), reads=[R[0]], writes=[sr])
        P.add("dve", lambda e: e.bn_stats(out=st[:, tb, 6:12], in_=x_tok[:, tb, 512:1024]), reads=[R[1]], writes=[sr])
        P.add("dve", lambda e: e.bn_aggr(out=st[:, tb, 12:14], in_=st[:, tb, 0:12]), reads=[sr], writes=[sr])
        P.add("pool", lambda e: e.tensor_scalar(out=st[:, tb, 14:15], in0=st[:, tb, 13:14], scalar1=LN_EPS, scalar2=None, op0=ALU.add),
              reads=[sr], writes=[sr])
        P.add("pool", lambda e: e.tensor_tensor(out=st[:, tb, 14:15], in0=st[:, tb, 14:15], in1=nh[:], op=ALU.pow),
              reads=[sr, "neghalf"], writes=[sr])
        P.add("dve", lambda e: e.tensor_scalar(out=st[:, tb, 15:16], in0=st[:, tb, 12:13], scalar1=st[:, tb, 14:15], scalar2=-1.0,
                                               op0=ALU.mult, op1=ALU.mult), reads=[sr], writes=[sr])
        P.add("act", lambda e: e.activation(out=x_tok[:, tb, :], in_=x_tok[:, tb, :], func=AF.Identity,
                                            scale=st[:, tb, 14:15], bias=st[:, tb, 15:16]), reads=[sr] + R, writes=R)
        P.add("pool", lambda e: e.tensor_tensor(out=x_tok[:, tb, :], in0=x_tok[:, tb, :], in1=lnp[:, 0, :], op=ALU.mult),
              reads=R + ["S:lnp"], writes=R)
        P.add("dve", lambda e: e.tensor_tensor(out=x_tok[:, tb, :], in0=x_tok[:, tb, :], in1=lnp[:, 1, :], op=ALU.add),
              reads=R + ["S:lnp"], writes=R)

    def ffn(self, l, final=False):
        P, xTp, x_tok, cwb, fix = self.P, self.xTp, self.x_tok, self.cwb, self.fix
        self.reset_scratch()
        hid = self.carve(NJ * 1024 // 2, BF16).rearrange("p (j t) -> p j t", j=NJ)
        tmp = [[self.carve(1024), self.carve(1024)] for _ in range(2)]
        cwsrc = self.dram["ffn_cwb"][l]
        P.add("sp", lambda e: e.dma_start(out=cwb[:], in_=cwsrc), writes=["cwb"], dma_key="cwb")
        self.load_ln(2 * l + 1)
        wup, wdn = self.dram["wup"], self.dram["wdn"]
        for h in range(2):
            c0 = PAD + 1024 * h
            xres = [f"xT{tb}" for tb in range(8 * h, 8 * h + 8)] + ["xTpad"]
            pre = {}
            PD = int(os.environ.get('MK_PD', '1'))
            if PREFETCH:
                for s0 in range(PD):
                    pre[s0] = self.wload(wup[l, s0])
            for s in range(11):
                if PREFETCH:
                    tile, res = pre[s]
                    if s + PD < 11:
                        pre[s + PD] = self.wload(wup[l, s + PD])
                else:
                    tile, res = self.wload(wup[l, s])
                sv = tile[:].rearrange("p (k j c) -> p k j c", k=8, j=2)
                for jj in range(2):
                    j = 2 * s + jj
                    pset = j % 2
                    for ug in range(2):
                        pst = 2 * pset + ug
                        ps = self.PS[pst]
                        for t in range(2):
                            P.begin_group()
                            for kc in range(8):
                                P.add("pe", lambda e, ps=ps, t=t, kc=kc, jj=jj, ug=ug, sv=sv, c0=c0: e.matmul(
                                    ps[:, t * 512:(t + 1) * 512], lhsT=sv[:, kc, jj, ug * 128:(ug + 1) * 128],
                                    rhs=xTp[:, kc, c0 + t * 512:c0 + (t + 1) * 512], start=(kc == 0), stop=(kc == 7)),
                                    reads=[res] + xres, writes=[self.BK(2 * pst + t)])
                            P.end_group()
                    for ug in range(2):
                        pst = 2 * pset + ug
                        ps = self.PS[pst]
                        a = tmp[pset][ug]
                        ar = f"S:a{pset}{ug}"
                        ch = ug * NJ + j
                        bks = [self.BK(2 * pst), self.BK(2 * pst + 1)]
                        P.add("act", lambda e, ps=ps, a=a, ch=ch: e.activation(out=a[:, 0:1024], in_=ps[:, 0:1024], func=AF.Identity,
                                                                                scale=cwb[:, ch, 2:3], bias=cwb[:, ch, 3:4]),
                              reads=bks + ["cwb"], writes=[ar])
                        P.add("dve", lambda e, ps=ps, a=a, ch=ch: e.scalar_tensor_tensor(
                            out=a[:, 1:1024], in0=ps[:, 0:1023], scalar=cwb[:, ch, 1:2], in1=a[:, 1:1024], op0=ALU.mult, op1=ALU.add),
                            reads=bks + ["cwb", ar], writes=[ar])
                        P.add("dve", lambda e, ps=ps, a=a, ch=ch: e.scalar_tensor_tensor(
                            out=a[:, 2:1024], in0=ps[:, 0:1022], scalar=cwb[:, ch, 0:1], in1=a[:, 2:1024], op0=ALU.mult, op1=ALU.add),
                            reads=bks + ["cwb", ar], writes=[ar])
                        if h == 0:
                            P.add("dve", lambda e, ps=ps, ch=ch: e.tensor_scalar(out=fix[:, ch, 0:2], in0=ps[:, 1022:1024], scalar1=cwb[:, ch, 0:1],
                                                                                 scalar2=None, op0=ALU.mult), reads=bks + ["cwb"], writes=[f"fix{ch}"])
                            P.add("dve", lambda e, ps=ps, ch=ch: e.scalar_tensor_tensor(
                                out=fix[:, ch, 0:1], in0=ps[:, 1023:1024], scalar=cwb[:, ch, 1:2], in1=fix[:, ch, 0:1], op0=ALU.mult, op1=ALU.add),
                                reads=bks + ["cwb", f"fix{ch}"], writes=[f"fix{ch}"])
                        else:
                            P.add("dve", lambda e, a=a, ch=ch: e.tensor_tensor(out=a[:, 0:2], in0=a[:, 0:2], in1=fix[:, ch, 0:2], op=ALU.add),
                                  reads=[ar, f"fix{ch}"], writes=[ar])
                    au, ag = tmp[pset]
                    P.add("act", lambda e, ag=ag: e.activation(out=ag[:, 0:1024], in_=ag[:, 0:1024], func=AF.Silu),
                          reads=[f"S:a{pset}1"], writes=[f"S:a{pset}1"])
                    P.add("pool", lambda e, au=au, ag=ag, j=j: e.tensor_tensor(out=hid[:, j, :], in0=au[:, 0:1024], in1=ag[:, 0:1024], op=ALU.mult),
                          reads=[f"S:a{pset}0", f"S:a{pset}1"], writes=[f"S:hid{j}"])
            for dh in range(2):
                for g in range(3):
                    tile, res = self.wload(wdn[l, dh, g])
                    sv = tile[:].rearrange("p (j c) -> p j c", j=8)
                    for jj in range(8 if g < 2 else 6):
                        j = 8 * g + jj
                        for tb in range(8):
                            P.add("pe", lambda e, tb=tb, j=j, jj=jj, sv=sv: e.matmul(
                                self.bank(tb), lhsT=hid[:, j, tb * 128:(tb + 1) * 128], rhs=sv[:, jj, :], start=(j == 0), stop=(j == NJ - 1)),
                                reads=[res, f"S:hid{j}"], writes=[self.BK(tb)])
                for tb in range(8):
                    gtb = 8 * h + tb
                    P.add("dve", lambda e, tb=tb, gtb=gtb, dh=dh: e.scalar_tensor_tensor(
                        out=x_tok[:, gtb, dh * 512:(dh + 1) * 512], in0=x_tok[:, gtb, dh * 512:(dh + 1) * 512], scalar=ALPHA,
                        in1=self.bank(tb), op0=ALU.mult, op1=ALU.add), reads=[self.BK(tb), f"xt{gtb}_{dh}"], writes=[f"xt{gtb}_{dh}"])
            for tb in range(8):
                gtb = 8 * h + tb
                self.ln_tb(gtb)
                if final:
                    self.store_tb(gtb)
                else:
                    self.transpose_tb(gtb, tb // 2)


    def mix1(self):
        P, xTp, x_tok, dram = self.P, self.xTp, self.x_tok, self.dram
        ones_f, ident_b, maskneg, small = self.ones_f, self.ident_b, self.maskneg, self.small
        xall = [f"xT{tb}" for tb in range(NTB)]
        self.reset_scratch()
        fl = self.carve(2048)
        cum = self.carve(2048)
        ones = self.carve(512)
        QG = self.carve(6 * 2048 // 2, BF16).rearrange("p (r t) -> p r t", r=6)
        KG = self.carve(6 * 2048 // 2, BF16).rearrange("p (r t) -> p r t", r=6)
        bsrc = dram["fox_bf"]
        P.add("sp", lambda e: e.dma_start(out=small[0:16, 0:1], in_=bsrc), writes=["small"], dma_key="small")
        P.add("dve", lambda e: e.tensor_scalar(out=small[0:16, 1:2], in0=small[0:16, 0:1], scalar1=-1.0, scalar2=None, op0=ALU.mult),
              reads=["small"], writes=["small"])
        P.add("pool", lambda e: e.memset(ones[0:16, :], 1.0), writes=["S:ones"])
        P.add("pool", lambda e: e.memset(QG[0:16, 3:6, :], 1.0), writes=["S:QG1"])
        P.add("pool", lambda e: e.memset(KG[0:16, 0:3, :], 1.0), writes=["S:KG1"])
        tile, res = self.wload(dram["fox_f"], words=128)
        fv = tile[:, 0:128].rearrange("p (k c) -> p k c", k=8)
        for t in range(4):
            P.begin_group()
            for kc in range(8):
                P.add("pe", lambda e, t=t, kc=kc: e.matmul(self.bank(t)[0:16, :], lhsT=fv[:, kc, :], rhs=xTp[:, kc, PAD + t * 512:PAD + (t + 1) * 512],
                                                           start=(kc == 0), stop=(kc == 7)), reads=[res] + xall, writes=[self.BK(t)])
            P.end_group()
            P.add("act", lambda e, t=t: e.activation(out=fl[0:16, t * 512:(t + 1) * 512], in_=self.bank(t)[0:16, :], func=AF.Exp,
                                                     scale=-1.0, bias=small[0:16, 1:2]), reads=[self.BK(t), "small"], writes=[f"S:fl{t}"])
        for t in range(4):
            P.add("act", lambda e, t=t: e.activation(out=fl[0:16, t * 512:(t + 1) * 512], in_=fl[0:16, t * 512:(t + 1) * 512], func=AF.Ln,
                                                     scale=1.0, bias=1.0), reads=[f"S:fl{t}"], writes=[f"S:fl{t}"])
        for t in range(4):
            init = 0.0 if t == 0 else cum[0:16, t * 512 - 1:t * 512]
            P.add("dve", lambda e, t=t, init=init: e.tensor_tensor_scan(out=cum[0:16, t * 512:(t + 1) * 512], data0=ones[0:16, :],
                                                                        data1=fl[0:16, t * 512:(t + 1) * 512], initial=init,
                                                                        op0=ALU.mult, op1=ALU.subtract),
                  reads=[f"S:fl{t}", "S:ones", "S:cum"], writes=["S:cum"])
        for r in range(3):
            P.add("dve", lambda e, r=r: e.tensor_copy(out=QG[0:16, r, :], in_=cum[0:16, :]), reads=["S:cum"], writes=[f"S:QG0{r}"])
            if r < 2:
                P.add("dve", lambda e, r=r: e.tensor_tensor(out=cum[0:16, :], in0=cum[0:16, :], in1=QG[0:16, r, :], op=ALU.subtract),
                      reads=["S:cum", f"S:QG0{r}"], writes=["S:cum"])
        P.add("dve", lambda e: e.tensor_scalar(out=KG[0:16, 3:6, :], in0=QG[0:16, 0:3, :], scalar1=-1.0, scalar2=None, op0=ALU.mult),
              reads=["S:QG00", "S:QG01", "S:QG02"], writes=["S:KG0"])
        gq, gk = dram["augq"], dram["augk"]
        P.add("sp", lambda e: e.dma_start(out=gq, in_=QG[0:16, :, :]), reads=["S:QG00", "S:QG01", "S:QG02", "S:QG1"], writes=["augq"], dma_key="augq")
        P.add("sp", lambda e: e.dma_start(out=gk, in_=KG[0:16, :, :]), reads=["S:KG0", "S:KG1"], writes=["augk"], dma_key="augk")
        self.reset_scratch()
        AUG = [[[self.carve(1024, BF16) for qk in range(2)] for sub in range(2)] for st in range(2)]
        VP = [self.carve(1040, BF16).rearrange("p (t s d) -> p t s d", t=16, s=2) for st in range(2)]
        OTP = [self.carve(1024, BF16) for st in range(2)]
        PT = [self.carve(256, BF16) for _ in range(4)]
        PTD = [self.carve(256, BF16) for _ in range(4)]
        rc = self.carve(512)
        bcs = self.carve(512)
        self.load_ln(2)
        for st in range(2):
            P.add("pool", lambda e, st=st: e.memset(VP[st][:, :, :, 64:65], 1.0), writes=[f"S:VPone{st}"])
        for i4 in range(1, 4):
            P.add("pool", lambda e, i4=i4: e.memset(PTD[i4][:, 0:128 * i4], 0.0), writes=[f"S:PTD{i4}"])
        ptc = 0
        for hp in range(8):
            st = hp % 2
            qk_t, qk_r = self.wload(dram["fox_qk"][hp], words=2048)
            v_t, v_r = self.wload(dram["fox_v"][hp], words=1024)
            wo_t, wo_r = self.wload(dram["fox_wo"][hp], words=1024)
            qkv = qk_t[:, 0:2048].rearrange("p (k c) -> p k c", k=8)
            vv = v_t[:, 0:1024].rearrange("p (k c) -> p k c", k=8)
            for sub in range(2):
                h = 2 * hp + sub
                P.add("sp", lambda e, st=st, sub=sub, h=h: e.dma_start(out=AUG[st][sub][0][64:70, :], in_=gq[h]),
                      reads=["augq"], writes=[f"S:AQa{st}{sub}"], dma_key=f"aq{st}{sub}")
                P.add("sp", lambda e, st=st, sub=sub, h=h: e.dma_start(out=AUG[st][sub][1][64:70, :], in_=gk[h]),
                      reads=["augk"], writes=[f"S:AKa{st}{sub}"], dma_key=f"ak{st}{sub}")
            for t in range(4):
                for qk in range(2):
                    b = self.rotbank("misc", (0, 1, 7))
                    P.begin_group()
                    for kc in range(8):
                        P.add("pe", lambda e, b=b, kc=kc, qk=qk, t=t, qkv=qkv: e.matmul(
                            self.bank(b), lhsT=qkv[:, kc, qk * 128:(qk + 1) * 128], rhs=xTp[:, kc, PAD + t * 512:PAD + (t + 1) * 512],
                            start=(kc == 0), stop=(kc == 7)), reads=[qk_r] + xall, writes=[self.BK(b)])
                    P.end_group()
                    for sub in range(2):
                        dst = AUG[st][sub][qk]
                        nm = f"S:A{'QK'[qk]}{st}{sub}t{t}"
                        if qk == 0:
                            P.add("dve", lambda e, b=b, sub=sub, dst=dst, t=t: e.tensor_scalar(
                                out=dst[0:64, t * 512:(t + 1) * 512], in0=self.bank(b)[sub * 64:(sub + 1) * 64, :], scalar1=0.125, scalar2=None,
                                op0=ALU.mult), reads=[self.BK(b)], writes=[nm])
                        else:
                            P.add("dve", lambda e, b=b, sub=sub, dst=dst, t=t: e.tensor_copy(
                                out=dst[0:64, t * 512:(t + 1) * 512], in_=self.bank(b)[sub * 64:(sub + 1) * 64, :]),
                                reads=[self.BK(b)], writes=[nm])
            for g4 in range(4):
                b = self.rotbank("misc", (0, 1, 7))
                P.begin_group()
                for ti in range(4):
                    tb = 4 * g4 + ti
                    for kc in range(8):
                        P.add("pe", lambda e, b=b, ti=ti, tb=tb, kc=kc, vv=vv: e.matmul(
                            self.bank(b)[:, ti * 128:(ti + 1) * 128], lhsT=xTp[:, kc, PAD + tb * 128:PAD + (tb + 1) * 128], rhs=vv[:, kc, :],
                            start=(kc == 0), stop=(kc == 7)), reads=[v_r, f"xT{tb}"], writes=[self.BK(b)])
                P.end_group()
                P.add("act", lambda e, b=b, g4=g4, st=st: e.activation(
                    out=VP[st][:, 4 * g4:4 * g4 + 4, :, 0:64], in_=self.bank(b).rearrange("p (t s d) -> p t s d", t=4, s=2), func=AF.Identity),
                    reads=[self.BK(b)], writes=[f"S:VP{st}g{g4}"])
            for sub in range(2):
                QA, KA = AUG[st][sub][0], AUG[st][sub][1]
                for qt in range(4):
                    ob = self.rotbank("O", (2, 3))
                    nkb = 4 * qt + 4
                    for kb in range(nkb):
                        i = kb - 4 * qt
                        co = 128 * i if i > 0 else 0
                        sb_ = self.rotbank("S", (4, 5, 6))
                        if i >= 0:
                            pt, ptr = PTD[i], f"S:PTD{i}"
                        else:
                            pt, ptr = PT[ptc % 4], f"S:PT{ptc % 4}"
                            ptc += 1
                        kres = [f"S:AK{st}{sub}t{kb // 4}", f"S:AKa{st}{sub}", f"S:AQ{st}{sub}t{qt}", f"S:AQa{st}{sub}"]
                        P.begin_group()
                        if i < 0:
                            P.add("pe", lambda e, sb_=sb_, kb=kb, qt=qt, QA=QA, KA=KA: e.matmul(
                                self.bank(sb_)[:, 0:512], lhsT=KA[0:70, kb * 128:(kb + 1) * 128], rhs=QA[0:70, qt * 512:(qt + 1) * 512],
                                start=True, stop=True), reads=kres, writes=[self.BK(sb_)])
                        else:
                            P.add("pe", lambda e, sb_=sb_, co=co, kb=kb, qt=qt, QA=QA, KA=KA: e.matmul(
                                self.bank(sb_)[:, co:co + 128], lhsT=KA[0:70, kb * 128:(kb + 1) * 128], rhs=QA[0:70, qt * 512 + co:qt * 512 + co + 128],
                                start=True, stop=False), reads=kres, writes=[self.BK(sb_)])
                            P.add("pe", lambda e, sb_=sb_, co=co: e.matmul(self.bank(sb_)[:, co:co + 128], lhsT=ident_b[:], rhs=maskneg[:],
                                                                           start=False, stop=True),
                                  reads=["ident_b", "maskneg"], writes=[self.BK(sb_)])
                            if co + 128 < 512:
                                P.add("pe", lambda e, sb_=sb_, co=co, kb=kb, qt=qt, QA=QA, KA=KA: e.matmul(
                                    self.bank(sb_)[:, co + 128:512], lhsT=KA[0:70, kb * 128:(kb + 1) * 128], rhs=QA[0:70, qt * 512 + co + 128:(qt + 1) * 512],
                                    start=True, stop=True), reads=kres, writes=[self.BK(sb_)])
                        P.end_group()
                        P.add("act", lambda e, sb_=sb_, co=co, pt=pt: e.activation(out=pt[:, co:512], in_=self.bank(sb_)[:, co:512], func=AF.Exp),
                              reads=[self.BK(sb_)], writes=[ptr])
                        P.add("pe", lambda e, ob=ob, kb=kb, pt=pt, st=st, sub=sub, nkb=nkb: e.matmul(
                            self.bank(ob)[0:65, 0:512], lhsT=VP[st][:, kb, sub, 0:65], rhs=pt[:, 0:512], start=(kb == 0), stop=(kb == nkb - 1)),
                            reads=[ptr, f"S:VP{st}g{kb // 4}", f"S:VPone{st}"], writes=[self.BK(ob)])
                    P.add("dve", lambda e, ob=ob: e.reciprocal(out=rc[64:65, :], in_=self.bank(ob)[64:65, :]), reads=[self.BK(ob)], writes=["S:rc"])
                    bb = self.rotbank("misc", (0, 1, 7))
                    P.add("pe", lambda e, bb=bb: e.matmul(self.bank(bb)[0:64, :], lhsT=ones_f[64:65, 0:64], rhs=rc[64:65, :], start=True, stop=True),
                          reads=["ones_f", "S:rc"], writes=[self.BK(bb)])
                    P.add("dve", lambda e, bb=bb: e.tensor_copy(out=bcs[0:64, :], in_=self.bank(bb)[0:64, :]), reads=[self.BK(bb)], writes=["S:bcs"])
                    P.add("dve", lambda e, ob=ob, st=st, sub=sub, qt=qt: e.tensor_tensor(
                        out=OTP[st][sub * 64:(sub + 1) * 64, qt * 512:(qt + 1) * 512], in0=self.bank(ob)[0:64, :], in1=bcs[0:64, :], op=ALU.mult),
                        reads=[self.BK(ob), "S:bcs"], writes=[f"S:OT{st}q{qt}"])
            for tb in range(NTB):
                for dh in range(2):
                    b = self.rotbank("misc", (0, 1, 7))
                    P.add("pe", lambda e, b=b, tb=tb, dh=dh, st=st, wo_t=wo_t: e.matmul(
                        self.bank(b), lhsT=OTP[st][:, tb * 128:(tb + 1) * 128], rhs=wo_t[:, dh * 512:(dh + 1) * 512], start=True, stop=True),
                        reads=[wo_r, f"S:OT{st}q{tb // 4}"], writes=[self.BK(b)])
                    xr = f"xt{tb}_{dh}"
                    if hp == 0:
                        P.add("dve", lambda e, b=b, tb=tb, dh=dh: e.scalar_tensor_tensor(
                            out=x_tok[:, tb, dh * 512:(dh + 1) * 512], in0=x_tok[:, tb, dh * 512:(dh + 1) * 512], scalar=ALPHA,
                            in1=self.bank(b), op0=ALU.mult, op1=ALU.add), reads=[self.BK(b), xr], writes=[xr])
                    else:
                        P.add("dve", lambda e, b=b, tb=tb, dh=dh: e.tensor_tensor(
                            out=x_tok[:, tb, dh * 512:(dh + 1) * 512], in0=self.bank(b), in1=x_tok[:, tb, dh * 512:(dh + 1) * 512], op=ALU.add),
                            reads=[self.BK(b), xr], writes=[xr])
        for tb in range(NTB):
            self.ln_tb(tb)
            self.transpose_tb(tb, tb % 4)


    def mix0(self):
        P, xTp, x_tok, dram = self.P, self.xTp, self.x_tok, self.dram
        ones_f, ident_f, ident_b, maskneg = self.ones_f, self.ident_f, self.ident_b, self.maskneg
        self.reset_scratch()
        H = self.carve(1024)
        prevbf = self.carve(512, BF16)
        R = self.carve(2048)
        L = [self.carve(128), self.carve(128)]
        halo = self.carve(16).rearrange("p (i k) -> p i k", i=8)
        fixx = self.carve(36).rearrange("p (i k) -> p i k", i=12)
        cst = self.carve(128)
        biasbc, Dbc = cst[:, 0:16], cst[:, 16:32]
        gT = cst[:, 32:40]
        scw = cst[:, 40:64].rearrange("p (i k) -> p i k", i=8)
        xcw = cst[:, 64:124].rearrange("p (i k) -> p i k", i=12)
        s_tok, s_feat, s_head = dram["m0_tokc"].rearrange("a h -> (a h)").partition_broadcast(128), dram["m0_featc"], dram["m0_headc"]
        P.add("sp", lambda e: e.dma_start(out=cst[:, 0:32], in_=s_tok), writes=["S:cst"], dma_key="m0c0")
        P.add("sp", lambda e: e.dma_start(out=cst[:, 32:124], in_=s_feat), writes=["S:cst"], dma_key="m0c1")
        P.add("sp", lambda e: e.dma_start(out=cst[0:16, 124:126], in_=s_head), writes=["S:cst"], dma_key="m0c2")
        P.add("act", lambda e: e.activation(out=cst[0:16, 126:127], in_=cst[0:16, 125:126], func=AF.Exp), reads=["S:cst"], writes=["S:cst"])
        P.add("dve", lambda e: e.tensor_scalar(out=cst[0:16, 127:128], in0=cst[0:16, 126:127], scalar1=-1.0, scalar2=None, op0=ALU.mult),
              reads=["S:cst"], writes=["S:cst"])
        dtb, acol = cst[0:16, 124:125], cst[0:16, 127:128]
        P.add("pool", lambda e: e.memset(H[:, :], 0.0), writes=["S:H"])
        P.add("pool", lambda e: e.memset(prevbf[:, :], 0.0), writes=["S:prev"])
        P.add("pool", lambda e: e.memset(R[0:48, :], 0.0), writes=["S:R"])
        P.add("pool", lambda e: e.memset(R[32:48, :], 1.0), reads=["S:R"], writes=["S:R"])
        P.add("pool", lambda e: e.affine_select(out=R[32:48, :].rearrange("p (h t) -> p h t", h=16), in_=R[32:48, :].rearrange("p (h t) -> p h t", h=16),
                                                 pattern=[[-1, 16], [0, 128]], compare_op=ALU.is_equal, fill=0.0, base=0, channel_multiplier=1),
              reads=["S:R"], writes=["S:R"])
        for k in range(2):
            P.add("pool", lambda e, k=k: e.memset(L[k][0:48, :], 0.0), writes=[f"S:L{k}"])
            P.add("pool", lambda e, k=k: e.memset(L[k][0:16, :], 1.0), reads=[f"S:L{k}"], writes=[f"S:L{k}"])
        base = self.sp_
        wx, wz, wsc, wo = dram["m0_wx"], dram["m0_wz"], dram["m0_wsc"], dram["m0_wo"]

        for qi in range(4):
            self.P.fence()
            self.sp_ = base
            yaT = self.carve(2048, BF16).rearrange("p (i t) -> p i t", i=8)
            ybT = self.carve(2048, BF16).rearrange("p (i t) -> p i t", i=8)
            xdt = self.carve(2048, BF16).rearrange("p (b f) -> p b f", b=4)
            zs = self.carve(2048, BF16).rearrange("p (b f) -> p b f", b=4)
            BcT = self.carve(512, BF16).rearrange("p (g t) -> p g t", g=2)
            CcT = self.carve(512, BF16).rearrange("p (g t) -> p g t", g=2)
            Btok = self.carve(512, BF16).rearrange("p (b f) -> p b f", b=4)
            dtok = self.carve(64).rearrange("p (b h) -> p b h", b=4)
            Ddt = self.carve(64).rearrange("p (b h) -> p b h", b=4)
            acsT = self.carve(512)
            dtT = self.carve(512)
            qbase = self.sp_
            a_ = [self.carve(512), self.carve(512)]
            prod = [self.carve(516), self.carve(516)]
            cs = [self.carve(512), self.carve(512)]
            c0 = PAD + 512 * qi
            xres = [f"xT{tb}" for tb in range(4 * qi, 4 * qi + 4)]
            win = lambda kc, c0=c0: xTp[:, kc, c0:c0 + 512]

            dt_t, dt_r = self.wload(dram["m0_wdt"], words=128)
            dv = dt_t[:, 0:128].rearrange("p (k c) -> p k c", k=8)
            P.begin_group()
            for kc in range(8):
                P.add("pe", lambda e, kc=kc, dv=dv, win=win: e.matmul(self.bank(0)[0:16, :], lhsT=dv[:, kc, :], rhs=win(kc), start=(kc == 0), stop=(kc == 7)),
                      reads=[dt_r] + xres, writes=[self.BK(0)])
            P.end_group()
            P.begin_group()
            for tbl in range(4):
                for kc in range(8):
                    P.add("pe", lambda e, kc=kc, tbl=tbl, dv=dv, c0=c0: e.matmul(self.bank(1)[:, tbl * 16:(tbl + 1) * 16],
                                                                               lhsT=xTp[:, kc, c0 + tbl * 128:c0 + (tbl + 1) * 128], rhs=dv[:, kc, :],
                                                                               start=(kc == 0), stop=(kc == 7)),
                          reads=[dt_r] + xres, writes=[self.BK(1)])
            P.end_group()
            P.add("act", lambda e, dtT=dtT: e.activation(out=dtT[0:16, :], in_=self.bank(0)[0:16, :], func=AF.Exp, bias=dtb, scale=1.0),
                  reads=[self.BK(0), "S:cst"], writes=["S:dtT"])
            P.add("dve", lambda e, dtok=dtok: e.tensor_tensor(out=dtok[:, :, :], in0=self.bank(1)[:, 0:64].rearrange("p (b h) -> p b h", b=4),
                                                             in1=biasbc.unsqueeze(1).to_broadcast([128, 4, 16]), op=ALU.add),
                  reads=[self.BK(1), "S:cst"], writes=["S:dtok"])
            P.add("act", lambda e, dtok=dtok: e.activation(out=dtok[:, :, :], in_=dtok[:, :, :], func=AF.Exp), reads=["S:dtok"], writes=["S:dtok"])
            P.add("act", lambda e, dtT=dtT: e.activation(out=dtT[0:16, :], in_=dtT[0:16, :], func=AF.Ln, bias=1.0, scale=1.0), reads=["S:dtT"], writes=["S:dtT"])
            P.add("act", lambda e, dtok=dtok: e.activation(out=dtok[:, :, :], in_=dtok[:, :, :], func=AF.Ln, bias=1.0, scale=1.0),
                  reads=["S:dtok"], writes=["S:dtok"])
            P.add("dve", lambda e, dtok=dtok, Ddt=Ddt: e.reciprocal(out=Ddt[:, :, :], in_=dtok[:, :, :]), reads=["S:dtok"], writes=["S:Ddt"])
            P.add("dve", lambda e, Ddt=Ddt: e.tensor_tensor(out=Ddt[:, :, :], in0=Ddt[:, :, :], in1=Dbc.unsqueeze(1).to_broadcast([128, 4, 16]), op=ALU.mult),
                  reads=["S:Ddt", "S:cst"], writes=["S:Ddt"])
            P.add("dve", lambda e, dtT=dtT: e.tensor_scalar(out=dtT[0:16, :], in0=dtT[0:16, :], scalar1=acol, scalar2=None, op0=ALU.mult),
                  reads=["S:dtT", "S:cst"], writes=["S:dtT"])
            for c in range(4):
                P.add("dve", lambda e, c=c, dtT=dtT, acsT=acsT: e.tensor_tensor_scan(out=acsT[0:16, c * 128:(c + 1) * 128], data0=ones_f[0:16, 0:128],
                                                                                   data1=dtT[0:16, c * 128:(c + 1) * 128], initial=0.0,
                                                                                   op0=ALU.mult, op1=ALU.add),
                      reads=["S:dtT", "ones_f"], writes=[f"S:acs{c}"])

            ak = 0
            for sl in range(3):
                x_t, x_r = self.wload(wx[sl])
                xv = x_t[:].rearrange("p (k c) -> p k c", k=8)
                for cc in range(4):
                    if sl == 0:
                        ci = 8 + cc
                    else:
                        ci = 4 * (sl - 1) + cc
                    b = self.rotbank("m0", (0, 1, 2, 3, 4, 5))
                    P.begin_group()
                    for kc in range(8):
                        P.add("pe", lambda e, b=b, kc=kc, cc=cc, xv=xv, win=win: e.matmul(self.bank(b), lhsT=xv[:, kc, cc * 128:(cc + 1) * 128], rhs=win(kc),
                                                                                         start=(kc == 0), stop=(kc == 7)),
                              reads=[x_r] + xres, writes=[self.BK(b)])
                    P.end_group()
                    a = a_[ak % 2]
                    ar = f"S:a{ak % 2}"
                    ak += 1
                    bk = [self.BK(b)]
                    P.add("act", lambda e, b=b, a=a, ci=ci: e.activation(out=a[:, 0:512], in_=self.bank(b), func=AF.Identity,
                                                                        scale=xcw[:, ci, 3:4], bias=xcw[:, ci, 4:5]), reads=bk + ["S:cst"], writes=[ar])
                    for sh in range(1, 4):
                        P.add("dve", lambda e, b=b, a=a, ci=ci, sh=sh: e.scalar_tensor_tensor(
                            out=a[:, sh:512], in0=self.bank(b)[:, 0:512 - sh], scalar=xcw[:, ci, 3 - sh:4 - sh], in1=a[:, sh:512], op0=ALU.mult, op1=ALU.add),
                            reads=bk + ["S:cst", ar], writes=[ar])
                    if qi > 0:
                        P.add("dve", lambda e, a=a, ci=ci: e.tensor_tensor(out=a[:, 0:3], in0=a[:, 0:3], in1=fixx[:, ci, 0:3], op=ALU.add),
                              reads=[ar, f"S:fx{ci}"], writes=[ar])
                    if qi < 3:
                        P.add("dve", lambda e, b=b, ci=ci: e.tensor_scalar(out=fixx[:, ci, 0:3], in0=self.bank(b)[:, 509:512], scalar1=xcw[:, ci, 0:1],
                                                                          scalar2=None, op0=ALU.mult), reads=bk + ["S:cst"], writes=[f"S:fx{ci}"])
                        P.add("dve", lambda e, b=b, ci=ci: e.scalar_tensor_tensor(out=fixx[:, ci, 0:2], in0=self.bank(b)[:, 510:512], scalar=xcw[:, ci, 1:2],
                                                                                 in1=fixx[:, ci, 0:2], op0=ALU.mult, op1=ALU.add),
                              reads=bk + ["S:cst", f"S:fx{ci}"], writes=[f"S:fx{ci}"])
                        P.add("dve", lambda e, b=b, ci=ci: e.scalar_tensor_tensor(out=fixx[:, ci, 0:1], in0=self.bank(b)[:, 511:512], scalar=xcw[:, ci, 2:3],
                                                                                 in1=fixx[:, ci, 0:1], op0=ALU.mult, op1=ALU.add),
                              reads=bk + ["S:cst", f"S:fx{ci}"], writes=[f"S:fx{ci}"])
                    if ci >= 10:
                        g = ci - 10
                        P.add("act", lambda e, a=a, g=g, CcT=CcT: e.activation(out=CcT[:, g, :], in_=a[:, 0:512], func=AF.Silu), reads=[ar], writes=[f"S:Cc{g}"])
                        continue
                    P.add("act", lambda e, a=a: e.activation(out=a[:, 0:512], in_=a[:, 0:512], func=AF.Silu), reads=[ar], writes=[ar])
                    tbk = self.rotbank("m0t", (6, 7))
                    P.begin_group()
                    for tbl in range(4):
                        P.add("pe", lambda e, tbk=tbk, tbl=tbl, a=a: e.transpose(self.bank(tbk)[:, tbl * 128:(tbl + 1) * 128], a[:, tbl * 128:(tbl + 1) * 128], ident_f[:]),
                              reads=[ar, "ident_f"], writes=[self.BK(tbk)])
                    P.end_group()
                    if ci >= 8:
                        g = ci - 8
                        P.add("pool", lambda e, a=a, g=g, BcT=BcT: e.tensor_copy(out=BcT[:, g, :], in_=a[:, 0:512]), reads=[ar], writes=[f"S:Bc{g}"])
                        P.add("act", lambda e, tbk=tbk, g=g, Btok=Btok: e.activation(out=Btok[:, :, g * 128:(g + 1) * 128],
                                                                                    in_=self.bank(tbk).rearrange("p (b f) -> p b f", b=4), func=AF.Identity),
                              reads=[self.BK(tbk)], writes=[f"S:Bt{g}"])
                    else:
                        P.add("dve", lambda e, tbk=tbk, ci=ci, xdt=xdt, dtok=dtok: e.tensor_tensor(
                            out=xdt[:, :, ci * 128:(ci + 1) * 128].rearrange("p b (h d) -> p b h d", h=2),
                            in0=self.bank(tbk).rearrange("p (b h d) -> p b h d", b=4, h=2),
                            in1=dtok[:, :, 2 * ci:2 * ci + 2].unsqueeze(3).to_broadcast([128, 4, 2, 64]), op=ALU.mult),
                            reads=[self.BK(tbk), "S:dtok"], writes=[f"S:xdt{ci}"])

            for zsl in range(2):
                z_t, z_r = self.wload(wz[zsl])
                zv = z_t[:].rearrange("p (k c) -> p k c", k=8)
                for tbl in range(4):
                    b = self.rotbank("m0", (0, 1, 2, 3, 4, 5))
                    P.begin_group()
                    for kc in range(8):
                        P.add("pe", lambda e, b=b, kc=kc, tbl=tbl, zv=zv, c0=c0: e.matmul(self.bank(b), lhsT=xTp[:, kc, c0 + tbl * 128:c0 + (tbl + 1) * 128],
                                                                                         rhs=zv[:, kc, :], start=(kc == 0), stop=(kc == 7)),
                              reads=[z_r] + xres, writes=[self.BK(b)])
                    P.end_group()
                    P.add("act", lambda e, b=b, tbl=tbl, zsl=zsl, zs=zs: e.activation(out=zs[:, tbl, zsl * 512:(zsl + 1) * 512], in_=self.bank(b), func=AF.Silu),
                          reads=[self.BK(b)], writes=[f"S:zs{tbl}"])

            for i in range(8):
                s_t, s_r = self.wload(wsc[i], words=3072)
                sv = s_t[:, 0:3072].rearrange("p (k c) -> p k c", k=8)
                bks = []
                for part in range(3):
                    b = self.rotbank("m0", (0, 1, 2, 3, 4, 5))
                    bks.append(b)
                    P.begin_group()
                    for kc in range(8):
                        P.add("pe", lambda e, b=b, kc=kc, part=part, sv=sv, win=win: e.matmul(self.bank(b), lhsT=sv[:, kc, part * 128:(part + 1) * 128], rhs=win(kc),
                                                                                             start=(kc == 0), stop=(kc == 7)),
                              reads=[s_r] + xres, writes=[self.BK(b)])
                    P.end_group()
                bc_, bh_, bb_ = bks
                k2 = i % 2
                pr, csb, a = prod[k2], cs[k2], a_[k2]
                prr, csr, ar = f"S:pr{k2}", f"S:cs{k2}", f"S:a{k2}"
                P.add("act", lambda e, bc_=bc_, csb=csb: e.activation(out=csb[:, 0:512], in_=self.bank(bc_), func=AF.Identity), reads=[self.BK(bc_)], writes=[csr])
                if qi == 0:
                    P.add("pool", lambda e, pr=pr: e.memset(pr[:, 0:2], 0.0), writes=[prr + "h"])
                else:
                    P.add("pool", lambda e, pr=pr, i=i: e.tensor_copy(out=pr[:, 0:2], in_=halo[:, i, :]), reads=[f"S:halo{i}"], writes=[prr + "h"])
                P.add("dve", lambda e, bh_=bh_, pr=pr, csb=csb: e.tensor_tensor(out=pr[:, 2:514], in0=self.bank(bh_), in1=csb[:, 0:512], op=ALU.mult),
                      reads=[self.BK(bh_), csr], writes=[prr])
                if qi < 3:
                    P.add("pool", lambda e, pr=pr, i=i: e.tensor_copy(out=halo[:, i, :], in_=pr[:, 512:514]), reads=[prr], writes=[f"S:halo{i}"])
                P.add("act", lambda e, pr=pr, a=a, i=i: e.activation(out=a[:, 0:512], in_=pr[:, 2:514], func=AF.Identity, scale=scw[:, i, 2:3]),
                      reads=[prr, "S:cst"], writes=[ar])
                P.add("dve", lambda e, pr=pr, a=a, i=i: e.scalar_tensor_tensor(out=a[:, 0:512], in0=pr[:, 1:513], scalar=scw[:, i, 1:2], in1=a[:, 0:512],
                                                                              op0=ALU.mult, op1=ALU.add), reads=[prr, prr + "h", "S:cst", ar], writes=[ar])
                P.add("dve", lambda e, pr=pr, a=a, i=i: e.scalar_tensor_tensor(out=a[:, 0:512], in0=pr[:, 0:512], scalar=scw[:, i, 0:1], in1=a[:, 0:512],
                                                                              op0=ALU.mult, op1=ALU.add), reads=[prr, prr + "h", "S:cst", ar], writes=[ar])
                P.add("dve", lambda e, bb_=bb_, a=a, i=i, yaT=yaT: e.tensor_tensor(out=yaT[:, i, :], in0=self.bank(bb_), in1=a[:, 0:512], op=ALU.mult),
                      reads=[self.BK(bb_), ar], writes=[f"S:ya{i}"])

            self.P.fence()
            self.sp_ = qbase
            segT = self.carve(1024, BF16).rearrange("p (h t) -> p h t", h=16)
            MT = self.carve(1024, BF16).rearrange("p (h t) -> p h t", h=16)
            xdtd = self.carve(512, BF16)
            yt_ = [self.carve(1024), self.carve(1024)]
            junk = self.carve(256, BF16)
            sm_ = [self.carve(64), self.carve(64)]
            X16 = self.carve(16)
            for c in range(4):
                gc = 4 * qi + c
                cols = slice(c * 128, (c + 1) * 128)
                Lm, Lr = L[gc % 2], f"S:L{gc % 2}"
                yt, ytr = yt_[gc % 2], f"S:yt{gc % 2}"
                sm, smr = sm_[gc % 2], f"S:sm{gc % 2}"
                acr = f"S:acs{c}"
                P.add("dve", lambda e, Lm=Lm, cols=cols, acsT=acsT: e.tensor_scalar(out=Lm[32:48, :], in0=acsT[0:16, cols], scalar1=-1.0, scalar2=None, op0=ALU.mult),
                      reads=[acr], writes=[Lr])
                P.add("dve", lambda e, cols=cols, acsT=acsT: e.tensor_tensor(
                    out=R[0:16, :].rearrange("p (h t) -> p h t", h=16), in0=acsT[0:16, cols].unsqueeze(1).to_broadcast([16, 16, 128]),
                    in1=ident_f[0:16, 0:16].unsqueeze(2).to_broadcast([16, 16, 128]), op=ALU.mult), reads=[acr, "ident_f"], writes=["S:R"])
                for hg in range(4):
                    b = hg % 2
                    P.begin_group()
                    for hh in range(4):
                        P.add("pe", lambda e, b=b, hg=hg, hh=hh, Lm=Lm: e.matmul(self.bank(b)[:, hh * 128:(hh + 1) * 128], lhsT=Lm[0:48, :],
                                                                                rhs=R[0:48, hg * 512 + hh * 128:hg * 512 + (hh + 1) * 128], start=True, stop=False),
                              reads=[Lr, "S:R"], writes=[self.BK(b)])
                        P.add("pe", lambda e, b=b, hh=hh: e.matmul(self.bank(b)[:, hh * 128:(hh + 1) * 128], lhsT=ident_b[:], rhs=maskneg[:], start=False, stop=True),
                              reads=["ident_b", "maskneg"], writes=[self.BK(b)])
                    P.end_group()
                    P.add("act", lambda e, b=b, hg=hg, segT=segT: e.activation(out=segT[:, 4 * hg:4 * hg + 4, :], in_=self.bank(b).rearrange("p (h t) -> p h t", h=4),
                                                                              func=AF.Exp), reads=[self.BK(b)], writes=[f"S:seg{hg}"])
                segr = [f"S:seg{hg}" for hg in range(4)]
                P.begin_group()
                for g in range(2):
                    P.add("pe", lambda e, g=g, cols=cols, BcT=BcT, CcT=CcT: e.matmul(self.bank(2)[:, g * 128:(g + 1) * 128], lhsT=BcT[:, g, cols], rhs=CcT[:, g, cols],
                                                                                    start=True, stop=True), reads=[f"S:Bc{g}", f"S:Cc{g}"], writes=["bk2"])
                P.end_group()
                for g in range(2):
                    P.add("dve", lambda e, g=g, MT=MT, segT=segT: e.tensor_tensor(
                        out=MT[:, 8 * g:8 * g + 8, :], in0=self.bank(2)[:, g * 128:(g + 1) * 128].unsqueeze(1).to_broadcast([128, 8, 128]),
                        in1=segT[:, 8 * g:8 * g + 8, :], op=ALU.mult), reads=["bk2"] + segr, writes=[f"S:MT{g}"])
                P.add("dve", lambda e, c=c, xdt=xdt, xdtd=xdtd, segT=segT: e.tensor_tensor(
                    out=xdtd[:, :].rearrange("p (h d) -> p h d", h=16), in0=xdt[:, c, :].rearrange("p (h d) -> p h d", h=16),
                    in1=segT[:, :, 127:128].to_broadcast([128, 16, 64]), op=ALU.mult),
                    reads=[f"S:xdt{i}" for i in range(8)] + segr, writes=["S:xdtd"])
                P.add("dve", lambda e, cols=cols, acsT=acsT: e.tensor_scalar(out=X16[0:16, 0:16], in0=ident_f[0:16, 0:16],
                                                                            scalar1=acsT[0:16, cols][:, 127:128], scalar2=None, op0=ALU.mult),
                      reads=[acr, "ident_f"], writes=["S:X16"])
                P.begin_group()
                P.add("pe", lambda e: e.matmul(self.bank(3)[:, 0:16], lhsT=ones_f[0:16, :], rhs=X16[0:16, 0:16], start=True, stop=True),
                      reads=["ones_f", "S:X16"], writes=["bk3"])
                P.add("pe", lambda e, cols=cols, acsT=acsT: e.transpose(self.bank(3)[:, 16:32], acsT[0:16, cols], ident_f[0:16, 0:16]),
                      reads=[acr, "ident_f"], writes=["bk3"])
                P.end_group()
                P.add("act", lambda e, sm=sm: e.activation(out=sm[:, 0:32], in_=self.bank(3)[:, 0:32], func=AF.Exp), reads=["bk3"], writes=[smr])
                P.begin_group()
                for g in range(2):
                    P.add("pe", lambda e, g=g, cols=cols, CcT=CcT: e.matmul(self.PS[2][:, g * 512:(g + 1) * 512], lhsT=CcT[:, g, cols], rhs=prevbf[:, g * 512:(g + 1) * 512],
                                                                           start=True, stop=True), reads=[f"S:Cc{g}", "S:prev"], writes=[self.BK(4 + g)])
                P.end_group()
                P.begin_group()
                for g in range(2):
                    P.add("pe", lambda e, g=g, c=c, Btok=Btok, xdtd=xdtd: e.matmul(self.PS[0][:, g * 512:(g + 1) * 512], lhsT=Btok[:, c, g * 128:(g + 1) * 128],
                                                                                  rhs=xdtd[:, g * 512:(g + 1) * 512], start=True, stop=True),
                          reads=[f"S:Bt{g}", "S:xdtd"], writes=[self.BK(g)])
                P.end_group()
                P.add("dve", lambda e, sm=sm: e.tensor_tensor(out=H[:, :].rearrange("p (h d) -> p h d", h=16), in0=H[:, :].rearrange("p (h d) -> p h d", h=16),
                                                              in1=sm[:, 0:16].unsqueeze(2).to_broadcast([128, 16, 64]), op=ALU.mult),
                      reads=["S:H", smr], writes=["S:H"])
                P.add("dve", lambda e: e.tensor_tensor(out=H[:, :], in0=self.PS[0][:, :], in1=H[:, :], op=ALU.add), reads=["S:H", self.BK(0), self.BK(1)], writes=["S:H"])
                P.add("act", lambda e: e.activation(out=prevbf[:, :], in_=H[:, :], func=AF.Identity), reads=["S:H"], writes=["S:prev"])
                P.begin_group()
                for h in range(16):
                    P.add("pe", lambda e, h=h, c=c, MT=MT, xdt=xdt: e.matmul(self.PS[3][:, h * 64:(h + 1) * 64], lhsT=MT[:, h, :], rhs=xdt[:, c, h * 64:(h + 1) * 64],
                                                                            start=True, stop=True),
                          reads=[f"S:MT{h // 8}", f"S:xdt{h // 2}"], writes=[self.BK(6 + h // 8)])
                P.end_group()
                P.add("dve", lambda e, yt=yt, sm=sm: e.tensor_tensor(out=yt[:, :].rearrange("p (h d) -> p h d", h=16), in0=self.PS[2][:, :].rearrange("p (h d) -> p h d", h=16),
                                                                     in1=sm[:, 16:32].unsqueeze(2).to_broadcast([128, 16, 64]), op=ALU.mult),
                      reads=[self.BK(4), self.BK(5), smr], writes=[ytr])
                P.add("dve", lambda e, yt=yt: e.tensor_tensor(out=yt[:, :], in0=self.PS[3][:, :], in1=yt[:, :], op=ALU.add), reads=[self.BK(6), self.BK(7), ytr], writes=[ytr])
                sk = yt_[(gc + 1) % 2]
                P.add("pool", lambda e, c=c, xdtd=xdtd, xdt=xdt, Ddt=Ddt, sm=sm: e.tensor_tensor(
                    out=MT[:, 0:8, :].rearrange("p h t -> p (h t)").rearrange("p (h d) -> p h d", h=16), in0=xdt[:, c, :].rearrange("p (h d) -> p h d", h=16),
                    in1=Ddt[:, c, :].unsqueeze(2).to_broadcast([128, 16, 64]), op=ALU.mult),
                    reads=[f"S:xdt{i}" for i in range(8)] + ["S:Ddt", "S:MT0"], writes=["S:MT0"])
                P.add("pool", lambda e, yt=yt: e.tensor_tensor(out=yt[:, :], in0=yt[:, :], in1=MT[:, 0:8, :].rearrange("p h t -> p (h t)"), op=ALU.add),
                      reads=[ytr, "S:MT0"], writes=[ytr])
                P.add("pool", lambda e, yt=yt, c=c, zs=zs: e.tensor_tensor(out=yt[:, :], in0=yt[:, :], in1=zs[:, c, :], op=ALU.mult), reads=[ytr, f"S:zs{c}"], writes=[ytr])
                for g in range(2):
                    P.add("act", lambda e, g=g, yt=yt, sm=sm: e.activation(out=junk[:, 0:512], in_=yt[:, g * 512:(g + 1) * 512], func=AF.Square,
                                                                          accum_out=sm[:, 32 + g:33 + g]), reads=[ytr], writes=[smr + "s", "S:junk"])
                P.add("pool", lambda e, sm=sm: e.tensor_scalar(out=sm[:, 34:36], in0=sm[:, 32:34], scalar1=1.0 / 512.0, scalar2=LN_EPS, op0=ALU.mult, op1=ALU.add),
                      reads=[smr + "s"], writes=[smr + "r"])
                P.add("pool", lambda e, sm=sm: e.tensor_tensor(out=sm[:, 34:36], in0=sm[:, 34:36], in1=self.neghalf[:, 0:1].to_broadcast([128, 2]), op=ALU.pow),
                      reads=[smr + "r", "neghalf"], writes=[smr + "r"])
                for g in range(2):
                    P.add("act", lambda e, g=g, yt=yt, sm=sm: e.activation(out=yt[:, g * 512:(g + 1) * 512], in_=yt[:, g * 512:(g + 1) * 512], func=AF.Identity,
                                                                          scale=sm[:, 34 + g:35 + g]), reads=[ytr, smr + "r"], writes=[ytr])
                P.begin_group()
                for i in range(8):
                    P.add("pe", lambda e, i=i, yt=yt: e.transpose(self.PS[2][:, i * 128:(i + 1) * 128], yt[:, i * 128:(i + 1) * 128], ident_f[:]),
                          reads=[ytr, "ident_f"], writes=[self.BK(4 + i // 4)])
                P.end_group()
                for half in range(2):
                    P.add("dve", lambda e, half=half, cols=cols, ybT=ybT: e.tensor_tensor(
                        out=ybT[:, 4 * half:4 * half + 4, cols], in0=self.PS[2][:, half * 512:(half + 1) * 512].rearrange("p (i t) -> p i t", i=4),
                        in1=gT[:, 4 * half:4 * half + 4].unsqueeze(2).to_broadcast([128, 4, 128]), op=ALU.mult),
                        reads=[self.BK(4 + half), "S:cst"], writes=[f"S:yb{c}"])

            for dh in range(2):
                for part in range(2):
                    o_t, o_r = self.wload(wo[dh, part])
                    ov = o_t[:].rearrange("p (k c) -> p k c", k=8)
                    src = yaT if part == 0 else ybT
                    for tbl in range(4):
                        P.begin_group()
                        for kc in range(8):
                            rd = [o_r, (f"S:ya{kc}" if part == 0 else f"S:yb{tbl}")]
                            P.add("pe", lambda e, dh=dh, part=part, tbl=tbl, kc=kc, ov=ov, src=src: e.matmul(
                                self.bank(4 * dh + tbl), lhsT=src[:, kc, tbl * 128:(tbl + 1) * 128], rhs=ov[:, kc, :],
                                start=(part == 0 and kc == 0), stop=(part == 1 and kc == 7)), reads=rd, writes=[self.BK(4 * dh + tbl)])
                        P.end_group()
                for tbl in range(4):
                    gtb = 4 * qi + tbl
                    P.add("dve", lambda e, dh=dh, tbl=tbl, gtb=gtb: e.scalar_tensor_tensor(
                        out=x_tok[:, gtb, dh * 512:(dh + 1) * 512], in0=x_tok[:, gtb, dh * 512:(dh + 1) * 512], scalar=ALPHA,
                        in1=self.bank(4 * dh + tbl), op0=ALU.mult, op1=ALU.add), reads=[self.BK(4 * dh + tbl), f"xt{gtb}_{dh}"], writes=[f"xt{gtb}_{dh}"])
        self.P.fence()
        self.sp_ = base
        self.load_ln(0)
        for tb in range(NTB):
            self.ln_tb(tb)
            self.transpose_tb(tb, tb % 4)


def declare_dram(nc, phases):
    d = {}
    d["x"] = nc.dram_tensor("x", [T, D], F32, kind="ExternalInput").ap()
    d["out"] = nc.dram_tensor("out", [T, D], F32, kind="ExternalOutput").ap()
    d["lnp"] = nc.dram_tensor("lnp", [4, 2, D], F32, kind="ExternalInput").ap()
    d["ffn_cwb"] = nc.dram_tensor("ffn_cwb", [2, 128, 44, 4], F32, kind="ExternalInput").ap()
    d["wup"] = nc.dram_tensor("wup", [2, 11, 128, 4096], F32, kind="ExternalInput").ap()
    d["wdn"] = nc.dram_tensor("wdn", [2, 2, 3, 128, 4096], F32, kind="ExternalInput").ap()
    d["m0_wdt"] = nc.dram_tensor("m0_wdt", [128, 128], F32, kind="ExternalInput").ap()
    d["m0_wx"] = nc.dram_tensor("m0_wx", [3, 128, 4096], F32, kind="ExternalInput").ap()
    d["m0_wz"] = nc.dram_tensor("m0_wz", [2, 128, 4096], F32, kind="ExternalInput").ap()
    d["m0_wsc"] = nc.dram_tensor("m0_wsc", [8, 128, 3072], F32, kind="ExternalInput").ap()
    d["m0_wo"] = nc.dram_tensor("m0_wo", [2, 2, 128, 4096], F32, kind="ExternalInput").ap()
    d["m0_tokc"] = nc.dram_tensor("m0_tokc", [2, 16], F32, kind="ExternalInput").ap()
    d["m0_featc"] = nc.dram_tensor("m0_featc", [128, 92], F32, kind="ExternalInput").ap()
    d["m0_headc"] = nc.dram_tensor("m0_headc", [16, 2], F32, kind="ExternalInput").ap()
    d["fox_f"] = nc.dram_tensor("fox_f", [128, 128], F32, kind="ExternalInput").ap()
    d["fox_bf"] = nc.dram_tensor("fox_bf", [16, 1], F32, kind="ExternalInput").ap()
    d["fox_qk"] = nc.dram_tensor("fox_qk", [8, 128, 2048], F32, kind="ExternalInput").ap()
    d["fox_v"] = nc.dram_tensor("fox_v", [8, 128, 1024], F32, kind="ExternalInput").ap()
    d["fox_wo"] = nc.dram_tensor("fox_wo", [8, 128, 1024], F32, kind="ExternalInput").ap()
    d["augq"] = nc.dram_tensor("augq", [16, 6, T], BF16, kind="Internal").ap()
    d["augk"] = nc.dram_tensor("augk", [16, 6, T], BF16, kind="Internal").ap()
    return d


def build_program(phases=("mix0", "ffn0", "mix1", "ffn1")):
    nc = bass.Bass("TRN2", target_bir_lowering=False)
    dram = declare_dram(nc, phases)
    P = Prog(nc)
    B = Builder(nc, P, dram)
    B.load_x()
    for tb in range(NTB):
        B.transpose_tb(tb, tb % 4)
    last = phases[-1]
    for ph in phases:
        if ph == "ffn0":
            B.ffn(0, final=(ph == last))
        elif ph == "ffn1":
            B.ffn(1, final=(ph == last))
        elif ph == "mix0":
            P.pin = tuple(os.environ.get("MK_PIN0", "dve,pe").split(","))
            B.mix0()
            P.pin = ("dve",)
            if ph == last:
                for tb in range(NTB):
                    B.store_tb(tb)
        elif ph == "mix1":
            B.mix1()
            if ph == last:
                for tb in range(NTB):
                    B.store_tb(tb)
        else:
            raise NotImplementedError(ph)
    if SCHEDULE:
        P.schedule()
    P.finalize(B.out_dmas)
    P.emit(B.out_dmas)
    P.close()
    return nc


def host_layouts(inp):
    f = np.float32
    o = {}
    o["lnp"] = np.ascontiguousarray(np.stack([
        np.stack([inp["ln_mix_g"][0], inp["ln_mix_b"][0]]), np.stack([inp["ln_ffn_g"][0], inp["ln_ffn_b"][0]]),
        np.stack([inp["ln_mix_g"][1], inp["ln_mix_b"][1]]), np.stack([inp["ln_ffn_g"][1], inp["ln_ffn_b"][1]])]).astype(f))
    cw = inp["ffn_conv_w"].astype(f)
    cb = inp["ffn_conv_b"].astype(f)
    cwb = np.concatenate([cw.transpose(0, 2, 1), cb[:, :, None]], axis=2)
    o["ffn_cwb"] = np.ascontiguousarray(cwb.reshape(2, 44, 128, 4).transpose(0, 2, 1, 3))
    wu = inp["ffn_w_up"].astype(f)
    u = wu[:, :, :DFF].reshape(2, 8, 128, 11, 2, 128)
    g = wu[:, :, DFF:].reshape(2, 8, 128, 11, 2, 128)
    ug = np.stack([u, g], axis=5)
    o["wup"] = np.ascontiguousarray(ug.transpose(0, 3, 2, 1, 4, 5, 6).reshape(2, 11, 128, 4096))
    wd = inp["ffn_w_down"].astype(f)
    wdp = np.zeros((2, 24 * 128, 1024), f)
    wdp[:, :DFF] = wd
    wdp = wdp.reshape(2, 3, 8, 128, 2, 512)
    o["wdn"] = np.ascontiguousarray(wdp.transpose(0, 4, 1, 3, 2, 5).reshape(2, 2, 3, 128, 4096))
    w0 = inp["sc_ssm_w_in"][0].astype(f).reshape(8, 128, 5648)
    lay = lambda cols: np.ascontiguousarray(w0[:, :, cols].transpose(1, 0, 2).reshape(128, -1))
    o["m0_wdt"] = lay(slice(5632, 5648))
    o["m0_wx"] = np.stack([lay(slice(5120, 5632)), lay(slice(4096, 4608)), lay(slice(4608, 5120))])
    o["m0_wz"] = np.stack([lay(slice(3072, 3584)), lay(slice(3584, 4096))])
    o["m0_wsc"] = np.stack([lay(np.r_[1024 + 128 * i:1152 + 128 * i, 2048 + 128 * i:2176 + 128 * i, 128 * i:128 + 128 * i]) for i in range(8)])
    wo0 = inp["sc_ssm_w_out"][0].astype(f).reshape(2, 8, 128, 2, 512)
    o["m0_wo"] = np.ascontiguousarray(wo0.transpose(3, 0, 2, 1, 4).reshape(2, 2, 128, 4096))
    o["m0_tokc"] = np.ascontiguousarray(np.stack([inp["ssm_dt_bias"][0], inp["ssm_d"][0]]).astype(f))
    o["m0_headc"] = np.ascontiguousarray(np.stack([inp["ssm_dt_bias"][0], inp["ssm_a_log"][0]], axis=1).astype(f))
    gTh = inp["ssm_norm_g"][0].astype(f).reshape(8, 128).T
    scwh = inp["sc_conv_w"][0].astype(f).reshape(3, 8, 128).transpose(2, 1, 0)
    xw = inp["ssm_conv_w"][0].astype(f).reshape(4, 12, 128).transpose(2, 1, 0)
    xb = inp["ssm_conv_b"][0].astype(f).reshape(12, 128).T[:, :, None]
    o["m0_featc"] = np.ascontiguousarray(np.concatenate([gTh, scwh.reshape(128, 24), np.concatenate([xw, xb], axis=2).reshape(128, 60)], axis=1))
    wi = inp["fox_w_in"][0].astype(f)
    wk = wi.reshape(8, 128, 3088)
    o["fox_f"] = np.ascontiguousarray(wk[:, :, 3072:3088].transpose(1, 0, 2).reshape(128, 128))
    q = wk[:, :, 0:1024].reshape(8, 128, 8, 128)
    k = wk[:, :, 1024:2048].reshape(8, 128, 8, 128)
    v = wk[:, :, 2048:3072].reshape(8, 128, 8, 128)
    qk = np.concatenate([q, k], axis=3)
    o["fox_qk"] = np.ascontiguousarray(qk.transpose(2, 1, 0, 3).reshape(8, 128, 2048))
    o["fox_v"] = np.ascontiguousarray(v.transpose(2, 1, 0, 3).reshape(8, 128, 1024))
    o["fox_wo"] = np.ascontiguousarray(inp["fox_w_out"][0].astype(f).reshape(8, 128, 1024))
    o["fox_bf"] = np.ascontiguousarray(inp["fox_b_f"][0].astype(f).reshape(16, 1))
    return o


_NC_CACHE = {}


def kernel(**inputs):
    phases = ("mix0", "ffn0", "mix1", "ffn1")
    if phases not in _NC_CACHE:
        _NC_CACHE[phases] = build_program(phases)
    nc = _NC_CACHE[phases]
    lay = host_layouts(inputs)
    x = np.asarray(inputs["x"], dtype=np.float32)
    in_maps = [dict(lay, x=np.ascontiguousarray(x[b])) for b in range(8)]
    res = run_bass_kernel_spmd(nc, in_maps, core_ids=list(range(8)))
    return np.stack([np.asarray(r["out"], dtype=np.float32) for r in res.results], axis=0)
```

```python
from contextlib import ExitStack
import numpy as np
import concourse.bass as bass
import concourse.mybir as mybir
from concourse.bass_utils import run_bass_kernel_spmd

F32 = mybir.dt.float32
BF16 = mybir.dt.bfloat16
AF = mybir.ActivationFunctionType
ALU = mybir.AluOpType

COMPUTE = ("pe", "act", "dve", "pool")
QUEUES = ("pe", "act", "dve", "pool", "sp")

ALPHA = 4.0 ** 0.25
LN_EPS = 1e-5
T = 2048
D = 1024
NTB = 16
PAD = 4
DFF = 2816
NJ = 22
import os
SCHEDULE = os.environ.get('MK_SCHED', '1') == '1'
PREFETCH = os.environ.get('MK_PREFETCH', '0') == '1'


class Ins:
    __slots__ = ("eng", "fn", "deps", "idx", "dma_key", "dma_val", "signal", "sigval", "clock", "waits", "is_dma", "pinned")


class Prog:
    def __init__(self, nc):
        self.nc = nc
        self.es = ExitStack()
        self.ins = []
        self.q = {e: [] for e in QUEUES}
        self.last_w = {}
        self.readers = {}
        self.dma_cum = {}
        self.dma_sems = {}
        self.sems = {}
        self.fence_deps = []
        self.scratch_touch = {}
        self.pin = ("dve",)

    def sbuf(self, name, shape, dtype):
        return self.es.enter_context(self.nc.sbuf_tensor(name, list(shape), dtype))

    def psum(self, name, shape, dtype=F32):
        return self.es.enter_context(self.nc.psum_tensor(name, list(shape), dtype))

    def begin_group(self):
        self._grp = []

    def end_group(self):
        g, self._grp = self._grp, None
        fns = [x[0] for x in g]
        reads, writes = [], []
        for _, r, w in g:
            for x in r:
                if x not in reads:
                    reads.append(x)
            for x in w:
                if x not in writes:
                    writes.append(x)

        def run(e, fns=fns):
            h = None
            for f in fns:
                h = f(e)
            return h
        return self.add("pe", run, reads=reads, writes=writes)

    def add(self, eng, fn, reads=(), writes=(), dma_key=None):
        if getattr(self, "_grp", None) is not None:
            assert eng == "pe" and dma_key is None
            self._grp.append((fn, list(reads), list(writes)))
            return None
        i = Ins()
        i.eng = eng
        i.fn = fn
        i.is_dma = dma_key is not None
        i.dma_key = dma_key
        i.signal = False
        i.pinned = eng in self.pin
        deps = set()
        scratch = False
        if any(r.startswith("bk") for r in reads):
            writes = list(writes) + [r for r in reads if r.startswith("bk") and r not in writes]
            reads = [r for r in reads if not r.startswith("bk")]
        for r in reads:
            w = self.last_w.get(r)
            if w is not None:
                deps.add(w)
            if r.startswith("S:"):
                scratch = True
        for w_ in writes:
            w = self.last_w.get(w_)
            if w is not None:
                deps.add(w)
            for rd in self.readers.get(w_, ()):
                deps.add(rd)
            if w_.startswith("S:"):
                scratch = True
        if scratch:
            deps.update(self.fence_deps)
        i.deps = deps
        i.idx = len(self.ins)
        self.ins.append(i)
        self.q[eng].append(i)
        for r in reads:
            self.readers.setdefault(r, []).append(i)
        for w_ in writes:
            self.last_w[w_] = i
            self.readers[w_] = []
        if i.is_dma:
            self.dma_cum[dma_key] = self.dma_cum.get(dma_key, 0) + 16
            i.dma_val = self.dma_cum[dma_key]
        if scratch:
            self.scratch_touch[i.idx] = i
        return i

    def fence(self):
        touched = list(self.scratch_touch.values())
        self.scratch_touch = {}
        if not hasattr(self, "_fdummy"):
            self._fdummy = self.sbuf("fence_dummy", [128, 8], F32)
        fd = self._fdummy
        join = self.add("dve", lambda e: e.memset(fd[:, 0:1], 0.0), writes=["fence_dummy"])
        join.deps.update(touched)
        self.fence_deps = [join]
        for k in [k for k in self.last_w if k.startswith("S:")]:
            del self.last_w[k]
        for k in [k for k in self.readers if k.startswith("S:")]:
            del self.readers[k]


    def schedule(self):
        import heapq

        class _Probe:
            def __init__(self):
                self.recs = []

            def __getattr__(self, name):
                def f(*a, **k):
                    self.recs.append((name, a, k))
                    return None
                return f

        def prod(sh):
            n = 1
            for v in sh:
                n *= int(v)
            return n

        cost, lat = {}, {}
        for i in self.ins:
            p = _Probe()
            i.fn(p)
            name, a, k = p.recs[-1]
            out = k.get("out", a[0] if a else None)
            n = prod(out.shape[1:]) if out is not None and hasattr(out, "shape") else 512
            L = 0.0
            if i.is_dma:
                by = n * out.shape[0] * 4 if out is not None else 0
                c = 0.6 if i.eng == "pool" else 0.15
                L = 2.5 + by / 150e3
            elif i.eng == "pe":
                c = 0.0
                for name, a, k in p.recs:
                    if name == "transpose":
                        c += 0.12
                    else:
                        rhs = k.get("rhs", a[2] if len(a) > 2 else None)
                        nn = prod(rhs.shape[1:]) if rhs is not None else 512
                        lhs = k.get("lhsT", a[1] if len(a) > 1 else None)
                        c1 = 0.035 + max(nn, 64) / 2000.0
                        if lhs is not None and lhs.dtype == F32:
                            c1 *= 4
                        c += c1
            elif i.eng == "act":
                c = 0.22 + n / 1400.0
            elif i.eng == "dve":
                c = 0.12 + n / 960.0
            else:
                c = 0.25 + n / 600.0
            cost[i.idx] = c
            lat[i.idx] = L
        succ = {i.idx: [] for i in self.ins}
        indeg = {}
        import os
        chain = {}
        for e in QUEUES:
            prev = None
            for i in self.q[e]:
                if prev is not None and i.pinned:
                    chain[i.idx] = prev
                prev = i
        for i in self.ins:
            ds = [d for d in i.deps if d is not i]
            if i.idx in chain and chain[i.idx] not in ds:
                ds.append(chain[i.idx])
            indeg[i.idx] = len(ds)
            for d in ds:
                succ[d.idx].append(i)
        byidx = {i.idx: i for i in self.ins}
        pending = {e: [] for e in QUEUES}
        avail = {e: [] for e in QUEUES}
        free = {e: 0.0 for e in QUEUES}
        fin = {}
        ready = {}
        for i in self.ins:
            if indeg[i.idx] == 0:
                ready[i.idx] = 0.0
                heapq.heappush(pending[i.eng], (0.0, i.idx))
        order = []
        newq = {e: [] for e in QUEUES}
        SYNC = 0.12
        n_left = len(self.ins)
        while n_left:
            best = None
            for e in QUEUES:
                pe_, av = pending[e], avail[e]
                while pe_ and pe_[0][0] <= free[e]:
                    r, ix = heapq.heappop(pe_)
                    heapq.heappush(av, ix)
                if av:
                    cand = (free[e], av[0], e, True)
                elif pe_:
                    cand = (pe_[0][0], pe_[0][1], e, False)
                else:
                    continue
                if best is None or cand[:2] < best[:2]:
                    best = cand
            st, ix, e, from_av = best
            if from_av:
                heapq.heappop(avail[e])
            else:
                heapq.heappop(pending[e])
            i = byidx[ix]
            f = st + cost[ix]
            free[e] = f
            fin[ix] = f + lat[ix]
            order.append(i)
            newq[e].append(i)
            n_left -= 1
            for sx in succ[ix]:
                indeg[sx.idx] -= 1
                r = max(ready.get(sx.idx, 0.0), fin[ix] + (0.0 if sx.eng == e and not i.is_dma else SYNC))
                ready[sx.idx] = r
                if indeg[sx.idx] == 0:
                    heapq.heappush(pending[sx.eng], (r, sx.idx))
        self.ins = order
        self.q = newq
        for k, i in enumerate(self.ins):
            i.idx = k
        self.est_us = max(fin.values()) if fin else 0.0

    def finalize(self, tail):
        nc = self.nc
        pos = {}
        for e in QUEUES:
            for k, i in enumerate(self.q[e]):
                pos[i.idx] = k
        prev_clock = {e: ({c: -1 for c in COMPUTE}, frozenset()) for e in QUEUES}
        for i in self.ins:
            clk, dseen = prev_clock[i.eng]
            clk = dict(clk)
            dseen = set(dseen)
            waits = []
            for d in sorted(i.deps, key=lambda d: -d.idx):
                if d is i:
                    continue
                if d.is_dma:
                    if d.idx in dseen:
                        continue
                    waits.append(d)
                    dseen.add(d.idx)
                else:
                    if d.eng == i.eng and d.eng == "pe":
                        continue
                    if clk[d.eng] >= pos[d.idx]:
                        continue
                    waits.append(d)
                    clk[d.eng] = max(clk[d.eng], pos[d.idx])
                dc, dd = d.clock
                for c in COMPUTE:
                    if dc[c] > clk[c]:
                        clk[c] = dc[c]
                dseen |= dd
            final = []
            for d in waits:
                if d.is_dma:
                    final.append(d)
                elif clk[d.eng] == pos[d.idx]:
                    final.append(d)
            i.waits = final
            for d in final:
                d.signal = True
            i.clock = (clk, frozenset(dseen))
            prev_clock[i.eng] = i.clock
        for d in tail:
            d.signal = True
        for e in COMPUTE:
            self.sems[e] = self.es.enter_context(nc.semaphore("s_" + e))
            n = 0
            for i in self.q[e]:
                if i.is_dma:
                    continue
                if i.signal:
                    n += 1
                    i.sigval = n
        for k in self.dma_cum:
            self.dma_sems[k] = self.es.enter_context(nc.semaphore("d_" + str(k).replace(":", "_")))

    def emit(self, tail):
        nc = self.nc
        prog = self

        def wait(eng, d):
            if d.is_dma:
                eng.wait_ge(prog.dma_sems[d.dma_key], d.dma_val)
            else:
                eng.wait_ge(prog.sems[d.eng], d.sigval)

        def run(engname, eng):
            for i in prog.q[engname]:
                for d in i.waits:
                    wait(eng, d)
                h = i.fn(eng)
                if i.is_dma:
                    h.then_inc(prog.dma_sems[i.dma_key], 16)
                elif i.signal:
                    h.then_inc(prog.sems[i.eng], 1)
            if engname == "sp":
                for d in tail:
                    wait(eng, d)

        with nc.Block() as block:
            @block.tensor
            def _(e):
                run("pe", e)

            @block.scalar
            def _(e):
                run("act", e)

            @block.vector
            def _(e):
                run("dve", e)

            @block.gpsimd
            def _(e):
                run("pool", e)

            @block.sync
            def _(e):
                run("sp", e)

    def close(self):
        self.es.close()


class Builder:
    def __init__(self, nc, P, dram):
        self.nc, self.P, self.dram = nc, P, dram
        P_ = P
        self.x_tok = P_.sbuf("x_tok_sb", [128, NTB, D], F32)
        self.xTp = P_.sbuf("xTp", [128, 8, PAD + T], BF16)
        self.ident_f = P_.sbuf("ident_f", [128, 128], F32)
        self.ones_f = P_.sbuf("ones_f", [128, 128], F32)
        self.neghalf = P_.sbuf("neghalf", [128, 1], F32)
        self.lnp = None
        self.stats = P_.sbuf("stats", [128, NTB, 16], F32)
        self.cwb = P_.sbuf("cwb", [128, 44, 4], F32)
        self.fix = P_.sbuf("fix", [128, 44, 2], F32)
        self.ring = [P_.sbuf(f"ring{i}", [128, 4096], BF16) for i in range(3)]
        self.ring_cnt = 0
        self.SW = 20480
        self.S = P_.sbuf("S", [128, self.SW], F32)
        self.sp_ = 0
        self.PS = [P_.psum(f"ps{i}", [128, 1024], F32) for i in range(4)]
        self.out_dmas = []
        self.ident_b = P_.sbuf("ident_b", [128, 128], BF16)
        self.maskneg = P_.sbuf("maskneg", [128, 128], BF16)
        self.small = P_.sbuf("small", [128, 64], F32)
        self.rot = {}
        self.consts()

    def rotbank(self, group, banks):
        k = self.rot.get(group, 0)
        self.rot[group] = k + 1
        return banks[k % len(banks)]

    def reset_scratch(self):
        self.P.fence()
        self.sp_ = 0

    def carve(self, words, dtype=F32):
        a = self.sp_
        self.sp_ += words
        assert self.sp_ <= self.SW, (self.sp_, self.SW)
        v = self.S[:, a:a + words]
        if dtype == BF16:
            v = v.bitcast(BF16)
        return v

    def bank(self, b):
        return self.PS[b // 2][:, (b % 2) * 512:(b % 2) * 512 + 512]

    @staticmethod
    def BK(b):
        return f"bk{b}"

    def ring_next(self):
        i = self.ring_cnt % 3
        self.ring_cnt += 1
        return self.ring[i], f"ring{i}"

    def wload(self, src_ap, words=4096):
        tile, res = self.ring_next()
        self.P.add("pool", lambda e, t=tile, s=src_ap, w=words: e.dma_start(out=t[:, 0:w], in_=s, max_dma_last_dim=8192),
                   writes=[res], dma_key=res)
        return tile, res

    def consts(self):
        P = self.P
        ones_f, ident_f = self.ones_f, self.ident_f
        P.add("pool", lambda e: e.memset(ones_f[:], 1.0), writes=["ones_f"])
        P.add("pool", lambda e: e.affine_select(out=ident_f[:], in_=ones_f[:], pattern=[[-1, 128]], compare_op=ALU.is_equal,
                                                 fill=0.0, base=0, channel_multiplier=1), reads=["ones_f"], writes=["ident_f"])
        nh = self.neghalf
        P.add("pool", lambda e: e.memset(nh[:], -0.5), writes=["neghalf"])
        xTp = self.xTp
        P.add("pool", lambda e: e.memset(xTp[:, :, 0:PAD], 0.0), writes=["xTpad"])
        ident_b, maskneg = self.ident_b, self.maskneg
        P.add("pool", lambda e: e.tensor_copy(out=ident_b[:], in_=ident_f[:]), reads=["ident_f"], writes=["ident_b"])
        P.add("pool", lambda e: e.memset(maskneg[:], -30000.0), writes=["maskneg"])
        P.add("pool", lambda e: e.affine_select(out=maskneg[:], in_=maskneg[:], pattern=[[-1, 128]], compare_op=ALU.is_gt,
                                                 fill=0.0, base=0, channel_multiplier=1), reads=["maskneg"], writes=["maskneg"])

    def load_x(self):
        P, x_tok = self.P, self.x_tok
        xv = self.dram["x"].rearrange("(tb p) d -> p tb d", p=128)
        for g in range(4):
            P.add("sp", lambda e, g=g: e.dma_start(out=x_tok[:, 4 * g:4 * g + 4, :], in_=xv[:, 4 * g:4 * g + 4, :]),
                  writes=[f"xt{tb}_{dh}" for tb in range(4 * g, 4 * g + 4) for dh in range(2)], dma_key=f"xin{g}")

    def store_tb(self, tb):
        P, x_tok = self.P, self.x_tok
        ov = self.dram["out"].rearrange("(tb p) d -> p tb d", p=128)
        i = P.add("sp", lambda e: e.dma_start(out=ov[:, tb, :], in_=x_tok[:, tb, :]), reads=[f"xt{tb}_0", f"xt{tb}_1"], dma_key=f"xout{tb}")
        self.out_dmas.append(i)

    def transpose_tb(self, tb, pst):
        P, x_tok, xTp, ident_f = self.P, self.x_tok, self.xTp, self.ident_f
        ps = self.PS[pst]
        P.begin_group()
        for kc in range(8):
            P.add("pe", lambda e, kc=kc: e.transpose(ps[:, kc * 128:(kc + 1) * 128], x_tok[:, tb, kc * 128:(kc + 1) * 128], ident_f[:]),
                  reads=[f"xt{tb}_{kc // 4}", "ident_f"], writes=[self.BK(2 * pst + kc // 4)])
        P.end_group()
        c0 = PAD + tb * 128
        P.add("act", lambda e: e.activation(out=xTp[:, 0:4, c0:c0 + 128], in_=ps[:, 0:512].rearrange("p (k t) -> p k t", k=4), func=AF.Identity),
              reads=[self.BK(2 * pst)], writes=[f"xT{tb}"])
        P.add("dve", lambda e: e.tensor_copy(out=xTp[:, 4:8, c0:c0 + 128], in_=ps[:, 512:1024].rearrange("p (k t) -> p k t", k=4)),
              reads=[self.BK(2 * pst + 1)], writes=[f"xT{tb}"])

    def load_ln(self, idx):
        self.lnp = self.carve(2 * D).rearrange("p (a d) -> p a d", a=2)
        lnp = self.lnp
        src = self.dram["lnp"][idx].partition_broadcast(128)
        self.P.add("sp", lambda e: e.dma_start(out=lnp[:], in_=src), writes=["S:lnp"], dma_key="lnp")

    def ln_tb(self, tb):
        P, x_tok, st, lnp, nh = self.P, self.x_tok, self.stats, self.lnp, self.neghalf
        R = [f"xt{tb}_0", f"xt{tb}_1"]
        sr = f"st{tb}"
        P.add("dve", lambda e: e.bn_stats(out=st[:, tb, 0:6], in_=x_tok[:, tb, 0:512]), reads=[R[0]], writes=[sr])
        P.add("dve", lambda e: e.bn_stats(out=st[:, tb, 6:12], in_=x_tok[:, tb, 512:1024]), reads=[R[1]], writes=[sr])
        P.add("dve", lambda e: e.bn_aggr(out=st[:, tb, 12:14], in_=st[:, tb, 0:12]), reads=[sr], writes=[sr])
        P.add("pool", lambda e: e.tensor_scalar(out=st[:, tb, 14:15], in0=st[:, tb, 13:14], scalar1=LN_EPS, scalar2=None, op0=ALU.add),
              reads=[sr], writes=[sr])
        P.add("pool", lambda e: e.tensor_tensor(out=st[:, tb, 14:15], in0=st[:, tb, 14:15], in1=nh[:], op=ALU.pow),
              reads=[sr, "neghalf"], writes=[sr])
        P.add("dve", lambda e: e.tensor_scalar(out=st[:, tb, 15:16], in0=st[:, tb, 12:13], scalar1=st[:, tb, 14:15], scalar2=-1.0,
                                               op0=ALU.mult, op1=ALU.mult), reads=[sr], writes=[sr])
        P.add("act", lambda e: e.activation(out=x_tok[:, tb, :], in_=x_tok[:, tb, :], func=AF.Identity,
                                            scale=st[:, tb, 14:15], bias=st[:, tb, 15:16]), reads=[sr] + R, writes=R)
        P.add("pool", lambda e: e.tensor_tensor(out=x_tok[:, tb, :], in0=x_tok[:, tb, :], in1=lnp[:, 0, :], op=ALU.mult),
              reads=R + ["S:lnp"], writes=R)
        P.add("dve", lambda e: e.tensor_tensor(out=x_tok[:, tb, :], in0=x_tok[:, tb, :], in1=lnp[:, 1, :], op=ALU.add),
              reads=R + ["S:lnp"], writes=R)

    def ffn(self, l, final=False):
        P, xTp, x_tok, cwb, fix = self.P, self.xTp, self.x_tok, self.cwb, self.fix
        self.reset_scratch()
        hid = self.carve(NJ * 1024 // 2, BF16).rearrange("p (j t) -> p j t", j=NJ)
        tmp = [[self.carve(1024), self.carve(1024)] for _ in range(2)]
        cwsrc = self.dram["ffn_cwb"][l]
        P.add("sp", lambda e: e.dma_start(out=cwb[:], in_=cwsrc), writes=["cwb"], dma_key="cwb")
        self.load_ln(2 * l + 1)
        wup, wdn = self.dram["wup"], self.dram["wdn"]
        for h in range(2):
            c0 = PAD + 1024 * h
            xres = [f"xT{tb}" for tb in range(8 * h, 8 * h + 8)] + ["xTpad"]
            pre = {}
            PD = int(os.environ.get('MK_PD', '1'))
            if PREFETCH:
                for s0 in range(PD):
                    pre[s0] = self.wload(wup[l, s0])
            for s in range(11):
                if PREFETCH:
                    tile, res = pre[s]
                    if s + PD < 11:
                        pre[s + PD] = self.wload(wup[l, s + PD])
                else:
                    tile, res = self.wload(wup[l, s])
                sv = tile[:].rearrange("p (k j c) -> p k j c", k=8, j=2)
                for jj in range(2):
                    j = 2 * s + jj
                    pset = j % 2
                    for ug in range(2):
                        pst = 2 * pset + ug
                        ps = self.PS[pst]
                        for t in range(2):
                            P.begin_group()
                            for kc in range(8):
                                P.add("pe", lambda e, ps=ps, t=t, kc=kc, jj=jj, ug=ug, sv=sv, c0=c0: e.matmul(
                                    ps[:, t * 512:(t + 1) * 512], lhsT=sv[:, kc, jj, ug * 128:(ug + 1) * 128],
                                    rhs=xTp[:, kc, c0 + t * 512:c0 + (t + 1) * 512], start=(kc == 0), stop=(kc == 7)),
                                    reads=[res] + xres, writes=[self.BK(2 * pst + t)])
                            P.end_group()
                    for ug in range(2):
                        pst = 2 * pset + ug
                        ps = self.PS[pst]
                        a = tmp[pset][ug]
                        ar = f"S:a{pset}{ug}"
                        ch = ug * NJ + j
                        bks = [self.BK(2 * pst), self.BK(2 * pst + 1)]
                        P.add("act", lambda e, ps=ps, a=a, ch=ch: e.activation(out=a[:, 0:1024], in_=ps[:, 0:1024], func=AF.Identity,
                                                                                scale=cwb[:, ch, 2:3], bias=cwb[:, ch, 3:4]),
                              reads=bks + ["cwb"], writes=[ar])
                        P.add("dve", lambda e, ps=ps, a=a, ch=ch: e.scalar_tensor_tensor(
                            out=a[:, 1:1024], in0=ps[:, 0:1023], scalar=cwb[:, ch, 1:2], in1=a[:, 1:1024], op0=ALU.mult, op1=ALU.add),
                            reads=bks + ["cwb", ar], writes=[ar])
                        P.add("dve", lambda e, ps=ps, a=a, ch=ch: e.scalar_tensor_tensor(
                            out=a[:, 2:1024], in0=ps[:, 0:1022], scalar=cwb[:, ch, 0:1], in1=a[:, 2:1024], op0=ALU.mult, op1=ALU.add),
                            reads=bks + ["cwb", ar], writes=[ar])
                        if h == 0:
                            P.add("dve", lambda e, ps=ps, ch=ch: e.tensor_scalar(out=fix[:, ch, 0:2], in0=ps[:, 1022:1024], scalar1=cwb[:, ch, 0:1],
                                                                                 scalar2=None, op0=ALU.mult), reads=bks + ["cwb"], writes=[f"fix{ch}"])
                            P.add("dve", lambda e, ps=ps, ch=ch: e.scalar_tensor_tensor(
                                out=fix[:, ch, 0:1], in0=ps[:, 1023:1024], scalar=cwb[:, ch, 1:2], in1=fix[:, ch, 0:1], op0=ALU.mult, op1=ALU.add),
                                reads=bks + ["cwb", f"fix{ch}"], writes=[f"fix{ch}"])
                        else:
                            P.add("dve", lambda e, a=a, ch=ch: e.tensor_tensor(out=a[:, 0:2], in0=a[:, 0:2], in1=fix[:, ch, 0:2], op=ALU.add),
                                  reads=[ar, f"fix{ch}"], writes=[ar])
                    au, ag = tmp[pset]
                    P.add("act", lambda e, ag=ag: e.activation(out=ag[:, 0:1024], in_=ag[:, 0:1024], func=AF.Silu),
                          reads=[f"S:a{pset}1"], writes=[f"S:a{pset}1"])
                    P.add("pool", lambda e, au=au, ag=ag, j=j: e.tensor_tensor(out=hid[:, j, :], in0=au[:, 0:1024], in1=ag[:, 0:1024], op=ALU.mult),
                          reads=[f"S:a{pset}0", f"S:a{pset}1"], writes=[f"S:hid{j}"])
            for dh in range(2):
                for g in range(3):
                    tile, res = self.wload(wdn[l, dh, g])
                    sv = tile[:].rearrange("p (j c) -> p j c", j=8)
                    for jj in range(8 if g < 2 else 6):
                        j = 8 * g + jj
                        for tb in range(8):
                            P.add("pe", lambda e, tb=tb, j=j, jj=jj, sv=sv: e.matmul(
                                self.bank(tb), lhsT=hid[:, j, tb * 128:(tb + 1) * 128], rhs=sv[:, jj, :], start=(j == 0), stop=(j == NJ - 1)),
                                reads=[res, f"S:hid{j}"], writes=[self.BK(tb)])
                for tb in range(8):
                    gtb = 8 * h + tb
                    P.add("dve", lambda e, tb=tb, gtb=gtb, dh=dh: e.scalar_tensor_tensor(
                        out=x_tok[:, gtb, dh * 512:(dh + 1) * 512], in0=x_tok[:, gtb, dh * 512:(dh + 1) * 512], scalar=ALPHA,
                        in1=self.bank(tb), op0=ALU.mult, op1=ALU.add), reads=[self.BK(tb), f"xt{gtb}_{dh}"], writes=[f"xt{gtb}_{dh}"])
            for tb in range(8):
                gtb = 8 * h + tb
                self.ln_tb(gtb)
                if final:
                    self.store_tb(gtb)
                else:
                    self.transpose_tb(gtb, tb // 2)


    def mix1(self):
        P, xTp, x_tok, dram = self.P, self.xTp, self.x_tok, self.dram
        ones_f, ident_b, maskneg, small = self.ones_f, self.ident_b, self.maskneg, self.small
        xall = [f"xT{tb}" for tb in range(NTB)]
        self.reset_scratch()
        fl = self.carve(2048)
        cum = self.carve(2048)
        ones = self.carve(512)
        QG = self.carve(6 * 2048 // 2, BF16).rearrange("p (r t) -> p r t", r=6)
        KG = self.carve(6 * 2048 // 2, BF16).rearrange("p (r t) -> p r t", r=6)
        bsrc = dram["fox_bf"]
        P.add("sp", lambda e: e.dma_start(out=small[0:16, 0:1], in_=bsrc), writes=["small"], dma_key="small")
        P.add("dve", lambda e: e.tensor_scalar(out=small[0:16, 1:2], in0=small[0:16, 0:1], scalar1=-1.0, scalar2=None, op0=ALU.mult),
              reads=["small"], writes=["small"])
        P.add("pool", lambda e: e.memset(ones[0:16, :], 1.0), writes=["S:ones"])
        P.add("pool", lambda e: e.memset(QG[0:16, 3:6, :], 1.0), writes=["S:QG1"])
        P.add("pool", lambda e: e.memset(KG[0:16, 0:3, :], 1.0), writes=["S:KG1"])
        tile, res = self.wload(dram["fox_f"], words=128)
        fv = tile[:, 0:128].rearrange("p (k c) -> p k c", k=8)
        for t in range(4):
            P.begin_group()
            for kc in range(8):
                P.add("pe", lambda e, t=t, kc=kc: e.matmul(self.bank(t)[0:16, :], lhsT=fv[:, kc, :], rhs=xTp[:, kc, PAD + t * 512:PAD + (t + 1) * 512],
                                                           start=(kc == 0), stop=(kc == 7)), reads=[res] + xall, writes=[self.BK(t)])
            P.end_group()
            P.add("act", lambda e, t=t: e.activation(out=fl[0:16, t * 512:(t + 1) * 512], in_=self.bank(t)[0:16, :], func=AF.Exp,
                                                     scale=-1.0, bias=small[0:16, 1:2]), reads=[self.BK(t), "small"], writes=[f"S:fl{t}"])
        for t in range(4):
            P.add("act", lambda e, t=t: e.activation(out=fl[0:16, t * 512:(t + 1) * 512], in_=fl[0:16, t * 512:(t + 1) * 512], func=AF.Ln,
                                                     scale=1.0, bias=1.0), reads=[f"S:fl{t}"], writes=[f"S:fl{t}"])
        for t in range(4):
            init = 0.0 if t == 0 else cum[0:16, t * 512 - 1:t * 512]
            P.add("dve", lambda e, t=t, init=init: e.tensor_tensor_scan(out=cum[0:16, t * 512:(t + 1) * 512], data0=ones[0:16, :],
                                                                        data1=fl[0:16, t * 512:(t + 1) * 512], initial=init,
                                                                        op0=ALU.mult, op1=ALU.subtract),
                  reads=[f"S:fl{t}", "S:ones", "S:cum"], writes=["S:cum"])
        for r in range(3):
            P.add("dve", lambda e, r=r: e.tensor_copy(out=QG[0:16, r, :], in_=cum[0:16, :]), reads=["S:cum"], writes=[f"S:QG0{r}"])
            if r < 2:
                P.add("dve", lambda e, r=r: e.tensor_tensor(out=cum[0:16, :], in0=cum[0:16, :], in1=QG[0:16, r, :], op=ALU.subtract),
                      reads=["S:cum", f"S:QG0{r}"], writes=["S:cum"])
        P.add("dve", lambda e: e.tensor_scalar(out=KG[0:16, 3:6, :], in0=QG[0:16, 0:3, :], scalar1=-1.0, scalar2=None, op0=ALU.mult),
              reads=["S:QG00", "S:QG01", "S:QG02"], writes=["S:KG0"])
        gq, gk = dram["augq"], dram["augk"]
        P.add("sp", lambda e: e.dma_start(out=gq, in_=QG[0:16, :, :]), reads=["S:QG00", "S:QG01", "S:QG02", "S:QG1"], writes=["augq"], dma_key="augq")
        P.add("sp", lambda e: e.dma_start(out=gk, in_=KG[0:16, :, :]), reads=["S:KG0", "S:KG1"], writes=["augk"], dma_key="augk")
        self.reset_scratch()
        AUG = [[[self.carve(1024, BF16) for qk in range(2)] for sub in range(2)] for st in range(2)]
        VP = [self.carve(1040, BF16).rearrange("p (t s d) -> p t s d", t=16, s=2) for st in range(2)]
        OTP = [self.carve(1024, BF16) for st in range(2)]
        PT = [self.carve(256, BF16) for _ in range(4)]
        PTD = [self.carve(256, BF16) for _ in range(4)]
        rc = self.carve(512)
        bcs = self.carve(512)
        self.load_ln(2)
        for st in range(2):
            P.add("pool", lambda e, st=st: e.memset(VP[st][:, :, :, 64:65], 1.0), writes=[f"S:VPone{st}"])
        for i4 in range(1, 4):
            P.add("pool", lambda e, i4=i4: e.memset(PTD[i4][:, 0:128 * i4], 0.0), writes=[f"S:PTD{i4}"])
        ptc = 0
        for hp in range(8):
            st = hp % 2
            qk_t, qk_r = self.wload(dram["fox_qk"][hp], words=2048)
            v_t, v_r = self.wload(dram["fox_v"][hp], words=1024)
            wo_t, wo_r = self.wload(dram["fox_wo"][hp], words=1024)
            qkv = qk_t[:, 0:2048].rearrange("p (k c) -> p k c", k=8)
            vv = v_t[:, 0:1024].rearrange("p (k c) -> p k c", k=8)
            for sub in range(2):
                h = 2 * hp + sub
                P.add("sp", lambda e, st=st, sub=sub, h=h: e.dma_start(out=AUG[st][sub][0][64:70, :], in_=gq[h]),
                      reads=["augq"], writes=[f"S:AQa{st}{sub}"], dma_key=f"aq{st}{sub}")
                P.add("sp", lambda e, st=st, sub=sub, h=h: e.dma_start(out=AUG[st][sub][1][64:70, :], in_=gk[h]),
                      reads=["augk"], writes=[f"S:AKa{st}{sub}"], dma_key=f"ak{st}{sub}")
            for t in range(4):
                for qk in range(2):
                    b = self.rotbank("misc", (0, 1, 7))
                    P.begin_group()
                    for kc in range(8):
                        P.add("pe", lambda e, b=b, kc=kc, qk=qk, t=t, qkv=qkv: e.matmul(
                            self.bank(b), lhsT=qkv[:, kc, qk * 128:(qk + 1) * 128], rhs=xTp[:, kc, PAD + t * 512:PAD + (t + 1) * 512],
                            start=(kc == 0), stop=(kc == 7)), reads=[qk_r] + xall, writes=[self.BK(b)])
                    P.end_group()
                    for sub in range(2):
                        dst = AUG[st][sub][qk]
                        nm = f"S:A{'QK'[qk]}{st}{sub}t{t}"
                        if qk == 0:
                            P.add("dve", lambda e, b=b, sub=sub, dst=dst, t=t: e.tensor_scalar(
                                out=dst[0:64, t * 512:(t + 1) * 512], in0=self.bank(b)[sub * 64:(sub + 1) * 64, :], scalar1=0.125, scalar2=None,
                                op0=ALU.mult), reads=[self.BK(b)], writes=[nm])
                        else:
                            P.add("dve", lambda e, b=b, sub=sub, dst=dst, t=t: e.tensor_copy(
                                out=dst[0:64, t * 512:(t + 1) * 512], in_=self.bank(b)[sub * 64:(sub + 1) * 64, :]),
                                reads=[self.BK(b)], writes=[nm])
            for g4 in range(4):
                b = self.rotbank("misc", (0, 1, 7))
                P.begin_group()
                for ti in range(4):
                    tb = 4 * g4 + ti
                    for kc in range(8):
                        P.add("pe", lambda e, b=b, ti=ti, tb=tb, kc=kc, vv=vv: e.matmul(
                            self.bank(b)[:, ti * 128:(ti + 1) * 128], lhsT=xTp[:, kc, PAD + tb * 128:PAD + (tb + 1) * 128], rhs=vv[:, kc, :],
                            start=(kc == 0), stop=(kc == 7)), reads=[v_r, f"xT{tb}"], writes=[self.BK(b)])
                P.end_group()
                P.add("act", lambda e, b=b, g4=g4, st=st: e.activation(
                    out=VP[st][:, 4 * g4:4 * g4 + 4, :, 0:64], in_=self.bank(b).rearrange("p (t s d) -> p t s d", t=4, s=2), func=AF.Identity),
                    reads=[self.BK(b)], writes=[f"S:VP{st}g{g4}"])
            for sub in range(2):
                QA, KA = AUG[st][sub][0], AUG[st][sub][1]
                for qt in range(4):
                    ob = self.rotbank("O", (2, 3))
                    nkb = 4 * qt + 4
                    for kb in range(nkb):
                        i = kb - 4 * qt
                        co = 128 * i if i > 0 else 0
                        sb_ = self.rotbank("S", (4, 5, 6))
                        if i >= 0:
                            pt, ptr = PTD[i], f"S:PTD{i}"
                        else:
                            pt, ptr = PT[ptc % 4], f"S:PT{ptc % 4}"
                            ptc += 1
                        kres = [f"S:AK{st}{sub}t{kb // 4}", f"S:AKa{st}{sub}", f"S:AQ{st}{sub}t{qt}", f"S:AQa{st}{sub}"]
                        P.begin_group()
                        if i < 0:
                            P.add("pe", lambda e, sb_=sb_, kb=kb, qt=qt, QA=QA, KA=KA: e.matmul(
                                self.bank(sb_)[:, 0:512], lhsT=KA[0:70, kb * 128:(kb + 1) * 128], rhs=QA[0:70, qt * 512:(qt + 1) * 512],
                                start=True, stop=True), reads=kres, writes=[self.BK(sb_)])
                        else:
                            P.add("pe", lambda e, sb_=sb_, co=co, kb=kb, qt=qt, QA=QA, KA=KA: e.matmul(
                                self.bank(sb_)[:, co:co + 128], lhsT=KA[0:70, kb * 128:(kb + 1) * 128], rhs=QA[0:70, qt * 512 + co:qt * 512 + co + 128],
                                start=True, stop=False), reads=kres, writes=[self.BK(sb_)])
                            P.add("pe", lambda e, sb_=sb_, co=co: e.matmul(self.bank(sb_)[:, co:co + 128], lhsT=ident_b[:], rhs=maskneg[:],
                                                                           start=False, stop=True),
                                  reads=["ident_b", "maskneg"], writes=[self.BK(sb_)])
                            if co + 128 < 512:
                                P.add("pe", lambda e, sb_=sb_, co=co, kb=kb, qt=qt, QA=QA, KA=KA: e.matmul(
                                    self.bank(sb_)[:, co + 128:512], lhsT=KA[0:70, kb * 128:(kb + 1) * 128], rhs=QA[0:70, qt * 512 + co + 128:(qt + 1) * 512],
                                    start=True, stop=True), reads=kres, writes=[self.BK(sb_)])
                        P.end_group()
                        P.add("act", lambda e, sb_=sb_, co=co, pt=pt: e.activation(out=pt[:, co:512], in_=self.bank(sb_)[:, co:512], func=AF.Exp),
                              reads=[self.BK(sb_)], writes=[ptr])
                        P.add("pe", lambda e, ob=ob, kb=kb, pt=pt, st=st, sub=sub, nkb=nkb: e.matmul(
                            self.bank(ob)[0:65, 0:512], lhsT=VP[st][:, kb, sub, 0:65], rhs=pt[:, 0:512], start=(kb == 0), stop=(kb == nkb - 1)),
                            reads=[ptr, f"S:VP{st}g{kb // 4}", f"S:VPone{st}"], writes=[self.BK(ob)])
                    P.add("dve", lambda e, ob=ob: e.reciprocal(out=rc[64:65, :], in_=self.bank(ob)[64:65, :]), reads=[self.BK(ob)], writes=["S:rc"])
                    bb = self.rotbank("misc", (0, 1, 7))
                    P.add("pe", lambda e, bb=bb: e.matmul(self.bank(bb)[0:64, :], lhsT=ones_f[64:65, 0:64], rhs=rc[64:65, :], start=True, stop=True),
                          reads=["ones_f", "S:rc"], writes=[self.BK(bb)])
                    P.add("dve", lambda e, bb=bb: e.tensor_copy(out=bcs[0:64, :], in_=self.bank(bb)[0:64, :]), reads=[self.BK(bb)], writes=["S:bcs"])
                    P.add("dve", lambda e, ob=ob, st=st, sub=sub, qt=qt: e.tensor_tensor(
                        out=OTP[st][sub * 64:(sub + 1) * 64, qt * 512:(qt + 1) * 512], in0=self.bank(ob)[0:64, :], in1=bcs[0:64, :], op=ALU.mult),
                        reads=[self.BK(ob), "S:bcs"], writes=[f"S:OT{st}q{qt}"])
            for tb in range(NTB):
                for dh in range(2):
                    b = self.rotbank("misc", (0, 1, 7))
                    P.add("pe", lambda e, b=b, tb=tb, dh=dh, st=st, wo_t=wo_t: e.matmul(
                        self.bank(b), lhsT=OTP[st][:, tb * 128:(tb + 1) * 128], rhs=wo_t[:, dh * 512:(dh + 1) * 512], start=True, stop=True),
                        reads=[wo_r, f"S:OT{st}q{tb // 4}"], writes=[self.BK(b)])
                    xr = f"xt{tb}_{dh}"
                    if hp == 0:
                        P.add("dve", lambda e, b=b, tb=tb, dh=dh: e.scalar_tensor_tensor(
                            out=x_tok[:, tb, dh * 512:(dh + 1) * 512], in0=x_tok[:, tb, dh * 512:(dh + 1) * 512], scalar=ALPHA,
                            in1=self.bank(b), op0=ALU.mult, op1=ALU.add), reads=[self.BK(b), xr], writes=[xr])
                    else:
                        P.add("dve", lambda e, b=b, tb=tb, dh=dh: e.tensor_tensor(
                            out=x_tok[:, tb, dh * 512:(dh + 1) * 512], in0=self.bank(b), in1=x_tok[:, tb, dh * 512:(dh + 1) * 512], op=ALU.add),
                            reads=[self.BK(b), xr], writes=[xr])
        for tb in range(NTB):
            self.ln_tb(tb)
            self.transpose_tb(tb, tb % 4)


    def mix0(self):
        P, xTp, x_tok, dram = self.P, self.xTp, self.x_tok, self.dram
        ones_f, ident_f, ident_b, maskneg = self.ones_f, self.ident_f, self.ident_b, self.maskneg
        self.reset_scratch()
        H = self.carve(1024)
        prevbf = self.carve(512, BF16)
        R = self.carve(2048)
        L = [self.carve(128), self.carve(128)]
        halo = self.carve(16).rearrange("p (i k) -> p i k", i=8)
        fixx = self.carve(36).rearrange("p (i k) -> p i k", i=12)
        cst = self.carve(128)
        biasbc, Dbc = cst[:, 0:16], cst[:, 16:32]
        gT = cst[:, 32:40]
        scw = cst[:, 40:64].rearrange("p (i k) -> p i k", i=8)
        xcw = cst[:, 64:124].rearrange("p (i k) -> p i k", i=12)
        s_tok, s_feat, s_head = dram["m0_tokc"].rearrange("a h -> (a h)").partition_broadcast(128), dram["m0_featc"], dram["m0_headc"]
        P.add("sp", lambda e: e.dma_start(out=cst[:, 0:32], in_=s_tok), writes=["S:cst"], dma_key="m0c0")
        P.add("sp", lambda e: e.dma_start(out=cst[:, 32:124], in_=s_feat), writes=["S:cst"], dma_key="m0c1")
        P.add("sp", lambda e: e.dma_start(out=cst[0:16, 124:126], in_=s_head), writes=["S:cst"], dma_key="m0c2")
        P.add("act", lambda e: e.activation(out=cst[0:16, 126:127], in_=cst[0:16, 125:126], func=AF.Exp), reads=["S:cst"], writes=["S:cst"])
        P.add("dve", lambda e: e.tensor_scalar(out=cst[0:16, 127:128], in0=cst[0:16, 126:127], scalar1=-1.0, scalar2=None, op0=ALU.mult),
              reads=["S:cst"], writes=["S:cst"])
        dtb, acol = cst[0:16, 124:125], cst[0:16, 127:128]
        P.add("pool", lambda e: e.memset(H[:, :], 0.0), writes=["S:H"])
        P.add("pool", lambda e: e.memset(prevbf[:, :], 0.0), writes=["S:prev"])
        P.add("pool", lambda e: e.memset(R[0:48, :], 0.0), writes=["S:R"])
        P.add("pool", lambda e: e.memset(R[32:48, :], 1.0), reads=["S:R"], writes=["S:R"])
        P.add("pool", lambda e: e.affine_select(out=R[32:48, :].rearrange("p (h t) -> p h t", h=16), in_=R[32:48, :].rearrange("p (h t) -> p h t", h=16),
                                                 pattern=[[-1, 16], [0, 128]], compare_op=ALU.is_equal, fill=0.0, base=0, channel_multiplier=1),
              reads=["S:R"], writes=["S:R"])
        for k in range(2):
            P.add("pool", lambda e, k=k: e.memset(L[k][0:48, :], 0.0), writes=[f"S:L{k}"])
            P.add("pool", lambda e, k=k: e.memset(L[k][0:16, :], 1.0), reads=[f"S:L{k}"], writes=[f"S:L{k}"])
        base = self.sp_
        wx, wz, wsc, wo = dram["m0_wx"], dram["m0_wz"], dram["m0_wsc"], dram["m0_wo"]

        for qi in range(4):
            self.P.fence()
            self.sp_ = base
            yaT = self.carve(2048, BF16).rearrange("p (i t) -> p i t", i=8)
            ybT = self.carve(2048, BF16).rearrange("p (i t) -> p i t", i=8)
            xdt = self.carve(2048, BF16).rearrange("p (b f) -> p b f", b=4)
            zs = self.carve(2048, BF16).rearrange("p (b f) -> p b f", b=4)
            BcT = self.carve(512, BF16).rearrange("p (g t) -> p g t", g=2)
            CcT = self.carve(512, BF16).rearrange("p (g t) -> p g t", g=2)
            Btok = self.carve(512, BF16).rearrange("p (b f) -> p b f", b=4)
            dtok = self.carve(64).rearrange("p (b h) -> p b h", b=4)
            Ddt = self.carve(64).rearrange("p (b h) -> p b h", b=4)
            acsT = self.carve(512)
            dtT = self.carve(512)
            qbase = self.sp_
            a_ = [self.carve(512), self.carve(512)]
            prod = [self.carve(516), self.carve(516)]
            cs = [self.carve(512), self.carve(512)]
            c0 = PAD + 512 * qi
            xres = [f"xT{tb}" for tb in range(4 * qi, 4 * qi + 4)]
            win = lambda kc, c0=c0: xTp[:, kc, c0:c0 + 512]

            dt_t, dt_r = self.wload(dram["m0_wdt"], words=128)
            dv = dt_t[:, 0:128].rearrange("p (k c) -> p k c", k=8)
            P.begin_group()
            for kc in range(8):
                P.add("pe", lambda e, kc=kc, dv=dv, win=win: e.matmul(self.bank(0)[0:16, :], lhsT=dv[:, kc, :], rhs=win(kc), start=(kc == 0), stop=(kc == 7)),
                      reads=[dt_r] + xres, writes=[self.BK(0)])
            P.end_group()
            P.begin_group()
            for tbl in range(4):
                for kc in range(8):
                    P.add("pe", lambda e, kc=kc, tbl=tbl, dv=dv, c0=c0: e.matmul(self.bank(1)[:, tbl * 16:(tbl + 1) * 16],
                                                                               lhsT=xTp[:, kc, c0 + tbl * 128:c0 + (tbl + 1) * 128], rhs=dv[:, kc, :],
                                                                               start=(kc == 0), stop=(kc == 7)),
                          reads=[dt_r] + xres, writes=[self.BK(1)])
            P.end_group()
            P.add("act", lambda e, dtT=dtT: e.activation(out=dtT[0:16, :], in_=self.bank(0)[0:16, :], func=AF.Exp, bias=dtb, scale=1.0),
                  reads=[self.BK(0), "S:cst"], writes=["S:dtT"])
            P.add("dve", lambda e, dtok=dtok: e.tensor_tensor(out=dtok[:, :, :], in0=self.bank(1)[:, 0:64].rearrange("p (b h) -> p b h", b=4),
                                                             in1=biasbc.unsqueeze(1).to_broadcast([128, 4, 16]), op=ALU.add),
                  reads=[self.BK(1), "S:cst"], writes=["S:dtok"])
            P.add("act", lambda e, dtok=dtok: e.activation(out=dtok[:, :, :], in_=dtok[:, :, :], func=AF.Exp), reads=["S:dtok"], writes=["S:dtok"])
            P.add("act", lambda e, dtT=dtT: e.activation(out=dtT[0:16, :], in_=dtT[0:16, :], func=AF.Ln, bias=1.0, scale=1.0), reads=["S:dtT"], writes=["S:dtT"])
            P.add("act", lambda e, dtok=dtok: e.activation(out=dtok[:, :, :], in_=dtok[:, :, :], func=AF.Ln, bias=1.0, scale=1.0),
                  reads=["S:dtok"], writes=["S:dtok"])
            P.add("dve", lambda e, dtok=dtok, Ddt=Ddt: e.reciprocal(out=Ddt[:, :, :], in_=dtok[:, :, :]), reads=["S:dtok"], writes=["S:Ddt"])
            P.add("dve", lambda e, Ddt=Ddt: e.tensor_tensor(out=Ddt[:, :, :], in0=Ddt[:, :, :], in1=Dbc.unsqueeze(1).to_broadcast([128, 4, 16]), op=ALU.mult),
                  reads=["S:Ddt", "S:cst"], writes=["S:Ddt"])
            P.add("dve", lambda e, dtT=dtT: e.tensor_scalar(out=dtT[0:16, :], in0=dtT[0:16, :], scalar1=acol, scalar2=None, op0=ALU.mult),
                  reads=["S:dtT", "S:cst"], writes=["S:dtT"])
            for c in range(4):
                P.add("dve", lambda e, c=c, dtT=dtT, acsT=acsT: e.tensor_tensor_scan(out=acsT[0:16, c * 128:(c + 1) * 128], data0=ones_f[0:16, 0:128],
                                                                                   data1=dtT[0:16, c * 128:(c + 1) * 128], initial=0.0,
                                                                                   op0=ALU.mult, op1=ALU.add),
                      reads=["S:dtT", "ones_f"], writes=[f"S:acs{c}"])

            def xchunk(sl, cc, ak, xv, x_r):
                if True:
                    if sl == 0:
                        ci = 8 + cc
                    else:
                        ci = 4 * (sl - 1) + cc
                    b = self.rotbank("m0", (0, 1, 2, 3, 4, 5))
                    P.begin_group()
                    for kc in range(8):
                        P.add("pe", lambda e, b=b, kc=kc, cc=cc, xv=xv, win=win: e.matmul(self.bank(b), lhsT=xv[:, kc, cc * 128:(cc + 1) * 128], rhs=win(kc),
                                                                                         start=(kc == 0), stop=(kc == 7)),
                              reads=[x_r] + xres, writes=[self.BK(b)])
                    P.end_group()
                    a = a_[ak % 2]
                    ar = f"S:a{ak % 2}"
                    bk = [self.BK(b)]
                    P.add("act", lambda e, b=b, a=a, ci=ci: e.activation(out=a[:, 0:512], in_=self.bank(b), func=AF.Identity,
                                                                        scale=xcw[:, ci, 3:4], bias=xcw[:, ci, 4:5]), reads=bk + ["S:cst"], writes=[ar])
                    for sh in range(1, 4):
                        P.add("dve", lambda e, b=b, a=a, ci=ci, sh=sh: e.scalar_tensor_tensor(
                            out=a[:, sh:512], in0=self.bank(b)[:, 0:512 - sh], scalar=xcw[:, ci, 3 - sh:4 - sh], in1=a[:, sh:512], op0=ALU.mult, op1=ALU.add),
                            reads=bk + ["S:cst", ar], writes=[ar])
                    if qi > 0:
                        P.add("dve", lambda e, a=a, ci=ci: e.tensor_tensor(out=a[:, 0:3], in0=a[:, 0:3], in1=fixx[:, ci, 0:3], op=ALU.add),
                              reads=[ar, f"S:fx{ci}"], writes=[ar])
                    if qi < 3:
                        P.add("dve", lambda e, b=b, ci=ci: e.tensor_scalar(out=fixx[:, ci, 0:3], in0=self.bank(b)[:, 509:512], scalar1=xcw[:, ci, 0:1],
                                                                          scalar2=None, op0=ALU.mult), reads=bk + ["S:cst"], writes=[f"S:fx{ci}"])
                        P.add("dve", lambda e, b=b, ci=ci: e.scalar_tensor_tensor(out=fixx[:, ci, 0:2], in0=self.bank(b)[:, 510:512], scalar=xcw[:, ci, 1:2],
                                                                                 in1=fixx[:, ci, 0:2], op0=ALU.mult, op1=ALU.add),
                              reads=bk + ["S:cst", f"S:fx{ci}"], writes=[f"S:fx{ci}"])
                        P.add("dve", lambda e, b=b, ci=ci: e.scalar_tensor_tensor(out=fixx[:, ci, 0:1], in0=self.bank(b)[:, 511:512], scalar=xcw[:, ci, 2:3],
                                                                                 in1=fixx[:, ci, 0:1], op0=ALU.mult, op1=ALU.add),
                              reads=bk + ["S:cst", f"S:fx{ci}"], writes=[f"S:fx{ci}"])
                    if ci >= 10:
                        g = ci - 10
                        yield
                        P.add("act", lambda e, a=a, g=g, CcT=CcT: e.activation(out=CcT[:, g, :], in_=a[:, 0:512], func=AF.Silu), reads=[ar], writes=[f"S:Cc{g}"])
                        yield
                        yield
                        return
                    yield
                    P.add("act", lambda e, a=a: e.activation(out=a[:, 0:512], in_=a[:, 0:512], func=AF.Silu), reads=[ar], writes=[ar])
                    yield
                    tbk = self.rotbank("m0t", (6, 7))
                    P.begin_group()
                    for tbl in range(4):
                        P.add("pe", lambda e, tbk=tbk, tbl=tbl, a=a: e.transpose(self.bank(tbk)[:, tbl * 128:(tbl + 1) * 128], a[:, tbl * 128:(tbl + 1) * 128], ident_f[:]),
                              reads=[ar, "ident_f"], writes=[self.BK(tbk)])
                    P.end_group()
                    if ci >= 8:
                        g = ci - 8
                        P.add("pool", lambda e, a=a, g=g, BcT=BcT: e.tensor_copy(out=BcT[:, g, :], in_=a[:, 0:512]), reads=[ar], writes=[f"S:Bc{g}"])
                        P.add("act", lambda e, tbk=tbk, g=g, Btok=Btok: e.activation(out=Btok[:, :, g * 128:(g + 1) * 128],
                                                                                    in_=self.bank(tbk).rearrange("p (b f) -> p b f", b=4), func=AF.Identity),
                              reads=[self.BK(tbk)], writes=[f"S:Bt{g}"])
                    else:
                        P.add("dve", lambda e, tbk=tbk, ci=ci, xdt=xdt, dtok=dtok: e.tensor_tensor(
                            out=xdt[:, :, ci * 128:(ci + 1) * 128].rearrange("p b (h d) -> p b h d", h=2),
                            in0=self.bank(tbk).rearrange("p (b h d) -> p b h d", b=4, h=2),
                            in1=dtok[:, :, 2 * ci:2 * ci + 2].unsqueeze(3).to_broadcast([128, 4, 2, 64]), op=ALU.mult),
                            reads=[self.BK(tbk), "S:dtok"], writes=[f"S:xdt{ci}"])
                    yield
            xg = []
            for sl in range(3):
                x_t, x_r = self.wload(wx[sl])
                xv = x_t[:].rearrange("p (k c) -> p k c", k=8)
                for cc in range(4):
                    k = len(xg)
                    xg.append(xchunk(sl, cc, k, xv, x_r))
                    next(xg[k])
                    if k >= 1:
                        next(xg[k - 1])
                    next(xg[k])
            next(xg[-1])

            for zsl in range(2):
                z_t, z_r = self.wload(wz[zsl])
                zv = z_t[:].rearrange("p (k c) -> p k c", k=8)
                for tbl in range(4):
                    b = self.rotbank("m0", (0, 1, 2, 3, 4, 5))
                    P.begin_group()
                    for kc in range(8):
                        P.add("pe", lambda e, b=b, kc=kc, tbl=tbl, zv=zv, c0=c0: e.matmul(self.bank(b), lhsT=xTp[:, kc, c0 + tbl * 128:c0 + (tbl + 1) * 128],
                                                                                         rhs=zv[:, kc, :], start=(kc == 0), stop=(kc == 7)),
                              reads=[z_r] + xres, writes=[self.BK(b)])
                    P.end_group()
                    P.add("act", lambda e, b=b, tbl=tbl, zsl=zsl, zs=zs: e.activation(out=zs[:, tbl, zsl * 512:(zsl + 1) * 512], in_=self.bank(b), func=AF.Silu),
                          reads=[self.BK(b)], writes=[f"S:zs{tbl}"])

            for i in range(8):
                s_t, s_r = self.wload(wsc[i], words=3072)
                sv = s_t[:, 0:3072].rearrange("p (k c) -> p k c", k=8)
                bks = []
                for part in range(3):
                    b = self.rotbank("m0", (0, 1, 2, 3, 4, 5))
                    bks.append(b)
                    P.begin_group()
                    for kc in range(8):
                        P.add("pe", lambda e, b=b, kc=kc, part=part, sv=sv, win=win: e.matmul(self.bank(b), lhsT=sv[:, kc, part * 128:(part + 1) * 128], rhs=win(kc),
                                                                                             start=(kc == 0), stop=(kc == 7)),
                              reads=[s_r] + xres, writes=[self.BK(b)])
                    P.end_group()
                bc_, bh_, bb_ = bks
                k2 = i % 2
                pr, csb, a = prod[k2], cs[k2], a_[k2]
                prr, csr, ar = f"S:pr{k2}", f"S:cs{k2}", f"S:a{k2}"
                P.add("act", lambda e, bc_=bc_, csb=csb: e.activation(out=csb[:, 0:512], in_=self.bank(bc_), func=AF.Identity), reads=[self.BK(bc_)], writes=[csr])
                if qi == 0:
                    P.add("pool", lambda e, pr=pr: e.memset(pr[:, 0:2], 0.0), writes=[prr + "h"])
                else:
                    P.add("pool", lambda e, pr=pr, i=i: e.tensor_copy(out=pr[:, 0:2], in_=halo[:, i, :]), reads=[f"S:halo{i}"], writes=[prr + "h"])
                P.add("dve", lambda e, bh_=bh_, pr=pr, csb=csb: e.tensor_tensor(out=pr[:, 2:514], in0=self.bank(bh_), in1=csb[:, 0:512], op=ALU.mult),
                      reads=[self.BK(bh_), csr], writes=[prr])
                if qi < 3:
                    P.add("pool", lambda e, pr=pr, i=i: e.tensor_copy(out=halo[:, i, :], in_=pr[:, 512:514]), reads=[prr], writes=[f"S:halo{i}"])
                P.add("act", lambda e, pr=pr, a=a, i=i: e.activation(out=a[:, 0:512], in_=pr[:, 2:514], func=AF.Identity, scale=scw[:, i, 2:3]),
                      reads=[prr, "S:cst"], writes=[ar])
                P.add("dve", lambda e, pr=pr, a=a, i=i: e.scalar_tensor_tensor(out=a[:, 0:512], in0=pr[:, 1:513], scalar=scw[:, i, 1:2], in1=a[:, 0:512],
                                                                              op0=ALU.mult, op1=ALU.add), reads=[prr, prr + "h", "S:cst", ar], writes=[ar])
                P.add("dve", lambda e, pr=pr, a=a, i=i: e.scalar_tensor_tensor(out=a[:, 0:512], in0=pr[:, 0:512], scalar=scw[:, i, 0:1], in1=a[:, 0:512],
                                                                              op0=ALU.mult, op1=ALU.add), reads=[prr, prr + "h", "S:cst", ar], writes=[ar])
                P.add("dve", lambda e, bb_=bb_, a=a, i=i, yaT=yaT: e.tensor_tensor(out=yaT[:, i, :], in0=self.bank(bb_), in1=a[:, 0:512], op=ALU.mult),
                      reads=[self.BK(bb_), ar], writes=[f"S:ya{i}"])

            self.P.fence()
            self.sp_ = qbase
            segT = self.carve(1024, BF16).rearrange("p (h t) -> p h t", h=16)
            MT = self.carve(1024, BF16).rearrange("p (h t) -> p h t", h=16)
            xdtd = self.carve(512, BF16)
            yt_ = [self.carve(1024), self.carve(1024)]
            junk = self.carve(256, BF16)
            sm_ = [self.carve(64), self.carve(64)]
            X16 = self.carve(16)
            def chunk(c):
                gc = 4 * qi + c
                cols = slice(c * 128, (c + 1) * 128)
                Lm, Lr = L[gc % 2], f"S:L{gc % 2}"
                yt, ytr = yt_[gc % 2], f"S:yt{gc % 2}"
                sm, smr = sm_[gc % 2], f"S:sm{gc % 2}"
                acr = f"S:acs{c}"
                P.add("dve", lambda e, Lm=Lm, cols=cols, acsT=acsT: e.tensor_scalar(out=Lm[32:48, :], in0=acsT[0:16, cols], scalar1=-1.0, scalar2=None, op0=ALU.mult),
                      reads=[acr], writes=[Lr])
                P.add("dve", lambda e, cols=cols, acsT=acsT: e.tensor_tensor(
                    out=R[0:16, :].rearrange("p (h t) -> p h t", h=16), in0=acsT[0:16, cols].unsqueeze(1).to_broadcast([16, 16, 128]),
                    in1=ident_f[0:16, 0:16].unsqueeze(2).to_broadcast([16, 16, 128]), op=ALU.mult), reads=[acr, "ident_f"], writes=["S:R"])
                for hg in range(4):
                    b = hg % 2
                    P.begin_group()
                    for hh in range(4):
                        P.add("pe", lambda e, b=b, hg=hg, hh=hh, Lm=Lm: e.matmul(self.bank(b)[:, hh * 128:(hh + 1) * 128], lhsT=Lm[0:48, :],
                                                                                rhs=R[0:48, hg * 512 + hh * 128:hg * 512 + (hh + 1) * 128], start=True, stop=False),
                              reads=[Lr, "S:R"], writes=[self.BK(b)])
                        P.add("pe", lambda e, b=b, hh=hh: e.matmul(self.bank(b)[:, hh * 128:(hh + 1) * 128], lhsT=ident_b[:], rhs=maskneg[:], start=False, stop=True),
                              reads=["ident_b", "maskneg"], writes=[self.BK(b)])
                    P.end_group()
                    P.add("act", lambda e, b=b, hg=hg, segT=segT: e.activation(out=segT[:, 4 * hg:4 * hg + 4, :], in_=self.bank(b).rearrange("p (h t) -> p h t", h=4),
                                                                              func=AF.Exp), reads=[self.BK(b)], writes=[f"S:seg{hg}"])
                segr = [f"S:seg{hg}" for hg in range(4)]
                P.begin_group()
                for g in range(2):
                    P.add("pe", lambda e, g=g, cols=cols, BcT=BcT, CcT=CcT: e.matmul(self.bank(2)[:, g * 128:(g + 1) * 128], lhsT=BcT[:, g, cols], rhs=CcT[:, g, cols],
                                                                                    start=True, stop=True), reads=[f"S:Bc{g}", f"S:Cc{g}"], writes=["bk2"])
                P.end_group()
                P.add("dve", lambda e, cols=cols, acsT=acsT: e.tensor_scalar(out=X16[0:16, 0:16], in0=ident_f[0:16, 0:16],
                                                                            scalar1=acsT[0:16, cols][:, 127:128], scalar2=None, op0=ALU.mult),
                      reads=[acr, "ident_f"], writes=["S:X16"])
                P.begin_group()
                P.add("pe", lambda e: e.matmul(self.bank(3)[:, 0:16], lhsT=ones_f[0:16, :], rhs=X16[0:16, 0:16], start=True, stop=True),
                      reads=["ones_f", "S:X16"], writes=["bk3"])
                P.add("pe", lambda e, cols=cols, acsT=acsT: e.transpose(self.bank(3)[:, 16:32], acsT[0:16, cols], ident_f[0:16, 0:16]),
                      reads=[acr, "ident_f"], writes=["bk3"])
                P.end_group()
                P.add("act", lambda e, sm=sm: e.activation(out=sm[:, 0:32], in_=self.bank(3)[:, 0:32], func=AF.Exp), reads=["bk3"], writes=[smr])
                yield
                for g in range(2):
                    P.add("dve", lambda e, g=g, MT=MT, segT=segT: e.tensor_tensor(
                        out=MT[:, 8 * g:8 * g + 8, :], in0=self.bank(2)[:, g * 128:(g + 1) * 128].unsqueeze(1).to_broadcast([128, 8, 128]),
                        in1=segT[:, 8 * g:8 * g + 8, :], op=ALU.mult), reads=["bk2"] + segr, writes=[f"S:MT{g}"])
                P.add("dve", lambda e, c=c, xdt=xdt, xdtd=xdtd, segT=segT: e.tensor_tensor(
                    out=xdtd[:, :].rearrange("p (h d) -> p h d", h=16), in0=xdt[:, c, :].rearrange("p (h d) -> p h d", h=16),
                    in1=segT[:, :, 127:128].to_broadcast([128, 16, 64]), op=ALU.mult),
                    reads=[f"S:xdt{i}" for i in range(8)] + segr, writes=["S:xdtd"])
                yield
                P.begin_group()
                for g in range(2):
                    P.add("pe", lambda e, g=g, cols=cols, CcT=CcT: e.matmul(self.PS[2][:, g * 512:(g + 1) * 512], lhsT=CcT[:, g, cols], rhs=prevbf[:, g * 512:(g + 1) * 512],
                                                                           start=True, stop=True), reads=[f"S:Cc{g}", "S:prev"], writes=[self.BK(4 + g)])
                P.end_group()
                P.begin_group()
                for g in range(2):
                    P.add("pe", lambda e, g=g, c=c, Btok=Btok, xdtd=xdtd: e.matmul(self.PS[0][:, g * 512:(g + 1) * 512], lhsT=Btok[:, c, g * 128:(g + 1) * 128],
                                                                                  rhs=xdtd[:, g * 512:(g + 1) * 512], start=True, stop=True),
                          reads=[f"S:Bt{g}", "S:xdtd"], writes=[self.BK(g)])
                P.end_group()
                P.add("dve", lambda e, sm=sm: e.tensor_tensor(out=H[:, :].rearrange("p (h d) -> p h d", h=16), in0=H[:, :].rearrange("p (h d) -> p h d", h=16),
                                                              in1=sm[:, 0:16].unsqueeze(2).to_broadcast([128, 16, 64]), op=ALU.mult),
                      reads=["S:H", smr], writes=["S:H"])
                P.add("dve", lambda e: e.tensor_tensor(out=H[:, :], in0=self.PS[0][:, :], in1=H[:, :], op=ALU.add), reads=["S:H", self.BK(0), self.BK(1)], writes=["S:H"])
                P.add("act", lambda e: e.activation(out=prevbf[:, :], in_=H[:, :], func=AF.Identity), reads=["S:H"], writes=["S:prev"])
                P.begin_group()
                for h in range(16):
                    P.add("pe", lambda e, h=h, c=c, MT=MT, xdt=xdt: e.matmul(self.PS[3][:, h * 64:(h + 1) * 64], lhsT=MT[:, h, :], rhs=xdt[:, c, h * 64:(h + 1) * 64],
                                                                            start=True, stop=True),
                          reads=[f"S:MT{h // 8}", f"S:xdt{h // 2}"], writes=[self.BK(6 + h // 8)])
                P.end_group()
                yield
                P.add("dve", lambda e, yt=yt, sm=sm: e.tensor_tensor(out=yt[:, :].rearrange("p (h d) -> p h d", h=16), in0=self.PS[2][:, :].rearrange("p (h d) -> p h d", h=16),
                                                                     in1=sm[:, 16:32].unsqueeze(2).to_broadcast([128, 16, 64]), op=ALU.mult),
                      reads=[self.BK(4), self.BK(5), smr], writes=[ytr])
                P.add("dve", lambda e, yt=yt: e.tensor_tensor(out=yt[:, :], in0=self.PS[3][:, :], in1=yt[:, :], op=ALU.add), reads=[self.BK(6), self.BK(7), ytr], writes=[ytr])
                for half in range(2):
                    P.add("pool", lambda e, c=c, half=half, xdt=xdt, Ddt=Ddt: e.tensor_tensor(
                        out=junk[:, 0:512].rearrange("p (h d) -> p h d", h=8), in0=xdt[:, c, half * 512:(half + 1) * 512].rearrange("p (h d) -> p h d", h=8),
                        in1=Ddt[:, c, 8 * half:8 * half + 8].unsqueeze(2).to_broadcast([128, 8, 64]), op=ALU.mult),
                        reads=[f"S:xdt{i}" for i in range(8)] + ["S:Ddt"], writes=["S:junk"])
                    P.add("pool", lambda e, yt=yt, half=half: e.tensor_tensor(out=yt[:, half * 512:(half + 1) * 512], in0=yt[:, half * 512:(half + 1) * 512],
                                                                           in1=junk[:, 0:512], op=ALU.add), reads=[ytr, "S:junk"], writes=[ytr])
                P.add("pool", lambda e, yt=yt, c=c, zs=zs: e.tensor_tensor(out=yt[:, :], in0=yt[:, :], in1=zs[:, c, :], op=ALU.mult), reads=[ytr, f"S:zs{c}"], writes=[ytr])
                for g in range(2):
                    P.add("act", lambda e, g=g, yt=yt, sm=sm: e.activation(out=junk[:, 0:512], in_=yt[:, g * 512:(g + 1) * 512], func=AF.Square,
                                                                          accum_out=sm[:, 32 + g:33 + g]), reads=[ytr], writes=[smr + "s", "S:junk"])
                P.add("pool", lambda e, sm=sm: e.tensor_scalar(out=sm[:, 34:36], in0=sm[:, 32:34], scalar1=1.0 / 512.0, scalar2=LN_EPS, op0=ALU.mult, op1=ALU.add),
                      reads=[smr + "s"], writes=[smr + "r"])
                P.add("pool", lambda e, sm=sm: e.tensor_tensor(out=sm[:, 34:36], in0=sm[:, 34:36], in1=self.neghalf[:, 0:1].to_broadcast([128, 2]), op=ALU.pow),
                      reads=[smr + "r", "neghalf"], writes=[smr + "r"])
                for g in range(2):
                    P.add("act", lambda e, g=g, yt=yt, sm=sm: e.activation(out=yt[:, g * 512:(g + 1) * 512], in_=yt[:, g * 512:(g + 1) * 512], func=AF.Identity,
                                                                          scale=sm[:, 34 + g:35 + g]), reads=[ytr, smr + "r"], writes=[ytr])
                yield
                P.begin_group()
                for i in range(8):
                    P.add("pe", lambda e, i=i, yt=yt: e.transpose(self.PS[2][:, i * 128:(i + 1) * 128], yt[:, i * 128:(i + 1) * 128], ident_f[:]),
                          reads=[ytr, "ident_f"], writes=[self.BK(4 + i // 4)])
                P.end_group()
                for half in range(2):
                    P.add("dve", lambda e, half=half, cols=cols, ybT=ybT: e.tensor_tensor(
                        out=ybT[:, 4 * half:4 * half + 4, cols], in0=self.PS[2][:, half * 512:(half + 1) * 512].rearrange("p (i t) -> p i t", i=4),
                        in1=gT[:, 4 * half:4 * half + 4].unsqueeze(2).to_broadcast([128, 4, 128]), op=ALU.mult),
                        reads=[self.BK(4 + half), "S:cst"], writes=[f"S:yb{c}"])

                yield
            gens = [chunk(c) for c in range(4)]
            order = [0, 0, 0, 1, 0, 1, 0, 1, 2, 1, 2, 1, 2, 3, 2, 3, 2, 3, 3, 3]
            for gi in order:
                next(gens[gi])
            for dh in range(2):
                for part in range(2):
                    o_t, o_r = self.wload(wo[dh, part])
                    ov = o_t[:].rearrange("p (k c) -> p k c", k=8)
                    src = yaT if part == 0 else ybT
                    for tbl in range(4):
                        P.begin_group()
                        for kc in range(8):
                            rd = [o_r, (f"S:ya{kc}" if part == 0 else f"S:yb{tbl}")]
                            P.add("pe", lambda e, dh=dh, part=part, tbl=tbl, kc=kc, ov=ov, src=src: e.matmul(
                                self.bank(4 * dh + tbl), lhsT=src[:, kc, tbl * 128:(tbl + 1) * 128], rhs=ov[:, kc, :],
                                start=(part == 0 and kc == 0), stop=(part == 1 and kc == 7)), reads=rd, writes=[self.BK(4 * dh + tbl)])
                        P.end_group()
                for tbl in range(4):
                    gtb = 4 * qi + tbl
                    P.add("dve", lambda e, dh=dh, tbl=tbl, gtb=gtb: e.scalar_tensor_tensor(
                        out=x_tok[:, gtb, dh * 512:(dh + 1) * 512], in0=x_tok[:, gtb, dh * 512:(dh + 1) * 512], scalar=ALPHA,
                        in1=self.bank(4 * dh + tbl), op0=ALU.mult, op1=ALU.add), reads=[self.BK(4 * dh + tbl), f"xt{gtb}_{dh}"], writes=[f"xt{gtb}_{dh}"])
        self.P.fence()
        self.sp_ = base
        self.load_ln(0)
        for tb in range(NTB):
            self.ln_tb(tb)
            self.transpose_tb(tb, tb % 4)


def declare_dram(nc, phases):
    d = {}
    d["x"] = nc.dram_tensor("x", [T, D], F32, kind="ExternalInput").ap()
    d["out"] = nc.dram_tensor("out", [T, D], F32, kind="ExternalOutput").ap()
    d["lnp"] = nc.dram_tensor("lnp", [4, 2, D], F32, kind="ExternalInput").ap()
    d["ffn_cwb"] = nc.dram_tensor("ffn_cwb", [2, 128, 44, 4], F32, kind="ExternalInput").ap()
    d["wup"] = nc.dram_tensor("wup", [2, 11, 128, 4096], F32, kind="ExternalInput").ap()
    d["wdn"] = nc.dram_tensor("wdn", [2, 2, 3, 128, 4096], F32, kind="ExternalInput").ap()
    d["m0_wdt"] = nc.dram_tensor("m0_wdt", [128, 128], F32, kind="ExternalInput").ap()
    d["m0_wx"] = nc.dram_tensor("m0_wx", [3, 128, 4096], F32, kind="ExternalInput").ap()
    d["m0_wz"] = nc.dram_tensor("m0_wz", [2, 128, 4096], F32, kind="ExternalInput").ap()
    d["m0_wsc"] = nc.dram_tensor("m0_wsc", [8, 128, 3072], F32, kind="ExternalInput").ap()
    d["m0_wo"] = nc.dram_tensor("m0_wo", [2, 2, 128, 4096], F32, kind="ExternalInput").ap()
    d["m0_tokc"] = nc.dram_tensor("m0_tokc", [2, 16], F32, kind="ExternalInput").ap()
    d["m0_featc"] = nc.dram_tensor("m0_featc", [128, 92], F32, kind="ExternalInput").ap()
    d["m0_headc"] = nc.dram_tensor("m0_headc", [16, 2], F32, kind="ExternalInput").ap()
    d["fox_f"] = nc.dram_tensor("fox_f", [128, 128], F32, kind="ExternalInput").ap()
    d["fox_bf"] = nc.dram_tensor("fox_bf", [16, 1], F32, kind="ExternalInput").ap()
    d["fox_qk"] = nc.dram_tensor("fox_qk", [8, 128, 2048], F32, kind="ExternalInput").ap()
    d["fox_v"] = nc.dram_tensor("fox_v", [8, 128, 1024], F32, kind="ExternalInput").ap()
    d["fox_wo"] = nc.dram_tensor("fox_wo", [8, 128, 1024], F32, kind="ExternalInput").ap()
    d["augq"] = nc.dram_tensor("augq", [16, 6, T], BF16, kind="Internal").ap()
    d["augk"] = nc.dram_tensor("augk", [16, 6, T], BF16, kind="Internal").ap()
    return d


def build_program(phases=("mix0", "ffn0", "mix1", "ffn1")):
    nc = bass.Bass("TRN2", target_bir_lowering=False)
    dram = declare_dram(nc, phases)
    P = Prog(nc)
    B = Builder(nc, P, dram)
    B.load_x()
    for tb in range(NTB):
        B.transpose_tb(tb, tb % 4)
    last = phases[-1]
    for ph in phases:
        if ph == "ffn0":
            B.ffn(0, final=(ph == last))
        elif ph == "ffn1":
            B.ffn(1, final=(ph == last))
        elif ph == "mix0":
            P.pin = tuple(os.environ.get("MK_PIN0", "dve,pe").split(","))
            B.mix0()
            P.pin = ("dve",)
            if ph == last:
                for tb in range(NTB):
                    B.store_tb(tb)
        elif ph == "mix1":
            B.mix1()
            if ph == last:
                for tb in range(NTB):
                    B.store_tb(tb)
        else:
            raise NotImplementedError(ph)
    if SCHEDULE:
        P.schedule()
    P.finalize(B.out_dmas)
    P.emit(B.out_dmas)
    P.close()
    return nc


def host_layouts(inp):
    f = np.float32
    o = {}
    o["lnp"] = np.ascontiguousarray(np.stack([
        np.stack([inp["ln_mix_g"][0], inp["ln_mix_b"][0]]), np.stack([inp["ln_ffn_g"][0], inp["ln_ffn_b"][0]]),
        np.stack([inp["ln_mix_g"][1], inp["ln_mix_b"][1]]), np.stack([inp["ln_ffn_g"][1], inp["ln_ffn_b"][1]])]).astype(f))
    cw = inp["ffn_conv_w"].astype(f)
    cb = inp["ffn_conv_b"].astype(f)
    cwb = np.concatenate([cw.transpose(0, 2, 1), cb[:, :, None]], axis=2)
    o["ffn_cwb"] = np.ascontiguousarray(cwb.reshape(2, 44, 128, 4).transpose(0, 2, 1, 3))
    wu = inp["ffn_w_up"].astype(f)
    u = wu[:, :, :DFF].reshape(2, 8, 128, 11, 2, 128)
    g = wu[:, :, DFF:].reshape(2, 8, 128, 11, 2, 128)
    ug = np.stack([u, g], axis=5)
    o["wup"] = np.ascontiguousarray(ug.transpose(0, 3, 2, 1, 4, 5, 6).reshape(2, 11, 128, 4096))
    wd = inp["ffn_w_down"].astype(f)
    wdp = np.zeros((2, 24 * 128, 1024), f)
    wdp[:, :DFF] = wd
    wdp = wdp.reshape(2, 3, 8, 128, 2, 512)
    o["wdn"] = np.ascontiguousarray(wdp.transpose(0, 4, 1, 3, 2, 5).reshape(2, 2, 3, 128, 4096))
    w0 = inp["sc_ssm_w_in"][0].astype(f).reshape(8, 128, 5648)
    lay = lambda cols: np.ascontiguousarray(w0[:, :, cols].transpose(1, 0, 2).reshape(128, -1))
    o["m0_wdt"] = lay(slice(5632, 5648))
    o["m0_wx"] = np.stack([lay(slice(5120, 5632)), lay(slice(4096, 4608)), lay(slice(4608, 5120))])
    o["m0_wz"] = np.stack([lay(slice(3072, 3584)), lay(slice(3584, 4096))])
    o["m0_wsc"] = np.stack([lay(np.r_[1024 + 128 * i:1152 + 128 * i, 2048 + 128 * i:2176 + 128 * i, 128 * i:128 + 128 * i]) for i in range(8)])
    wo0 = inp["sc_ssm_w_out"][0].astype(f).reshape(2, 8, 128, 2, 512)
    o["m0_wo"] = np.ascontiguousarray(wo0.transpose(3, 0, 2, 1, 4).reshape(2, 2, 128, 4096))
    o["m0_tokc"] = np.ascontiguousarray(np.stack([inp["ssm_dt_bias"][0], inp["ssm_d"][0]]).astype(f))
    o["m0_headc"] = np.ascontiguousarray(np.stack([inp["ssm_dt_bias"][0], inp["ssm_a_log"][0]], axis=1).astype(f))
    gTh = inp["ssm_norm_g"][0].astype(f).reshape(8, 128).T
    scwh = inp["sc_conv_w"][0].astype(f).reshape(3, 8, 128).transpose(2, 1, 0)
    xw = inp["ssm_conv_w"][0].astype(f).reshape(4, 12, 128).transpose(2, 1, 0)
    xb = inp["ssm_conv_b"][0].astype(f).reshape(12, 128).T[:, :, None]
    o["m0_featc"] = np.ascontiguousarray(np.concatenate([gTh, scwh.reshape(128, 24), np.concatenate([xw, xb], axis=2).reshape(128, 60)], axis=1))
    wi = inp["fox_w_in"][0].astype(f)
    wk = wi.reshape(8, 128, 3088)
    o["fox_f"] = np.ascontiguousarray(wk[:, :, 3072:3088].transpose(1, 0, 2).reshape(128, 128))
    q = wk[:, :, 0:1024].reshape(8, 128, 8, 128)
    k = wk[:, :, 1024:2048].reshape(8, 128, 8, 128)
    v = wk[:, :, 2048:3072].reshape(8, 128, 8, 128)
    qk = np.concatenate([q, k], axis=3)
    o["fox_qk"] = np.ascontiguousarray(qk.transpose(2, 1, 0, 3).reshape(8, 128, 2048))
    o["fox_v"] = np.ascontiguousarray(v.transpose(2, 1, 0, 3).reshape(8, 128, 1024))
    o["fox_wo"] = np.ascontiguousarray(inp["fox_w_out"][0].astype(f).reshape(8, 128, 1024))
    o["fox_bf"] = np.ascontiguousarray(inp["fox_b_f"][0].astype(f).reshape(16, 1))
    return o


_NC_CACHE = {}


def kernel(**inputs):
    phases = ("mix0", "ffn0", "mix1", "ffn1")
    if phases not in _NC_CACHE:
        _NC_CACHE[phases] = build_program(phases)
    nc = _NC_CACHE[phases]
    lay = host_layouts(inputs)
    x = np.asarray(inputs["x"], dtype=np.float32)
    in_maps = [dict(lay, x=np.ascontiguousarray(x[b])) for b in range(8)]
    res = run_bass_kernel_spmd(nc, in_maps, core_ids=list(range(8)))
    return np.stack([np.asarray(r["out"], dtype=np.float32) for r in res.results], axis=0)
```

```python
from contextlib import ExitStack
import numpy as np
import concourse.bass as bass
import concourse.mybir as mybir
from concourse.bass_utils import run_bass_kernel_spmd

F32 = mybir.dt.float32
BF16 = mybir.dt.bfloat16
AF = mybir.ActivationFunctionType
ALU = mybir.AluOpType

COMPUTE = ("pe", "act", "dve", "pool")
QUEUES = ("pe", "act", "dve", "pool", "sp")

ALPHA = 4.0 ** 0.25
LN_EPS = 1e-5
T = 2048
D = 1024
NTB = 16
PAD = 4
DFF = 2816
NJ = 22
import os
SCHEDULE = os.environ.get('MK_SCHED', '1') == '1'
PREFETCH = os.environ.get('MK_PREFETCH', '0') == '1'


class Ins:
    __slots__ = ("eng", "fn", "deps", "idx", "dma_key", "dma_val", "signal", "sigval", "clock", "waits", "is_dma", "pinned")


class Prog:
    def __init__(self, nc):
        self.nc = nc
        self.es = ExitStack()
        self.ins = []
        self.q = {e: [] for e in QUEUES}
        self.last_w = {}
        self.readers = {}
        self.dma_cum = {}
        self.dma_sems = {}
        self.sems = {}
        self.fence_deps = []
        self.scratch_touch = {}
        self.pin = ("dve",)

    def sbuf(self, name, shape, dtype):
        return self.es.enter_context(self.nc.sbuf_tensor(name, list(shape), dtype))

    def psum(self, name, shape, dtype=F32):
        return self.es.enter_context(self.nc.psum_tensor(name, list(shape), dtype))

    def begin_group(self):
        self._grp = []

    def end_group(self):
        g, self._grp = self._grp, None
        fns = [x[0] for x in g]
        reads, writes = [], []
        for _, r, w in g:
            for x in r:
                if x not in reads:
                    reads.append(x)
            for x in w:
                if x not in writes:
                    writes.append(x)

        def run(e, fns=fns):
            h = None
            for f in fns:
                h = f(e)
            return h
        return self.add("pe", run, reads=reads, writes=writes)

    def add(self, eng, fn, reads=(), writes=(), dma_key=None):
        if getattr(self, "_grp", None) is not None:
            assert eng == "pe" and dma_key is None
            self._grp.append((fn, list(reads), list(writes)))
            return None
        i = Ins()
        i.eng = eng
        i.fn = fn
        i.is_dma = dma_key is not None
        i.dma_key = dma_key
        i.signal = False
        i.pinned = eng in self.pin
        deps = set()
        scratch = False
        if any(r.startswith("bk") for r in reads):
            writes = list(writes) + [r for r in reads if r.startswith("bk") and r not in writes]
            reads = [r for r in reads if not r.startswith("bk")]
        for r in reads:
            w = self.last_w.get(r)
            if w is not None:
                deps.add(w)
            if r.startswith("S:"):
                scratch = True
        for w_ in writes:
            w = self.last_w.get(w_)
            if w is not None:
                deps.add(w)
            for rd in self.readers.get(w_, ()):
                deps.add(rd)
            if w_.startswith("S:"):
                scratch = True
        if scratch:
            deps.update(self.fence_deps)
        i.deps = deps
        i.idx = len(self.ins)
        self.ins.append(i)
        self.q[eng].append(i)
        for r in reads:
            self.readers.setdefault(r, []).append(i)
        for w_ in writes:
            self.last_w[w_] = i
            self.readers[w_] = []
        if i.is_dma:
            self.dma_cum[dma_key] = self.dma_cum.get(dma_key, 0) + 16
            i.dma_val = self.dma_cum[dma_key]
        if scratch:
            self.scratch_touch[i.idx] = i
        return i

    def fence(self):
        touched = list(self.scratch_touch.values())
        self.scratch_touch = {}
        if not hasattr(self, "_fdummy"):
            self._fdummy = self.sbuf("fence_dummy", [128, 8], F32)
        fd = self._fdummy
        join = self.add("dve", lambda e: e.memset(fd[:, 0:1], 0.0), writes=["fence_dummy"])
        join.deps.update(touched)
        self.fence_deps = [join]
        for k in [k for k in self.last_w if k.startswith("S:")]:
            del self.last_w[k]
        for k in [k for k in self.readers if k.startswith("S:")]:
            del self.readers[k]


    def schedule(self):
        import heapq

        class _Probe:
            def __init__(self):
                self.recs = []

            def __getattr__(self, name):
                def f(*a, **k):
                    self.recs.append((name, a, k))
                    return None
                return f

        def prod(sh):
            n = 1
            for v in sh:
                n *= int(v)
            return n

        cost, lat = {}, {}
        for i in self.ins:
            p = _Probe()
            i.fn(p)
            name, a, k = p.recs[-1]
            out = k.get("out", a[0] if a else None)
            n = prod(out.shape[1:]) if out is not None and hasattr(out, "shape") else 512
            L = 0.0
            if i.is_dma:
                by = n * out.shape[0] * 4 if out is not None else 0
                c = 0.6 if i.eng == "pool" else 0.15
                L = 2.5 + by / 150e3
            elif i.eng == "pe":
                c = 0.0
                for name, a, k in p.recs:
                    if name == "transpose":
                        c += 0.12
                    else:
                        rhs = k.get("rhs", a[2] if len(a) > 2 else None)
                        nn = prod(rhs.shape[1:]) if rhs is not None else 512
                        lhs = k.get("lhsT", a[1] if len(a) > 1 else None)
                        c1 = 0.035 + max(nn, 64) / 2000.0
                        if lhs is not None and lhs.dtype == F32:
                            c1 *= 4
                        c += c1
            elif i.eng == "act":
                c = 0.22 + n / 1400.0
            elif i.eng == "dve":
                c = 0.12 + n / 960.0
            else:
                c = 0.25 + n / 600.0
            cost[i.idx] = c
            lat[i.idx] = L
        succ = {i.idx: [] for i in self.ins}
        indeg = {}
        import os
        chain = {}
        for e in QUEUES:
            prev = None
            for i in self.q[e]:
                if prev is not None and i.pinned:
                    chain[i.idx] = prev
                prev = i
        for i in self.ins:
            ds = [d for d in i.deps if d is not i]
            if i.idx in chain and chain[i.idx] not in ds:
                ds.append(chain[i.idx])
            indeg[i.idx] = len(ds)
            for d in ds:
                succ[d.idx].append(i)
        byidx = {i.idx: i for i in self.ins}
        pending = {e: [] for e in QUEUES}
        avail = {e: [] for e in QUEUES}
        free = {e: 0.0 for e in QUEUES}
        fin = {}
        ready = {}
        for i in self.ins:
            if indeg[i.idx] == 0:
                ready[i.idx] = 0.0
                heapq.heappush(pending[i.eng], (0.0, i.idx))
        order = []
        newq = {e: [] for e in QUEUES}
        SYNC = 0.12
        n_left = len(self.ins)
        while n_left:
            best = None
            for e in QUEUES:
                pe_, av = pending[e], avail[e]
                while pe_ and pe_[0][0] <= free[e]:
                    r, ix = heapq.heappop(pe_)
                    heapq.heappush(av, ix)
                if av:
                    cand = (free[e], av[0], e, True)
                elif pe_:
                    cand = (pe_[0][0], pe_[0][1], e, False)
                else:
                    continue
                if best is None or cand[:2] < best[:2]:
                    best = cand
            st, ix, e, from_av = best
            if from_av:
                heapq.heappop(avail[e])
            else:
                heapq.heappop(pending[e])
            i = byidx[ix]
            f = st + cost[ix]
            free[e] = f
            fin[ix] = f + lat[ix]
            order.append(i)
            newq[e].append(i)
            n_left -= 1
            for sx in succ[ix]:
                indeg[sx.idx] -= 1
                r = max(ready.get(sx.idx, 0.0), fin[ix] + (0.0 if sx.eng == e and not i.is_dma else SYNC))
                ready[sx.idx] = r
                if indeg[sx.idx] == 0:
                    heapq.heappush(pending[sx.eng], (r, sx.idx))
        self.ins = order
        self.q = newq
        for k, i in enumerate(self.ins):
            i.idx = k
        self.est_us = max(fin.values()) if fin else 0.0

    def finalize(self, tail):
        nc = self.nc
        pos = {}
        for e in QUEUES:
            for k, i in enumerate(self.q[e]):
                pos[i.idx] = k
        prev_clock = {e: ({c: -1 for c in COMPUTE}, frozenset()) for e in QUEUES}
        for i in self.ins:
            clk, dseen = prev_clock[i.eng]
            clk = dict(clk)
            dseen = set(dseen)
            waits = []
            for d in sorted(i.deps, key=lambda d: -d.idx):
                if d is i:
                    continue
                if d.is_dma:
                    if d.idx in dseen:
                        continue
                    waits.append(d)
                    dseen.add(d.idx)
                else:
                    if d.eng == i.eng and d.eng == "pe":
                        continue
                    if clk[d.eng] >= pos[d.idx]:
                        continue
                    waits.append(d)
                    clk[d.eng] = max(clk[d.eng], pos[d.idx])
                dc, dd = d.clock
                for c in COMPUTE:
                    if dc[c] > clk[c]:
                        clk[c] = dc[c]
                dseen |= dd
            final = []
            for d in waits:
                if d.is_dma:
                    final.append(d)
                elif clk[d.eng] == pos[d.idx]:
                    final.append(d)
            i.waits = final
            for d in final:
                d.signal = True
            i.clock = (clk, frozenset(dseen))
            prev_clock[i.eng] = i.clock
        for d in tail:
            d.signal = True
        for e in COMPUTE:
            self.sems[e] = self.es.enter_context(nc.semaphore("s_" + e))
            n = 0
            for i in self.q[e]:
                if i.is_dma:
                    continue
                if i.signal:
                    n += 1
                    i.sigval = n
        for k in self.dma_cum:
            self.dma_sems[k] = self.es.enter_context(nc.semaphore("d_" + str(k).replace(":", "_")))

    def emit(self, tail):
        nc = self.nc
        prog = self

        def wait(eng, d):
            if d.is_dma:
                eng.wait_ge(prog.dma_sems[d.dma_key], d.dma_val)
            else:
                eng.wait_ge(prog.sems[d.eng], d.sigval)

        def run(engname, eng):
            for i in prog.q[engname]:
                for d in i.waits:
                    wait(eng, d)
                h = i.fn(eng)
                if i.is_dma:
                    h.then_inc(prog.dma_sems[i.dma_key], 16)
                elif i.signal:
                    h.then_inc(prog.sems[i.eng], 1)
            if engname == "sp":
                for d in tail:
                    wait(eng, d)

        with nc.Block() as block:
            @block.tensor
            def _(e):
                run("pe", e)

            @block.scalar
            def _(e):
                run("act", e)

            @block.vector
            def _(e):
                run("dve", e)

            @block.gpsimd
            def _(e):
                run("pool", e)

            @block.sync
            def _(e):
                run("sp", e)

    def close(self):
        self.es.close()


class Builder:
    def __init__(self, nc, P, dram):
        self.nc, self.P, self.dram = nc, P, dram
        P_ = P
        self.x_tok = P_.sbuf("x_tok_sb", [128, NTB, D], F32)
        self.xTp = P_.sbuf("xTp", [128, 8, PAD + T], BF16)
        self.ident_f = P_.sbuf("ident_f", [128, 128], F32)
        self.ones_f = P_.sbuf("ones_f", [128, 128], F32)
        self.neghalf = P_.sbuf("neghalf", [128, 1], F32)
        self.lnp = None
        self.stats = P_.sbuf("stats", [128, NTB, 16], F32)
        self.cwb = P_.sbuf("cwb", [128, 44, 4], F32)
        self.fix = P_.sbuf("fix", [128, 44, 2], F32)
        self.ring = [P_.sbuf(f"ring{i}", [128, 4096], BF16) for i in range(3)]
        self.ring_cnt = 0
        self.SW = 20480
        self.S = P_.sbuf("S", [128, self.SW], F32)
        self.sp_ = 0
        self.PS = [P_.psum(f"ps{i}", [128, 1024], F32) for i in range(4)]
        self.out_dmas = []
        self.ident_b = P_.sbuf("ident_b", [128, 128], BF16)
        self.maskneg = P_.sbuf("maskneg", [128, 128], BF16)
        self.small = P_.sbuf("small", [128, 64], F32)
        self.lnT = P_.sbuf("lnT_sb", [128, 16], F32)
        self.rot = {}
        self.consts()

    def rotbank(self, group, banks):
        k = self.rot.get(group, 0)
        self.rot[group] = k + 1
        return banks[k % len(banks)]

    def reset_scratch(self):
        self.P.fence()
        self.sp_ = 0

    def carve(self, words, dtype=F32):
        a = self.sp_
        self.sp_ += words
        assert self.sp_ <= self.SW, (self.sp_, self.SW)
        v = self.S[:, a:a + words]
        if dtype == BF16:
            v = v.bitcast(BF16)
        return v

    def bank(self, b):
        return self.PS[b // 2][:, (b % 2) * 512:(b % 2) * 512 + 512]

    @staticmethod
    def BK(b):
        return f"bk{b}"

    def ring_next(self):
        i = self.ring_cnt % 3
        self.ring_cnt += 1
        return self.ring[i], f"ring{i}"

    def wload(self, src_ap, words=4096):
        tile, res = self.ring_next()
        self.P.add("pool", lambda e, t=tile, s=src_ap, w=words: e.dma_start(out=t[:, 0:w], in_=s, max_dma_last_dim=8192),
                   writes=[res], dma_key=res)
        return tile, res

    def consts(self):
        P = self.P
        ones_f, ident_f = self.ones_f, self.ident_f
        P.add("pool", lambda e: e.memset(ones_f[:], 1.0), writes=["ones_f"])
        P.add("pool", lambda e: e.affine_select(out=ident_f[:], in_=ones_f[:], pattern=[[-1, 128]], compare_op=ALU.is_equal,
                                                 fill=0.0, base=0, channel_multiplier=1), reads=["ones_f"], writes=["ident_f"])
        nh = self.neghalf
        P.add("pool", lambda e: e.memset(nh[:], -0.5), writes=["neghalf"])
        xTp = self.xTp
        P.add("pool", lambda e: e.memset(xTp[:, :, 0:PAD], 0.0), writes=["xTpad"])
        ident_b, maskneg = self.ident_b, self.maskneg
        P.add("pool", lambda e: e.tensor_copy(out=ident_b[:], in_=ident_f[:]), reads=["ident_f"], writes=["ident_b"])
        P.add("pool", lambda e: e.memset(maskneg[:], -30000.0), writes=["maskneg"])
        P.add("pool", lambda e: e.affine_select(out=maskneg[:], in_=maskneg[:], pattern=[[-1, 128]], compare_op=ALU.is_gt,
                                                 fill=0.0, base=0, channel_multiplier=1), reads=["maskneg"], writes=["maskneg"])

    def load_x(self):
        P, x_tok = self.P, self.x_tok
        xv = self.dram["x"].rearrange("(tb p) d -> p tb d", p=128)
        for g in range(4):
            P.add("sp", lambda e, g=g: e.dma_start(out=x_tok[:, 4 * g:4 * g + 4, :], in_=xv[:, 4 * g:4 * g + 4, :]),
                  writes=[f"xt{tb}_{dh}" for tb in range(4 * g, 4 * g + 4) for dh in range(2)], dma_key=f"xin{g}")

    def store_tb(self, tb):
        P, x_tok = self.P, self.x_tok
        ov = self.dram["out"].rearrange("(tb p) d -> p tb d", p=128)
        i = P.add("sp", lambda e: e.dma_start(out=ov[:, tb, :], in_=x_tok[:, tb, :]), reads=[f"xt{tb}_0", f"xt{tb}_1"], dma_key=f"xout{tb}")
        self.out_dmas.append(i)

    def transpose_tb(self, tb, pst, affine=False):
        P, x_tok, xTp, ident_f, lnT = self.P, self.x_tok, self.xTp, self.ident_f, self.lnT
        ps = self.PS[pst]
        P.begin_group()
        for kc in range(8):
            P.add("pe", lambda e, kc=kc: e.transpose(ps[:, kc * 128:(kc + 1) * 128], x_tok[:, tb, kc * 128:(kc + 1) * 128], ident_f[:]),
                  reads=[f"xt{tb}_{kc // 4}", "ident_f"], writes=[self.BK(2 * pst + kc // 4)])
        P.end_group()
        c0 = PAD + tb * 128
        if affine:
            for kc in range(8):
                if kc < 4:
                    P.add("act", lambda e, kc=kc: e.activation(out=xTp[:, kc, c0:c0 + 128], in_=ps[:, kc * 128:(kc + 1) * 128], func=AF.Identity,
                                                               scale=lnT[:, kc:kc + 1], bias=lnT[:, 8 + kc:9 + kc]),
                          reads=[self.BK(2 * pst), "lnT"], writes=[f"xT{tb}"])
                else:
                    P.add("dve", lambda e, kc=kc: e.tensor_scalar(out=xTp[:, kc, c0:c0 + 128], in0=ps[:, kc * 128:(kc + 1) * 128],
                                                                  scalar1=lnT[:, kc:kc + 1], scalar2=lnT[:, 8 + kc:9 + kc], op0=ALU.mult, op1=ALU.add),
                          reads=[self.BK(2 * pst + 1), "lnT"], writes=[f"xT{tb}"])
            return
        P.add("act", lambda e: e.activation(out=xTp[:, 0:4, c0:c0 + 128], in_=ps[:, 0:512].rearrange("p (k t) -> p k t", k=4), func=AF.Identity),
              reads=[self.BK(2 * pst)], writes=[f"xT{tb}"])
        P.add("dve", lambda e: e.tensor_copy(out=xTp[:, 4:8, c0:c0 + 128], in_=ps[:, 512:1024].rearrange("p (k t) -> p k t", k=4)),
              reads=[self.BK(2 * pst + 1)], writes=[f"xT{tb}"])

    def load_lnT(self, idx):
        lnT = self.lnT
        src = self.dram["lnpT"][idx]
        self.P.add("sp", lambda e: e.dma_start(out=lnT[:], in_=src), writes=["lnT"], dma_key="lnT")

    def load_ln(self, idx, with_T=True):
        if with_T:
            self.load_lnT(idx)
        self.lnp = self.carve(2 * D).rearrange("p (a d) -> p a d", a=2)
        lnp = self.lnp
        src = self.dram["lnp"][idx].partition_broadcast(128)
        self.P.add("sp", lambda e: e.dma_start(out=lnp[:], in_=src), writes=["S:lnp"], dma_key="lnp")

    def ln_tb(self, tb):
        P, x_tok, st, lnp, nh = self.P, self.x_tok, self.stats, self.lnp, self.neghalf
        R = [f"xt{tb}_0", f"xt{tb}_1"]
        sr = f"st{tb}"
        P.add("dve", lambda e: e.bn_stats(out=st[:, tb, 0:6], in_=x_tok[:, tb, 0:512]), reads=[R[0]], writes=[sr])
        P.add("dve", lambda e: e.bn_stats(out=st[:, tb, 6:12], in_=x_tok[:, tb, 512:1024]), reads=[R[1]], writes=[sr])
        P.add("dve", lambda e: e.bn_aggr(out=st[:, tb, 12:14], in_=st[:, tb, 0:12]), reads=[sr], writes=[sr])
        P.add("pool", lambda e: e.tensor_scalar(out=st[:, tb, 14:15], in0=st[:, tb, 13:14], scalar1=LN_EPS, scalar2=None, op0=ALU.add),
              reads=[sr], writes=[sr])
        P.add("pool", lambda e: e.tensor_tensor(out=st[:, tb, 14:15], in0=st[:, tb, 14:15], in1=nh[:], op=ALU.pow),
              reads=[sr, "neghalf"], writes=[sr])
        P.add("dve", lambda e: e.tensor_scalar(out=st[:, tb, 15:16], in0=st[:, tb, 12:13], scalar1=st[:, tb, 14:15], scalar2=-1.0,
                                               op0=ALU.mult, op1=ALU.mult), reads=[sr], writes=[sr])
        P.add("act", lambda e: e.activation(out=x_tok[:, tb, :], in_=x_tok[:, tb, :], func=AF.Identity,
                                            scale=st[:, tb, 14:15], bias=st[:, tb, 15:16]), reads=[sr] + R, writes=R)

    def ln_affine(self, tb):
        P, x_tok, lnp = self.P, self.x_tok, self.lnp
        R = [f"xt{tb}_0", f"xt{tb}_1"]
        P.add("pool", lambda e: e.tensor_tensor(out=x_tok[:, tb, :], in0=x_tok[:, tb, :], in1=lnp[:, 0, :], op=ALU.mult),
              reads=R + ["S:lnp"], writes=R)
        P.add("pool", lambda e: e.tensor_tensor(out=x_tok[:, tb, :], in0=x_tok[:, tb, :], in1=lnp[:, 1, :], op=ALU.add),
              reads=R + ["S:lnp"], writes=R)

    def ffn(self, l, final=False):
        P, xTp, x_tok, cwb, fix = self.P, self.xTp, self.x_tok, self.cwb, self.fix
        self.reset_scratch()
        hid = self.carve(NJ * 1024 // 2, BF16).rearrange("p (j t) -> p j t", j=NJ)
        tmp = [[self.carve(1024), self.carve(1024)] for _ in range(2)]
        cwsrc = self.dram["ffn_cwb"][l]
        P.add("sp", lambda e: e.dma_start(out=cwb[:], in_=cwsrc), writes=["cwb"], dma_key="cwb")
        self.load_ln(2 * l + 1)
        wup, wdn = self.dram["wup"], self.dram["wdn"]
        for h in range(2):
            c0 = PAD + 1024 * h
            xres = [f"xT{tb}" for tb in range(8 * h, 8 * h + 8)] + ["xTpad"]
            pre = {}
            PD = int(os.environ.get('MK_PD', '1'))
            if PREFETCH:
                for s0 in range(PD):
                    pre[s0] = self.wload(wup[l, s0])
            for s in range(11):
                if PREFETCH:
                    tile, res = pre[s]
                    if s + PD < 11:
                        pre[s + PD] = self.wload(wup[l, s + PD])
                else:
                    tile, res = self.wload(wup[l, s])
                sv = tile[:].rearrange("p (k j c) -> p k j c", k=8, j=2)
                for jj in range(2):
                    j = 2 * s + jj
                    pset = j % 2
                    for ug in range(2):
                        pst = 2 * pset + ug
                        ps = self.PS[pst]
                        for t in range(2):
                            P.begin_group()
                            for kc in range(8):
                                P.add("pe", lambda e, ps=ps, t=t, kc=kc, jj=jj, ug=ug, sv=sv, c0=c0: e.matmul(
                                    ps[:, t * 512:(t + 1) * 512], lhsT=sv[:, kc, jj, ug * 128:(ug + 1) * 128],
                                    rhs=xTp[:, kc, c0 + t * 512:c0 + (t + 1) * 512], start=(kc == 0), stop=(kc == 7)),
                                    reads=[res] + xres, writes=[self.BK(2 * pst + t)])
                            P.end_group()
                    for ug in range(2):
                        pst = 2 * pset + ug
                        ps = self.PS[pst]
                        a = tmp[pset][ug]
                        ar = f"S:a{pset}{ug}"
                        ch = ug * NJ + j
                        bks = [self.BK(2 * pst), self.BK(2 * pst + 1)]
                        P.add("act", lambda e, ps=ps, a=a, ch=ch: e.activation(out=a[:, 0:1024], in_=ps[:, 0:1024], func=AF.Identity,
                                                                                scale=cwb[:, ch, 2:3], bias=cwb[:, ch, 3:4]),
                              reads=bks + ["cwb"], writes=[ar])
                        P.add("dve", lambda e, ps=ps, a=a, ch=ch: e.scalar_tensor_tensor(
                            out=a[:, 1:1024], in0=ps[:, 0:1023], scalar=cwb[:, ch, 1:2], in1=a[:, 1:1024], op0=ALU.mult, op1=ALU.add),
                            reads=bks + ["cwb", ar], writes=[ar])
                        P.add("dve", lambda e, ps=ps, a=a, ch=ch: e.scalar_tensor_tensor(
                            out=a[:, 2:1024], in0=ps[:, 0:1022], scalar=cwb[:, ch, 0:1], in1=a[:, 2:1024], op0=ALU.mult, op1=ALU.add),
                            reads=bks + ["cwb", ar], writes=[ar])
                        if h == 0:
                            P.add("dve", lambda e, ps=ps, ch=ch: e.tensor_scalar(out=fix[:, ch, 0:2], in0=ps[:, 1022:1024], scalar1=cwb[:, ch, 0:1],
                                                                                 scalar2=None, op0=ALU.mult), reads=bks + ["cwb"], writes=[f"fix{ch}"])
                            P.add("dve", lambda e, ps=ps, ch=ch: e.scalar_tensor_tensor(
                                out=fix[:, ch, 0:1], in0=ps[:, 1023:1024], scalar=cwb[:, ch, 1:2], in1=fix[:, ch, 0:1], op0=ALU.mult, op1=ALU.add),
                                reads=bks + ["cwb", f"fix{ch}"], writes=[f"fix{ch}"])
                        else:
                            P.add("dve", lambda e, a=a, ch=ch: e.tensor_tensor(out=a[:, 0:2], in0=a[:, 0:2], in1=fix[:, ch, 0:2], op=ALU.add),
                                  reads=[ar, f"fix{ch}"], writes=[ar])
                    au, ag = tmp[pset]
                    P.add("act", lambda e, ag=ag: e.activation(out=ag[:, 0:1024], in_=ag[:, 0:1024], func=AF.Silu),
                          reads=[f"S:a{pset}1"], writes=[f"S:a{pset}1"])
                    P.add("pool", lambda e, au=au, ag=ag, j=j: e.tensor_tensor(out=hid[:, j, :], in0=au[:, 0:1024], in1=ag[:, 0:1024], op=ALU.mult),
                          reads=[f"S:a{pset}0", f"S:a{pset}1"], writes=[f"S:hid{j}"])
            for dh in range(2):
                for g in range(3):
                    tile, res = self.wload(wdn[l, dh, g])
                    sv = tile[:].rearrange("p (j c) -> p j c", j=8)
                    for jj in range(8 if g < 2 else 6):
                        j = 8 * g + jj
                        for tb in range(8):
                            P.add("pe", lambda e, tb=tb, j=j, jj=jj, sv=sv: e.matmul(
                                self.bank(tb), lhsT=hid[:, j, tb * 128:(tb + 1) * 128], rhs=sv[:, jj, :], start=(j == 0), stop=(j == NJ - 1)),
                                reads=[res, f"S:hid{j}"], writes=[self.BK(tb)])
                for tb in range(8):
                    gtb = 8 * h + tb
                    P.add("dve", lambda e, tb=tb, gtb=gtb, dh=dh: e.scalar_tensor_tensor(
                        out=x_tok[:, gtb, dh * 512:(dh + 1) * 512], in0=x_tok[:, gtb, dh * 512:(dh + 1) * 512], scalar=ALPHA,
                        in1=self.bank(tb), op0=ALU.mult, op1=ALU.add), reads=[self.BK(tb), f"xt{gtb}_{dh}"], writes=[f"xt{gtb}_{dh}"])
            for tb in range(8):
                gtb = 8 * h + tb
                self.ln_tb(gtb)
                if not final:
                    self.transpose_tb(gtb, tb // 2, affine=True)
                self.ln_affine(gtb)
                if final:
                    self.store_tb(gtb)


    def mix1(self):
        P, xTp, x_tok, dram = self.P, self.xTp, self.x_tok, self.dram
        ones_f, ident_b, maskneg, small = self.ones_f, self.ident_b, self.maskneg, self.small
        xall = [f"xT{tb}" for tb in range(NTB)]
        self.reset_scratch()
        fl = self.carve(2048)
        cum = self.carve(2048)
        ones = self.carve(512)
        QG = self.carve(6 * 2048 // 2, BF16).rearrange("p (r t) -> p r t", r=6)
        KG = self.carve(6 * 2048 // 2, BF16).rearrange("p (r t) -> p r t", r=6)
        bsrc = dram["fox_bf"]
        P.add("sp", lambda e: e.dma_start(out=small[0:16, 0:1], in_=bsrc), writes=["small"], dma_key="small")
        P.add("dve", lambda e: e.tensor_scalar(out=small[0:16, 1:2], in0=small[0:16, 0:1], scalar1=-1.0, scalar2=None, op0=ALU.mult),
              reads=["small"], writes=["small"])
        P.add("pool", lambda e: e.memset(ones[0:16, :], 1.0), writes=["S:ones"])
        P.add("pool", lambda e: e.memset(QG[0:16, 3:6, :], 1.0), writes=["S:QG1"])
        P.add("pool", lambda e: e.memset(KG[0:16, 0:3, :], 1.0), writes=["S:KG1"])
        tile, res = self.wload(dram["fox_f"], words=128)
        fv = tile[:, 0:128].rearrange("p (k c) -> p k c", k=8)
        for t in range(4):
            P.begin_group()
            for kc in range(8):
                P.add("pe", lambda e, t=t, kc=kc: e.matmul(self.bank(t)[0:16, :], lhsT=fv[:, kc, :], rhs=xTp[:, kc, PAD + t * 512:PAD + (t + 1) * 512],
                                                           start=(kc == 0), stop=(kc == 7)), reads=[res] + xall, writes=[self.BK(t)])
            P.end_group()
            P.add("act", lambda e, t=t: e.activation(out=fl[0:16, t * 512:(t + 1) * 512], in_=self.bank(t)[0:16, :], func=AF.Exp,
                                                     scale=-1.0, bias=small[0:16, 1:2]), reads=[self.BK(t), "small"], writes=[f"S:fl{t}"])
        for t in range(4):
            P.add("act", lambda e, t=t: e.activation(out=fl[0:16, t * 512:(t + 1) * 512], in_=fl[0:16, t * 512:(t + 1) * 512], func=AF.Ln,
                                                     scale=1.0, bias=1.0), reads=[f"S:fl{t}"], writes=[f"S:fl{t}"])
        for t in range(4):
            init = 0.0 if t == 0 else cum[0:16, t * 512 - 1:t * 512]
            P.add("dve", lambda e, t=t, init=init: e.tensor_tensor_scan(out=cum[0:16, t * 512:(t + 1) * 512], data0=ones[0:16, :],
                                                                        data1=fl[0:16, t * 512:(t + 1) * 512], initial=init,
                                                                        op0=ALU.mult, op1=ALU.subtract),
                  reads=[f"S:fl{t}", "S:ones", "S:cum"], writes=["S:cum"])
        for r in range(3):
            P.add("dve", lambda e, r=r: e.tensor_copy(out=QG[0:16, r, :], in_=cum[0:16, :]), reads=["S:cum"], writes=[f"S:QG0{r}"])
            if r < 2:
                P.add("dve", lambda e, r=r: e.tensor_tensor(out=cum[0:16, :], in0=cum[0:16, :], in1=QG[0:16, r, :], op=ALU.subtract),
                      reads=["S:cum", f"S:QG0{r}"], writes=["S:cum"])
        P.add("dve", lambda e: e.tensor_scalar(out=KG[0:16, 3:6, :], in0=QG[0:16, 0:3, :], scalar1=-1.0, scalar2=None, op0=ALU.mult),
              reads=["S:QG00", "S:QG01", "S:QG02"], writes=["S:KG0"])
        gq, gk = dram["augq"], dram["augk"]
        P.add("sp", lambda e: e.dma_start(out=gq, in_=QG[0:16, :, :]), reads=["S:QG00", "S:QG01", "S:QG02", "S:QG1"], writes=["augq"], dma_key="augq")
        P.add("sp", lambda e: e.dma_start(out=gk, in_=KG[0:16, :, :]), reads=["S:KG0", "S:KG1"], writes=["augk"], dma_key="augk")
        self.reset_scratch()
        AUG = [[[self.carve(1024, BF16) for qk in range(2)] for sub in range(2)] for st in range(2)]
        VP = [self.carve(1040, BF16).rearrange("p (t s d) -> p t s d", t=16, s=2) for st in range(2)]
        OTP = [self.carve(1024, BF16) for st in range(2)]
        PT = [self.carve(256, BF16) for _ in range(4)]
        PTD = [self.carve(256, BF16) for _ in range(4)]
        rc = self.carve(512)
        bcs = self.carve(512)
        self.load_ln(2)
        for st in range(2):
            P.add("pool", lambda e, st=st: e.memset(VP[st][:, :, :, 64:65], 1.0), writes=[f"S:VPone{st}"])
        for i4 in range(1, 4):
            P.add("pool", lambda e, i4=i4: e.memset(PTD[i4][:, 0:128 * i4], 0.0), writes=[f"S:PTD{i4}"])
        ptc = 0
        for hp in range(8):
            st = hp % 2
            qk_t, qk_r = self.wload(dram["fox_qk"][hp], words=2048)
            v_t, v_r = self.wload(dram["fox_v"][hp], words=1024)
            wo_t, wo_r = self.wload(dram["fox_wo"][hp], words=1024)
            qkv = qk_t[:, 0:2048].rearrange("p (k c) -> p k c", k=8)
            vv = v_t[:, 0:1024].rearrange("p (k c) -> p k c", k=8)
            for sub in range(2):
                h = 2 * hp + sub
                P.add("sp", lambda e, st=st, sub=sub, h=h: e.dma_start(out=AUG[st][sub][0][64:70, :], in_=gq[h]),
                      reads=["augq"], writes=[f"S:AQa{st}{sub}"], dma_key=f"aq{st}{sub}")
                P.add("sp", lambda e, st=st, sub=sub, h=h: e.dma_start(out=AUG[st][sub][1][64:70, :], in_=gk[h]),
                      reads=["augk"], writes=[f"S:AKa{st}{sub}"], dma_key=f"ak{st}{sub}")
            for t in range(4):
                for qk in range(2):
                    b = self.rotbank("misc", (0, 1, 7))
                    P.begin_group()
                    for kc in range(8):
                        P.add("pe", lambda e, b=b, kc=kc, qk=qk, t=t, qkv=qkv: e.matmul(
                            self.bank(b), lhsT=qkv[:, kc, qk * 128:(qk + 1) * 128], rhs=xTp[:, kc, PAD + t * 512:PAD + (t + 1) * 512],
                            start=(kc == 0), stop=(kc == 7)), reads=[qk_r] + xall, writes=[self.BK(b)])
                    P.end_group()
                    for sub in range(2):
                        dst = AUG[st][sub][qk]
                        nm = f"S:A{'QK'[qk]}{st}{sub}t{t}"
                        if qk == 0:
                            P.add("dve", lambda e, b=b, sub=sub, dst=dst, t=t: e.tensor_scalar(
                                out=dst[0:64, t * 512:(t + 1) * 512], in0=self.bank(b)[sub * 64:(sub + 1) * 64, :], scalar1=0.125, scalar2=None,
                                op0=ALU.mult), reads=[self.BK(b)], writes=[nm])
                        else:
                            P.add("dve", lambda e, b=b, sub=sub, dst=dst, t=t: e.tensor_copy(
                                out=dst[0:64, t * 512:(t + 1) * 512], in_=self.bank(b)[sub * 64:(sub + 1) * 64, :]),
                                reads=[self.BK(b)], writes=[nm])
            for g4 in range(4):
                b = self.rotbank("misc", (0, 1, 7))
                P.begin_group()
                for ti in range(4):
                    tb = 4 * g4 + ti
                    for kc in range(8):
                        P.add("pe", lambda e, b=b, ti=ti, tb=tb, kc=kc, vv=vv: e.matmul(
                            self.bank(b)[:, ti * 128:(ti + 1) * 128], lhsT=xTp[:, kc, PAD + tb * 128:PAD + (tb + 1) * 128], rhs=vv[:, kc, :],
                            start=(kc == 0), stop=(kc == 7)), reads=[v_r, f"xT{tb}"], writes=[self.BK(b)])
                P.end_group()
                P.add("act", lambda e, b=b, g4=g4, st=st: e.activation(
                    out=VP[st][:, 4 * g4:4 * g4 + 4, :, 0:64], in_=self.bank(b).rearrange("p (t s d) -> p t s d", t=4, s=2), func=AF.Identity),
                    reads=[self.BK(b)], writes=[f"S:VP{st}g{g4}"])
            for sub in range(2):
                QA, KA = AUG[st][sub][0], AUG[st][sub][1]
                for qt in range(4):
                    ob = self.rotbank("O", (2, 3))
                    nkb = 4 * qt + 4
                    for kb in range(nkb):
                        i = kb - 4 * qt
                        co = 128 * i if i > 0 else 0
                        sb_ = self.rotbank("S", (4, 5, 6))
                        if i >= 0:
                            pt, ptr = PTD[i], f"S:PTD{i}"
                        else:
                            pt, ptr = PT[ptc % 4], f"S:PT{ptc % 4}"
                            ptc += 1
                        kres = [f"S:AK{st}{sub}t{kb // 4}", f"S:AKa{st}{sub}", f"S:AQ{st}{sub}t{qt}", f"S:AQa{st}{sub}"]
                        P.begin_group()
                        if i < 0:
                            P.add("pe", lambda e, sb_=sb_, kb=kb, qt=qt, QA=QA, KA=KA: e.matmul(
                                self.bank(sb_)[:, 0:512], lhsT=KA[0:70, kb * 128:(kb + 1) * 128], rhs=QA[0:70, qt * 512:(qt + 1) * 512],
                                start=True, stop=True), reads=kres, writes=[self.BK(sb_)])
                        else:
                            P.add("pe", lambda e, sb_=sb_, co=co, kb=kb, qt=qt, QA=QA, KA=KA: e.matmul(
                                self.bank(sb_)[:, co:co + 128], lhsT=KA[0:70, kb * 128:(kb + 1) * 128], rhs=QA[0:70, qt * 512 + co:qt * 512 + co + 128],
                                start=True, stop=False), reads=kres, writes=[self.BK(sb_)])
                            P.add("pe", lambda e, sb_=sb_, co=co: e.matmul(self.bank(sb_)[:, co:co + 128], lhsT=ident_b[:], rhs=maskneg[:],
                                                                           start=False, stop=True),
                                  reads=["ident_b", "maskneg"], writes=[self.BK(sb_)])
                            if co + 128 < 512:
                                P.add("pe", lambda e, sb_=sb_, co=co, kb=kb, qt=qt, QA=QA, KA=KA: e.matmul(
                                    self.bank(sb_)[:, co + 128:512], lhsT=KA[0:70, kb * 128:(kb + 1) * 128], rhs=QA[0:70, qt * 512 + co + 128:(qt + 1) * 512],
                                    start=True, stop=True), reads=kres, writes=[self.BK(sb_)])
                        P.end_group()
                        P.add("act", lambda e, sb_=sb_, co=co, pt=pt: e.activation(out=pt[:, co:512], in_=self.bank(sb_)[:, co:512], func=AF.Exp),
                              reads=[self.BK(sb_)], writes=[ptr])
                        P.add("pe", lambda e, ob=ob, kb=kb, pt=pt, st=st, sub=sub, nkb=nkb: e.matmul(
                            self.bank(ob)[0:65, 0:512], lhsT=VP[st][:, kb, sub, 0:65], rhs=pt[:, 0:512], start=(kb == 0), stop=(kb == nkb - 1)),
                            reads=[ptr, f"S:VP{st}g{kb // 4}", f"S:VPone{st}"], writes=[self.BK(ob)])
                    P.add("dve", lambda e, ob=ob: e.reciprocal(out=rc[64:65, :], in_=self.bank(ob)[64:65, :]), reads=[self.BK(ob)], writes=["S:rc"])
                    bb = self.rotbank("misc", (0, 1, 7))
                    P.add("pe", lambda e, bb=bb: e.matmul(self.bank(bb)[0:64, :], lhsT=ones_f[64:65, 0:64], rhs=rc[64:65, :], start=True, stop=True),
                          reads=["ones_f", "S:rc"], writes=[self.BK(bb)])
                    P.add("dve", lambda e, bb=bb: e.tensor_copy(out=bcs[0:64, :], in_=self.bank(bb)[0:64, :]), reads=[self.BK(bb)], writes=["S:bcs"])
                    P.add("dve", lambda e, ob=ob, st=st, sub=sub, qt=qt: e.tensor_tensor(
                        out=OTP[st][sub * 64:(sub + 1) * 64, qt * 512:(qt + 1) * 512], in0=self.bank(ob)[0:64, :], in1=bcs[0:64, :], op=ALU.mult),
                        reads=[self.BK(ob), "S:bcs"], writes=[f"S:OT{st}q{qt}"])
            for tb in range(NTB):
                for dh in range(2):
                    b = self.rotbank("misc", (0, 1, 7))
                    P.add("pe", lambda e, b=b, tb=tb, dh=dh, st=st, wo_t=wo_t: e.matmul(
                        self.bank(b), lhsT=OTP[st][:, tb * 128:(tb + 1) * 128], rhs=wo_t[:, dh * 512:(dh + 1) * 512], start=True, stop=True),
                        reads=[wo_r, f"S:OT{st}q{tb // 4}"], writes=[self.BK(b)])
                    xr = f"xt{tb}_{dh}"
                    if hp == 0:
                        P.add("dve", lambda e, b=b, tb=tb, dh=dh: e.scalar_tensor_tensor(
                            out=x_tok[:, tb, dh * 512:(dh + 1) * 512], in0=x_tok[:, tb, dh * 512:(dh + 1) * 512], scalar=ALPHA,
                            in1=self.bank(b), op0=ALU.mult, op1=ALU.add), reads=[self.BK(b), xr], writes=[xr])
                    else:
                        P.add("dve", lambda e, b=b, tb=tb, dh=dh: e.tensor_tensor(
                            out=x_tok[:, tb, dh * 512:(dh + 1) * 512], in0=self.bank(b), in1=x_tok[:, tb, dh * 512:(dh + 1) * 512], op=ALU.add),
                            reads=[self.BK(b), xr], writes=[xr])
        for tb in range(NTB):
            self.ln_tb(tb)
            self.transpose_tb(tb, tb % 4, affine=True)
            self.ln_affine(tb)


    def mix0(self):
        P, xTp, x_tok, dram = self.P, self.xTp, self.x_tok, self.dram
        ones_f, ident_f, ident_b, maskneg = self.ones_f, self.ident_f, self.ident_b, self.maskneg
        self.reset_scratch()
        H = self.carve(1024)
        prevbf = self.carve(512, BF16)
        R = self.carve(2048)
        L = [self.carve(128), self.carve(128)]
        halo = self.carve(16).rearrange("p (i k) -> p i k", i=8)
        fixx = self.carve(36).rearrange("p (i k) -> p i k", i=12)
        cst = self.carve(128)
        biasbc, Dbc = cst[:, 0:16], cst[:, 16:32]
        gT = cst[:, 32:40]
        scw = cst[:, 40:64].rearrange("p (i k) -> p i k", i=8)
        xcw = cst[:, 64:124].rearrange("p (i k) -> p i k", i=12)
        s_tok, s_feat, s_head = dram["m0_tokc"].rearrange("a h -> (a h)").partition_broadcast(128), dram["m0_featc"], dram["m0_headc"]
        P.add("sp", lambda e: e.dma_start(out=cst[:, 0:32], in_=s_tok), writes=["S:cst"], dma_key="m0c0")
        P.add("sp", lambda e: e.dma_start(out=cst[:, 32:124], in_=s_feat), writes=["S:cst"], dma_key="m0c1")
        P.add("sp", lambda e: e.dma_start(out=cst[0:16, 124:126], in_=s_head), writes=["S:cst"], dma_key="m0c2")
        P.add("act", lambda e: e.activation(out=cst[0:16, 126:127], in_=cst[0:16, 125:126], func=AF.Exp), reads=["S:cst"], writes=["S:cst"])
        P.add("dve", lambda e: e.tensor_scalar(out=cst[0:16, 127:128], in0=cst[0:16, 126:127], scalar1=-1.0, scalar2=None, op0=ALU.mult),
              reads=["S:cst"], writes=["S:cst"])
        dtb, acol = cst[0:16, 124:125], cst[0:16, 127:128]
        P.add("pool", lambda e: e.memset(H[:, :], 0.0), writes=["S:H"])
        P.add("pool", lambda e: e.memset(prevbf[:, :], 0.0), writes=["S:prev"])
        P.add("pool", lambda e: e.memset(R[0:48, :], 0.0), writes=["S:R"])
        P.add("pool", lambda e: e.memset(R[32:48, :], 1.0), reads=["S:R"], writes=["S:R"])
        P.add("pool", lambda e: e.affine_select(out=R[32:48, :].rearrange("p (h t) -> p h t", h=16), in_=R[32:48, :].rearrange("p (h t) -> p h t", h=16),
                                                 pattern=[[-1, 16], [0, 128]], compare_op=ALU.is_equal, fill=0.0, base=0, channel_multiplier=1),
              reads=["S:R"], writes=["S:R"])
        for k in range(2):
            P.add("pool", lambda e, k=k: e.memset(L[k][0:48, :], 0.0), writes=[f"S:L{k}"])
            P.add("pool", lambda e, k=k: e.memset(L[k][0:16, :], 1.0), reads=[f"S:L{k}"], writes=[f"S:L{k}"])
        base = self.sp_
        self.load_lnT(0)
        wx, wz, wsc, wo = dram["m0_wx"], dram["m0_wz"], dram["m0_wsc"], dram["m0_wo"]

        for qi in range(4):
            self.P.fence()
            self.sp_ = base
            yaT = self.carve(2048, BF16).rearrange("p (i t) -> p i t", i=8)
            ybT = self.carve(2048, BF16).rearrange("p (i t) -> p i t", i=8)
            xdt = self.carve(2048, BF16).rearrange("p (b f) -> p b f", b=4)
            zs = self.carve(2048, BF16).rearrange("p (b f) -> p b f", b=4)
            BcT = self.carve(512, BF16).rearrange("p (g t) -> p g t", g=2)
            CcT = self.carve(512, BF16).rearrange("p (g t) -> p g t", g=2)
            Btok = self.carve(512, BF16).rearrange("p (b f) -> p b f", b=4)
            dtok = self.carve(64).rearrange("p (b h) -> p b h", b=4)
            Ddt = self.carve(64).rearrange("p (b h) -> p b h", b=4)
            acsT = self.carve(512)
            dtT = self.carve(512)
            qbase = self.sp_
            a_ = [self.carve(512), self.carve(512)]
            prod = [self.carve(516), self.carve(516)]
            cs = [self.carve(512), self.carve(512)]
            c0 = PAD + 512 * qi
            xres = [f"xT{tb}" for tb in range(4 * qi, 4 * qi + 4)]
            win = lambda kc, c0=c0: xTp[:, kc, c0:c0 + 512]

            dt_t, dt_r = self.wload(dram["m0_wdt"], words=128)
            dv = dt_t[:, 0:128].rearrange("p (k c) -> p k c", k=8)
            P.begin_group()
            for kc in range(8):
                P.add("pe", lambda e, kc=kc, dv=dv, win=win: e.matmul(self.bank(0)[0:16, :], lhsT=dv[:, kc, :], rhs=win(kc), start=(kc == 0), stop=(kc == 7)),
                      reads=[dt_r] + xres, writes=[self.BK(0)])
            P.end_group()
            P.begin_group()
            for tbl in range(4):
                for kc in range(8):
                    P.add("pe", lambda e, kc=kc, tbl=tbl, dv=dv, c0=c0: e.matmul(self.bank(1)[:, tbl * 16:(tbl + 1) * 16],
                                                                               lhsT=xTp[:, kc, c0 + tbl * 128:c0 + (tbl + 1) * 128], rhs=dv[:, kc, :],
                                                                               start=(kc == 0), stop=(kc == 7)),
                          reads=[dt_r] + xres, writes=[self.BK(1)])
            P.end_group()
            P.add("act", lambda e, dtT=dtT: e.activation(out=dtT[0:16, :], in_=self.bank(0)[0:16, :], func=AF.Exp, bias=dtb, scale=1.0),
                  reads=[self.BK(0), "S:cst"], writes=["S:dtT"])
            P.add("dve", lambda e, dtok=dtok: e.tensor_tensor(out=dtok[:, :, :], in0=self.bank(1)[:, 0:64].rearrange("p (b h) -> p b h", b=4),
                                                             in1=biasbc.unsqueeze(1).to_broadcast([128, 4, 16]), op=ALU.add),
                  reads=[self.BK(1), "S:cst"], writes=["S:dtok"])
            P.add("act", lambda e, dtok=dtok: e.activation(out=dtok[:, :, :], in_=dtok[:, :, :], func=AF.Exp), reads=["S:dtok"], writes=["S:dtok"])
            P.add("act", lambda e, dtT=dtT: e.activation(out=dtT[0:16, :], in_=dtT[0:16, :], func=AF.Ln, bias=1.0, scale=1.0), reads=["S:dtT"], writes=["S:dtT"])
            P.add("act", lambda e, dtok=dtok: e.activation(out=dtok[:, :, :], in_=dtok[:, :, :], func=AF.Ln, bias=1.0, scale=1.0),
                  reads=["S:dtok"], writes=["S:dtok"])
            P.add("dve", lambda e, dtok=dtok, Ddt=Ddt: e.reciprocal(out=Ddt[:, :, :], in_=dtok[:, :, :]), reads=["S:dtok"], writes=["S:Ddt"])
            P.add("dve", lambda e, Ddt=Ddt: e.tensor_tensor(out=Ddt[:, :, :], in0=Ddt[:, :, :], in1=Dbc.unsqueeze(1).to_broadcast([128, 4, 16]), op=ALU.mult),
                  reads=["S:Ddt", "S:cst"], writes=["S:Ddt"])
            P.add("dve", lambda e, dtT=dtT: e.tensor_scalar(out=dtT[0:16, :], in0=dtT[0:16, :], scalar1=acol, scalar2=None, op0=ALU.mult),
                  reads=["S:dtT", "S:cst"], writes=["S:dtT"])
            for c in range(4):
                P.add("dve", lambda e, c=c, dtT=dtT, acsT=acsT: e.tensor_tensor_scan(out=acsT[0:16, c * 128:(c + 1) * 128], data0=ones_f[0:16, 0:128],
                                                                                   data1=dtT[0:16, c * 128:(c + 1) * 128], initial=0.0,
                                                                                   op0=ALU.mult, op1=ALU.add),
                      reads=["S:dtT", "ones_f"], writes=[f"S:acs{c}"])

            def xchunk(sl, cc, ak, xv, x_r):
                if True:
                    if sl == 0:
                        ci = 8 + cc
                    else:
                        ci = 4 * (sl - 1) + cc
                    b = self.rotbank("m0", (0, 1, 2, 3, 4, 5))
                    P.begin_group()
                    for kc in range(8):
                        P.add("pe", lambda e, b=b, kc=kc, cc=cc, xv=xv, win=win: e.matmul(self.bank(b), lhsT=xv[:, kc, cc * 128:(cc + 1) * 128], rhs=win(kc),
                                                                                         start=(kc == 0), stop=(kc == 7)),
                              reads=[x_r] + xres, writes=[self.BK(b)])
                    P.end_group()
                    a = a_[ak % 2]
                    ar = f"S:a{ak % 2}"
                    bk = [self.BK(b)]
                    P.add("act", lambda e, b=b, a=a, ci=ci: e.activation(out=a[:, 0:512], in_=self.bank(b), func=AF.Identity,
                                                                        scale=xcw[:, ci, 3:4], bias=xcw[:, ci, 4:5]), reads=bk + ["S:cst"], writes=[ar])
                    for sh in range(1, 4):
                        P.add("dve", lambda e, b=b, a=a, ci=ci, sh=sh: e.scalar_tensor_tensor(
                            out=a[:, sh:512], in0=self.bank(b)[:, 0:512 - sh], scalar=xcw[:, ci, 3 - sh:4 - sh], in1=a[:, sh:512], op0=ALU.mult, op1=ALU.add),
                            reads=bk + ["S:cst", ar], writes=[ar])
                    if qi > 0:
                        P.add("dve", lambda e, a=a, ci=ci: e.tensor_tensor(out=a[:, 0:3], in0=a[:, 0:3], in1=fixx[:, ci, 0:3], op=ALU.add),
                              reads=[ar, f"S:fx{ci}"], writes=[ar])
                    if qi < 3:
                        P.add("dve", lambda e, b=b, ci=ci: e.tensor_scalar(out=fixx[:, ci, 0:3], in0=self.bank(b)[:, 509:512], scalar1=xcw[:, ci, 0:1],
                                                                          scalar2=None, op0=ALU.mult), reads=bk + ["S:cst"], writes=[f"S:fx{ci}"])
                        P.add("dve", lambda e, b=b, ci=ci: e.scalar_tensor_tensor(out=fixx[:, ci, 0:2], in0=self.bank(b)[:, 510:512], scalar=xcw[:, ci, 1:2],
                                                                                 in1=fixx[:, ci, 0:2], op0=ALU.mult, op1=ALU.add),
                              reads=bk + ["S:cst", f"S:fx{ci}"], writes=[f"S:fx{ci}"])
                        P.add("dve", lambda e, b=b, ci=ci: e.scalar_tensor_tensor(out=fixx[:, ci, 0:1], in0=self.bank(b)[:, 511:512], scalar=xcw[:, ci, 2:3],
                                                                                 in1=fixx[:, ci, 0:1], op0=ALU.mult, op1=ALU.add),
                              reads=bk + ["S:cst", f"S:fx{ci}"], writes=[f"S:fx{ci}"])
                    if ci >= 10:
                        g = ci - 10
                        yield
                        P.add("act", lambda e, a=a, g=g, CcT=CcT: e.activation(out=CcT[:, g, :], in_=a[:, 0:512], func=AF.Silu), reads=[ar], writes=[f"S:Cc{g}"])
                        yield
                        yield
                        return
                    yield
                    P.add("act", lambda e, a=a: e.activation(out=a[:, 0:512], in_=a[:, 0:512], func=AF.Silu), reads=[ar], writes=[ar])
                    yield
                    tbk = self.rotbank("m0t", (6, 7))
                    P.begin_group()
                    for tbl in range(4):
                        P.add("pe", lambda e, tbk=tbk, tbl=tbl, a=a: e.transpose(self.bank(tbk)[:, tbl * 128:(tbl + 1) * 128], a[:, tbl * 128:(tbl + 1) * 128], ident_f[:]),
                              reads=[ar, "ident_f"], writes=[self.BK(tbk)])
                    P.end_group()
                    if ci >= 8:
                        g = ci - 8
                        P.add("pool", lambda e, a=a, g=g, BcT=BcT: e.tensor_copy(out=BcT[:, g, :], in_=a[:, 0:512]), reads=[ar], writes=[f"S:Bc{g}"])
                        P.add("act", lambda e, tbk=tbk, g=g, Btok=Btok: e.activation(out=Btok[:, :, g * 128:(g + 1) * 128],
                                                                                    in_=self.bank(tbk).rearrange("p (b f) -> p b f", b=4), func=AF.Identity),
                              reads=[self.BK(tbk)], writes=[f"S:Bt{g}"])
                    else:
                        P.add("dve", lambda e, tbk=tbk, ci=ci, xdt=xdt, dtok=dtok: e.tensor_tensor(
                            out=xdt[:, :, ci * 128:(ci + 1) * 128].rearrange("p b (h d) -> p b h d", h=2),
                            in0=self.bank(tbk).rearrange("p (b h d) -> p b h d", b=4, h=2),
                            in1=dtok[:, :, 2 * ci:2 * ci + 2].unsqueeze(3).to_broadcast([128, 4, 2, 64]), op=ALU.mult),
                            reads=[self.BK(tbk), "S:dtok"], writes=[f"S:xdt{ci}"])
                    yield
            xg = []
            for sl in range(3):
                x_t, x_r = self.wload(wx[sl])
                xv = x_t[:].rearrange("p (k c) -> p k c", k=8)
                for cc in range(4):
                    k = len(xg)
                    xg.append(xchunk(sl, cc, k, xv, x_r))
                    next(xg[k])
                    if k >= 1:
                        next(xg[k - 1])
                    next(xg[k])
            next(xg[-1])

            for zsl in range(2):
                z_t, z_r = self.wload(wz[zsl])
                zv = z_t[:].rearrange("p (k c) -> p k c", k=8)
                for tbl in range(4):
                    b = self.rotbank("m0", (0, 1, 2, 3, 4, 5))
                    P.begin_group()
                    for kc in range(8):
                        P.add("pe", lambda e, b=b, kc=kc, tbl=tbl, zv=zv, c0=c0: e.matmul(self.bank(b), lhsT=xTp[:, kc, c0 + tbl * 128:c0 + (tbl + 1) * 128],
                                                                                         rhs=zv[:, kc, :], start=(kc == 0), stop=(kc == 7)),
                              reads=[z_r] + xres, writes=[self.BK(b)])
                    P.end_group()
                    P.add("act", lambda e, b=b, tbl=tbl, zsl=zsl, zs=zs: e.activation(out=zs[:, tbl, zsl * 512:(zsl + 1) * 512], in_=self.bank(b), func=AF.Silu),
                          reads=[self.BK(b)], writes=[f"S:zs{tbl}"])

            for i in range(8):
                s_t, s_r = self.wload(wsc[i], words=3072)
                sv = s_t[:, 0:3072].rearrange("p (k c) -> p k c", k=8)
                bks = []
                for part in range(3):
                    b = self.rotbank("m0", (0, 1, 2, 3, 4, 5))
                    bks.append(b)
                    P.begin_group()
                    for kc in range(8):
                        P.add("pe", lambda e, b=b, kc=kc, part=part, sv=sv, win=win: e.matmul(self.bank(b), lhsT=sv[:, kc, part * 128:(part + 1) * 128], rhs=win(kc),
                                                                                             start=(kc == 0), stop=(kc == 7)),
                              reads=[s_r] + xres, writes=[self.BK(b)])
                    P.end_group()
                bc_, bh_, bb_ = bks
                k2 = i % 2
                pr, csb, a = prod[k2], cs[k2], a_[k2]
                prr, csr, ar = f"S:pr{k2}", f"S:cs{k2}", f"S:a{k2}"
                P.add("act", lambda e, bc_=bc_, csb=csb: e.activation(out=csb[:, 0:512], in_=self.bank(bc_), func=AF.Identity), reads=[self.BK(bc_)], writes=[csr])
                if qi == 0:
                    P.add("pool", lambda e, pr=pr: e.memset(pr[:, 0:2], 0.0), writes=[prr + "h"])
                else:
                    P.add("pool", lambda e, pr=pr, i=i: e.tensor_copy(out=pr[:, 0:2], in_=halo[:, i, :]), reads=[f"S:halo{i}"], writes=[prr + "h"])
                P.add("dve", lambda e, bh_=bh_, pr=pr, csb=csb: e.tensor_tensor(out=pr[:, 2:514], in0=self.bank(bh_), in1=csb[:, 0:512], op=ALU.mult),
                      reads=[self.BK(bh_), csr], writes=[prr])
                if qi < 3:
                    P.add("pool", lambda e, pr=pr, i=i: e.tensor_copy(out=halo[:, i, :], in_=pr[:, 512:514]), reads=[prr], writes=[f"S:halo{i}"])
                P.add("act", lambda e, pr=pr, a=a, i=i: e.activation(out=a[:, 0:512], in_=pr[:, 2:514], func=AF.Identity, scale=scw[:, i, 2:3]),
                      reads=[prr, "S:cst"], writes=[ar])
                P.add("dve", lambda e, pr=pr, a=a, i=i: e.scalar_tensor_tensor(out=a[:, 0:512], in0=pr[:, 1:513], scalar=scw[:, i, 1:2], in1=a[:, 0:512],
                                                                              op0=ALU.mult, op1=ALU.add), reads=[prr, prr + "h", "S:cst", ar], writes=[ar])
                P.add("dve", lambda e, pr=pr, a=a, i=i: e.scalar_tensor_tensor(out=a[:, 0:512], in0=pr[:, 0:512], scalar=scw[:, i, 0:1], in1=a[:, 0:512],
                                                                              op0=ALU.mult, op1=ALU.add), reads=[prr, prr + "h", "S:cst", ar], writes=[ar])
                P.add("dve", lambda e, bb_=bb_, a=a, i=i, yaT=yaT: e.tensor_tensor(out=yaT[:, i, :], in0=self.bank(bb_), in1=a[:, 0:512], op=ALU.mult),
                      reads=[self.BK(bb_), ar], writes=[f"S:ya{i}"])

            self.P.fence()
            self.sp_ = qbase
            segT = self.carve(1024, BF16).rearrange("p (h t) -> p h t", h=16)
            MT = self.carve(1024, BF16).rearrange("p (h t) -> p h t", h=16)
            xdtd = self.carve(512, BF16)
            yt_ = [self.carve(1024), self.carve(1024)]
            junk = self.carve(256, BF16)
            sm_ = [self.carve(64), self.carve(64)]
            X16 = self.carve(16)
            def chunk(c):
                gc = 4 * qi + c
                cols = slice(c * 128, (c + 1) * 128)
                Lm, Lr = L[gc % 2], f"S:L{gc % 2}"
                yt, ytr = yt_[gc % 2], f"S:yt{gc % 2}"
                sm, smr = sm_[gc % 2], f"S:sm{gc % 2}"
                acr = f"S:acs{c}"
                P.add("dve", lambda e, Lm=Lm, cols=cols, acsT=acsT: e.tensor_scalar(out=Lm[32:48, :], in0=acsT[0:16, cols], scalar1=-1.0, scalar2=None, op0=ALU.mult),
                      reads=[acr], writes=[Lr])
                P.add("dve", lambda e, cols=cols, acsT=acsT: e.tensor_tensor(
                    out=R[0:16, :].rearrange("p (h t) -> p h t", h=16), in0=acsT[0:16, cols].unsqueeze(1).to_broadcast([16, 16, 128]),
                    in1=ident_f[0:16, 0:16].unsqueeze(2).to_broadcast([16, 16, 128]), op=ALU.mult), reads=[acr, "ident_f"], writes=["S:R"])
                for hg in range(4):
                    b = hg % 2
                    P.begin_group()
                    for hh in range(4):
                        P.add("pe", lambda e, b=b, hg=hg, hh=hh, Lm=Lm: e.matmul(self.bank(b)[:, hh * 128:(hh + 1) * 128], lhsT=Lm[0:48, :],
                                                                                rhs=R[0:48, hg * 512 + hh * 128:hg * 512 + (hh + 1) * 128], start=True, stop=False),
                              reads=[Lr, "S:R"], writes=[self.BK(b)])
                        P.add("pe", lambda e, b=b, hh=hh: e.matmul(self.bank(b)[:, hh * 128:(hh + 1) * 128], lhsT=ident_b[:], rhs=maskneg[:], start=False, stop=True),
                              reads=["ident_b", "maskneg"], writes=[self.BK(b)])
                    P.end_group()
                    P.add("act", lambda e, b=b, hg=hg, segT=segT: e.activation(out=segT[:, 4 * hg:4 * hg + 4, :], in_=self.bank(b).rearrange("p (h t) -> p h t", h=4),
                                                                              func=AF.Exp), reads=[self.BK(b)], writes=[f"S:seg{hg}"])
                segr = [f"S:seg{hg}" for hg in range(4)]
                P.begin_group()
                for g in range(2):
                    P.add("pe", lambda e, g=g, cols=cols, BcT=BcT, CcT=CcT: e.matmul(self.bank(2)[:, g * 128:(g + 1) * 128], lhsT=BcT[:, g, cols], rhs=CcT[:, g, cols],
                                                                                    start=True, stop=True), reads=[f"S:Bc{g}", f"S:Cc{g}"], writes=["bk2"])
                P.end_group()
                P.add("dve", lambda e, cols=cols, acsT=acsT: e.tensor_scalar(out=X16[0:16, 0:16], in0=ident_f[0:16, 0:16],
                                                                            scalar1=acsT[0:16, cols][:, 127:128], scalar2=None, op0=ALU.mult),
                      reads=[acr, "ident_f"], writes=["S:X16"])
                P.begin_group()
                P.add("pe", lambda e: e.matmul(self.bank(3)[:, 0:16], lhsT=ones_f[0:16, :], rhs=X16[0:16, 0:16], start=True, stop=True),
                      reads=["ones_f", "S:X16"], writes=["bk3"])
                P.add("pe", lambda e, cols=cols, acsT=acsT: e.transpose(self.bank(3)[:, 16:32], acsT[0:16, cols], ident_f[0:16, 0:16]),
                      reads=[acr, "ident_f"], writes=["bk3"])
                P.end_group()
                P.add("act", lambda e, sm=sm: e.activation(out=sm[:, 0:32], in_=self.bank(3)[:, 0:32], func=AF.Exp), reads=["bk3"], writes=[smr])
                yield
                for g in range(2):
                    P.add("dve", lambda e, g=g, MT=MT, segT=segT: e.tensor_tensor(
                        out=MT[:, 8 * g:8 * g + 8, :], in0=self.bank(2)[:, g * 128:(g + 1) * 128].unsqueeze(1).to_broadcast([128, 8, 128]),
                        in1=segT[:, 8 * g:8 * g + 8, :], op=ALU.mult), reads=["bk2"] + segr, writes=[f"S:MT{g}"])
                P.add("dve", lambda e, c=c, xdt=xdt, xdtd=xdtd, segT=segT: e.tensor_tensor(
                    out=xdtd[:, :].rearrange("p (h d) -> p h d", h=16), in0=xdt[:, c, :].rearrange("p (h d) -> p h d", h=16),
                    in1=segT[:, :, 127:128].to_broadcast([128, 16, 64]), op=ALU.mult),
                    reads=[f"S:xdt{i}" for i in range(8)] + segr, writes=["S:xdtd"])
                yield
                P.begin_group()
                for g in range(2):
                    P.add("pe", lambda e, g=g, cols=cols, CcT=CcT: e.matmul(self.PS[2][:, g * 512:(g + 1) * 512], lhsT=CcT[:, g, cols], rhs=prevbf[:, g * 512:(g + 1) * 512],
                                                                           start=True, stop=True), reads=[f"S:Cc{g}", "S:prev"], writes=[self.BK(4 + g)])
                P.end_group()
                P.begin_group()
                for g in range(2):
                    P.add("pe", lambda e, g=g, c=c, Btok=Btok, xdtd=xdtd: e.matmul(self.PS[0][:, g * 512:(g + 1) * 512], lhsT=Btok[:, c, g * 128:(g + 1) * 128],
                                                                                  rhs=xdtd[:, g * 512:(g + 1) * 512], start=True, stop=True),
                          reads=[f"S:Bt{g}", "S:xdtd"], writes=[self.BK(g)])
                P.end_group()
                P.add("dve", lambda e, sm=sm: e.tensor_tensor(out=H[:, :].rearrange("p (h d) -> p h d", h=16), in0=H[:, :].rearrange("p (h d) -> p h d", h=16),
                                                              in1=sm[:, 0:16].unsqueeze(2).to_broadcast([128, 16, 64]), op=ALU.mult),
                      reads=["S:H", smr], writes=["S:H"])
                P.add("dve", lambda e: e.tensor_tensor(out=H[:, :], in0=self.PS[0][:, :], in1=H[:, :], op=ALU.add), reads=["S:H", self.BK(0), self.BK(1)], writes=["S:H"])
                P.add("act", lambda e: e.activation(out=prevbf[:, :], in_=H[:, :], func=AF.Identity), reads=["S:H"], writes=["S:prev"])
                P.begin_group()
                for h in range(16):
                    P.add("pe", lambda e, h=h, c=c, MT=MT, xdt=xdt: e.matmul(self.PS[3][:, h * 64:(h + 1) * 64], lhsT=MT[:, h, :], rhs=xdt[:, c, h * 64:(h + 1) * 64],
                                                                            start=True, stop=True),
                          reads=[f"S:MT{h // 8}", f"S:xdt{h // 2}"], writes=[self.BK(6 + h // 8)])
                P.end_group()
                yield
                P.add("dve", lambda e, yt=yt, sm=sm: e.tensor_tensor(out=yt[:, :].rearrange("p (h d) -> p h d", h=16), in0=self.PS[2][:, :].rearrange("p (h d) -> p h d", h=16),
                                                                     in1=sm[:, 16:32].unsqueeze(2).to_broadcast([128, 16, 64]), op=ALU.mult),
                      reads=[self.BK(4), self.BK(5), smr], writes=[ytr])
                P.add("dve", lambda e, yt=yt: e.tensor_tensor(out=yt[:, :], in0=self.PS[3][:, :], in1=yt[:, :], op=ALU.add), reads=[self.BK(6), self.BK(7), ytr], writes=[ytr])
                for half in range(2):
                    P.add("pool", lambda e, c=c, half=half, xdt=xdt, Ddt=Ddt: e.tensor_tensor(
                        out=junk[:, 0:512].rearrange("p (h d) -> p h d", h=8), in0=xdt[:, c, half * 512:(half + 1) * 512].rearrange("p (h d) -> p h d", h=8),
                        in1=Ddt[:, c, 8 * half:8 * half + 8].unsqueeze(2).to_broadcast([128, 8, 64]), op=ALU.mult),
                        reads=[f"S:xdt{i}" for i in range(8)] + ["S:Ddt"], writes=["S:junk"])
                    P.add("pool", lambda e, yt=yt, half=half: e.tensor_tensor(out=yt[:, half * 512:(half + 1) * 512], in0=yt[:, half * 512:(half + 1) * 512],
                                                                           in1=junk[:, 0:512], op=ALU.add), reads=[ytr, "S:junk"], writes=[ytr])
                P.add("pool", lambda e, yt=yt, c=c, zs=zs: e.tensor_tensor(out=yt[:, :], in0=yt[:, :], in1=zs[:, c, :], op=ALU.mult), reads=[ytr, f"S:zs{c}"], writes=[ytr])
                for g in range(2):
                    P.add("act", lambda e, g=g, yt=yt, sm=sm: e.activation(out=junk[:, 0:512], in_=yt[:, g * 512:(g + 1) * 512], func=AF.Square,
                                                                          accum_out=sm[:, 32 + g:33 + g]), reads=[ytr], writes=[smr + "s", "S:junk"])
                P.add("pool", lambda e, sm=sm: e.tensor_scalar(out=sm[:, 34:36], in0=sm[:, 32:34], scalar1=1.0 / 512.0, scalar2=LN_EPS, op0=ALU.mult, op1=ALU.add),
                      reads=[smr + "s"], writes=[smr + "r"])
                P.add("pool", lambda e, sm=sm: e.tensor_tensor(out=sm[:, 34:36], in0=sm[:, 34:36], in1=self.neghalf[:, 0:1].to_broadcast([128, 2]), op=ALU.pow),
                      reads=[smr + "r", "neghalf"], writes=[smr + "r"])
                for g in range(2):
                    P.add("act", lambda e, g=g, yt=yt, sm=sm: e.activation(out=yt[:, g * 512:(g + 1) * 512], in_=yt[:, g * 512:(g + 1) * 512], func=AF.Identity,
                                                                          scale=sm[:, 34 + g:35 + g]), reads=[ytr, smr + "r"], writes=[ytr])
                yield
                P.begin_group()
                for i in range(8):
                    P.add("pe", lambda e, i=i, yt=yt: e.transpose(self.PS[2][:, i * 128:(i + 1) * 128], yt[:, i * 128:(i + 1) * 128], ident_f[:]),
                          reads=[ytr, "ident_f"], writes=[self.BK(4 + i // 4)])
                P.end_group()
                for half in range(2):
                    P.add("dve", lambda e, half=half, cols=cols, ybT=ybT: e.tensor_tensor(
                        out=ybT[:, 4 * half:4 * half + 4, cols], in0=self.PS[2][:, half * 512:(half + 1) * 512].rearrange("p (i t) -> p i t", i=4),
                        in1=gT[:, 4 * half:4 * half + 4].unsqueeze(2).to_broadcast([128, 4, 128]), op=ALU.mult),
                        reads=[self.BK(4 + half), "S:cst"], writes=[f"S:yb{c}"])

                yield
            gens = [chunk(c) for c in range(4)]
            order = [0, 0, 0, 1, 0, 1, 0, 1, 2, 1, 2, 1, 2, 3, 2, 3, 2, 3, 3, 3]
            for gi in order:
                next(gens[gi])
            for dh in range(2):
                for part in range(2):
                    o_t, o_r = self.wload(wo[dh, part])
                    ov = o_t[:].rearrange("p (k c) -> p k c", k=8)
                    src = yaT if part == 0 else ybT
                    for tbl in range(4):
                        P.begin_group()
                        for kc in range(8):
                            rd = [o_r, (f"S:ya{kc}" if part == 0 else f"S:yb{tbl}")]
                            P.add("pe", lambda e, dh=dh, part=part, tbl=tbl, kc=kc, ov=ov, src=src: e.matmul(
                                self.bank(4 * dh + tbl), lhsT=src[:, kc, tbl * 128:(tbl + 1) * 128], rhs=ov[:, kc, :],
                                start=(part == 0 and kc == 0), stop=(part == 1 and kc == 7)), reads=rd, writes=[self.BK(4 * dh + tbl)])
                        P.end_group()
                for tbl in range(4):
                    gtb = 4 * qi + tbl
                    P.add("dve", lambda e, dh=dh, tbl=tbl, gtb=gtb: e.scalar_tensor_tensor(
                        out=x_tok[:, gtb, dh * 512:(dh + 1) * 512], in0=x_tok[:, gtb, dh * 512:(dh + 1) * 512], scalar=ALPHA,
                        in1=self.bank(4 * dh + tbl), op0=ALU.mult, op1=ALU.add), reads=[self.BK(4 * dh + tbl), f"xt{gtb}_{dh}"], writes=[f"xt{gtb}_{dh}"])
            for tbl in range(4):
                gtb = 4 * qi + tbl
                self.ln_tb(gtb)
                self.transpose_tb(gtb, tbl, affine=True)
        self.P.fence()
        self.sp_ = base
        self.load_ln(0, with_T=False)
        for tb in range(NTB):
            self.ln_affine(tb)


def declare_dram(nc, phases):
    d = {}
    d["x"] = nc.dram_tensor("x", [T, D], F32, kind="ExternalInput").ap()
    d["out"] = nc.dram_tensor("out", [T, D], F32, kind="ExternalOutput").ap()
    d["lnp"] = nc.dram_tensor("lnp", [4, 2, D], F32, kind="ExternalInput").ap()
    d["lnpT"] = nc.dram_tensor("lnpT", [4, 128, 16], F32, kind="ExternalInput").ap()
    d["ffn_cwb"] = nc.dram_tensor("ffn_cwb", [2, 128, 44, 4], F32, kind="ExternalInput").ap()
    d["wup"] = nc.dram_tensor("wup", [2, 11, 128, 4096], F32, kind="ExternalInput").ap()
    d["wdn"] = nc.dram_tensor("wdn", [2, 2, 3, 128, 4096], F32, kind="ExternalInput").ap()
    d["m0_wdt"] = nc.dram_tensor("m0_wdt", [128, 128], F32, kind="ExternalInput").ap()
    d["m0_wx"] = nc.dram_tensor("m0_wx", [3, 128, 4096], F32, kind="ExternalInput").ap()
    d["m0_wz"] = nc.dram_tensor("m0_wz", [2, 128, 4096], F32, kind="ExternalInput").ap()
    d["m0_wsc"] = nc.dram_tensor("m0_wsc", [8, 128, 3072], F32, kind="ExternalInput").ap()
    d["m0_wo"] = nc.dram_tensor("m0_wo", [2, 2, 128, 4096], F32, kind="ExternalInput").ap()
    d["m0_tokc"] = nc.dram_tensor("m0_tokc", [2, 16], F32, kind="ExternalInput").ap()
    d["m0_featc"] = nc.dram_tensor("m0_featc", [128, 92], F32, kind="ExternalInput").ap()
    d["m0_headc"] = nc.dram_tensor("m0_headc", [16, 2], F32, kind="ExternalInput").ap()
    d["fox_f"] = nc.dram_tensor("fox_f", [128, 128], F32, kind="ExternalInput").ap()
    d["fox_bf"] = nc.dram_tensor("fox_bf", [16, 1], F32, kind="ExternalInput").ap()
    d["fox_qk"] = nc.dram_tensor("fox_qk", [8, 128, 2048], F32, kind="ExternalInput").ap()
    d["fox_v"] = nc.dram_tensor("fox_v", [8, 128, 1024], F32, kind="ExternalInput").ap()
    d["fox_wo"] = nc.dram_tensor("fox_wo", [8, 128, 1024], F32, kind="ExternalInput").ap()
    d["augq"] = nc.dram_tensor("augq", [16, 6, T], BF16, kind="Internal").ap()
    d["augk"] = nc.dram_tensor("augk", [16, 6, T], BF16, kind="Internal").ap()
    return d


def build_program(phases=("mix0", "ffn0", "mix1", "ffn1")):
    nc = bass.Bass("TRN2", target_bir_lowering=False)
    dram = declare_dram(nc, phases)
    P = Prog(nc)
    B = Builder(nc, P, dram)
    B.load_x()
    for tb in range(NTB):
        B.transpose_tb(tb, tb % 4)
    last = phases[-1]
    for ph in phases:
        if ph == "ffn0":
            B.ffn(0, final=(ph == last))
        elif ph == "ffn1":
            B.ffn(1, final=(ph == last))
        elif ph == "mix0":
            P.pin = tuple(os.environ.get("MK_PIN0", "dve,pe").split(","))
            B.mix0()
            P.pin = ("dve",)
            if ph == last:
                for tb in range(NTB):
                    B.store_tb(tb)
        elif ph == "mix1":
            B.mix1()
            if ph == last:
                for tb in range(NTB):
                    B.store_tb(tb)
        else:
            raise NotImplementedError(ph)
    if SCHEDULE:
        P.schedule()
    P.finalize(B.out_dmas)
    P.emit(B.out_dmas)
    P.close()
    return nc


def host_layouts(inp):
    f = np.float32
    o = {}
    o["lnp"] = np.ascontiguousarray(np.stack([
        np.stack([inp["ln_mix_g"][0], inp["ln_mix_b"][0]]), np.stack([inp["ln_ffn_g"][0], inp["ln_ffn_b"][0]]),
        np.stack([inp["ln_mix_g"][1], inp["ln_mix_b"][1]]), np.stack([inp["ln_ffn_g"][1], inp["ln_ffn_b"][1]])]).astype(f))
    o["lnpT"] = np.ascontiguousarray(o["lnp"].reshape(4, 2, 8, 128).transpose(0, 3, 1, 2).reshape(4, 128, 16))
    cw = inp["ffn_conv_w"].astype(f)
    cb = inp["ffn_conv_b"].astype(f)
    cwb = np.concatenate([cw.transpose(0, 2, 1), cb[:, :, None]], axis=2)
    o["ffn_cwb"] = np.ascontiguousarray(cwb.reshape(2, 44, 128, 4).transpose(0, 2, 1, 3))
    wu = inp["ffn_w_up"].astype(f)
    u = wu[:, :, :DFF].reshape(2, 8, 128, 11, 2, 128)
    g = wu[:, :, DFF:].reshape(2, 8, 128, 11, 2, 128)
    ug = np.stack([u, g], axis=5)
    o["wup"] = np.ascontiguousarray(ug.transpose(0, 3, 2, 1, 4, 5, 6).reshape(2, 11, 128, 4096))
    wd = inp["ffn_w_down"].astype(f)
    wdp = np.zeros((2, 24 * 128, 1024), f)
    wdp[:, :DFF] = wd
    wdp = wdp.reshape(2, 3, 8, 128, 2, 512)
    o["wdn"] = np.ascontiguousarray(wdp.transpose(0, 4, 1, 3, 2, 5).reshape(2, 2, 3, 128, 4096))
    w0 = inp["sc_ssm_w_in"][0].astype(f).reshape(8, 128, 5648)
    lay = lambda cols: np.ascontiguousarray(w0[:, :, cols].transpose(1, 0, 2).reshape(128, -1))
    o["m0_wdt"] = lay(slice(5632, 5648))
    o["m0_wx"] = np.stack([lay(slice(5120, 5632)), lay(slice(4096, 4608)), lay(slice(4608, 5120))])
    o["m0_wz"] = np.stack([lay(slice(3072, 3584)), lay(slice(3584, 4096))])
    o["m0_wsc"] = np.stack([lay(np.r_[1024 + 128 * i:1152 + 128 * i, 2048 + 128 * i:2176 + 128 * i, 128 * i:128 + 128 * i]) for i in range(8)])
    wo0 = inp["sc_ssm_w_out"][0].astype(f).reshape(2, 8, 128, 2, 512)
    o["m0_wo"] = np.ascontiguousarray(wo0.transpose(3, 0, 2, 1, 4).reshape(2, 2, 128, 4096))
    o["m0_tokc"] = np.ascontiguousarray(np.stack([inp["ssm_dt_bias"][0], inp["ssm_d"][0]]).astype(f))
    o["m0_headc"] = np.ascontiguousarray(np.stack([inp["ssm_dt_bias"][0], inp["ssm_a_log"][0]], axis=1).astype(f))
    gTh = inp["ssm_norm_g"][0].astype(f).reshape(8, 128).T
    scwh = inp["sc_conv_w"][0].astype(f).reshape(3, 8, 128).transpose(2, 1, 0)
    xw = inp["ssm_conv_w"][0].astype(f).reshape(4, 12, 128).transpose(2, 1, 0)
    xb = inp["ssm_conv_b"][0].astype(f).reshape(12, 128).T[:, :, None]
    o["m0_featc"] = np.ascontiguousarray(np.concatenate([gTh, scwh.reshape(128, 24), np.concatenate([xw, xb], axis=2).reshape(128, 60)], axis=1))
    wi = inp["fox_w_in"][0].astype(f)
    wk = wi.reshape(8, 128, 3088)
    o["fox_f"] = np.ascontiguousarray(wk[:, :, 3072:3088].transpose(1, 0, 2).reshape(128, 128))
    q = wk[:, :, 0:1024].reshape(8, 128, 8, 128)
    k = wk[:, :, 1024:2048].reshape(8, 128, 8, 128)
    v = wk[:, :, 2048:3072].reshape(8, 128, 8, 128)
    qk = np.concatenate([q, k], axis=3)
    o["fox_qk"] = np.ascontiguousarray(qk.transpose(2, 1, 0, 3).reshape(8, 128, 2048))
    o["fox_v"] = np.ascontiguousarray(v.transpose(2, 1, 0, 3).reshape(8, 128, 1024))
    o["fox_wo"] = np.ascontiguousarray(inp["fox_w_out"][0].astype(f).reshape(8, 128, 1024))
    o["fox_bf"] = np.ascontiguousarray(inp["fox_b_f"][0].astype(f).reshape(16, 1))
    return o


_NC_CACHE = {}


def kernel(**inputs):
    phases = ("mix0", "ffn0", "mix1", "ffn1")
    if phases not in _NC_CACHE:
        _NC_CACHE[phases] = build_program(phases)
    nc = _NC_CACHE[phases]
    lay = host_layouts(inputs)
    x = np.asarray(inputs["x"], dtype=np.float32)
    in_maps = [dict(lay, x=np.ascontiguousarray(x[b])) for b in range(8)]
    res = run_bass_kernel_spmd(nc, in_maps, core_ids=list(range(8)))
    return np.stack([np.asarray(r["out"], dtype=np.float32) for r in res.results], axis=0)
```

```python
from contextlib import ExitStack
import numpy as np
import concourse.bass as bass
import concourse.mybir as mybir
from concourse.bass_utils import run_bass_kernel_spmd

F32 = mybir.dt.float32
BF16 = mybir.dt.bfloat16
AF = mybir.ActivationFunctionType
ALU = mybir.AluOpType

COMPUTE = ("pe", "act", "dve", "pool")
QUEUES = ("pe", "act", "dve", "pool", "sp")

ALPHA = 4.0 ** 0.25
LN_EPS = 1e-5
T = 2048
D = 1024
NTB = 16
PAD = 4
DFF = 2816
NJ = 22
import os
SCHEDULE = os.environ.get('MK_SCHED', '1') == '1'
PREFETCH = os.environ.get('MK_PREFETCH', '0') == '1'


class Ins:
    __slots__ = ("eng", "fn", "deps", "idx", "dma_key", "dma_val", "signal", "sigval", "clock", "waits", "is_dma", "pinned")


class Prog:
    def __init__(self, nc):
        self.nc = nc
        self.es = ExitStack()
        self.ins = []
        self.q = {e: [] for e in QUEUES}
        self.last_w = {}
        self.readers = {}
        self.dma_cum = {}
        self.dma_sems = {}
        self.sems = {}
        self.fence_deps = []
        self.scratch_touch = {}
        self.pin = ("dve",)

    def sbuf(self, name, shape, dtype):
        return self.es.enter_context(self.nc.sbuf_tensor(name, list(shape), dtype))

    def psum(self, name, shape, dtype=F32):
        return self.es.enter_context(self.nc.psum_tensor(name, list(shape), dtype))

    def begin_group(self):
        self._grp = []

    def end_group(self):
        g, self._grp = self._grp, None
        fns = [x[0] for x in g]
        reads, writes = [], []
        for _, r, w in g:
            for x in r:
                if x not in reads:
                    reads.append(x)
            for x in w:
                if x not in writes:
                    writes.append(x)

        def run(e, fns=fns):
            h = None
            for f in fns:
                h = f(e)
            return h
        return self.add("pe", run, reads=reads, writes=writes)

    def add(self, eng, fn, reads=(), writes=(), dma_key=None):
        if getattr(self, "_grp", None) is not None:
            assert eng == "pe" and dma_key is None
            self._grp.append((fn, list(reads), list(writes)))
            return None
        i = Ins()
        i.eng = eng
        i.fn = fn
        i.is_dma = dma_key is not None
        i.dma_key = dma_key
        i.signal = False
        i.pinned = eng in self.pin
        deps = set()
        scratch = False
        if any(r.startswith("bk") for r in reads):
            writes = list(writes) + [r for r in reads if r.startswith("bk") and r not in writes]
            reads = [r for r in reads if not r.startswith("bk")]
        for r in reads:
            w = self.last_w.get(r)
            if w is not None:
                deps.add(w)
            if r.startswith("S:"):
                scratch = True
        for w_ in writes:
            w = self.last_w.get(w_)
            if w is not None:
                deps.add(w)
            for rd in self.readers.get(w_, ()):
                deps.add(rd)
            if w_.startswith("S:"):
                scratch = True
        if scratch:
            deps.update(self.fence_deps)
        i.deps = deps
        i.idx = len(self.ins)
        self.ins.append(i)
        self.q[eng].append(i)
        for r in reads:
            self.readers.setdefault(r, []).append(i)
        for w_ in writes:
            self.last_w[w_] = i
            self.readers[w_] = []
        if i.is_dma:
            self.dma_cum[dma_key] = self.dma_cum.get(dma_key, 0) + 16
            i.dma_val = self.dma_cum[dma_key]
        if scratch:
            self.scratch_touch[i.idx] = i
        return i

    def fence(self):
        touched = list(self.scratch_touch.values())
        self.scratch_touch = {}
        if not hasattr(self, "_fdummy"):
            self._fdummy = self.sbuf("fence_dummy", [128, 8], F32)
        fd = self._fdummy
        join = self.add("dve", lambda e: e.memset(fd[:, 0:1], 0.0), writes=["fence_dummy"])
        join.deps.update(touched)
        self.fence_deps = [join]
        for k in [k for k in self.last_w if k.startswith("S:")]:
            del self.last_w[k]
        for k in [k for k in self.readers if k.startswith("S:")]:
            del self.readers[k]


    def schedule(self):
        import heapq

        class _Probe:
            def __init__(self):
                self.recs = []

            def __getattr__(self, name):
                def f(*a, **k):
                    self.recs.append((name, a, k))
                    return None
                return f

        def prod(sh):
            n = 1
            for v in sh:
                n *= int(v)
            return n

        cost, lat = {}, {}
        for i in self.ins:
            p = _Probe()
            i.fn(p)
            name, a, k = p.recs[-1]
            out = k.get("out", a[0] if a else None)
            n = prod(out.shape[1:]) if out is not None and hasattr(out, "shape") else 512
            L = 0.0
            if i.is_dma:
                by = n * out.shape[0] * 4 if out is not None else 0
                c = 0.6 if i.eng == "pool" else 0.15
                L = 2.5 + by / 150e3
            elif i.eng == "pe":
                c = 0.0
                for name, a, k in p.recs:
                    if name == "transpose":
                        c += 0.12
                    else:
                        rhs = k.get("rhs", a[2] if len(a) > 2 else None)
                        nn = prod(rhs.shape[1:]) if rhs is not None else 512
                        lhs = k.get("lhsT", a[1] if len(a) > 1 else None)
                        c1 = 0.035 + max(nn, 64) / 2000.0
                        if lhs is not None and lhs.dtype == F32:
                            c1 *= 4
                        c += c1
            elif i.eng == "act":
                c = 0.22 + n / 1400.0
            elif i.eng == "dve":
                c = 0.12 + n / 960.0
            else:
                c = 0.25 + n / 600.0
            cost[i.idx] = c
            lat[i.idx] = L
        succ = {i.idx: [] for i in self.ins}
        indeg = {}
        import os
        chain = {}
        for e in QUEUES:
            prev = None
            for i in self.q[e]:
                if prev is not None and i.pinned:
                    chain[i.idx] = prev
                prev = i
        for i in self.ins:
            ds = [d for d in i.deps if d is not i]
            if i.idx in chain and chain[i.idx] not in ds:
                ds.append(chain[i.idx])
            indeg[i.idx] = len(ds)
            for d in ds:
                succ[d.idx].append(i)
        byidx = {i.idx: i for i in self.ins}
        pending = {e: [] for e in QUEUES}
        avail = {e: [] for e in QUEUES}
        free = {e: 0.0 for e in QUEUES}
        fin = {}
        ready = {}
        for i in self.ins:
            if indeg[i.idx] == 0:
                ready[i.idx] = 0.0
                heapq.heappush(pending[i.eng], (0.0, i.idx))
        order = []
        newq = {e: [] for e in QUEUES}
        SYNC = 0.12
        n_left = len(self.ins)
        while n_left:
            best = None
            for e in QUEUES:
                pe_, av = pending[e], avail[e]
                while pe_ and pe_[0][0] <= free[e]:
                    r, ix = heapq.heappop(pe_)
                    heapq.heappush(av, ix)
                if av:
                    cand = (free[e], av[0], e, True)
                elif pe_:
                    cand = (pe_[0][0], pe_[0][1], e, False)
                else:
                    continue
                if best is None or cand[:2] < best[:2]:
                    best = cand
            st, ix, e, from_av = best
            if from_av:
                heapq.heappop(avail[e])
            else:
                heapq.heappop(pending[e])
            i = byidx[ix]
            f = st + cost[ix]
            free[e] = f
            fin[ix] = f + lat[ix]
            order.append(i)
            newq[e].append(i)
            n_left -= 1
            for sx in succ[ix]:
                indeg[sx.idx] -= 1
                r = max(ready.get(sx.idx, 0.0), fin[ix] + (0.0 if sx.eng == e and not i.is_dma else SYNC))
                ready[sx.idx] = r
                if indeg[sx.idx] == 0:
                    heapq.heappush(pending[sx.eng], (r, sx.idx))
        self.ins = order
        self.q = newq
        for k, i in enumerate(self.ins):
            i.idx = k
        self.est_us = max(fin.values()) if fin else 0.0

    def finalize(self, tail):
        nc = self.nc
        pos = {}
        for e in QUEUES:
            for k, i in enumerate(self.q[e]):
                pos[i.idx] = k
        prev_clock = {e: ({c: -1 for c in COMPUTE}, frozenset()) for e in QUEUES}
        for i in self.ins:
            clk, dseen = prev_clock[i.eng]
            clk = dict(clk)
            dseen = set(dseen)
            waits = []
            for d in sorted(i.deps, key=lambda d: -d.idx):
                if d is i:
                    continue
                if d.is_dma:
                    if d.idx in dseen:
                        continue
                    waits.append(d)
                    dseen.add(d.idx)
                else:
                    if d.eng == i.eng and d.eng == "pe":
                        continue
                    if clk[d.eng] >= pos[d.idx]:
                        continue
                    waits.append(d)
                    clk[d.eng] = max(clk[d.eng], pos[d.idx])
                dc, dd = d.clock
                for c in COMPUTE:
                    if dc[c] > clk[c]:
                        clk[c] = dc[c]
                dseen |= dd
            final = []
            for d in waits:
                if d.is_dma:
                    final.append(d)
                elif clk[d.eng] == pos[d.idx]:
                    final.append(d)
            i.waits = final
            for d in final:
                d.signal = True
            i.clock = (clk, frozenset(dseen))
            prev_clock[i.eng] = i.clock
        for d in tail:
            d.signal = True
        for e in COMPUTE:
            self.sems[e] = self.es.enter_context(nc.semaphore("s_" + e))
            n = 0
            for i in self.q[e]:
                if i.is_dma:
                    continue
                if i.signal:
                    n += 1
                    i.sigval = n
        for k in self.dma_cum:
            self.dma_sems[k] = self.es.enter_context(nc.semaphore("d_" + str(k).replace(":", "_")))

    def emit(self, tail):
        nc = self.nc
        prog = self

        def wait(eng, d):
            if d.is_dma:
                eng.wait_ge(prog.dma_sems[d.dma_key], d.dma_val)
            else:
                eng.wait_ge(prog.sems[d.eng], d.sigval)

        def run(engname, eng):
            for i in prog.q[engname]:
                for d in i.waits:
                    wait(eng, d)
                h = i.fn(eng)
                if i.is_dma:
                    h.then_inc(prog.dma_sems[i.dma_key], 16)
                elif i.signal:
                    h.then_inc(prog.sems[i.eng], 1)
            if engname == "sp":
                for d in tail:
                    wait(eng, d)

        with nc.Block() as block:
            @block.tensor
            def _(e):
                run("pe", e)

            @block.scalar
            def _(e):
                run("act", e)

            @block.vector
            def _(e):
                run("dve", e)

            @block.gpsimd
            def _(e):
                run("pool", e)

            @block.sync
            def _(e):
                run("sp", e)

    def close(self):
        self.es.close()


class Builder:
    def __init__(self, nc, P, dram):
        self.nc, self.P, self.dram = nc, P, dram
        P_ = P
        self.x_tok = P_.sbuf("x_tok_sb", [128, NTB, D], F32)
        self.xTp = P_.sbuf("xTp", [128, 8, PAD + T], BF16)
        self.ident_f = P_.sbuf("ident_f", [128, 128], F32)
        self.ones_f = P_.sbuf("ones_f", [128, 128], F32)
        self.neghalf = P_.sbuf("neghalf", [128, 1], F32)
        self.lnp = None
        self.stats = P_.sbuf("stats", [128, NTB, 16], F32)
        self.cwb = P_.sbuf("cwb", [128, 44, 4], F32)
        self.fix = P_.sbuf("fix", [128, 44, 2], F32)
        self.ring = [P_.sbuf(f"ring{i}", [128, 4096], BF16) for i in range(3)]
        self.ring_cnt = 0
        self.SW = 20480
        self.S = P_.sbuf("S", [128, self.SW], F32)
        self.sp_ = 0
        self.PS = [P_.psum(f"ps{i}", [128, 1024], F32) for i in range(4)]
        self.out_dmas = []
        self.ident_b = P_.sbuf("ident_b", [128, 128], BF16)
        self.maskneg = P_.sbuf("maskneg", [128, 128], BF16)
        self.small = P_.sbuf("small", [128, 64], F32)
        self.lnT = P_.sbuf("lnT_sb", [128, 16], F32)
        self.rot = {}
        self.consts()

    def rotbank(self, group, banks):
        k = self.rot.get(group, 0)
        self.rot[group] = k + 1
        return banks[k % len(banks)]

    def reset_scratch(self):
        self.P.fence()
        self.sp_ = 0

    def carve(self, words, dtype=F32):
        a = self.sp_
        self.sp_ += words
        assert self.sp_ <= self.SW, (self.sp_, self.SW)
        v = self.S[:, a:a + words]
        if dtype == BF16:
            v = v.bitcast(BF16)
        return v

    def bank(self, b):
        return self.PS[b // 2][:, (b % 2) * 512:(b % 2) * 512 + 512]

    @staticmethod
    def BK(b):
        return f"bk{b}"

    def ring_next(self):
        i = self.ring_cnt % 3
        self.ring_cnt += 1
        return self.ring[i], f"ring{i}"

    def wload(self, src_ap, words=4096):
        tile, res = self.ring_next()
        self.P.add("pool", lambda e, t=tile, s=src_ap, w=words: e.dma_start(out=t[:, 0:w], in_=s, max_dma_last_dim=8192),
                   writes=[res], dma_key=res)
        return tile, res

    def consts(self):
        P = self.P
        ones_f, ident_f = self.ones_f, self.ident_f
        P.add("pool", lambda e: e.memset(ones_f[:], 1.0), writes=["ones_f"])
        P.add("pool", lambda e: e.affine_select(out=ident_f[:], in_=ones_f[:], pattern=[[-1, 128]], compare_op=ALU.is_equal,
                                                 fill=0.0, base=0, channel_multiplier=1), reads=["ones_f"], writes=["ident_f"])
        nh = self.neghalf
        P.add("pool", lambda e: e.memset(nh[:], -0.5), writes=["neghalf"])
        xTp = self.xTp
        P.add("pool", lambda e: e.memset(xTp[:, :, 0:PAD], 0.0), writes=["xTpad"])
        ident_b, maskneg = self.ident_b, self.maskneg
        P.add("pool", lambda e: e.tensor_copy(out=ident_b[:], in_=ident_f[:]), reads=["ident_f"], writes=["ident_b"])
        P.add("pool", lambda e: e.memset(maskneg[:], -30000.0), writes=["maskneg"])
        P.add("pool", lambda e: e.affine_select(out=maskneg[:], in_=maskneg[:], pattern=[[-1, 128]], compare_op=ALU.is_gt,
                                                 fill=0.0, base=0, channel_multiplier=1), reads=["maskneg"], writes=["maskneg"])

    def load_x(self):
        P, x_tok = self.P, self.x_tok
        xv = self.dram["x"].rearrange("(tb p) d -> p tb d", p=128)
        for g in range(4):
            P.add("sp", lambda e, g=g: e.dma_start(out=x_tok[:, 4 * g:4 * g + 4, :], in_=xv[:, 4 * g:4 * g + 4, :]),
                  writes=[f"xt{tb}_{dh}" for tb in range(4 * g, 4 * g + 4) for dh in range(2)], dma_key=f"xin{g}")

    def store_tb(self, tb):
        P, x_tok = self.P, self.x_tok
        ov = self.dram["out"].rearrange("(tb p) d -> p tb d", p=128)
        i = P.add("sp", lambda e: e.dma_start(out=ov[:, tb, :], in_=x_tok[:, tb, :]), reads=[f"xt{tb}_0", f"xt{tb}_1"], dma_key=f"xout{tb}")
        self.out_dmas.append(i)

    def transpose_tb(self, tb, pst, affine=False):
        P, x_tok, xTp, ident_f, lnT = self.P, self.x_tok, self.xTp, self.ident_f, self.lnT
        ps = self.PS[pst]
        P.begin_group()
        for kc in range(8):
            P.add("pe", lambda e, kc=kc: e.transpose(ps[:, kc * 128:(kc + 1) * 128], x_tok[:, tb, kc * 128:(kc + 1) * 128], ident_f[:]),
                  reads=[f"xt{tb}_{kc // 4}", "ident_f"], writes=[self.BK(2 * pst + kc // 4)])
        P.end_group()
        c0 = PAD + tb * 128
        if affine:
            for kc in range(8):
                if kc < 4:
                    P.add("act", lambda e, kc=kc: e.activation(out=xTp[:, kc, c0:c0 + 128], in_=ps[:, kc * 128:(kc + 1) * 128], func=AF.Identity,
                                                               scale=lnT[:, kc:kc + 1], bias=lnT[:, 8 + kc:9 + kc]),
                          reads=[self.BK(2 * pst), "lnT"], writes=[f"xT{tb}"])
                else:
                    P.add("dve", lambda e, kc=kc: e.tensor_scalar(out=xTp[:, kc, c0:c0 + 128], in0=ps[:, kc * 128:(kc + 1) * 128],
                                                                  scalar1=lnT[:, kc:kc + 1], scalar2=lnT[:, 8 + kc:9 + kc], op0=ALU.mult, op1=ALU.add),
                          reads=[self.BK(2 * pst + 1), "lnT"], writes=[f"xT{tb}"])
            return
        P.add("act", lambda e: e.activation(out=xTp[:, 0:4, c0:c0 + 128], in_=ps[:, 0:512].rearrange("p (k t) -> p k t", k=4), func=AF.Identity),
              reads=[self.BK(2 * pst)], writes=[f"xT{tb}"])
        P.add("dve", lambda e: e.tensor_copy(out=xTp[:, 4:8, c0:c0 + 128], in_=ps[:, 512:1024].rearrange("p (k t) -> p k t", k=4)),
              reads=[self.BK(2 * pst + 1)], writes=[f"xT{tb}"])

    def load_lnT(self, idx):
        lnT = self.lnT
        src = self.dram["lnpT"][idx]
        self.P.add("sp", lambda e: e.dma_start(out=lnT[:], in_=src), writes=["lnT"], dma_key="lnT")

    def load_ln(self, idx, with_T=True):
        if with_T:
            self.load_lnT(idx)
        self.lnp = self.carve(2 * D).rearrange("p (a d) -> p a d", a=2)
        lnp = self.lnp
        src = self.dram["lnp"][idx].partition_broadcast(128)
        self.P.add("sp", lambda e: e.dma_start(out=lnp[:], in_=src), writes=["S:lnp"], dma_key="lnp")

    def ln_tb(self, tb):
        P, x_tok, st, lnp, nh = self.P, self.x_tok, self.stats, self.lnp, self.neghalf
        R = [f"xt{tb}_0", f"xt{tb}_1"]
        sr = f"st{tb}"
        P.add("dve", lambda e: e.bn_stats(out=st[:, tb, 0:6], in_=x_tok[:, tb, 0:512]), reads=[R[0]], writes=[sr])
        P.add("dve", lambda e: e.bn_stats(out=st[:, tb, 6:12], in_=x_tok[:, tb, 512:1024]), reads=[R[1]], writes=[sr])
        P.add("dve", lambda e: e.bn_aggr(out=st[:, tb, 12:14], in_=st[:, tb, 0:12]), reads=[sr], writes=[sr])
        P.add("pool", lambda e: e.tensor_scalar(out=st[:, tb, 14:15], in0=st[:, tb, 13:14], scalar1=LN_EPS, scalar2=None, op0=ALU.add),
              reads=[sr], writes=[sr])
        P.add("pool", lambda e: e.tensor_tensor(out=st[:, tb, 14:15], in0=st[:, tb, 14:15], in1=nh[:], op=ALU.pow),
              reads=[sr, "neghalf"], writes=[sr])
        P.add("dve", lambda e: e.tensor_scalar(out=st[:, tb, 15:16], in0=st[:, tb, 12:13], scalar1=st[:, tb, 14:15], scalar2=-1.0,
                                               op0=ALU.mult, op1=ALU.mult), reads=[sr], writes=[sr])
        P.add("act", lambda e: e.activation(out=x_tok[:, tb, :], in_=x_tok[:, tb, :], func=AF.Identity,
                                            scale=st[:, tb, 14:15], bias=st[:, tb, 15:16]), reads=[sr] + R, writes=R)

    def ln_affine(self, tb):
        P, x_tok, lnp = self.P, self.x_tok, self.lnp
        R = [f"xt{tb}_0", f"xt{tb}_1"]
        P.add("pool", lambda e: e.tensor_tensor(out=x_tok[:, tb, :], in0=x_tok[:, tb, :], in1=lnp[:, 0, :], op=ALU.mult),
              reads=R + ["S:lnp"], writes=R)
        P.add("pool", lambda e: e.tensor_tensor(out=x_tok[:, tb, :], in0=x_tok[:, tb, :], in1=lnp[:, 1, :], op=ALU.add),
              reads=R + ["S:lnp"], writes=R)

    def ffn(self, l, final=False):
        P, xTp, x_tok, cwb, fix = self.P, self.xTp, self.x_tok, self.cwb, self.fix
        self.reset_scratch()
        hid = self.carve(NJ * 1024 // 2, BF16).rearrange("p (j t) -> p j t", j=NJ)
        tmp = [[self.carve(1024), self.carve(1024)] for _ in range(2)]
        cwsrc = self.dram["ffn_cwb"][l]
        P.add("sp", lambda e: e.dma_start(out=cwb[:], in_=cwsrc), writes=["cwb"], dma_key="cwb")
        self.load_ln(2 * l + 1)
        wup, wdn = self.dram["wup"], self.dram["wdn"]
        for h in range(2):
            c0 = PAD + 1024 * h
            xres = [f"xT{tb}" for tb in range(8 * h, 8 * h + 8)] + ["xTpad"]
            pre = {}
            PD = int(os.environ.get('MK_PD', '1'))
            if PREFETCH:
                for s0 in range(PD):
                    pre[s0] = self.wload(wup[l, s0])
            for s in range(11):
                if PREFETCH:
                    tile, res = pre[s]
                    if s + PD < 11:
                        pre[s + PD] = self.wload(wup[l, s + PD])
                else:
                    tile, res = self.wload(wup[l, s])
                sv = tile[:].rearrange("p (k j c) -> p k j c", k=8, j=2)
                for jj in range(2):
                    j = 2 * s + jj
                    pset = j % 2
                    for ug in range(2):
                        pst = 2 * pset + ug
                        ps = self.PS[pst]
                        for t in range(2):
                            P.begin_group()
                            for kc in range(8):
                                P.add("pe", lambda e, ps=ps, t=t, kc=kc, jj=jj, ug=ug, sv=sv, c0=c0: e.matmul(
                                    ps[:, t * 512:(t + 1) * 512], lhsT=sv[:, kc, jj, ug * 128:(ug + 1) * 128],
                                    rhs=xTp[:, kc, c0 + t * 512:c0 + (t + 1) * 512], start=(kc == 0), stop=(kc == 7)),
                                    reads=[res] + xres, writes=[self.BK(2 * pst + t)])
                            P.end_group()
                    for ug in range(2):
                        pst = 2 * pset + ug
                        ps = self.PS[pst]
                        a = tmp[pset][ug]
                        ar = f"S:a{pset}{ug}"
                        ch = ug * NJ + j
                        bks = [self.BK(2 * pst), self.BK(2 * pst + 1)]
                        P.add("act", lambda e, ps=ps, a=a, ch=ch: e.activation(out=a[:, 0:1024], in_=ps[:, 0:1024], func=AF.Identity,
                                                                                scale=cwb[:, ch, 2:3], bias=cwb[:, ch, 3:4]),
                              reads=bks + ["cwb"], writes=[ar])
                        P.add("dve", lambda e, ps=ps, a=a, ch=ch: e.scalar_tensor_tensor(
                            out=a[:, 1:1024], in0=ps[:, 0:1023], scalar=cwb[:, ch, 1:2], in1=a[:, 1:1024], op0=ALU.mult, op1=ALU.add),
                            reads=bks + ["cwb", ar], writes=[ar])
                        P.add("dve", lambda e, ps=ps, a=a, ch=ch: e.scalar_tensor_tensor(
                            out=a[:, 2:1024], in0=ps[:, 0:1022], scalar=cwb[:, ch, 0:1], in1=a[:, 2:1024], op0=ALU.mult, op1=ALU.add),
                            reads=bks + ["cwb", ar], writes=[ar])
                        if h == 0:
                            P.add("dve", lambda e, ps=ps, ch=ch: e.tensor_scalar(out=fix[:, ch, 0:2], in0=ps[:, 1022:1024], scalar1=cwb[:, ch, 0:1],
                                                                                 scalar2=None, op0=ALU.mult), reads=bks + ["cwb"], writes=[f"fix{ch}"])
                            P.add("dve", lambda e, ps=ps, ch=ch: e.scalar_tensor_tensor(
                                out=fix[:, ch, 0:1], in0=ps[:, 1023:1024], scalar=cwb[:, ch, 1:2], in1=fix[:, ch, 0:1], op0=ALU.mult, op1=ALU.add),
                                reads=bks + ["cwb", f"fix{ch}"], writes=[f"fix{ch}"])
                        else:
                            P.add("dve", lambda e, a=a, ch=ch: e.tensor_tensor(out=a[:, 0:2], in0=a[:, 0:2], in1=fix[:, ch, 0:2], op=ALU.add),
                                  reads=[ar, f"fix{ch}"], writes=[ar])
                    au, ag = tmp[pset]
                    P.add("act", lambda e, ag=ag: e.activation(out=ag[:, 0:1024], in_=ag[:, 0:1024], func=AF.Silu),
                          reads=[f"S:a{pset}1"], writes=[f"S:a{pset}1"])
                    P.add("pool", lambda e, au=au, ag=ag, j=j: e.tensor_tensor(out=hid[:, j, :], in0=au[:, 0:1024], in1=ag[:, 0:1024], op=ALU.mult),
                          reads=[f"S:a{pset}0", f"S:a{pset}1"], writes=[f"S:hid{j}"])
            for dh in range(2):
                for g in range(3):
                    tile, res = self.wload(wdn[l, dh, g])
                    sv = tile[:].rearrange("p (j c) -> p j c", j=8)
                    for jj in range(8 if g < 2 else 6):
                        j = 8 * g + jj
                        for tb in range(8):
                            P.add("pe", lambda e, tb=tb, j=j, jj=jj, sv=sv: e.matmul(
                                self.bank(tb), lhsT=hid[:, j, tb * 128:(tb + 1) * 128], rhs=sv[:, jj, :], start=(j == 0), stop=(j == NJ - 1)),
                                reads=[res, f"S:hid{j}"], writes=[self.BK(tb)])
                for tb in range(8):
                    gtb = 8 * h + tb
                    P.add("dve", lambda e, tb=tb, gtb=gtb, dh=dh: e.scalar_tensor_tensor(
                        out=x_tok[:, gtb, dh * 512:(dh + 1) * 512], in0=x_tok[:, gtb, dh * 512:(dh + 1) * 512], scalar=ALPHA,
                        in1=self.bank(tb), op0=ALU.mult, op1=ALU.add), reads=[self.BK(tb), f"xt{gtb}_{dh}"], writes=[f"xt{gtb}_{dh}"])
            for tb in range(8):
                gtb = 8 * h + tb
                self.ln_tb(gtb)
                if not final:
                    self.transpose_tb(gtb, tb // 2, affine=True)
                self.ln_affine(gtb)
                if final:
                    self.store_tb(gtb)


    def mix1(self):
        P, xTp, x_tok, dram = self.P, self.xTp, self.x_tok, self.dram
        ones_f, ident_b, maskneg, small = self.ones_f, self.ident_b, self.maskneg, self.small
        xall = [f"xT{tb}" for tb in range(NTB)]
        self.reset_scratch()
        fl = self.carve(2048)
        cum = self.carve(2048)
        ones = self.carve(512)
        QG = self.carve(6 * 2048 // 2, BF16).rearrange("p (r t) -> p r t", r=6)
        KG = self.carve(6 * 2048 // 2, BF16).rearrange("p (r t) -> p r t", r=6)
        bsrc = dram["fox_bf"]
        P.add("sp", lambda e: e.dma_start(out=small[0:16, 0:1], in_=bsrc), writes=["small"], dma_key="small")
        P.add("dve", lambda e: e.tensor_scalar(out=small[0:16, 1:2], in0=small[0:16, 0:1], scalar1=-1.0, scalar2=None, op0=ALU.mult),
              reads=["small"], writes=["small"])
        P.add("pool", lambda e: e.memset(ones[0:16, :], 1.0), writes=["S:ones"])
        P.add("pool", lambda e: e.memset(QG[0:16, 3:6, :], 1.0), writes=["S:QG1"])
        P.add("pool", lambda e: e.memset(KG[0:16, 0:3, :], 1.0), writes=["S:KG1"])
        tile, res = self.wload(dram["fox_f"], words=128)
        fv = tile[:, 0:128].rearrange("p (k c) -> p k c", k=8)
        for t in range(4):
            P.begin_group()
            for kc in range(8):
                P.add("pe", lambda e, t=t, kc=kc: e.matmul(self.bank(t)[0:16, :], lhsT=fv[:, kc, :], rhs=xTp[:, kc, PAD + t * 512:PAD + (t + 1) * 512],
                                                           start=(kc == 0), stop=(kc == 7)), reads=[res] + xall, writes=[self.BK(t)])
            P.end_group()
            P.add("act", lambda e, t=t: e.activation(out=fl[0:16, t * 512:(t + 1) * 512], in_=self.bank(t)[0:16, :], func=AF.Exp,
                                                     scale=-1.0, bias=small[0:16, 1:2]), reads=[self.BK(t), "small"], writes=[f"S:fl{t}"])
        for t in range(4):
            P.add("act", lambda e, t=t: e.activation(out=fl[0:16, t * 512:(t + 1) * 512], in_=fl[0:16, t * 512:(t + 1) * 512], func=AF.Ln,
                                                     scale=1.0, bias=1.0), reads=[f"S:fl{t}"], writes=[f"S:fl{t}"])
        for t in range(4):
            init = 0.0 if t == 0 else cum[0:16, t * 512 - 1:t * 512]
            P.add("dve", lambda e, t=t, init=init: e.tensor_tensor_scan(out=cum[0:16, t * 512:(t + 1) * 512], data0=ones[0:16, :],
                                                                        data1=fl[0:16, t * 512:(t + 1) * 512], initial=init,
                                                                        op0=ALU.mult, op1=ALU.subtract),
                  reads=[f"S:fl{t}", "S:ones", "S:cum"], writes=["S:cum"])
        for r in range(3):
            P.add("dve", lambda e, r=r: e.tensor_copy(out=QG[0:16, r, :], in_=cum[0:16, :]), reads=["S:cum"], writes=[f"S:QG0{r}"])
            if r < 2:
                P.add("dve", lambda e, r=r: e.tensor_tensor(out=cum[0:16, :], in0=cum[0:16, :], in1=QG[0:16, r, :], op=ALU.subtract),
                      reads=["S:cum", f"S:QG0{r}"], writes=["S:cum"])
        P.add("dve", lambda e: e.tensor_scalar(out=KG[0:16, 3:6, :], in0=QG[0:16, 0:3, :], scalar1=-1.0, scalar2=None, op0=ALU.mult),
              reads=["S:QG00", "S:QG01", "S:QG02"], writes=["S:KG0"])
        gq, gk = dram["augq"], dram["augk"]
        P.add("sp", lambda e: e.dma_start(out=gq, in_=QG[0:16, :, :]), reads=["S:QG00", "S:QG01", "S:QG02", "S:QG1"], writes=["augq"], dma_key="augq")
        P.add("sp", lambda e: e.dma_start(out=gk, in_=KG[0:16, :, :]), reads=["S:KG0", "S:KG1"], writes=["augk"], dma_key="augk")
        self.reset_scratch()
        AUG = [[[self.carve(1024, BF16) for qk in range(2)] for sub in range(2)] for st in range(2)]
        VP = [self.carve(1040, BF16).rearrange("p (t s d) -> p t s d", t=16, s=2) for st in range(2)]
        OTP = [self.carve(1024, BF16) for st in range(2)]
        PT = [self.carve(256, BF16) for _ in range(4)]
        PTD = [self.carve(256, BF16) for _ in range(4)]
        rc = self.carve(512)
        bcs = self.carve(512)
        self.load_ln(2)
        for st in range(2):
            P.add("pool", lambda e, st=st: e.memset(VP[st][:, :, :, 64:65], 1.0), writes=[f"S:VPone{st}"])
        for i4 in range(1, 4):
            P.add("pool", lambda e, i4=i4: e.memset(PTD[i4][:, 0:128 * i4], 0.0), writes=[f"S:PTD{i4}"])
        ptc = [0]

        def pair(hp):
            st = hp % 2
            qk_t, qk_r = self.wload(dram["fox_qk"][hp], words=2048)
            v_t, v_r = self.wload(dram["fox_v"][hp], words=1024)
            qkv = qk_t[:, 0:2048].rearrange("p (k c) -> p k c", k=8)
            vv = v_t[:, 0:1024].rearrange("p (k c) -> p k c", k=8)
            for sub in range(2):
                h = 2 * hp + sub
                P.add("sp", lambda e, st=st, sub=sub, h=h: e.dma_start(out=AUG[st][sub][0][64:70, :], in_=gq[h]),
                      reads=["augq"], writes=[f"S:AQa{st}{sub}"], dma_key=f"aq{st}{sub}")
                P.add("sp", lambda e, st=st, sub=sub, h=h: e.dma_start(out=AUG[st][sub][1][64:70, :], in_=gk[h]),
                      reads=["augk"], writes=[f"S:AKa{st}{sub}"], dma_key=f"ak{st}{sub}")
            for t in range(4):
                for qk in range(2):
                    b = self.rotbank("misc", (0, 1, 7))
                    P.begin_group()
                    for kc in range(8):
                        P.add("pe", lambda e, b=b, kc=kc, qk=qk, t=t, qkv=qkv: e.matmul(
                            self.bank(b), lhsT=qkv[:, kc, qk * 128:(qk + 1) * 128], rhs=xTp[:, kc, PAD + t * 512:PAD + (t + 1) * 512],
                            start=(kc == 0), stop=(kc == 7)), reads=[qk_r] + xall, writes=[self.BK(b)])
                    P.end_group()
                    for sub in range(2):
                        dst = AUG[st][sub][qk]
                        nm = f"S:A{'QK'[qk]}{st}{sub}t{t}"
                        if qk == 0:
                            P.add("dve", lambda e, b=b, sub=sub, dst=dst, t=t: e.tensor_scalar(
                                out=dst[0:64, t * 512:(t + 1) * 512], in0=self.bank(b)[sub * 64:(sub + 1) * 64, :], scalar1=0.125, scalar2=None,
                                op0=ALU.mult), reads=[self.BK(b)], writes=[nm])
                        else:
                            P.add("dve", lambda e, b=b, sub=sub, dst=dst, t=t: e.tensor_copy(
                                out=dst[0:64, t * 512:(t + 1) * 512], in_=self.bank(b)[sub * 64:(sub + 1) * 64, :]),
                                reads=[self.BK(b)], writes=[nm])
            for g4 in range(4):
                b = self.rotbank("misc", (0, 1, 7))
                P.begin_group()
                for ti in range(4):
                    tb = 4 * g4 + ti
                    for kc in range(8):
                        P.add("pe", lambda e, b=b, ti=ti, tb=tb, kc=kc, vv=vv: e.matmul(
                            self.bank(b)[:, ti * 128:(ti + 1) * 128], lhsT=xTp[:, kc, PAD + tb * 128:PAD + (tb + 1) * 128], rhs=vv[:, kc, :],
                            start=(kc == 0), stop=(kc == 7)), reads=[v_r, f"xT{tb}"], writes=[self.BK(b)])
                P.end_group()
                P.add("act", lambda e, b=b, g4=g4, st=st: e.activation(
                    out=VP[st][:, 4 * g4:4 * g4 + 4, :, 0:64], in_=self.bank(b).rearrange("p (t s d) -> p t s d", t=4, s=2), func=AF.Identity),
                    reads=[self.BK(b)], writes=[f"S:VP{st}g{g4}"])
            yield
            for sub in range(2):
                if sub == 1:
                    wo_t, wo_r = self.wload(dram["fox_wo"][hp], words=1024)
                QA, KA = AUG[st][sub][0], AUG[st][sub][1]
                for qt in range(4):
                    ob = self.rotbank("O", (2, 3))
                    nkb = 4 * qt + 4
                    for kb in range(nkb):
                        i = kb - 4 * qt
                        co = 128 * i if i > 0 else 0
                        sb_ = self.rotbank("S", (4, 5, 6))
                        if i >= 0:
                            pt, ptr = PTD[i], f"S:PTD{i}"
                        else:
                            pt, ptr = PT[ptc[0] % 4], f"S:PT{ptc[0] % 4}"
                            ptc[0] += 1
                        kres = [f"S:AK{st}{sub}t{kb // 4}", f"S:AKa{st}{sub}", f"S:AQ{st}{sub}t{qt}", f"S:AQa{st}{sub}"]
                        P.begin_group()
                        if i < 0:
                            P.add("pe", lambda e, sb_=sb_, kb=kb, qt=qt, QA=QA, KA=KA: e.matmul(
                                self.bank(sb_)[:, 0:512], lhsT=KA[0:70, kb * 128:(kb + 1) * 128], rhs=QA[0:70, qt * 512:(qt + 1) * 512],
                                start=True, stop=True), reads=kres, writes=[self.BK(sb_)])
                        else:
                            P.add("pe", lambda e, sb_=sb_, co=co, kb=kb, qt=qt, QA=QA, KA=KA: e.matmul(
                                self.bank(sb_)[:, co:co + 128], lhsT=KA[0:70, kb * 128:(kb + 1) * 128], rhs=QA[0:70, qt * 512 + co:qt * 512 + co + 128],
                                start=True, stop=False), reads=kres, writes=[self.BK(sb_)])
                            P.add("pe", lambda e, sb_=sb_, co=co: e.matmul(self.bank(sb_)[:, co:co + 128], lhsT=ident_b[:], rhs=maskneg[:],
                                                                           start=False, stop=True),
                                  reads=["ident_b", "maskneg"], writes=[self.BK(sb_)])
                            if co + 128 < 512:
                                P.add("pe", lambda e, sb_=sb_, co=co, kb=kb, qt=qt, QA=QA, KA=KA: e.matmul(
                                    self.bank(sb_)[:, co + 128:512], lhsT=KA[0:70, kb * 128:(kb + 1) * 128], rhs=QA[0:70, qt * 512 + co + 128:(qt + 1) * 512],
                                    start=True, stop=True), reads=kres, writes=[self.BK(sb_)])
                        P.end_group()
                        P.add("act", lambda e, sb_=sb_, co=co, pt=pt: e.activation(out=pt[:, co:512], in_=self.bank(sb_)[:, co:512], func=AF.Exp),
                              reads=[self.BK(sb_)], writes=[ptr])
                        P.add("pe", lambda e, ob=ob, kb=kb, pt=pt, st=st, sub=sub, nkb=nkb: e.matmul(
                            self.bank(ob)[0:65, 0:512], lhsT=VP[st][:, kb, sub, 0:65], rhs=pt[:, 0:512], start=(kb == 0), stop=(kb == nkb - 1)),
                            reads=[ptr, f"S:VP{st}g{kb // 4}", f"S:VPone{st}"], writes=[self.BK(ob)])
                    P.add("dve", lambda e, ob=ob: e.reciprocal(out=rc[64:65, :], in_=self.bank(ob)[64:65, :]), reads=[self.BK(ob)], writes=["S:rc"])
                    bb = self.rotbank("misc", (0, 1, 7))
                    P.add("pe", lambda e, bb=bb: e.matmul(self.bank(bb)[0:64, :], lhsT=ones_f[64:65, 0:64], rhs=rc[64:65, :], start=True, stop=True),
                          reads=["ones_f", "S:rc"], writes=[self.BK(bb)])
                    P.add("dve", lambda e, bb=bb: e.tensor_copy(out=bcs[0:64, :], in_=self.bank(bb)[0:64, :]), reads=[self.BK(bb)], writes=["S:bcs"])
                    P.add("dve", lambda e, ob=ob, st=st, sub=sub, qt=qt: e.tensor_tensor(
                        out=OTP[st][sub * 64:(sub + 1) * 64, qt * 512:(qt + 1) * 512], in0=self.bank(ob)[0:64, :], in1=bcs[0:64, :], op=ALU.mult),
                        reads=[self.BK(ob), "S:bcs"], writes=[f"S:OT{st}q{qt}"])
                yield
            for tb in range(NTB):
                for dh in range(2):
                    b = self.rotbank("misc", (0, 1, 7))
                    P.add("pe", lambda e, b=b, tb=tb, dh=dh, st=st, wo_t=wo_t: e.matmul(
                        self.bank(b), lhsT=OTP[st][:, tb * 128:(tb + 1) * 128], rhs=wo_t[:, dh * 512:(dh + 1) * 512], start=True, stop=True),
                        reads=[wo_r, f"S:OT{st}q{tb // 4}"], writes=[self.BK(b)])
                    xr = f"xt{tb}_{dh}"
                    if hp == 0:
                        P.add("dve", lambda e, b=b, tb=tb, dh=dh: e.scalar_tensor_tensor(
                            out=x_tok[:, tb, dh * 512:(dh + 1) * 512], in0=x_tok[:, tb, dh * 512:(dh + 1) * 512], scalar=ALPHA,
                            in1=self.bank(b), op0=ALU.mult, op1=ALU.add), reads=[self.BK(b), xr], writes=[xr])
                    else:
                        P.add("dve", lambda e, b=b, tb=tb, dh=dh: e.tensor_tensor(
                            out=x_tok[:, tb, dh * 512:(dh + 1) * 512], in0=self.bank(b), in1=x_tok[:, tb, dh * 512:(dh + 1) * 512], op=ALU.add),
                            reads=[self.BK(b), xr], writes=[xr])
            yield
        gp = [pair(hp) for hp in range(8)]
        next(gp[0])
        for hp in range(8):
            next(gp[hp])
            if hp + 1 < 8:
                next(gp[hp + 1])
            next(gp[hp])
            next(gp[hp])
        for tb in range(NTB):
            self.ln_tb(tb)
            self.transpose_tb(tb, tb % 4, affine=True)
            self.ln_affine(tb)


    def mix0(self):
        P, xTp, x_tok, dram = self.P, self.xTp, self.x_tok, self.dram
        ones_f, ident_f, ident_b, maskneg = self.ones_f, self.ident_f, self.ident_b, self.maskneg
        self.reset_scratch()
        H = self.carve(1024)
        prevbf = self.carve(512, BF16)
        R = self.carve(2048)
        L = [self.carve(128), self.carve(128)]
        halo = self.carve(16).rearrange("p (i k) -> p i k", i=8)
        fixx = self.carve(36).rearrange("p (i k) -> p i k", i=12)
        cst = self.carve(128)
        biasbc, Dbc = cst[:, 0:16], cst[:, 16:32]
        gT = cst[:, 32:40]
        scw = cst[:, 40:64].rearrange("p (i k) -> p i k", i=8)
        xcw = cst[:, 64:124].rearrange("p (i k) -> p i k", i=12)
        s_tok, s_feat, s_head = dram["m0_tokc"].rearrange("a h -> (a h)").partition_broadcast(128), dram["m0_featc"], dram["m0_headc"]
        P.add("sp", lambda e: e.dma_start(out=cst[:, 0:32], in_=s_tok), writes=["S:cst"], dma_key="m0c0")
        P.add("sp", lambda e: e.dma_start(out=cst[:, 32:124], in_=s_feat), writes=["S:cst"], dma_key="m0c1")
        P.add("sp", lambda e: e.dma_start(out=cst[0:16, 124:126], in_=s_head), writes=["S:cst"], dma_key="m0c2")
        P.add("act", lambda e: e.activation(out=cst[0:16, 126:127], in_=cst[0:16, 125:126], func=AF.Exp), reads=["S:cst"], writes=["S:cst"])
        P.add("dve", lambda e: e.tensor_scalar(out=cst[0:16, 127:128], in0=cst[0:16, 126:127], scalar1=-1.0, scalar2=None, op0=ALU.mult),
              reads=["S:cst"], writes=["S:cst"])
        dtb, acol = cst[0:16, 124:125], cst[0:16, 127:128]
        P.add("pool", lambda e: e.memset(H[:, :], 0.0), writes=["S:H"])
        P.add("pool", lambda e: e.memset(prevbf[:, :], 0.0), writes=["S:prev"])
        P.add("pool", lambda e: e.memset(R[0:48, :], 0.0), writes=["S:R"])
        P.add("pool", lambda e: e.memset(R[32:48, :], 1.0), reads=["S:R"], writes=["S:R"])
        P.add("pool", lambda e: e.affine_select(out=R[32:48, :].rearrange("p (h t) -> p h t", h=16), in_=R[32:48, :].rearrange("p (h t) -> p h t", h=16),
                                                 pattern=[[-1, 16], [0, 128]], compare_op=ALU.is_equal, fill=0.0, base=0, channel_multiplier=1),
              reads=["S:R"], writes=["S:R"])
        for k in range(2):
            P.add("pool", lambda e, k=k: e.memset(L[k][0:48, :], 0.0), writes=[f"S:L{k}"])
            P.add("pool", lambda e, k=k: e.memset(L[k][0:16, :], 1.0), reads=[f"S:L{k}"], writes=[f"S:L{k}"])
        base = self.sp_
        self.load_lnT(0)
        wx, wz, wsc, wo = dram["m0_wx"], dram["m0_wz"], dram["m0_wsc"], dram["m0_wo"]

        for qi in range(4):
            self.P.fence()
            self.sp_ = base
            yaT = self.carve(2048, BF16).rearrange("p (i t) -> p i t", i=8)
            ybT = self.carve(2048, BF16).rearrange("p (i t) -> p i t", i=8)
            xdt = self.carve(2048, BF16).rearrange("p (b f) -> p b f", b=4)
            zs = self.carve(2048, BF16).rearrange("p (b f) -> p b f", b=4)
            BcT = self.carve(512, BF16).rearrange("p (g t) -> p g t", g=2)
            CcT = self.carve(512, BF16).rearrange("p (g t) -> p g t", g=2)
            Btok = self.carve(512, BF16).rearrange("p (b f) -> p b f", b=4)
            dtok = self.carve(64).rearrange("p (b h) -> p b h", b=4)
            Ddt = self.carve(64).rearrange("p (b h) -> p b h", b=4)
            acsT = self.carve(512)
            dtT = self.carve(512)
            qbase = self.sp_
            a_ = [self.carve(512), self.carve(512)]
            prod = [self.carve(516), self.carve(516)]
            cs = [self.carve(512), self.carve(512)]
            c0 = PAD + 512 * qi
            xres = [f"xT{tb}" for tb in range(4 * qi, 4 * qi + 4)]
            win = lambda kc, c0=c0: xTp[:, kc, c0:c0 + 512]

            dt_t, dt_r = self.wload(dram["m0_wdt"], words=128)
            dv = dt_t[:, 0:128].rearrange("p (k c) -> p k c", k=8)
            P.begin_group()
            for kc in range(8):
                P.add("pe", lambda e, kc=kc, dv=dv, win=win: e.matmul(self.bank(0)[0:16, :], lhsT=dv[:, kc, :], rhs=win(kc), start=(kc == 0), stop=(kc == 7)),
                      reads=[dt_r] + xres, writes=[self.BK(0)])
            P.end_group()
            P.begin_group()
            for tbl in range(4):
                for kc in range(8):
                    P.add("pe", lambda e, kc=kc, tbl=tbl, dv=dv, c0=c0: e.matmul(self.bank(1)[:, tbl * 16:(tbl + 1) * 16],
                                                                               lhsT=xTp[:, kc, c0 + tbl * 128:c0 + (tbl + 1) * 128], rhs=dv[:, kc, :],
                                                                               start=(kc == 0), stop=(kc == 7)),
                          reads=[dt_r] + xres, writes=[self.BK(1)])
            P.end_group()
            P.add("act", lambda e, dtT=dtT: e.activation(out=dtT[0:16, :], in_=self.bank(0)[0:16, :], func=AF.Exp, bias=dtb, scale=1.0),
                  reads=[self.BK(0), "S:cst"], writes=["S:dtT"])
            P.add("dve", lambda e, dtok=dtok: e.tensor_tensor(out=dtok[:, :, :], in0=self.bank(1)[:, 0:64].rearrange("p (b h) -> p b h", b=4),
                                                             in1=biasbc.unsqueeze(1).to_broadcast([128, 4, 16]), op=ALU.add),
                  reads=[self.BK(1), "S:cst"], writes=["S:dtok"])
            P.add("act", lambda e, dtok=dtok: e.activation(out=dtok[:, :, :], in_=dtok[:, :, :], func=AF.Exp), reads=["S:dtok"], writes=["S:dtok"])
            P.add("act", lambda e, dtT=dtT: e.activation(out=dtT[0:16, :], in_=dtT[0:16, :], func=AF.Ln, bias=1.0, scale=1.0), reads=["S:dtT"], writes=["S:dtT"])
            P.add("act", lambda e, dtok=dtok: e.activation(out=dtok[:, :, :], in_=dtok[:, :, :], func=AF.Ln, bias=1.0, scale=1.0),
                  reads=["S:dtok"], writes=["S:dtok"])
            P.add("dve", lambda e, dtok=dtok, Ddt=Ddt: e.reciprocal(out=Ddt[:, :, :], in_=dtok[:, :, :]), reads=["S:dtok"], writes=["S:Ddt"])
            P.add("dve", lambda e, Ddt=Ddt: e.tensor_tensor(out=Ddt[:, :, :], in0=Ddt[:, :, :], in1=Dbc.unsqueeze(1).to_broadcast([128, 4, 16]), op=ALU.mult),
                  reads=["S:Ddt", "S:cst"], writes=["S:Ddt"])
            P.add("dve", lambda e, dtT=dtT: e.tensor_scalar(out=dtT[0:16, :], in0=dtT[0:16, :], scalar1=acol, scalar2=None, op0=ALU.mult),
                  reads=["S:dtT", "S:cst"], writes=["S:dtT"])
            for c in range(4):
                P.add("dve", lambda e, c=c, dtT=dtT, acsT=acsT: e.tensor_tensor_scan(out=acsT[0:16, c * 128:(c + 1) * 128], data0=ones_f[0:16, 0:128],
                                                                                   data1=dtT[0:16, c * 128:(c + 1) * 128], initial=0.0,
                                                                                   op0=ALU.mult, op1=ALU.add),
                      reads=["S:dtT", "ones_f"], writes=[f"S:acs{c}"])

            def xchunk(sl, cc, ak, xv, x_r):
                if True:
                    if sl == 0:
                        ci = 8 + cc
                    else:
                        ci = 4 * (sl - 1) + cc
                    b = self.rotbank("m0", (0, 1, 2, 3, 4, 5))
                    P.begin_group()
                    for kc in range(8):
                        P.add("pe", lambda e, b=b, kc=kc, cc=cc, xv=xv, win=win: e.matmul(self.bank(b), lhsT=xv[:, kc, cc * 128:(cc + 1) * 128], rhs=win(kc),
                                                                                         start=(kc == 0), stop=(kc == 7)),
                              reads=[x_r] + xres, writes=[self.BK(b)])
                    P.end_group()
                    a = a_[ak % 2]
                    ar = f"S:a{ak % 2}"
                    bk = [self.BK(b)]
                    P.add("act", lambda e, b=b, a=a, ci=ci: e.activation(out=a[:, 0:512], in_=self.bank(b), func=AF.Identity,
                                                                        scale=xcw[:, ci, 3:4], bias=xcw[:, ci, 4:5]), reads=bk + ["S:cst"], writes=[ar])
                    for sh in range(1, 4):
                        P.add("dve", lambda e, b=b, a=a, ci=ci, sh=sh: e.scalar_tensor_tensor(
                            out=a[:, sh:512], in0=self.bank(b)[:, 0:512 - sh], scalar=xcw[:, ci, 3 - sh:4 - sh], in1=a[:, sh:512], op0=ALU.mult, op1=ALU.add),
                            reads=bk + ["S:cst", ar], writes=[ar])
                    if qi > 0:
                        P.add("dve", lambda e, a=a, ci=ci: e.tensor_tensor(out=a[:, 0:3], in0=a[:, 0:3], in1=fixx[:, ci, 0:3], op=ALU.add),
                              reads=[ar, f"S:fx{ci}"], writes=[ar])
                    if qi < 3:
                        P.add("dve", lambda e, b=b, ci=ci: e.tensor_scalar(out=fixx[:, ci, 0:3], in0=self.bank(b)[:, 509:512], scalar1=xcw[:, ci, 0:1],
                                                                          scalar2=None, op0=ALU.mult), reads=bk + ["S:cst"], writes=[f"S:fx{ci}"])
                        P.add("dve", lambda e, b=b, ci=ci: e.scalar_tensor_tensor(out=fixx[:, ci, 0:2], in0=self.bank(b)[:, 510:512], scalar=xcw[:, ci, 1:2],
                                                                                 in1=fixx[:, ci, 0:2], op0=ALU.mult, op1=ALU.add),
                              reads=bk + ["S:cst", f"S:fx{ci}"], writes=[f"S:fx{ci}"])
                        P.add("dve", lambda e, b=b, ci=ci: e.scalar_tensor_tensor(out=fixx[:, ci, 0:1], in0=self.bank(b)[:, 511:512], scalar=xcw[:, ci, 2:3],
                                                                                 in1=fixx[:, ci, 0:1], op0=ALU.mult, op1=ALU.add),
                              reads=bk + ["S:cst", f"S:fx{ci}"], writes=[f"S:fx{ci}"])
                    if ci >= 10:
                        g = ci - 10
                        yield
                        P.add("act", lambda e, a=a, g=g, CcT=CcT: e.activation(out=CcT[:, g, :], in_=a[:, 0:512], func=AF.Silu), reads=[ar], writes=[f"S:Cc{g}"])
                        yield
                        yield
                        return
                    yield
                    P.add("act", lambda e, a=a: e.activation(out=a[:, 0:512], in_=a[:, 0:512], func=AF.Silu), reads=[ar], writes=[ar])
                    yield
                    tbk = self.rotbank("m0t", (6, 7))
                    P.begin_group()
                    for tbl in range(4):
                        P.add("pe", lambda e, tbk=tbk, tbl=tbl, a=a: e.transpose(self.bank(tbk)[:, tbl * 128:(tbl + 1) * 128], a[:, tbl * 128:(tbl + 1) * 128], ident_f[:]),
                              reads=[ar, "ident_f"], writes=[self.BK(tbk)])
                    P.end_group()
                    if ci >= 8:
                        g = ci - 8
                        P.add("pool", lambda e, a=a, g=g, BcT=BcT: e.tensor_copy(out=BcT[:, g, :], in_=a[:, 0:512]), reads=[ar], writes=[f"S:Bc{g}"])
                        P.add("act", lambda e, tbk=tbk, g=g, Btok=Btok: e.activation(out=Btok[:, :, g * 128:(g + 1) * 128],
                                                                                    in_=self.bank(tbk).rearrange("p (b f) -> p b f", b=4), func=AF.Identity),
                              reads=[self.BK(tbk)], writes=[f"S:Bt{g}"])
                    else:
                        P.add("dve", lambda e, tbk=tbk, ci=ci, xdt=xdt, dtok=dtok: e.tensor_tensor(
                            out=xdt[:, :, ci * 128:(ci + 1) * 128].rearrange("p b (h d) -> p b h d", h=2),
                            in0=self.bank(tbk).rearrange("p (b h d) -> p b h d", b=4, h=2),
                            in1=dtok[:, :, 2 * ci:2 * ci + 2].unsqueeze(3).to_broadcast([128, 4, 2, 64]), op=ALU.mult),
                            reads=[self.BK(tbk), "S:dtok"], writes=[f"S:xdt{ci}"])
                    yield
            xg = []
            for sl in range(3):
                x_t, x_r = self.wload(wx[sl])
                xv = x_t[:].rearrange("p (k c) -> p k c", k=8)
                for cc in range(4):
                    k = len(xg)
                    xg.append(xchunk(sl, cc, k, xv, x_r))
                    next(xg[k])
                    if k >= 1:
                        next(xg[k - 1])
                    next(xg[k])
            next(xg[-1])

            for zsl in range(2):
                z_t, z_r = self.wload(wz[zsl])
                zv = z_t[:].rearrange("p (k c) -> p k c", k=8)
                for tbl in range(4):
                    b = self.rotbank("m0", (0, 1, 2, 3, 4, 5))
                    P.begin_group()
                    for kc in range(8):
                        P.add("pe", lambda e, b=b, kc=kc, tbl=tbl, zv=zv, c0=c0: e.matmul(self.bank(b), lhsT=xTp[:, kc, c0 + tbl * 128:c0 + (tbl + 1) * 128],
                                                                                         rhs=zv[:, kc, :], start=(kc == 0), stop=(kc == 7)),
                              reads=[z_r] + xres, writes=[self.BK(b)])
                    P.end_group()
                    P.add("act", lambda e, b=b, tbl=tbl, zsl=zsl, zs=zs: e.activation(out=zs[:, tbl, zsl * 512:(zsl + 1) * 512], in_=self.bank(b), func=AF.Silu),
                          reads=[self.BK(b)], writes=[f"S:zs{tbl}"])

            for i in range(8):
                s_t, s_r = self.wload(wsc[i], words=3072)
                sv = s_t[:, 0:3072].rearrange("p (k c) -> p k c", k=8)
                bks = []
                for part in range(3):
                    b = self.rotbank("m0", (0, 1, 2, 3, 4, 5))
                    bks.append(b)
                    P.begin_group()
                    for kc in range(8):
                        P.add("pe", lambda e, b=b, kc=kc, part=part, sv=sv, win=win: e.matmul(self.bank(b), lhsT=sv[:, kc, part * 128:(part + 1) * 128], rhs=win(kc),
                                                                                             start=(kc == 0), stop=(kc == 7)),
                              reads=[s_r] + xres, writes=[self.BK(b)])
                    P.end_group()
                bc_, bh_, bb_ = bks
                k2 = i % 2
                pr, csb, a = prod[k2], cs[k2], a_[k2]
                prr, csr, ar = f"S:pr{k2}", f"S:cs{k2}", f"S:a{k2}"
                P.add("act", lambda e, bc_=bc_, csb=csb: e.activation(out=csb[:, 0:512], in_=self.bank(bc_), func=AF.Identity), reads=[self.BK(bc_)], writes=[csr])
                if qi == 0:
                    P.add("pool", lambda e, pr=pr: e.memset(pr[:, 0:2], 0.0), writes=[prr + "h"])
                else:
                    P.add("pool", lambda e, pr=pr, i=i: e.tensor_copy(out=pr[:, 0:2], in_=halo[:, i, :]), reads=[f"S:halo{i}"], writes=[prr + "h"])
                P.add("dve", lambda e, bh_=bh_, pr=pr, csb=csb: e.tensor_tensor(out=pr[:, 2:514], in0=self.bank(bh_), in1=csb[:, 0:512], op=ALU.mult),
                      reads=[self.BK(bh_), csr], writes=[prr])
                if qi < 3:
                    P.add("pool", lambda e, pr=pr, i=i: e.tensor_copy(out=halo[:, i, :], in_=pr[:, 512:514]), reads=[prr], writes=[f"S:halo{i}"])
                P.add("act", lambda e, pr=pr, a=a, i=i: e.activation(out=a[:, 0:512], in_=pr[:, 2:514], func=AF.Identity, scale=scw[:, i, 2:3]),
                      reads=[prr, "S:cst"], writes=[ar])
                P.add("dve", lambda e, pr=pr, a=a, i=i: e.scalar_tensor_tensor(out=a[:, 0:512], in0=pr[:, 1:513], scalar=scw[:, i, 1:2], in1=a[:, 0:512],
                                                                              op0=ALU.mult, op1=ALU.add), reads=[prr, prr + "h", "S:cst", ar], writes=[ar])
                P.add("dve", lambda e, pr=pr, a=a, i=i: e.scalar_tensor_tensor(out=a[:, 0:512], in0=pr[:, 0:512], scalar=scw[:, i, 0:1], in1=a[:, 0:512],
                                                                              op0=ALU.mult, op1=ALU.add), reads=[prr, prr + "h", "S:cst", ar], writes=[ar])
                P.add("dve", lambda e, bb_=bb_, a=a, i=i, yaT=yaT: e.tensor_tensor(out=yaT[:, i, :], in0=self.bank(bb_), in1=a[:, 0:512], op=ALU.mult),
                      reads=[self.BK(bb_), ar], writes=[f"S:ya{i}"])

            self.P.fence()
            self.sp_ = qbase
            segT = self.carve(1024, BF16).rearrange("p (h t) -> p h t", h=16)
            MT = self.carve(1024, BF16).rearrange("p (h t) -> p h t", h=16)
            xdtd = self.carve(512, BF16)
            yt_ = [self.carve(1024), self.carve(1024)]
            junk = self.carve(256, BF16)
            sm_ = [self.carve(64), self.carve(64)]
            X16 = self.carve(16)
            def chunk(c):
                gc = 4 * qi + c
                cols = slice(c * 128, (c + 1) * 128)
                Lm, Lr = L[gc % 2], f"S:L{gc % 2}"
                yt, ytr = yt_[gc % 2], f"S:yt{gc % 2}"
                sm, smr = sm_[gc % 2], f"S:sm{gc % 2}"
                acr = f"S:acs{c}"
                P.add("dve", lambda e, Lm=Lm, cols=cols, acsT=acsT: e.tensor_scalar(out=Lm[32:48, :], in0=acsT[0:16, cols], scalar1=-1.0, scalar2=None, op0=ALU.mult),
                      reads=[acr], writes=[Lr])
                P.add("dve", lambda e, cols=cols, acsT=acsT: e.tensor_tensor(
                    out=R[0:16, :].rearrange("p (h t) -> p h t", h=16), in0=acsT[0:16, cols].unsqueeze(1).to_broadcast([16, 16, 128]),
                    in1=ident_f[0:16, 0:16].unsqueeze(2).to_broadcast([16, 16, 128]), op=ALU.mult), reads=[acr, "ident_f"], writes=["S:R"])
                for hg in range(4):
                    b = hg % 2
                    P.begin_group()
                    for hh in range(4):
                        P.add("pe", lambda e, b=b, hg=hg, hh=hh, Lm=Lm: e.matmul(self.bank(b)[:, hh * 128:(hh + 1) * 128], lhsT=Lm[0:48, :],
                                                                                rhs=R[0:48, hg * 512 + hh * 128:hg * 512 + (hh + 1) * 128], start=True, stop=False),
                              reads=[Lr, "S:R"], writes=[self.BK(b)])
                        P.add("pe", lambda e, b=b, hh=hh: e.matmul(self.bank(b)[:, hh * 128:(hh + 1) * 128], lhsT=ident_b[:], rhs=maskneg[:], start=False, stop=True),
                              reads=["ident_b", "maskneg"], writes=[self.BK(b)])
                    P.end_group()
                    P.add("act", lambda e, b=b, hg=hg, segT=segT: e.activation(out=segT[:, 4 * hg:4 * hg + 4, :], in_=self.bank(b).rearrange("p (h t) -> p h t", h=4),
                                                                              func=AF.Exp), reads=[self.BK(b)], writes=[f"S:seg{hg}"])
                segr = [f"S:seg{hg}" for hg in range(4)]
                P.begin_group()
                for g in range(2):
                    P.add("pe", lambda e, g=g, cols=cols, BcT=BcT, CcT=CcT: e.matmul(self.bank(2)[:, g * 128:(g + 1) * 128], lhsT=BcT[:, g, cols], rhs=CcT[:, g, cols],
                                                                                    start=True, stop=True), reads=[f"S:Bc{g}", f"S:Cc{g}"], writes=["bk2"])
                P.end_group()
                P.add("dve", lambda e, cols=cols, acsT=acsT: e.tensor_scalar(out=X16[0:16, 0:16], in0=ident_f[0:16, 0:16],
                                                                            scalar1=acsT[0:16, cols][:, 127:128], scalar2=None, op0=ALU.mult),
                      reads=[acr, "ident_f"], writes=["S:X16"])
                P.begin_group()
                P.add("pe", lambda e: e.matmul(self.bank(3)[:, 0:16], lhsT=ones_f[0:16, :], rhs=X16[0:16, 0:16], start=True, stop=True),
                      reads=["ones_f", "S:X16"], writes=["bk3"])
                P.add("pe", lambda e, cols=cols, acsT=acsT: e.transpose(self.bank(3)[:, 16:32], acsT[0:16, cols], ident_f[0:16, 0:16]),
                      reads=[acr, "ident_f"], writes=["bk3"])
                P.end_group()
                P.add("act", lambda e, sm=sm: e.activation(out=sm[:, 0:32], in_=self.bank(3)[:, 0:32], func=AF.Exp), reads=["bk3"], writes=[smr])
                yield
                for g in range(2):
                    P.add("dve", lambda e, g=g, MT=MT, segT=segT: e.tensor_tensor(
                        out=MT[:, 8 * g:8 * g + 8, :], in0=self.bank(2)[:, g * 128:(g + 1) * 128].unsqueeze(1).to_broadcast([128, 8, 128]),
                        in1=segT[:, 8 * g:8 * g + 8, :], op=ALU.mult), reads=["bk2"] + segr, writes=[f"S:MT{g}"])
                P.add("dve", lambda e, c=c, xdt=xdt, xdtd=xdtd, segT=segT: e.tensor_tensor(
                    out=xdtd[:, :].rearrange("p (h d) -> p h d", h=16), in0=xdt[:, c, :].rearrange("p (h d) -> p h d", h=16),
                    in1=segT[:, :, 127:128].to_broadcast([128, 16, 64]), op=ALU.mult),
                    reads=[f"S:xdt{i}" for i in range(8)] + segr, writes=["S:xdtd"])
                yield
                P.begin_group()
                for g in range(2):
                    P.add("pe", lambda e, g=g, cols=cols, CcT=CcT: e.matmul(self.PS[2][:, g * 512:(g + 1) * 512], lhsT=CcT[:, g, cols], rhs=prevbf[:, g * 512:(g + 1) * 512],
                                                                           start=True, stop=True), reads=[f"S:Cc{g}", "S:prev"], writes=[self.BK(4 + g)])
                P.end_group()
                P.begin_group()
                for g in range(2):
                    P.add("pe", lambda e, g=g, c=c, Btok=Btok, xdtd=xdtd: e.matmul(self.PS[0][:, g * 512:(g + 1) * 512], lhsT=Btok[:, c, g * 128:(g + 1) * 128],
                                                                                  rhs=xdtd[:, g * 512:(g + 1) * 512], start=True, stop=True),
                          reads=[f"S:Bt{g}", "S:xdtd"], writes=[self.BK(g)])
                P.end_group()
                P.add("dve", lambda e, sm=sm: e.tensor_tensor(out=H[:, :].rearrange("p (h d) -> p h d", h=16), in0=H[:, :].rearrange("p (h d) -> p h d", h=16),
                                                              in1=sm[:, 0:16].unsqueeze(2).to_broadcast([128, 16, 64]), op=ALU.mult),
                      reads=["S:H", smr], writes=["S:H"])
                P.add("dve", lambda e: e.tensor_tensor(out=H[:, :], in0=self.PS[0][:, :], in1=H[:, :], op=ALU.add), reads=["S:H", self.BK(0), self.BK(1)], writes=["S:H"])
                P.add("act", lambda e: e.activation(out=prevbf[:, :], in_=H[:, :], func=AF.Identity), reads=["S:H"], writes=["S:prev"])
                P.begin_group()
                for h in range(16):
                    P.add("pe", lambda e, h=h, c=c, MT=MT, xdt=xdt: e.matmul(self.PS[3][:, h * 64:(h + 1) * 64], lhsT=MT[:, h, :], rhs=xdt[:, c, h * 64:(h + 1) * 64],
                                                                            start=True, stop=True),
                          reads=[f"S:MT{h // 8}", f"S:xdt{h // 2}"], writes=[self.BK(6 + h // 8)])
                P.end_group()
                yield
                P.add("dve", lambda e, yt=yt, sm=sm: e.tensor_tensor(out=yt[:, :].rearrange("p (h d) -> p h d", h=16), in0=self.PS[2][:, :].rearrange("p (h d) -> p h d", h=16),
                                                                     in1=sm[:, 16:32].unsqueeze(2).to_broadcast([128, 16, 64]), op=ALU.mult),
                      reads=[self.BK(4), self.BK(5), smr], writes=[ytr])
                P.add("dve", lambda e, yt=yt: e.tensor_tensor(out=yt[:, :], in0=self.PS[3][:, :], in1=yt[:, :], op=ALU.add), reads=[self.BK(6), self.BK(7), ytr], writes=[ytr])
                for half in range(2):
                    P.add("pool", lambda e, c=c, half=half, xdt=xdt, Ddt=Ddt: e.tensor_tensor(
                        out=junk[:, 0:512].rearrange("p (h d) -> p h d", h=8), in0=xdt[:, c, half * 512:(half + 1) * 512].rearrange("p (h d) -> p h d", h=8),
                        in1=Ddt[:, c, 8 * half:8 * half + 8].unsqueeze(2).to_broadcast([128, 8, 64]), op=ALU.mult),
                        reads=[f"S:xdt{i}" for i in range(8)] + ["S:Ddt"], writes=["S:junk"])
                    P.add("pool", lambda e, yt=yt, half=half: e.tensor_tensor(out=yt[:, half * 512:(half + 1) * 512], in0=yt[:, half * 512:(half + 1) * 512],
                                                                           in1=junk[:, 0:512], op=ALU.add), reads=[ytr, "S:junk"], writes=[ytr])
                P.add("pool", lambda e, yt=yt, c=c, zs=zs: e.tensor_tensor(out=yt[:, :], in0=yt[:, :], in1=zs[:, c, :], op=ALU.mult), reads=[ytr, f"S:zs{c}"], writes=[ytr])
                for g in range(2):
                    P.add("act", lambda e, g=g, yt=yt, sm=sm: e.activation(out=junk[:, 0:512], in_=yt[:, g * 512:(g + 1) * 512], func=AF.Square,
                                                                          accum_out=sm[:, 32 + g:33 + g]), reads=[ytr], writes=[smr + "s", "S:junk"])
                P.add("pool", lambda e, sm=sm: e.tensor_scalar(out=sm[:, 34:36], in0=sm[:, 32:34], scalar1=1.0 / 512.0, scalar2=LN_EPS, op0=ALU.mult, op1=ALU.add),
                      reads=[smr + "s"], writes=[smr + "r"])
                P.add("pool", lambda e, sm=sm: e.tensor_tensor(out=sm[:, 34:36], in0=sm[:, 34:36], in1=self.neghalf[:, 0:1].to_broadcast([128, 2]), op=ALU.pow),
                      reads=[smr + "r", "neghalf"], writes=[smr + "r"])
                for g in range(2):
                    P.add("act", lambda e, g=g, yt=yt, sm=sm: e.activation(out=yt[:, g * 512:(g + 1) * 512], in_=yt[:, g * 512:(g + 1) * 512], func=AF.Identity,
                                                                          scale=sm[:, 34 + g:35 + g]), reads=[ytr, smr + "r"], writes=[ytr])
                yield
                P.begin_group()
                for i in range(8):
                    P.add("pe", lambda e, i=i, yt=yt: e.transpose(self.PS[2][:, i * 128:(i + 1) * 128], yt[:, i * 128:(i + 1) * 128], ident_f[:]),
                          reads=[ytr, "ident_f"], writes=[self.BK(4 + i // 4)])
                P.end_group()
                for half in range(2):
                    P.add("dve", lambda e, half=half, cols=cols, ybT=ybT: e.tensor_tensor(
                        out=ybT[:, 4 * half:4 * half + 4, cols], in0=self.PS[2][:, half * 512:(half + 1) * 512].rearrange("p (i t) -> p i t", i=4),
                        in1=gT[:, 4 * half:4 * half + 4].unsqueeze(2).to_broadcast([128, 4, 128]), op=ALU.mult),
                        reads=[self.BK(4 + half), "S:cst"], writes=[f"S:yb{c}"])

                yield
            gens = [chunk(c) for c in range(4)]
            order = [0, 0, 0, 1, 0, 1, 0, 1, 2, 1, 2, 1, 2, 3, 2, 3, 2, 3, 3, 3]
            for gi in order:
                next(gens[gi])
            for dh in range(2):
                for part in range(2):
                    o_t, o_r = self.wload(wo[dh, part])
                    ov = o_t[:].rearrange("p (k c) -> p k c", k=8)
                    src = yaT if part == 0 else ybT
                    for tbl in range(4):
                        P.begin_group()
                        for kc in range(8):
                            rd = [o_r, (f"S:ya{kc}" if part == 0 else f"S:yb{tbl}")]
                            P.add("pe", lambda e, dh=dh, part=part, tbl=tbl, kc=kc, ov=ov, src=src: e.matmul(
                                self.bank(4 * dh + tbl), lhsT=src[:, kc, tbl * 128:(tbl + 1) * 128], rhs=ov[:, kc, :],
                                start=(part == 0 and kc == 0), stop=(part == 1 and kc == 7)), reads=rd, writes=[self.BK(4 * dh + tbl)])
                        P.end_group()
                for tbl in range(4):
                    gtb = 4 * qi + tbl
                    P.add("dve", lambda e, dh=dh, tbl=tbl, gtb=gtb: e.scalar_tensor_tensor(
                        out=x_tok[:, gtb, dh * 512:(dh + 1) * 512], in0=x_tok[:, gtb, dh * 512:(dh + 1) * 512], scalar=ALPHA,
                        in1=self.bank(4 * dh + tbl), op0=ALU.mult, op1=ALU.add), reads=[self.BK(4 * dh + tbl), f"xt{gtb}_{dh}"], writes=[f"xt{gtb}_{dh}"])
            for tbl in range(4):
                gtb = 4 * qi + tbl
                self.ln_tb(gtb)
                self.transpose_tb(gtb, tbl, affine=True)
        self.P.fence()
        self.sp_ = base
        self.load_ln(0, with_T=False)
        for tb in range(NTB):
            self.ln_affine(tb)


def declare_dram(nc, phases):
    d = {}
    d["x"] = nc.dram_tensor("x", [T, D], F32, kind="ExternalInput").ap()
    d["out"] = nc.dram_tensor("out", [T, D], F32, kind="ExternalOutput").ap()
    d["lnp"] = nc.dram_tensor("lnp", [4, 2, D], F32, kind="ExternalInput").ap()
    d["lnpT"] = nc.dram_tensor("lnpT", [4, 128, 16], F32, kind="ExternalInput").ap()
    d["ffn_cwb"] = nc.dram_tensor("ffn_cwb", [2, 128, 44, 4], F32, kind="ExternalInput").ap()
    d["wup"] = nc.dram_tensor("wup", [2, 11, 128, 4096], F32, kind="ExternalInput").ap()
    d["wdn"] = nc.dram_tensor("wdn", [2, 2, 3, 128, 4096], F32, kind="ExternalInput").ap()
    d["m0_wdt"] = nc.dram_tensor("m0_wdt", [128, 128], F32, kind="ExternalInput").ap()
    d["m0_wx"] = nc.dram_tensor("m0_wx", [3, 128, 4096], F32, kind="ExternalInput").ap()
    d["m0_wz"] = nc.dram_tensor("m0_wz", [2, 128, 4096], F32, kind="ExternalInput").ap()
    d["m0_wsc"] = nc.dram_tensor("m0_wsc", [8, 128, 3072], F32, kind="ExternalInput").ap()
    d["m0_wo"] = nc.dram_tensor("m0_wo", [2, 2, 128, 4096], F32, kind="ExternalInput").ap()
    d["m0_tokc"] = nc.dram_tensor("m0_tokc", [2, 16], F32, kind="ExternalInput").ap()
    d["m0_featc"] = nc.dram_tensor("m0_featc", [128, 92], F32, kind="ExternalInput").ap()
    d["m0_headc"] = nc.dram_tensor("m0_headc", [16, 2], F32, kind="ExternalInput").ap()
    d["fox_f"] = nc.dram_tensor("fox_f", [128, 128], F32, kind="ExternalInput").ap()
    d["fox_bf"] = nc.dram_tensor("fox_bf", [16, 1], F32, kind="ExternalInput").ap()
    d["fox_qk"] = nc.dram_tensor("fox_qk", [8, 128, 2048], F32, kind="ExternalInput").ap()
    d["fox_v"] = nc.dram_tensor("fox_v", [8, 128, 1024], F32, kind="ExternalInput").ap()
    d["fox_wo"] = nc.dram_tensor("fox_wo", [8, 128, 1024], F32, kind="ExternalInput").ap()
    d["augq"] = nc.dram_tensor("augq", [16, 6, T], BF16, kind="Internal").ap()
    d["augk"] = nc.dram_tensor("augk", [16, 6, T], BF16, kind="Internal").ap()
    return d


def build_program(phases=("mix0", "ffn0", "mix1", "ffn1")):
    nc = bass.Bass("TRN2", target_bir_lowering=False)
    dram = declare_dram(nc, phases)
    P = Prog(nc)
    B = Builder(nc, P, dram)
    B.load_x()
    for tb in range(NTB):
        B.transpose_tb(tb, tb % 4)
    last = phases[-1]
    for ph in phases:
        if ph == "ffn0":
            B.ffn(0, final=(ph == last))
        elif ph == "ffn1":
            B.ffn(1, final=(ph == last))
        elif ph == "mix0":
            P.pin = tuple(os.environ.get("MK_PIN0", "dve,pe").split(","))
            B.mix0()
            P.pin = ("dve",)
            if ph == last:
                for tb in range(NTB):
                    B.store_tb(tb)
        elif ph == "mix1":
            B.mix1()
            if ph == last:
                for tb in range(NTB):
                    B.store_tb(tb)
        else:
            raise NotImplementedError(ph)
    if SCHEDULE:
        P.schedule()
    P.finalize(B.out_dmas)
    P.emit(B.out_dmas)
    P.close()
    return nc


def host_layouts(inp):
    f = np.float32
    o = {}
    o["lnp"] = np.ascontiguousarray(np.stack([
        np.stack([inp["ln_mix_g"][0], inp["ln_mix_b"][0]]), np.stack([inp["ln_ffn_g"][0], inp["ln_ffn_b"][0]]),
        np.stack([inp["ln_mix_g"][1], inp["ln_mix_b"][1]]), np.stack([inp["ln_ffn_g"][1], inp["ln_ffn_b"][1]])]).astype(f))
    o["lnpT"] = np.ascontiguousarray(o["lnp"].reshape(4, 2, 8, 128).transpose(0, 3, 1, 2).reshape(4, 128, 16))
    cw = inp["ffn_conv_w"].astype(f)
    cb = inp["ffn_conv_b"].astype(f)
    cwb = np.concatenate([cw.transpose(0, 2, 1), cb[:, :, None]], axis=2)
    o["ffn_cwb"] = np.ascontiguousarray(cwb.reshape(2, 44, 128, 4).transpose(0, 2, 1, 3))
    wu = inp["ffn_w_up"].astype(f)
    u = wu[:, :, :DFF].reshape(2, 8, 128, 11, 2, 128)
    g = wu[:, :, DFF:].reshape(2, 8, 128, 11, 2, 128)
    ug = np.stack([u, g], axis=5)
    o["wup"] = np.ascontiguousarray(ug.transpose(0, 3, 2, 1, 4, 5, 6).reshape(2, 11, 128, 4096))
    wd = inp["ffn_w_down"].astype(f)
    wdp = np.zeros((2, 24 * 128, 1024), f)
    wdp[:, :DFF] = wd
    wdp = wdp.reshape(2, 3, 8, 128, 2, 512)
    o["wdn"] = np.ascontiguousarray(wdp.transpose(0, 4, 1, 3, 2, 5).reshape(2, 2, 3, 128, 4096))
    w0 = inp["sc_ssm_w_in"][0].astype(f).reshape(8, 128, 5648)
    lay = lambda cols: np.ascontiguousarray(w0[:, :, cols].transpose(1, 0, 2).reshape(128, -1))
    o["m0_wdt"] = lay(slice(5632, 5648))
    o["m0_wx"] = np.stack([lay(slice(5120, 5632)), lay(slice(4096, 4608)), lay(slice(4608, 5120))])
    o["m0_wz"] = np.stack([lay(slice(3072, 3584)), lay(slice(3584, 4096))])
    o["m0_wsc"] = np.stack([lay(np.r_[1024 + 128 * i:1152 + 128 * i, 2048 + 128 * i:2176 + 128 * i, 128 * i:128 + 128 * i]) for i in range(8)])
    wo0 = inp["sc_ssm_w_out"][0].astype(f).reshape(2, 8, 128, 2, 512)
    o["m0_wo"] = np.ascontiguousarray(wo0.transpose(3, 0, 2, 1, 4).reshape(2, 2, 128, 4096))
    o["m0_tokc"] = np.ascontiguousarray(np.stack([inp["ssm_dt_bias"][0], inp["ssm_d"][0]]).astype(f))
    o["m0_headc"] = np.ascontiguousarray(np.stack([inp["ssm_dt_bias"][0], inp["ssm_a_log"][0]], axis=1).astype(f))
    gTh = inp["ssm_norm_g"][0].astype(f).reshape(8, 128).T
    scwh = inp["sc_conv_w"][0].astype(f).reshape(3, 8, 128).transpose(2, 1, 0)
    xw = inp["ssm_conv_w"][0].astype(f).reshape(4, 12, 128).transpose(2, 1, 0)
    xb = inp["ssm_conv_b"][0].astype(f).reshape(12, 128).T[:, :, None]
    o["m0_featc"] = np.ascontiguousarray(np.concatenate([gTh, scwh.reshape(128, 24), np.concatenate([xw, xb], axis=2).reshape(128, 60)], axis=1))
    wi = inp["fox_w_in"][0].astype(f)
    wk = wi.reshape(8, 128, 3088)
    o["fox_f"] = np.ascontiguousarray(wk[:, :, 3072:3088].transpose(1, 0, 2).reshape(128, 128))
    q = wk[:, :, 0:1024].reshape(8, 128, 8, 128)
    k = wk[:, :, 1024:2048].reshape(8, 128, 8, 128)
    v = wk[:, :, 2048:3072].reshape(8, 128, 8, 128)
    qk = np.concatenate([q, k], axis=3)
    o["fox_qk"] = np.ascontiguousarray(qk.transpose(2, 1, 0, 3).reshape(8, 128, 2048))
    o["fox_v"] = np.ascontiguousarray(v.transpose(2, 1, 0, 3).reshape(8, 128, 1024))
    o["fox_wo"] = np.ascontiguousarray(inp["fox_w_out"][0].astype(f).reshape(8, 128, 1024))
    o["fox_bf"] = np.ascontiguousarray(inp["fox_b_f"][0].astype(f).reshape(16, 1))
    return o


_NC_CACHE = {}


def kernel(**inputs):
    phases = ("mix0", "ffn0", "mix1", "ffn1")
    if phases not in _NC_CACHE:
        _NC_CACHE[phases] = build_program(phases)
    nc = _NC_CACHE[phases]
    lay = host_layouts(inputs)
    x = np.asarray(inputs["x"], dtype=np.float32)
    in_maps = [dict(lay, x=np.ascontiguousarray(x[b])) for b in range(8)]
    res = run_bass_kernel_spmd(nc, in_maps, core_ids=list(range(8)))
    return np.stack([np.asarray(r["out"], dtype=np.float32) for r in res.results], axis=0)
```

```python
from contextlib import ExitStack
import numpy as np
import concourse.bass as bass
import concourse.mybir as mybir
from concourse.bass_utils import run_bass_kernel_spmd

F32 = mybir.dt.float32
BF16 = mybir.dt.bfloat16
AF = mybir.ActivationFunctionType
ALU = mybir.AluOpType

COMPUTE = ("pe", "act", "dve", "pool")
QUEUES = ("pe", "act", "dve", "pool", "sp")

ALPHA = 4.0 ** 0.25
LN_EPS = 1e-5
T = 2048
D = 1024
NTB = 16
PAD = 4
DFF = 2816
NJ = 22
import os
SCHEDULE = os.environ.get('MK_SCHED', '1') == '1'
PREFETCH = os.environ.get('MK_PREFETCH', '0') == '1'


class Ins:
    __slots__ = ("eng", "fn", "deps", "idx", "dma_key", "dma_val", "signal", "sigval", "clock", "waits", "is_dma", "pinned")


class Prog:
    def __init__(self, nc):
        self.nc = nc
        self.es = ExitStack()
        self.ins = []
        self.q = {e: [] for e in QUEUES}
        self.last_w = {}
        self.readers = {}
        self.dma_cum = {}
        self.dma_sems = {}
        self.sems = {}
        self.fence_deps = []
        self.scratch_touch = {}
        self.pin = ("dve",)

    def sbuf(self, name, shape, dtype):
        return self.es.enter_context(self.nc.sbuf_tensor(name, list(shape), dtype))

    def psum(self, name, shape, dtype=F32):
        return self.es.enter_context(self.nc.psum_tensor(name, list(shape), dtype))

    def begin_group(self):
        self._grp = []

    def end_group(self):
        g, self._grp = self._grp, None
        fns = [x[0] for x in g]
        reads, writes = [], []
        for _, r, w in g:
            for x in r:
                if x not in reads:
                    reads.append(x)
            for x in w:
                if x not in writes:
                    writes.append(x)

        def run(e, fns=fns):
            h = None
            for f in fns:
                h = f(e)
            return h
        return self.add("pe", run, reads=reads, writes=writes)

    def add(self, eng, fn, reads=(), writes=(), dma_key=None):
        if getattr(self, "_grp", None) is not None:
            assert eng == "pe" and dma_key is None
            self._grp.append((fn, list(reads), list(writes)))
            return None
        i = Ins()
        i.eng = eng
        i.fn = fn
        i.is_dma = dma_key is not None
        i.dma_key = dma_key
        i.signal = False
        i.pinned = eng in self.pin
        deps = set()
        scratch = False
        if any(r.startswith("bk") for r in reads):
            writes = list(writes) + [r for r in reads if r.startswith("bk") and r not in writes]
            reads = [r for r in reads if not r.startswith("bk")]
        for r in reads:
            w = self.last_w.get(r)
            if w is not None:
                deps.add(w)
            if r.startswith("S:"):
                scratch = True
        for w_ in writes:
            w = self.last_w.get(w_)
            if w is not None:
                deps.add(w)
            for rd in self.readers.get(w_, ()):
                deps.add(rd)
            if w_.startswith("S:"):
                scratch = True
        if scratch:
            deps.update(self.fence_deps)
        i.deps = deps
        i.idx = len(self.ins)
        self.ins.append(i)
        self.q[eng].append(i)
        for r in reads:
            self.readers.setdefault(r, []).append(i)
        for w_ in writes:
            self.last_w[w_] = i
            self.readers[w_] = []
        if i.is_dma:
            self.dma_cum[dma_key] = self.dma_cum.get(dma_key, 0) + 16
            i.dma_val = self.dma_cum[dma_key]
        if scratch:
            self.scratch_touch[i.idx] = i
        return i

    def fence(self):
        touched = list(self.scratch_touch.values())
        self.scratch_touch = {}
        if not hasattr(self, "_fdummy"):
            self._fdummy = self.sbuf("fence_dummy", [128, 8], F32)
        fd = self._fdummy
        join = self.add("dve", lambda e: e.memset(fd[:, 0:1], 0.0), writes=["fence_dummy"])
        join.deps.update(touched)
        self.fence_deps = [join]
        for k in [k for k in self.last_w if k.startswith("S:")]:
            del self.last_w[k]
        for k in [k for k in self.readers if k.startswith("S:")]:
            del self.readers[k]


    def schedule(self):
        import heapq

        class _Probe:
            def __init__(self):
                self.recs = []

            def __getattr__(self, name):
                def f(*a, **k):
                    self.recs.append((name, a, k))
                    return None
                return f

        def prod(sh):
            n = 1
            for v in sh:
                n *= int(v)
            return n

        cost, lat = {}, {}
        for i in self.ins:
            p = _Probe()
            i.fn(p)
            name, a, k = p.recs[-1]
            out = k.get("out", a[0] if a else None)
            n = prod(out.shape[1:]) if out is not None and hasattr(out, "shape") else 512
            L = 0.0
            if i.is_dma:
                by = n * out.shape[0] * 4 if out is not None else 0
                c = 0.6 if i.eng == "pool" else 0.15
                L = 2.5 + by / 150e3
            elif i.eng == "pe":
                c = 0.0
                for name, a, k in p.recs:
                    if name == "transpose":
                        c += 0.12
                    else:
                        rhs = k.get("rhs", a[2] if len(a) > 2 else None)
                        nn = prod(rhs.shape[1:]) if rhs is not None else 512
                        lhs = k.get("lhsT", a[1] if len(a) > 1 else None)
                        c1 = 0.01 + max(nn, 64) / 2400.0
                        if lhs is not None and lhs.dtype == F32:
                            c1 *= 4
                        c += c1
            elif i.eng == "act":
                c = 0.22 + n / 1400.0
            elif i.eng == "dve":
                c = 0.12 + n / 960.0
            else:
                c = 0.25 + n / 600.0
            cost[i.idx] = c
            lat[i.idx] = L
        succ = {i.idx: [] for i in self.ins}
        indeg = {}
        import os
        chain = {}
        for e in QUEUES:
            prev = None
            for i in self.q[e]:
                if prev is not None and i.pinned:
                    chain[i.idx] = prev
                prev = i
        for i in self.ins:
            ds = [d for d in i.deps if d is not i]
            if i.idx in chain and chain[i.idx] not in ds:
                ds.append(chain[i.idx])
            indeg[i.idx] = len(ds)
            for d in ds:
                succ[d.idx].append(i)
        byidx = {i.idx: i for i in self.ins}
        pending = {e: [] for e in QUEUES}
        avail = {e: [] for e in QUEUES}
        free = {e: 0.0 for e in QUEUES}
        fin = {}
        ready = {}
        for i in self.ins:
            if indeg[i.idx] == 0:
                ready[i.idx] = 0.0
                heapq.heappush(pending[i.eng], (0.0, i.idx))
        order = []
        newq = {e: [] for e in QUEUES}
        SYNC = 0.12
        n_left = len(self.ins)
        while n_left:
            best = None
            for e in QUEUES:
                pe_, av = pending[e], avail[e]
                while pe_ and pe_[0][0] <= free[e]:
                    r, ix = heapq.heappop(pe_)
                    heapq.heappush(av, ix)
                if av:
                    cand = (free[e], av[0], e, True)
                elif pe_:
                    cand = (pe_[0][0], pe_[0][1], e, False)
                else:
                    continue
                if best is None or cand[:2] < best[:2]:
                    best = cand
            st, ix, e, from_av = best
            if from_av:
                heapq.heappop(avail[e])
            else:
                heapq.heappop(pending[e])
            i = byidx[ix]
            f = st + cost[ix]
            free[e] = f
            fin[ix] = f + lat[ix]
            order.append(i)
            newq[e].append(i)
            n_left -= 1
            for sx in succ[ix]:
                indeg[sx.idx] -= 1
                r = max(ready.get(sx.idx, 0.0), fin[ix] + (0.0 if sx.eng == e and not i.is_dma else SYNC))
                ready[sx.idx] = r
                if indeg[sx.idx] == 0:
                    heapq.heappush(pending[sx.eng], (r, sx.idx))
        self.ins = order
        self.q = newq
        for k, i in enumerate(self.ins):
            i.idx = k
        self.est_us = max(fin.values()) if fin else 0.0

    def finalize(self, tail):
        nc = self.nc
        pos = {}
        for e in QUEUES:
            for k, i in enumerate(self.q[e]):
                pos[i.idx] = k
        prev_clock = {e: ({c: -1 for c in COMPUTE}, frozenset()) for e in QUEUES}
        for i in self.ins:
            clk, dseen = prev_clock[i.eng]
            clk = dict(clk)
            dseen = set(dseen)
            waits = []
            for d in sorted(i.deps, key=lambda d: -d.idx):
                if d is i:
                    continue
                if d.is_dma:
                    if d.idx in dseen:
                        continue
                    waits.append(d)
                    dseen.add(d.idx)
                else:
                    if d.eng == i.eng and d.eng == "pe":
                        continue
                    if clk[d.eng] >= pos[d.idx]:
                        continue
                    waits.append(d)
                    clk[d.eng] = max(clk[d.eng], pos[d.idx])
                dc, dd = d.clock
                for c in COMPUTE:
                    if dc[c] > clk[c]:
                        clk[c] = dc[c]
                dseen |= dd
            final = []
            for d in waits:
                if d.is_dma:
                    final.append(d)
                elif clk[d.eng] == pos[d.idx]:
                    final.append(d)
            i.waits = final
            for d in final:
                d.signal = True
            i.clock = (clk, frozenset(dseen))
            prev_clock[i.eng] = i.clock
        for d in tail:
            d.signal = True
        for e in COMPUTE:
            self.sems[e] = self.es.enter_context(nc.semaphore("s_" + e))
            n = 0
            for i in self.q[e]:
                if i.is_dma:
                    continue
                if i.signal:
                    n += 1
                    i.sigval = n
        for k in self.dma_cum:
            self.dma_sems[k] = self.es.enter_context(nc.semaphore("d_" + str(k).replace(":", "_")))

    def emit(self, tail):
        nc = self.nc
        prog = self

        def wait(eng, d):
            if d.is_dma:
                eng.wait_ge(prog.dma_sems[d.dma_key], d.dma_val)
            else:
                eng.wait_ge(prog.sems[d.eng], d.sigval)

        def run(engname, eng):
            for i in prog.q[engname]:
                for d in i.waits:
                    wait(eng, d)
                h = i.fn(eng)
                if i.is_dma:
                    h.then_inc(prog.dma_sems[i.dma_key], 16)
                elif i.signal:
                    h.then_inc(prog.sems[i.eng], 1)
            if engname == "sp":
                for d in tail:
                    wait(eng, d)

        with nc.Block() as block:
            @block.tensor
            def _(e):
                run("pe", e)

            @block.scalar
            def _(e):
                run("act", e)

            @block.vector
            def _(e):
                run("dve", e)

            @block.gpsimd
            def _(e):
                run("pool", e)

            @block.sync
            def _(e):
                run("sp", e)

    def close(self):
        self.es.close()


class Builder:
    def __init__(self, nc, P, dram):
        self.nc, self.P, self.dram = nc, P, dram
        P_ = P
        self.x_tok = P_.sbuf("x_tok_sb", [128, NTB, D], F32)
        self.xTp = P_.sbuf("xTp", [128, 8, PAD + T], BF16)
        self.ident_f = P_.sbuf("ident_f", [128, 128], F32)
        self.ones_f = P_.sbuf("ones_f", [128, 128], F32)
        self.neghalf = P_.sbuf("neghalf", [128, 1], F32)
        self.lnp = None
        self.stats = P_.sbuf("stats", [128, NTB, 16], F32)
        self.cwb = P_.sbuf("cwb", [128, 44, 4], F32)
        self.fix = P_.sbuf("fix", [128, 44, 2], F32)
        self.ring = [P_.sbuf(f"ring{i}", [128, 4096], BF16) for i in range(3)]
        self.ring_cnt = 0
        self.SW = 20480
        self.S = P_.sbuf("S", [128, self.SW], F32)
        self.sp_ = 0
        self.PS = [P_.psum(f"ps{i}", [128, 1024], F32) for i in range(4)]
        self.out_dmas = []
        self.ident_b = P_.sbuf("ident_b", [128, 128], BF16)
        self.maskneg = P_.sbuf("maskneg", [128, 128], BF16)
        self.small = P_.sbuf("small", [128, 64], F32)
        self.lnT = P_.sbuf("lnT_sb", [128, 16], F32)
        self.rot = {}
        self.consts()

    def rotbank(self, group, banks):
        k = self.rot.get(group, 0)
        self.rot[group] = k + 1
        return banks[k % len(banks)]

    def reset_scratch(self):
        self.P.fence()
        self.sp_ = 0

    def carve(self, words, dtype=F32):
        a = self.sp_
        self.sp_ += words
        assert self.sp_ <= self.SW, (self.sp_, self.SW)
        v = self.S[:, a:a + words]
        if dtype == BF16:
            v = v.bitcast(BF16)
        return v

    def bank(self, b):
        return self.PS[b // 2][:, (b % 2) * 512:(b % 2) * 512 + 512]

    @staticmethod
    def BK(b):
        return f"bk{b}"

    def ring_next(self):
        i = self.ring_cnt % 3
        self.ring_cnt += 1
        return self.ring[i], f"ring{i}"

    def wload(self, src_ap, words=4096, slot=None):
        if slot is None:
            tile, res = self.ring_next()
        else:
            tile, res = self.ring[slot], f"ring{slot}"
        self.P.add("pool", lambda e, t=tile, s=src_ap, w=words: e.dma_start(out=t[:, 0:w], in_=s, max_dma_last_dim=8192),
                   writes=[res], dma_key=res)
        return tile, res

    def consts(self):
        P = self.P
        ones_f, ident_f = self.ones_f, self.ident_f
        P.add("pool", lambda e: e.memset(ones_f[:], 1.0), writes=["ones_f"])
        P.add("pool", lambda e: e.affine_select(out=ident_f[:], in_=ones_f[:], pattern=[[-1, 128]], compare_op=ALU.is_equal,
                                                 fill=0.0, base=0, channel_multiplier=1), reads=["ones_f"], writes=["ident_f"])
        nh = self.neghalf
        P.add("pool", lambda e: e.memset(nh[:], -0.5), writes=["neghalf"])
        xTp = self.xTp
        P.add("pool", lambda e: e.memset(xTp[:, :, 0:PAD], 0.0), writes=["xTpad"])
        ident_b, maskneg = self.ident_b, self.maskneg
        P.add("pool", lambda e: e.tensor_copy(out=ident_b[:], in_=ident_f[:]), reads=["ident_f"], writes=["ident_b"])
        P.add("pool", lambda e: e.memset(maskneg[:], -30000.0), writes=["maskneg"])
        P.add("pool", lambda e: e.affine_select(out=maskneg[:], in_=maskneg[:], pattern=[[-1, 128]], compare_op=ALU.is_gt,
                                                 fill=0.0, base=0, channel_multiplier=1), reads=["maskneg"], writes=["maskneg"])

    def load_x(self):
        P, x_tok = self.P, self.x_tok
        xv = self.dram["x"].rearrange("(tb p) d -> p tb d", p=128)
        for g in range(4):
            P.add("sp", lambda e, g=g: e.dma_start(out=x_tok[:, 4 * g:4 * g + 4, :], in_=xv[:, 4 * g:4 * g + 4, :]),
                  writes=[f"xt{tb}_{dh}" for tb in range(4 * g, 4 * g + 4) for dh in range(2)], dma_key=f"xin{g}")

    def store_tb(self, tb):
        P, x_tok = self.P, self.x_tok
        ov = self.dram["out"].rearrange("(tb p) d -> p tb d", p=128)
        i = P.add("sp", lambda e: e.dma_start(out=ov[:, tb, :], in_=x_tok[:, tb, :]), reads=[f"xt{tb}_0", f"xt{tb}_1"], dma_key=f"xout{tb}")
        self.out_dmas.append(i)

    def transpose_tb(self, tb, pst, affine=False):
        P, x_tok, xTp, ident_f, lnT = self.P, self.x_tok, self.xTp, self.ident_f, self.lnT
        ps = self.PS[pst]
        P.begin_group()
        for kc in range(8):
            P.add("pe", lambda e, kc=kc: e.transpose(ps[:, kc * 128:(kc + 1) * 128], x_tok[:, tb, kc * 128:(kc + 1) * 128], ident_f[:]),
                  reads=[f"xt{tb}_{kc // 4}", "ident_f"], writes=[self.BK(2 * pst + kc // 4)])
        P.end_group()
        c0 = PAD + tb * 128
        if affine:
            for kc in range(8):
                if kc < 4:
                    P.add("act", lambda e, kc=kc: e.activation(out=xTp[:, kc, c0:c0 + 128], in_=ps[:, kc * 128:(kc + 1) * 128], func=AF.Identity,
                                                               scale=lnT[:, kc:kc + 1], bias=lnT[:, 8 + kc:9 + kc]),
                          reads=[self.BK(2 * pst), "lnT"], writes=[f"xT{tb}"])
                else:
                    P.add("dve", lambda e, kc=kc: e.tensor_scalar(out=xTp[:, kc, c0:c0 + 128], in0=ps[:, kc * 128:(kc + 1) * 128],
                                                                  scalar1=lnT[:, kc:kc + 1], scalar2=lnT[:, 8 + kc:9 + kc], op0=ALU.mult, op1=ALU.add),
                          reads=[self.BK(2 * pst + 1), "lnT"], writes=[f"xT{tb}"])
            return
        P.add("act", lambda e: e.activation(out=xTp[:, 0:4, c0:c0 + 128], in_=ps[:, 0:512].rearrange("p (k t) -> p k t", k=4), func=AF.Identity),
              reads=[self.BK(2 * pst)], writes=[f"xT{tb}"])
        P.add("dve", lambda e: e.tensor_copy(out=xTp[:, 4:8, c0:c0 + 128], in_=ps[:, 512:1024].rearrange("p (k t) -> p k t", k=4)),
              reads=[self.BK(2 * pst + 1)], writes=[f"xT{tb}"])

    def load_lnT(self, idx):
        lnT = self.lnT
        src = self.dram["lnpT"][idx]
        self.P.add("sp", lambda e: e.dma_start(out=lnT[:], in_=src), writes=["lnT"], dma_key="lnT")

    def load_ln(self, idx, with_T=True):
        if with_T:
            self.load_lnT(idx)
        self.lnp = self.carve(2 * D).rearrange("p (a d) -> p a d", a=2)
        lnp = self.lnp
        src = self.dram["lnp"][idx].partition_broadcast(128)
        self.P.add("sp", lambda e: e.dma_start(out=lnp[:], in_=src), writes=["S:lnp"], dma_key="lnp")

    def ln_tb(self, tb):
        P, x_tok, st, lnp, nh = self.P, self.x_tok, self.stats, self.lnp, self.neghalf
        R = [f"xt{tb}_0", f"xt{tb}_1"]
        sr = f"st{tb}"
        P.add("dve", lambda e: e.bn_stats(out=st[:, tb, 0:6], in_=x_tok[:, tb, 0:512]), reads=[R[0]], writes=[sr])
        P.add("dve", lambda e: e.bn_stats(out=st[:, tb, 6:12], in_=x_tok[:, tb, 512:1024]), reads=[R[1]], writes=[sr])
        P.add("dve", lambda e: e.bn_aggr(out=st[:, tb, 12:14], in_=st[:, tb, 0:12]), reads=[sr], writes=[sr])
        P.add("pool", lambda e: e.tensor_scalar(out=st[:, tb, 14:15], in0=st[:, tb, 13:14], scalar1=LN_EPS, scalar2=None, op0=ALU.add),
              reads=[sr], writes=[sr])
        P.add("pool", lambda e: e.tensor_tensor(out=st[:, tb, 14:15], in0=st[:, tb, 14:15], in1=nh[:], op=ALU.pow),
              reads=[sr, "neghalf"], writes=[sr])
        P.add("dve", lambda e: e.tensor_scalar(out=st[:, tb, 15:16], in0=st[:, tb, 12:13], scalar1=st[:, tb, 14:15], scalar2=-1.0,
                                               op0=ALU.mult, op1=ALU.mult), reads=[sr], writes=[sr])
        P.add("act", lambda e: e.activation(out=x_tok[:, tb, :], in_=x_tok[:, tb, :], func=AF.Identity,
                                            scale=st[:, tb, 14:15], bias=st[:, tb, 15:16]), reads=[sr] + R, writes=R)

    def ln_affine(self, tb):
        P, x_tok, lnp = self.P, self.x_tok, self.lnp
        R = [f"xt{tb}_0", f"xt{tb}_1"]
        P.add("pool", lambda e: e.tensor_tensor(out=x_tok[:, tb, :], in0=x_tok[:, tb, :], in1=lnp[:, 0, :], op=ALU.mult),
              reads=R + ["S:lnp"], writes=R)
        P.add("pool", lambda e: e.tensor_tensor(out=x_tok[:, tb, :], in0=x_tok[:, tb, :], in1=lnp[:, 1, :], op=ALU.add),
              reads=R + ["S:lnp"], writes=R)

    def ffn(self, l, final=False):
        P, xTp, x_tok, cwb, fix = self.P, self.xTp, self.x_tok, self.cwb, self.fix
        self.reset_scratch()
        hid = self.carve(NJ * 1024 // 2, BF16).rearrange("p (j t) -> p j t", j=NJ)
        tmp = [[self.carve(1024), self.carve(1024)] for _ in range(2)]
        cwsrc = self.dram["ffn_cwb"][l]
        P.add("sp", lambda e: e.dma_start(out=cwb[:], in_=cwsrc), writes=["cwb"], dma_key="cwb")
        self.load_ln(2 * l + 1)
        wup, wdn = self.dram["wup"], self.dram["wdn"]
        for h in range(2):
            c0 = PAD + 1024 * h
            xres = [f"xT{tb}" for tb in range(8 * h, 8 * h + 8)] + ["xTpad"]
            for s in range(11):
                if s == 0 and h == 1:
                    tile, res = pre_up
                else:
                    tile, res = self.wload(wup[l, s])
                sv = tile[:].rearrange("p (k j c) -> p k j c", k=8, j=2)
                for jj in range(2):
                    j = 2 * s + jj
                    pset = j % 2
                    for ug in range(2):
                        pst = 2 * pset + ug
                        ps = self.PS[pst]
                        for t in range(2):
                            P.begin_group()
                            for kc in range(8):
                                P.add("pe", lambda e, ps=ps, t=t, kc=kc, jj=jj, ug=ug, sv=sv, c0=c0: e.matmul(
                                    ps[:, t * 512:(t + 1) * 512], lhsT=sv[:, kc, jj, ug * 128:(ug + 1) * 128],
                                    rhs=xTp[:, kc, c0 + t * 512:c0 + (t + 1) * 512], start=(kc == 0), stop=(kc == 7)),
                                    reads=[res] + xres, writes=[self.BK(2 * pst + t)])
                            P.end_group()
                    for ug in range(2):
                        pst = 2 * pset + ug
                        ps = self.PS[pst]
                        a = tmp[pset][ug]
                        ar = f"S:a{pset}{ug}"
                        ch = ug * NJ + j
                        bks = [self.BK(2 * pst), self.BK(2 * pst + 1)]
                        P.add("act", lambda e, ps=ps, a=a, ch=ch: e.activation(out=a[:, 0:1024], in_=ps[:, 0:1024], func=AF.Identity,
                                                                                scale=cwb[:, ch, 2:3], bias=cwb[:, ch, 3:4]),
                              reads=bks + ["cwb"], writes=[ar])
                        P.add("dve", lambda e, ps=ps, a=a, ch=ch: e.scalar_tensor_tensor(
                            out=a[:, 1:1024], in0=ps[:, 0:1023], scalar=cwb[:, ch, 1:2], in1=a[:, 1:1024], op0=ALU.mult, op1=ALU.add),
                            reads=bks + ["cwb", ar], writes=[ar])
                        P.add("dve", lambda e, ps=ps, a=a, ch=ch: e.scalar_tensor_tensor(
                            out=a[:, 2:1024], in0=ps[:, 0:1022], scalar=cwb[:, ch, 0:1], in1=a[:, 2:1024], op0=ALU.mult, op1=ALU.add),
                            reads=bks + ["cwb", ar], writes=[ar])
                        if h == 0:
                            P.add("dve", lambda e, ps=ps, ch=ch: e.tensor_scalar(out=fix[:, ch, 0:2], in0=ps[:, 1022:1024], scalar1=cwb[:, ch, 0:1],
                                                                                 scalar2=None, op0=ALU.mult), reads=bks + ["cwb"], writes=[f"fix{ch}"])
                            P.add("dve", lambda e, ps=ps, ch=ch: e.scalar_tensor_tensor(
                                out=fix[:, ch, 0:1], in0=ps[:, 1023:1024], scalar=cwb[:, ch, 1:2], in1=fix[:, ch, 0:1], op0=ALU.mult, op1=ALU.add),
                                reads=bks + ["cwb", f"fix{ch}"], writes=[f"fix{ch}"])
                        else:
                            P.add("dve", lambda e, a=a, ch=ch: e.tensor_tensor(out=a[:, 0:2], in0=a[:, 0:2], in1=fix[:, ch, 0:2], op=ALU.add),
                                  reads=[ar, f"fix{ch}"], writes=[ar])
                    au, ag = tmp[pset]
                    P.add("act", lambda e, ag=ag: e.activation(out=ag[:, 0:1024], in_=ag[:, 0:1024], func=AF.Silu),
                          reads=[f"S:a{pset}1"], writes=[f"S:a{pset}1"])
                    P.add("pool", lambda e, au=au, ag=ag, j=j: e.tensor_tensor(out=hid[:, j, :], in0=au[:, 0:1024], in1=ag[:, 0:1024], op=ALU.mult),
                          reads=[f"S:a{pset}0", f"S:a{pset}1"], writes=[f"S:hid{j}"])
            r0 = self.ring_cnt % 3
            if h == 0:
                pre_up = self.wload(wup[l, 0], slot=(r0 + 2) % 3)
            nd = 0
            for dh in range(2):
                for g in range(3):
                    tile, res = self.wload(wdn[l, dh, g], slot=(r0 + nd % 2) % 3)
                    nd += 1
                    sv = tile[:].rearrange("p (j c) -> p j c", j=8)
                    for jj in range(8 if g < 2 else 6):
                        j = 8 * g + jj
                        for tb in range(8):
                            P.add("pe", lambda e, tb=tb, j=j, jj=jj, sv=sv: e.matmul(
                                self.bank(tb), lhsT=hid[:, j, tb * 128:(tb + 1) * 128], rhs=sv[:, jj, :], start=(j == 0), stop=(j == NJ - 1)),
                                reads=[res, f"S:hid{j}"], writes=[self.BK(tb)])
                for tb in range(8):
                    gtb = 8 * h + tb
                    P.add("dve", lambda e, tb=tb, gtb=gtb, dh=dh: e.scalar_tensor_tensor(
                        out=x_tok[:, gtb, dh * 512:(dh + 1) * 512], in0=x_tok[:, gtb, dh * 512:(dh + 1) * 512], scalar=ALPHA,
                        in1=self.bank(tb), op0=ALU.mult, op1=ALU.add), reads=[self.BK(tb), f"xt{gtb}_{dh}"], writes=[f"xt{gtb}_{dh}"])
            for tb in range(8):
                gtb = 8 * h + tb
                self.ln_tb(gtb)
                if not final:
                    self.transpose_tb(gtb, tb // 2, affine=True)
                self.ln_affine(gtb)
                if final:
                    self.store_tb(gtb)


    def mix1(self):
        P, xTp, x_tok, dram = self.P, self.xTp, self.x_tok, self.dram
        ones_f, ident_b, maskneg, small = self.ones_f, self.ident_b, self.maskneg, self.small
        xall = [f"xT{tb}" for tb in range(NTB)]
        self.reset_scratch()
        fl = self.carve(2048)
        cum = self.carve(2048)
        ones = self.carve(512)
        QG = self.carve(6 * 2048 // 2, BF16).rearrange("p (r t) -> p r t", r=6)
        KG = self.carve(6 * 2048 // 2, BF16).rearrange("p (r t) -> p r t", r=6)
        bsrc = dram["fox_bf"]
        P.add("sp", lambda e: e.dma_start(out=small[0:16, 0:1], in_=bsrc), writes=["small"], dma_key="small")
        P.add("dve", lambda e: e.tensor_scalar(out=small[0:16, 1:2], in0=small[0:16, 0:1], scalar1=-1.0, scalar2=None, op0=ALU.mult),
              reads=["small"], writes=["small"])
        P.add("pool", lambda e: e.memset(ones[0:16, :], 1.0), writes=["S:ones"])
        P.add("pool", lambda e: e.memset(QG[0:16, 3:6, :], 1.0), writes=["S:QG1"])
        P.add("pool", lambda e: e.memset(KG[0:16, 0:3, :], 1.0), writes=["S:KG1"])
        tile, res = self.wload(dram["fox_f"], words=128)
        fv = tile[:, 0:128].rearrange("p (k c) -> p k c", k=8)
        for t in range(4):
            P.begin_group()
            for kc in range(8):
                P.add("pe", lambda e, t=t, kc=kc: e.matmul(self.bank(t)[0:16, :], lhsT=fv[:, kc, :], rhs=xTp[:, kc, PAD + t * 512:PAD + (t + 1) * 512],
                                                           start=(kc == 0), stop=(kc == 7)), reads=[res] + xall, writes=[self.BK(t)])
            P.end_group()
            P.add("act", lambda e, t=t: e.activation(out=fl[0:16, t * 512:(t + 1) * 512], in_=self.bank(t)[0:16, :], func=AF.Exp,
                                                     scale=-1.0, bias=small[0:16, 1:2]), reads=[self.BK(t), "small"], writes=[f"S:fl{t}"])
        for t in range(4):
            P.add("act", lambda e, t=t: e.activation(out=fl[0:16, t * 512:(t + 1) * 512], in_=fl[0:16, t * 512:(t + 1) * 512], func=AF.Ln,
                                                     scale=1.0, bias=1.0), reads=[f"S:fl{t}"], writes=[f"S:fl{t}"])
        for t in range(4):
            init = 0.0 if t == 0 else cum[0:16, t * 512 - 1:t * 512]
            P.add("dve", lambda e, t=t, init=init: e.tensor_tensor_scan(out=cum[0:16, t * 512:(t + 1) * 512], data0=ones[0:16, :],
                                                                        data1=fl[0:16, t * 512:(t + 1) * 512], initial=init,
                                                                        op0=ALU.mult, op1=ALU.subtract),
                  reads=[f"S:fl{t}", "S:ones", "S:cum"], writes=["S:cum"])
        for r in range(3):
            P.add("dve", lambda e, r=r: e.tensor_copy(out=QG[0:16, r, :], in_=cum[0:16, :]), reads=["S:cum"], writes=[f"S:QG0{r}"])
            if r < 2:
                P.add("dve", lambda e, r=r: e.tensor_tensor(out=cum[0:16, :], in0=cum[0:16, :], in1=QG[0:16, r, :], op=ALU.subtract),
                      reads=["S:cum", f"S:QG0{r}"], writes=["S:cum"])
        P.add("dve", lambda e: e.tensor_scalar(out=KG[0:16, 3:6, :], in0=QG[0:16, 0:3, :], scalar1=-1.0, scalar2=None, op0=ALU.mult),
              reads=["S:QG00", "S:QG01", "S:QG02"], writes=["S:KG0"])
        gq, gk = dram["augq"], dram["augk"]
        P.add("sp", lambda e: e.dma_start(out=gq, in_=QG[0:16, :, :]), reads=["S:QG00", "S:QG01", "S:QG02", "S:QG1"], writes=["augq"], dma_key="augq")
        P.add("sp", lambda e: e.dma_start(out=gk, in_=KG[0:16, :, :]), reads=["S:KG0", "S:KG1"], writes=["augk"], dma_key="augk")
        self.reset_scratch()
        AUG = [[[self.carve(1024, BF16) for qk in range(2)] for sub in range(2)] for st in range(2)]
        VP = [self.carve(1040, BF16).rearrange("p (t s d) -> p t s d", t=16, s=2) for st in range(2)]
        OTP = [self.carve(1024, BF16) for st in range(2)]
        PT = [self.carve(256, BF16) for _ in range(4)]
        PTD = [self.carve(256, BF16) for _ in range(4)]
        rc = self.carve(512)
        bcs = self.carve(512)
        self.load_ln(2)
        for st in range(2):
            P.add("pool", lambda e, st=st: e.memset(VP[st][:, :, :, 64:65], 1.0), writes=[f"S:VPone{st}"])
        for i4 in range(1, 4):
            P.add("pool", lambda e, i4=i4: e.memset(PTD[i4][:, 0:128 * i4], 0.0), writes=[f"S:PTD{i4}"])
        ptc = [0]

        def pair(hp):
            st = hp % 2
            qk_t, qk_r = self.wload(dram["fox_qk"][hp], words=2048)
            v_t, v_r = self.wload(dram["fox_v"][hp], words=1024)
            qkv = qk_t[:, 0:2048].rearrange("p (k c) -> p k c", k=8)
            vv = v_t[:, 0:1024].rearrange("p (k c) -> p k c", k=8)
            for sub in range(2):
                h = 2 * hp + sub
                P.add("sp", lambda e, st=st, sub=sub, h=h: e.dma_start(out=AUG[st][sub][0][64:70, :], in_=gq[h]),
                      reads=["augq"], writes=[f"S:AQa{st}{sub}"], dma_key=f"aq{st}{sub}")
                P.add("sp", lambda e, st=st, sub=sub, h=h: e.dma_start(out=AUG[st][sub][1][64:70, :], in_=gk[h]),
                      reads=["augk"], writes=[f"S:AKa{st}{sub}"], dma_key=f"ak{st}{sub}")
            for t in range(4):
                for qk in range(2):
                    b = self.rotbank("misc", (0, 1, 7))
                    P.begin_group()
                    for kc in range(8):
                        P.add("pe", lambda e, b=b, kc=kc, qk=qk, t=t, qkv=qkv: e.matmul(
                            self.bank(b), lhsT=qkv[:, kc, qk * 128:(qk + 1) * 128], rhs=xTp[:, kc, PAD + t * 512:PAD + (t + 1) * 512],
                            start=(kc == 0), stop=(kc == 7)), reads=[qk_r] + xall, writes=[self.BK(b)])
                    P.end_group()
                    for sub in range(2):
                        dst = AUG[st][sub][qk]
                        nm = f"S:A{'QK'[qk]}{st}{sub}t{t}"
                        if qk == 0:
                            P.add("dve", lambda e, b=b, sub=sub, dst=dst, t=t: e.tensor_scalar(
                                out=dst[0:64, t * 512:(t + 1) * 512], in0=self.bank(b)[sub * 64:(sub + 1) * 64, :], scalar1=0.125, scalar2=None,
                                op0=ALU.mult), reads=[self.BK(b)], writes=[nm])
                        else:
                            P.add("dve", lambda e, b=b, sub=sub, dst=dst, t=t: e.tensor_copy(
                                out=dst[0:64, t * 512:(t + 1) * 512], in_=self.bank(b)[sub * 64:(sub + 1) * 64, :]),
                                reads=[self.BK(b)], writes=[nm])
            for g4 in range(4):
                b = self.rotbank("misc", (0, 1, 7))
                P.begin_group()
                for ti in range(4):
                    tb = 4 * g4 + ti
                    for kc in range(8):
                        P.add("pe", lambda e, b=b, ti=ti, tb=tb, kc=kc, vv=vv: e.matmul(
                            self.bank(b)[:, ti * 128:(ti + 1) * 128], lhsT=xTp[:, kc, PAD + tb * 128:PAD + (tb + 1) * 128], rhs=vv[:, kc, :],
                            start=(kc == 0), stop=(kc == 7)), reads=[v_r, f"xT{tb}"], writes=[self.BK(b)])
                P.end_group()
                P.add("act", lambda e, b=b, g4=g4, st=st: e.activation(
                    out=VP[st][:, 4 * g4:4 * g4 + 4, :, 0:64], in_=self.bank(b).rearrange("p (t s d) -> p t s d", t=4, s=2), func=AF.Identity),
                    reads=[self.BK(b)], writes=[f"S:VP{st}g{g4}"])
            yield
            for sub in range(2):
                if sub == 1:
                    wo_t, wo_r = self.wload(dram["fox_wo"][hp], words=1024)
                QA, KA = AUG[st][sub][0], AUG[st][sub][1]
                for qt in range(4):
                    ob = self.rotbank("O", (2, 3))
                    nkb = 4 * qt + 4
                    for kb in range(nkb):
                        i = kb - 4 * qt
                        co = 128 * i if i > 0 else 0
                        sb_ = self.rotbank("S", (4, 5, 6))
                        if i >= 0:
                            pt, ptr = PTD[i], f"S:PTD{i}"
                        else:
                            pt, ptr = PT[ptc[0] % 4], f"S:PT{ptc[0] % 4}"
                            ptc[0] += 1
                        kres = [f"S:AK{st}{sub}t{kb // 4}", f"S:AKa{st}{sub}", f"S:AQ{st}{sub}t{qt}", f"S:AQa{st}{sub}"]
                        P.begin_group()
                        if i < 0:
                            P.add("pe", lambda e, sb_=sb_, kb=kb, qt=qt, QA=QA, KA=KA: e.matmul(
                                self.bank(sb_)[:, 0:512], lhsT=KA[0:70, kb * 128:(kb + 1) * 128], rhs=QA[0:70, qt * 512:(qt + 1) * 512],
                                start=True, stop=True), reads=kres, writes=[self.BK(sb_)])
                        else:
                            P.add("pe", lambda e, sb_=sb_, co=co, kb=kb, qt=qt, QA=QA, KA=KA: e.matmul(
                                self.bank(sb_)[:, co:co + 128], lhsT=KA[0:70, kb * 128:(kb + 1) * 128], rhs=QA[0:70, qt * 512 + co:qt * 512 + co + 128],
                                start=True, stop=False), reads=kres, writes=[self.BK(sb_)])
                            P.add("pe", lambda e, sb_=sb_, co=co: e.matmul(self.bank(sb_)[:, co:co + 128], lhsT=ident_b[:], rhs=maskneg[:],
                                                                           start=False, stop=True),
                                  reads=["ident_b", "maskneg"], writes=[self.BK(sb_)])
                            if co + 128 < 512:
                                P.add("pe", lambda e, sb_=sb_, co=co, kb=kb, qt=qt, QA=QA, KA=KA: e.matmul(
                                    self.bank(sb_)[:, co + 128:512], lhsT=KA[0:70, kb * 128:(kb + 1) * 128], rhs=QA[0:70, qt * 512 + co + 128:(qt + 1) * 512],
                                    start=True, stop=True), reads=kres, writes=[self.BK(sb_)])
                        P.end_group()
                        P.add("act", lambda e, sb_=sb_, co=co, pt=pt: e.activation(out=pt[:, co:512], in_=self.bank(sb_)[:, co:512], func=AF.Exp),
                              reads=[self.BK(sb_)], writes=[ptr])
                        P.add("pe", lambda e, ob=ob, kb=kb, pt=pt, st=st, sub=sub, nkb=nkb: e.matmul(
                            self.bank(ob)[0:65, 0:512], lhsT=VP[st][:, kb, sub, 0:65], rhs=pt[:, 0:512], start=(kb == 0), stop=(kb == nkb - 1)),
                            reads=[ptr, f"S:VP{st}g{kb // 4}", f"S:VPone{st}"], writes=[self.BK(ob)])
                    P.add("dve", lambda e, ob=ob: e.reciprocal(out=rc[64:65, :], in_=self.bank(ob)[64:65, :]), reads=[self.BK(ob)], writes=["S:rc"])
                    bb = self.rotbank("misc", (0, 1, 7))
                    P.add("pe", lambda e, bb=bb: e.matmul(self.bank(bb)[0:64, :], lhsT=ones_f[64:65, 0:64], rhs=rc[64:65, :], start=True, stop=True),
                          reads=["ones_f", "S:rc"], writes=[self.BK(bb)])
                    P.add("dve", lambda e, bb=bb: e.tensor_copy(out=bcs[0:64, :], in_=self.bank(bb)[0:64, :]), reads=[self.BK(bb)], writes=["S:bcs"])
                    P.add("dve", lambda e, ob=ob, st=st, sub=sub, qt=qt: e.tensor_tensor(
                        out=OTP[st][sub * 64:(sub + 1) * 64, qt * 512:(qt + 1) * 512], in0=self.bank(ob)[0:64, :], in1=bcs[0:64, :], op=ALU.mult),
                        reads=[self.BK(ob), "S:bcs"], writes=[f"S:OT{st}q{qt}"])
                yield
            for tb in range(NTB):
                for dh in range(2):
                    b = self.rotbank("misc", (0, 1, 7))
                    P.add("pe", lambda e, b=b, tb=tb, dh=dh, st=st, wo_t=wo_t: e.matmul(
                        self.bank(b), lhsT=OTP[st][:, tb * 128:(tb + 1) * 128], rhs=wo_t[:, dh * 512:(dh + 1) * 512], start=True, stop=True),
                        reads=[wo_r, f"S:OT{st}q{tb // 4}"], writes=[self.BK(b)])
                    xr = f"xt{tb}_{dh}"
                    if hp == 0:
                        P.add("dve", lambda e, b=b, tb=tb, dh=dh: e.scalar_tensor_tensor(
                            out=x_tok[:, tb, dh * 512:(dh + 1) * 512], in0=x_tok[:, tb, dh * 512:(dh + 1) * 512], scalar=ALPHA,
                            in1=self.bank(b), op0=ALU.mult, op1=ALU.add), reads=[self.BK(b), xr], writes=[xr])
                    else:
                        P.add("dve", lambda e, b=b, tb=tb, dh=dh: e.tensor_tensor(
                            out=x_tok[:, tb, dh * 512:(dh + 1) * 512], in0=self.bank(b), in1=x_tok[:, tb, dh * 512:(dh + 1) * 512], op=ALU.add),
                            reads=[self.BK(b), xr], writes=[xr])
            yield
        gp = [pair(hp) for hp in range(8)]
        next(gp[0])
        for hp in range(8):
            next(gp[hp])
            if hp + 1 < 8:
                next(gp[hp + 1])
            next(gp[hp])
            next(gp[hp])
        for tb in range(NTB):
            self.ln_tb(tb)
            self.transpose_tb(tb, tb % 4, affine=True)
            self.ln_affine(tb)


    def mix0(self):
        P, xTp, x_tok, dram = self.P, self.xTp, self.x_tok, self.dram
        ones_f, ident_f, ident_b, maskneg = self.ones_f, self.ident_f, self.ident_b, self.maskneg
        self.reset_scratch()
        H = self.carve(1024)
        prevbf = self.carve(512, BF16)
        R = self.carve(2048)
        L = [self.carve(128), self.carve(128)]
        halo = self.carve(16).rearrange("p (i k) -> p i k", i=8)
        fixx = self.carve(36).rearrange("p (i k) -> p i k", i=12)
        cst = self.carve(128)
        biasbc, Dbc = cst[:, 0:16], cst[:, 16:32]
        gT = cst[:, 32:40]
        scw = cst[:, 40:64].rearrange("p (i k) -> p i k", i=8)
        xcw = cst[:, 64:124].rearrange("p (i k) -> p i k", i=12)
        s_tok, s_feat, s_head = dram["m0_tokc"].rearrange("a h -> (a h)").partition_broadcast(128), dram["m0_featc"], dram["m0_headc"]
        P.add("sp", lambda e: e.dma_start(out=cst[:, 0:32], in_=s_tok), writes=["S:cst"], dma_key="m0c0")
        P.add("sp", lambda e: e.dma_start(out=cst[:, 32:124], in_=s_feat), writes=["S:cst"], dma_key="m0c1")
        P.add("sp", lambda e: e.dma_start(out=cst[0:16, 124:126], in_=s_head), writes=["S:cst"], dma_key="m0c2")
        P.add("act", lambda e: e.activation(out=cst[0:16, 126:127], in_=cst[0:16, 125:126], func=AF.Exp), reads=["S:cst"], writes=["S:cst"])
        P.add("dve", lambda e: e.tensor_scalar(out=cst[0:16, 127:128], in0=cst[0:16, 126:127], scalar1=-1.0, scalar2=None, op0=ALU.mult),
              reads=["S:cst"], writes=["S:cst"])
        dtb, acol = cst[0:16, 124:125], cst[0:16, 127:128]
        P.add("pool", lambda e: e.memset(H[:, :], 0.0), writes=["S:H"])
        P.add("pool", lambda e: e.memset(prevbf[:, :], 0.0), writes=["S:prev"])
        P.add("pool", lambda e: e.memset(R[0:48, :], 0.0), writes=["S:R"])
        P.add("pool", lambda e: e.memset(R[32:48, :], 1.0), reads=["S:R"], writes=["S:R"])
        P.add("pool", lambda e: e.affine_select(out=R[32:48, :].rearrange("p (h t) -> p h t", h=16), in_=R[32:48, :].rearrange("p (h t) -> p h t", h=16),
                                                 pattern=[[-1, 16], [0, 128]], compare_op=ALU.is_equal, fill=0.0, base=0, channel_multiplier=1),
              reads=["S:R"], writes=["S:R"])
        for k in range(2):
            P.add("pool", lambda e, k=k: e.memset(L[k][0:48, :], 0.0), writes=[f"S:L{k}"])
            P.add("pool", lambda e, k=k: e.memset(L[k][0:16, :], 1.0), reads=[f"S:L{k}"], writes=[f"S:L{k}"])
        base = self.sp_
        self.load_lnT(0)
        wx, wz, wsc, wo = dram["m0_wx"], dram["m0_wz"], dram["m0_wsc"], dram["m0_wo"]

        for qi in range(4):
            self.P.fence()
            self.sp_ = base
            yaT = self.carve(2048, BF16).rearrange("p (i t) -> p i t", i=8)
            ybT = self.carve(2048, BF16).rearrange("p (i t) -> p i t", i=8)
            xdt = self.carve(2048, BF16).rearrange("p (b f) -> p b f", b=4)
            zs = self.carve(2048, BF16).rearrange("p (b f) -> p b f", b=4)
            BcT = self.carve(512, BF16).rearrange("p (g t) -> p g t", g=2)
            CcT = self.carve(512, BF16).rearrange("p (g t) -> p g t", g=2)
            Btok = self.carve(512, BF16).rearrange("p (b f) -> p b f", b=4)
            dtok = self.carve(64).rearrange("p (b h) -> p b h", b=4)
            Ddt = self.carve(64).rearrange("p (b h) -> p b h", b=4)
            acsT = self.carve(512)
            dtT = self.carve(512)
            qbase = self.sp_
            a_ = [self.carve(512), self.carve(512)]
            prod = [self.carve(516), self.carve(516)]
            cs = [self.carve(512), self.carve(512)]
            c0 = PAD + 512 * qi
            xres = [f"xT{tb}" for tb in range(4 * qi, 4 * qi + 4)]
            win = lambda kc, c0=c0: xTp[:, kc, c0:c0 + 512]

            dt_t, dt_r = self.wload(dram["m0_wdt"], words=128)
            dv = dt_t[:, 0:128].rearrange("p (k c) -> p k c", k=8)
            P.begin_group()
            for kc in range(8):
                P.add("pe", lambda e, kc=kc, dv=dv, win=win: e.matmul(self.bank(0)[0:16, :], lhsT=dv[:, kc, :], rhs=win(kc), start=(kc == 0), stop=(kc == 7)),
                      reads=[dt_r] + xres, writes=[self.BK(0)])
            P.end_group()
            P.begin_group()
            for tbl in range(4):
                for kc in range(8):
                    P.add("pe", lambda e, kc=kc, tbl=tbl, dv=dv, c0=c0: e.matmul(self.bank(1)[:, tbl * 16:(tbl + 1) * 16],
                                                                               lhsT=xTp[:, kc, c0 + tbl * 128:c0 + (tbl + 1) * 128], rhs=dv[:, kc, :],
                                                                               start=(kc == 0), stop=(kc == 7)),
                          reads=[dt_r] + xres, writes=[self.BK(1)])
            P.end_group()
            P.add("act", lambda e, dtT=dtT: e.activation(out=dtT[0:16, :], in_=self.bank(0)[0:16, :], func=AF.Exp, bias=dtb, scale=1.0),
                  reads=[self.BK(0), "S:cst"], writes=["S:dtT"])
            P.add("dve", lambda e, dtok=dtok: e.tensor_tensor(out=dtok[:, :, :], in0=self.bank(1)[:, 0:64].rearrange("p (b h) -> p b h", b=4),
                                                             in1=biasbc.unsqueeze(1).to_broadcast([128, 4, 16]), op=ALU.add),
                  reads=[self.BK(1), "S:cst"], writes=["S:dtok"])
            P.add("act", lambda e, dtok=dtok: e.activation(out=dtok[:, :, :], in_=dtok[:, :, :], func=AF.Exp), reads=["S:dtok"], writes=["S:dtok"])
            P.add("act", lambda e, dtT=dtT: e.activation(out=dtT[0:16, :], in_=dtT[0:16, :], func=AF.Ln, bias=1.0, scale=1.0), reads=["S:dtT"], writes=["S:dtT"])
            P.add("act", lambda e, dtok=dtok: e.activation(out=dtok[:, :, :], in_=dtok[:, :, :], func=AF.Ln, bias=1.0, scale=1.0),
                  reads=["S:dtok"], writes=["S:dtok"])
            P.add("dve", lambda e, dtok=dtok, Ddt=Ddt: e.reciprocal(out=Ddt[:, :, :], in_=dtok[:, :, :]), reads=["S:dtok"], writes=["S:Ddt"])
            P.add("dve", lambda e, Ddt=Ddt: e.tensor_tensor(out=Ddt[:, :, :], in0=Ddt[:, :, :], in1=Dbc.unsqueeze(1).to_broadcast([128, 4, 16]), op=ALU.mult),
                  reads=["S:Ddt", "S:cst"], writes=["S:Ddt"])
            P.add("dve", lambda e, dtT=dtT: e.tensor_scalar(out=dtT[0:16, :], in0=dtT[0:16, :], scalar1=acol, scalar2=None, op0=ALU.mult),
                  reads=["S:dtT", "S:cst"], writes=["S:dtT"])
            for c in range(4):
                P.add("dve", lambda e, c=c, dtT=dtT, acsT=acsT: e.tensor_tensor_scan(out=acsT[0:16, c * 128:(c + 1) * 128], data0=ones_f[0:16, 0:128],
                                                                                   data1=dtT[0:16, c * 128:(c + 1) * 128], initial=0.0,
                                                                                   op0=ALU.mult, op1=ALU.add),
                      reads=["S:dtT", "ones_f"], writes=[f"S:acs{c}"])

            def xchunk(sl, cc, ak, xv, x_r):
                if True:
                    if sl == 0:
                        ci = 8 + cc
                    else:
                        ci = 4 * (sl - 1) + cc
                    b = self.rotbank("m0", (0, 1, 2, 3, 4, 5))
                    P.begin_group()
                    for kc in range(8):
                        P.add("pe", lambda e, b=b, kc=kc, cc=cc, xv=xv, win=win: e.matmul(self.bank(b), lhsT=xv[:, kc, cc * 128:(cc + 1) * 128], rhs=win(kc),
                                                                                         start=(kc == 0), stop=(kc == 7)),
                              reads=[x_r] + xres, writes=[self.BK(b)])
                    P.end_group()
                    a = a_[ak % 2]
                    ar = f"S:a{ak % 2}"
                    bk = [self.BK(b)]
                    P.add("act", lambda e, b=b, a=a, ci=ci: e.activation(out=a[:, 0:512], in_=self.bank(b), func=AF.Identity,
                                                                        scale=xcw[:, ci, 3:4], bias=xcw[:, ci, 4:5]), reads=bk + ["S:cst"], writes=[ar])
                    for sh in range(1, 4):
                        P.add("dve", lambda e, b=b, a=a, ci=ci, sh=sh: e.scalar_tensor_tensor(
                            out=a[:, sh:512], in0=self.bank(b)[:, 0:512 - sh], scalar=xcw[:, ci, 3 - sh:4 - sh], in1=a[:, sh:512], op0=ALU.mult, op1=ALU.add),
                            reads=bk + ["S:cst", ar], writes=[ar])
                    if qi > 0:
                        P.add("dve", lambda e, a=a, ci=ci: e.tensor_tensor(out=a[:, 0:3], in0=a[:, 0:3], in1=fixx[:, ci, 0:3], op=ALU.add),
                              reads=[ar, f"S:fx{ci}"], writes=[ar])
                    if qi < 3:
                        P.add("dve", lambda e, b=b, ci=ci: e.tensor_scalar(out=fixx[:, ci, 0:3], in0=self.bank(b)[:, 509:512], scalar1=xcw[:, ci, 0:1],
                                                                          scalar2=None, op0=ALU.mult), reads=bk + ["S:cst"], writes=[f"S:fx{ci}"])
                        P.add("dve", lambda e, b=b, ci=ci: e.scalar_tensor_tensor(out=fixx[:, ci, 0:2], in0=self.bank(b)[:, 510:512], scalar=xcw[:, ci, 1:2],
                                                                                 in1=fixx[:, ci, 0:2], op0=ALU.mult, op1=ALU.add),
                              reads=bk + ["S:cst", f"S:fx{ci}"], writes=[f"S:fx{ci}"])
                        P.add("dve", lambda e, b=b, ci=ci: e.scalar_tensor_tensor(out=fixx[:, ci, 0:1], in0=self.bank(b)[:, 511:512], scalar=xcw[:, ci, 2:3],
                                                                                 in1=fixx[:, ci, 0:1], op0=ALU.mult, op1=ALU.add),
                              reads=bk + ["S:cst", f"S:fx{ci}"], writes=[f"S:fx{ci}"])
                    if ci >= 10:
                        g = ci - 10
                        yield
                        P.add("act", lambda e, a=a, g=g, CcT=CcT: e.activation(out=CcT[:, g, :], in_=a[:, 0:512], func=AF.Silu), reads=[ar], writes=[f"S:Cc{g}"])
                        yield
                        yield
                        return
                    yield
                    P.add("act", lambda e, a=a: e.activation(out=a[:, 0:512], in_=a[:, 0:512], func=AF.Silu), reads=[ar], writes=[ar])
                    yield
                    tbk = self.rotbank("m0t", (6, 7))
                    P.begin_group()
                    for tbl in range(4):
                        P.add("pe", lambda e, tbk=tbk, tbl=tbl, a=a: e.transpose(self.bank(tbk)[:, tbl * 128:(tbl + 1) * 128], a[:, tbl * 128:(tbl + 1) * 128], ident_f[:]),
                              reads=[ar, "ident_f"], writes=[self.BK(tbk)])
                    P.end_group()
                    if ci >= 8:
                        g = ci - 8
                        P.add("pool", lambda e, a=a, g=g, BcT=BcT: e.tensor_copy(out=BcT[:, g, :], in_=a[:, 0:512]), reads=[ar], writes=[f"S:Bc{g}"])
                        P.add("act", lambda e, tbk=tbk, g=g, Btok=Btok: e.activation(out=Btok[:, :, g * 128:(g + 1) * 128],
                                                                                    in_=self.bank(tbk).rearrange("p (b f) -> p b f", b=4), func=AF.Identity),
                              reads=[self.BK(tbk)], writes=[f"S:Bt{g}"])
                    else:
                        P.add("dve", lambda e, tbk=tbk, ci=ci, xdt=xdt, dtok=dtok: e.tensor_tensor(
                            out=xdt[:, :, ci * 128:(ci + 1) * 128].rearrange("p b (h d) -> p b h d", h=2),
                            in0=self.bank(tbk).rearrange("p (b h d) -> p b h d", b=4, h=2),
                            in1=dtok[:, :, 2 * ci:2 * ci + 2].unsqueeze(3).to_broadcast([128, 4, 2, 64]), op=ALU.mult),
                            reads=[self.BK(tbk), "S:dtok"], writes=[f"S:xdt{ci}"])
                    yield
            xg = []
            for sl in range(3):
                x_t, x_r = self.wload(wx[sl])
                xv = x_t[:].rearrange("p (k c) -> p k c", k=8)
                for cc in range(4):
                    k = len(xg)
                    xg.append(xchunk(sl, cc, k, xv, x_r))
                    next(xg[k])
                    if k >= 1:
                        next(xg[k - 1])
                    next(xg[k])
            next(xg[-1])

            for zsl in range(2):
                z_t, z_r = self.wload(wz[zsl])
                zv = z_t[:].rearrange("p (k c) -> p k c", k=8)
                for tbl in range(4):
                    b = self.rotbank("m0", (0, 1, 2, 3, 4, 5))
                    P.begin_group()
                    for kc in range(8):
                        P.add("pe", lambda e, b=b, kc=kc, tbl=tbl, zv=zv, c0=c0: e.matmul(self.bank(b), lhsT=xTp[:, kc, c0 + tbl * 128:c0 + (tbl + 1) * 128],
                                                                                         rhs=zv[:, kc, :], start=(kc == 0), stop=(kc == 7)),
                              reads=[z_r] + xres, writes=[self.BK(b)])
                    P.end_group()
                    P.add("act", lambda e, b=b, tbl=tbl, zsl=zsl, zs=zs: e.activation(out=zs[:, tbl, zsl * 512:(zsl + 1) * 512], in_=self.bank(b), func=AF.Silu),
                          reads=[self.BK(b)], writes=[f"S:zs{tbl}"])

            for i in range(8):
                s_t, s_r = self.wload(wsc[i], words=3072)
                sv = s_t[:, 0:3072].rearrange("p (k c) -> p k c", k=8)
                bks = []
                for part in range(3):
                    b = self.rotbank("m0", (0, 1, 2, 3, 4, 5))
                    bks.append(b)
                    P.begin_group()
                    for kc in range(8):
                        P.add("pe", lambda e, b=b, kc=kc, part=part, sv=sv, win=win: e.matmul(self.bank(b), lhsT=sv[:, kc, part * 128:(part + 1) * 128], rhs=win(kc),
                                                                                             start=(kc == 0), stop=(kc == 7)),
                              reads=[s_r] + xres, writes=[self.BK(b)])
                    P.end_group()
                bc_, bh_, bb_ = bks
                k2 = i % 2
                pr, csb, a = prod[k2], cs[k2], a_[k2]
                prr, csr, ar = f"S:pr{k2}", f"S:cs{k2}", f"S:a{k2}"
                P.add("act", lambda e, bc_=bc_, csb=csb: e.activation(out=csb[:, 0:512], in_=self.bank(bc_), func=AF.Identity), reads=[self.BK(bc_)], writes=[csr])
                if qi == 0:
                    P.add("pool", lambda e, pr=pr: e.memset(pr[:, 0:2], 0.0), writes=[prr + "h"])
                else:
                    P.add("pool", lambda e, pr=pr, i=i: e.tensor_copy(out=pr[:, 0:2], in_=halo[:, i, :]), reads=[f"S:halo{i}"], writes=[prr + "h"])
                P.add("dve", lambda e, bh_=bh_, pr=pr, csb=csb: e.tensor_tensor(out=pr[:, 2:514], in0=self.bank(bh_), in1=csb[:, 0:512], op=ALU.mult),
                      reads=[self.BK(bh_), csr], writes=[prr])
                if qi < 3:
                    P.add("pool", lambda e, pr=pr, i=i: e.tensor_copy(out=halo[:, i, :], in_=pr[:, 512:514]), reads=[prr], writes=[f"S:halo{i}"])
                P.add("act", lambda e, pr=pr, a=a, i=i: e.activation(out=a[:, 0:512], in_=pr[:, 2:514], func=AF.Identity, scale=scw[:, i, 2:3]),
                      reads=[prr, "S:cst"], writes=[ar])
                P.add("dve", lambda e, pr=pr, a=a, i=i: e.scalar_tensor_tensor(out=a[:, 0:512], in0=pr[:, 1:513], scalar=scw[:, i, 1:2], in1=a[:, 0:512],
                                                                              op0=ALU.mult, op1=ALU.add), reads=[prr, prr + "h", "S:cst", ar], writes=[ar])
                P.add("dve", lambda e, pr=pr, a=a, i=i: e.scalar_tensor_tensor(out=a[:, 0:512], in0=pr[:, 0:512], scalar=scw[:, i, 0:1], in1=a[:, 0:512],
                                                                              op0=ALU.mult, op1=ALU.add), reads=[prr, prr + "h", "S:cst", ar], writes=[ar])
                P.add("dve", lambda e, bb_=bb_, a=a, i=i, yaT=yaT: e.tensor_tensor(out=yaT[:, i, :], in0=self.bank(bb_), in1=a[:, 0:512], op=ALU.mult),
                      reads=[self.BK(bb_), ar], writes=[f"S:ya{i}"])

            self.P.fence()
            self.sp_ = qbase
            segT = self.carve(1024, BF16).rearrange("p (h t) -> p h t", h=16)
            MT = self.carve(1024, BF16).rearrange("p (h t) -> p h t", h=16)
            xdtd = self.carve(512, BF16)
            yt_ = [self.carve(1024), self.carve(1024)]
            junk = self.carve(256, BF16)
            sm_ = [self.carve(64), self.carve(64)]
            X16 = self.carve(16)
            def chunk(c):
                gc = 4 * qi + c
                cols = slice(c * 128, (c + 1) * 128)
                Lm, Lr = L[gc % 2], f"S:L{gc % 2}"
                yt, ytr = yt_[gc % 2], f"S:yt{gc % 2}"
                sm, smr = sm_[gc % 2], f"S:sm{gc % 2}"
                acr = f"S:acs{c}"
                P.add("dve", lambda e, Lm=Lm, cols=cols, acsT=acsT: e.tensor_scalar(out=Lm[32:48, :], in0=acsT[0:16, cols], scalar1=-1.0, scalar2=None, op0=ALU.mult),
                      reads=[acr], writes=[Lr])
                P.add("dve", lambda e, cols=cols, acsT=acsT: e.tensor_tensor(
                    out=R[0:16, :].rearrange("p (h t) -> p h t", h=16), in0=acsT[0:16, cols].unsqueeze(1).to_broadcast([16, 16, 128]),
                    in1=ident_f[0:16, 0:16].unsqueeze(2).to_broadcast([16, 16, 128]), op=ALU.mult), reads=[acr, "ident_f"], writes=["S:R"])
                for hg in range(4):
                    b = hg % 2
                    P.begin_group()
                    for hh in range(4):
                        P.add("pe", lambda e, b=b, hg=hg, hh=hh, Lm=Lm: e.matmul(self.bank(b)[:, hh * 128:(hh + 1) * 128], lhsT=Lm[0:48, :],
                                                                                rhs=R[0:48, hg * 512 + hh * 128:hg * 512 + (hh + 1) * 128], start=True, stop=False),
                              reads=[Lr, "S:R"], writes=[self.BK(b)])
                        P.add("pe", lambda e, b=b, hh=hh: e.matmul(self.bank(b)[:, hh * 128:(hh + 1) * 128], lhsT=ident_b[:], rhs=maskneg[:], start=False, stop=True),
                              reads=["ident_b", "maskneg"], writes=[self.BK(b)])
                    P.end_group()
                    P.add("act", lambda e, b=b, hg=hg, segT=segT: e.activation(out=segT[:, 4 * hg:4 * hg + 4, :], in_=self.bank(b).rearrange("p (h t) -> p h t", h=4),
                                                                              func=AF.Exp), reads=[self.BK(b)], writes=[f"S:seg{hg}"])
                segr = [f"S:seg{hg}" for hg in range(4)]
                P.begin_group()
                for g in range(2):
                    P.add("pe", lambda e, g=g, cols=cols, BcT=BcT, CcT=CcT: e.matmul(self.bank(2)[:, g * 128:(g + 1) * 128], lhsT=BcT[:, g, cols], rhs=CcT[:, g, cols],
                                                                                    start=True, stop=True), reads=[f"S:Bc{g}", f"S:Cc{g}"], writes=["bk2"])
                P.end_group()
                P.add("dve", lambda e, cols=cols, acsT=acsT: e.tensor_scalar(out=X16[0:16, 0:16], in0=ident_f[0:16, 0:16],
                                                                            scalar1=acsT[0:16, cols][:, 127:128], scalar2=None, op0=ALU.mult),
                      reads=[acr, "ident_f"], writes=["S:X16"])
                P.begin_group()
                P.add("pe", lambda e: e.matmul(self.bank(3)[:, 0:16], lhsT=ones_f[0:16, :], rhs=X16[0:16, 0:16], start=True, stop=True),
                      reads=["ones_f", "S:X16"], writes=["bk3"])
                P.add("pe", lambda e, cols=cols, acsT=acsT: e.transpose(self.bank(3)[:, 16:32], acsT[0:16, cols], ident_f[0:16, 0:16]),
                      reads=[acr, "ident_f"], writes=["bk3"])
                P.end_group()
                P.add("act", lambda e, sm=sm: e.activation(out=sm[:, 0:32], in_=self.bank(3)[:, 0:32], func=AF.Exp), reads=["bk3"], writes=[smr])
                yield
                for g in range(2):
                    P.add("dve", lambda e, g=g, MT=MT, segT=segT: e.tensor_tensor(
                        out=MT[:, 8 * g:8 * g + 8, :], in0=self.bank(2)[:, g * 128:(g + 1) * 128].unsqueeze(1).to_broadcast([128, 8, 128]),
                        in1=segT[:, 8 * g:8 * g + 8, :], op=ALU.mult), reads=["bk2"] + segr, writes=[f"S:MT{g}"])
                P.add("dve", lambda e, c=c, xdt=xdt, xdtd=xdtd, segT=segT: e.tensor_tensor(
                    out=xdtd[:, :].rearrange("p (h d) -> p h d", h=16), in0=xdt[:, c, :].rearrange("p (h d) -> p h d", h=16),
                    in1=segT[:, :, 127:128].to_broadcast([128, 16, 64]), op=ALU.mult),
                    reads=[f"S:xdt{i}" for i in range(8)] + segr, writes=["S:xdtd"])
                yield
                P.begin_group()
                for g in range(2):
                    P.add("pe", lambda e, g=g, cols=cols, CcT=CcT: e.matmul(self.PS[2][:, g * 512:(g + 1) * 512], lhsT=CcT[:, g, cols], rhs=prevbf[:, g * 512:(g + 1) * 512],
                                                                           start=True, stop=True), reads=[f"S:Cc{g}", "S:prev"], writes=[self.BK(4 + g)])
                P.end_group()
                P.begin_group()
                for g in range(2):
                    P.add("pe", lambda e, g=g, c=c, Btok=Btok, xdtd=xdtd: e.matmul(self.PS[0][:, g * 512:(g + 1) * 512], lhsT=Btok[:, c, g * 128:(g + 1) * 128],
                                                                                  rhs=xdtd[:, g * 512:(g + 1) * 512], start=True, stop=True),
                          reads=[f"S:Bt{g}", "S:xdtd"], writes=[self.BK(g)])
                P.end_group()
                P.add("dve", lambda e, sm=sm: e.tensor_tensor(out=H[:, :].rearrange("p (h d) -> p h d", h=16), in0=H[:, :].rearrange("p (h d) -> p h d", h=16),
                                                              in1=sm[:, 0:16].unsqueeze(2).to_broadcast([128, 16, 64]), op=ALU.mult),
                      reads=["S:H", smr], writes=["S:H"])
                P.add("dve", lambda e: e.tensor_tensor(out=H[:, :], in0=self.PS[0][:, :], in1=H[:, :], op=ALU.add), reads=["S:H", self.BK(0), self.BK(1)], writes=["S:H"])
                P.add("act", lambda e: e.activation(out=prevbf[:, :], in_=H[:, :], func=AF.Identity), reads=["S:H"], writes=["S:prev"])
                P.begin_group()
                for h in range(16):
                    P.add("pe", lambda e, h=h, c=c, MT=MT, xdt=xdt: e.matmul(self.PS[3][:, h * 64:(h + 1) * 64], lhsT=MT[:, h, :], rhs=xdt[:, c, h * 64:(h + 1) * 64],
                                                                            start=True, stop=True),
                          reads=[f"S:MT{h // 8}", f"S:xdt{h // 2}"], writes=[self.BK(6 + h // 8)])
                P.end_group()
                yield
                P.add("dve", lambda e, yt=yt, sm=sm: e.tensor_tensor(out=yt[:, :].rearrange("p (h d) -> p h d", h=16), in0=self.PS[2][:, :].rearrange("p (h d) -> p h d", h=16),
                                                                     in1=sm[:, 16:32].unsqueeze(2).to_broadcast([128, 16, 64]), op=ALU.mult),
                      reads=[self.BK(4), self.BK(5), smr], writes=[ytr])
                P.add("dve", lambda e, yt=yt: e.tensor_tensor(out=yt[:, :], in0=self.PS[3][:, :], in1=yt[:, :], op=ALU.add), reads=[self.BK(6), self.BK(7), ytr], writes=[ytr])
                for half in range(2):
                    P.add("pool", lambda e, c=c, half=half, xdt=xdt, Ddt=Ddt: e.tensor_tensor(
                        out=junk[:, 0:512].rearrange("p (h d) -> p h d", h=8), in0=xdt[:, c, half * 512:(half + 1) * 512].rearrange("p (h d) -> p h d", h=8),
                        in1=Ddt[:, c, 8 * half:8 * half + 8].unsqueeze(2).to_broadcast([128, 8, 64]), op=ALU.mult),
                        reads=[f"S:xdt{i}" for i in range(8)] + ["S:Ddt"], writes=["S:junk"])
                    P.add("pool", lambda e, yt=yt, half=half: e.tensor_tensor(out=yt[:, half * 512:(half + 1) * 512], in0=yt[:, half * 512:(half + 1) * 512],
                                                                           in1=junk[:, 0:512], op=ALU.add), reads=[ytr, "S:junk"], writes=[ytr])
                P.add("pool", lambda e, yt=yt, c=c, zs=zs: e.tensor_tensor(out=yt[:, :], in0=yt[:, :], in1=zs[:, c, :], op=ALU.mult), reads=[ytr, f"S:zs{c}"], writes=[ytr])
                for g in range(2):
                    P.add("act", lambda e, g=g, yt=yt, sm=sm: e.activation(out=junk[:, 0:512], in_=yt[:, g * 512:(g + 1) * 512], func=AF.Square,
                                                                          accum_out=sm[:, 32 + g:33 + g]), reads=[ytr], writes=[smr + "s", "S:junk"])
                P.add("pool", lambda e, sm=sm: e.tensor_scalar(out=sm[:, 34:36], in0=sm[:, 32:34], scalar1=1.0 / 512.0, scalar2=LN_EPS, op0=ALU.mult, op1=ALU.add),
                      reads=[smr + "s"], writes=[smr + "r"])
                P.add("pool", lambda e, sm=sm: e.tensor_tensor(out=sm[:, 34:36], in0=sm[:, 34:36], in1=self.neghalf[:, 0:1].to_broadcast([128, 2]), op=ALU.pow),
                      reads=[smr + "r", "neghalf"], writes=[smr + "r"])
                for g in range(2):
                    P.add("act", lambda e, g=g, yt=yt, sm=sm: e.activation(out=yt[:, g * 512:(g + 1) * 512], in_=yt[:, g * 512:(g + 1) * 512], func=AF.Identity,
                                                                          scale=sm[:, 34 + g:35 + g]), reads=[ytr, smr + "r"], writes=[ytr])
                yield
                P.begin_group()
                for i in range(8):
                    P.add("pe", lambda e, i=i, yt=yt: e.transpose(self.PS[2][:, i * 128:(i + 1) * 128], yt[:, i * 128:(i + 1) * 128], ident_f[:]),
                          reads=[ytr, "ident_f"], writes=[self.BK(4 + i // 4)])
                P.end_group()
                for half in range(2):
                    P.add("dve", lambda e, half=half, cols=cols, ybT=ybT: e.tensor_tensor(
                        out=ybT[:, 4 * half:4 * half + 4, cols], in0=self.PS[2][:, half * 512:(half + 1) * 512].rearrange("p (i t) -> p i t", i=4),
                        in1=gT[:, 4 * half:4 * half + 4].unsqueeze(2).to_broadcast([128, 4, 128]), op=ALU.mult),
                        reads=[self.BK(4 + half), "S:cst"], writes=[f"S:yb{c}"])

                yield
            gens = [chunk(c) for c in range(4)]
            order = [0, 0, 0, 1, 0, 1, 0, 1, 2, 1, 2, 1, 2, 3, 2, 3, 2, 3, 3, 3]
            for gi in order:
                next(gens[gi])
            for dh in range(2):
                for part in range(2):
                    o_t, o_r = self.wload(wo[dh, part])
                    ov = o_t[:].rearrange("p (k c) -> p k c", k=8)
                    src = yaT if part == 0 else ybT
                    for tbl in range(4):
                        P.begin_group()
                        for kc in range(8):
                            rd = [o_r, (f"S:ya{kc}" if part == 0 else f"S:yb{tbl}")]
                            P.add("pe", lambda e, dh=dh, part=part, tbl=tbl, kc=kc, ov=ov, src=src: e.matmul(
                                self.bank(4 * dh + tbl), lhsT=src[:, kc, tbl * 128:(tbl + 1) * 128], rhs=ov[:, kc, :],
                                start=(part == 0 and kc == 0), stop=(part == 1 and kc == 7)), reads=rd, writes=[self.BK(4 * dh + tbl)])
                        P.end_group()
                for tbl in range(4):
                    gtb = 4 * qi + tbl
                    P.add("dve", lambda e, dh=dh, tbl=tbl, gtb=gtb: e.scalar_tensor_tensor(
                        out=x_tok[:, gtb, dh * 512:(dh + 1) * 512], in0=x_tok[:, gtb, dh * 512:(dh + 1) * 512], scalar=ALPHA,
                        in1=self.bank(4 * dh + tbl), op0=ALU.mult, op1=ALU.add), reads=[self.BK(4 * dh + tbl), f"xt{gtb}_{dh}"], writes=[f"xt{gtb}_{dh}"])
            for tbl in range(4):
                gtb = 4 * qi + tbl
                self.ln_tb(gtb)
                self.transpose_tb(gtb, tbl, affine=True)
        self.P.fence()
        self.sp_ = base
        self.load_ln(0, with_T=False)
        for tb in range(NTB):
            self.ln_affine(tb)


def declare_dram(nc, phases):
    d = {}
    d["x"] = nc.dram_tensor("x", [T, D], F32, kind="ExternalInput").ap()
    d["out"] = nc.dram_tensor("out", [T, D], F32, kind="ExternalOutput").ap()
    d["lnp"] = nc.dram_tensor("lnp", [4, 2, D], F32, kind="ExternalInput").ap()
    d["lnpT"] = nc.dram_tensor("lnpT", [4, 128, 16], F32, kind="ExternalInput").ap()
    d["ffn_cwb"] = nc.dram_tensor("ffn_cwb", [2, 128, 44, 4], F32, kind="ExternalInput").ap()
    d["wup"] = nc.dram_tensor("wup", [2, 11, 128, 4096], F32, kind="ExternalInput").ap()
    d["wdn"] = nc.dram_tensor("wdn", [2, 2, 3, 128, 4096], F32, kind="ExternalInput").ap()
    d["m0_wdt"] = nc.dram_tensor("m0_wdt", [128, 128], F32, kind="ExternalInput").ap()
    d["m0_wx"] = nc.dram_tensor("m0_wx", [3, 128, 4096], F32, kind="ExternalInput").ap()
    d["m0_wz"] = nc.dram_tensor("m0_wz", [2, 128, 4096], F32, kind="ExternalInput").ap()
    d["m0_wsc"] = nc.dram_tensor("m0_wsc", [8, 128, 3072], F32, kind="ExternalInput").ap()
    d["m0_wo"] = nc.dram_tensor("m0_wo", [2, 2, 128, 4096], F32, kind="ExternalInput").ap()
    d["m0_tokc"] = nc.dram_tensor("m0_tokc", [2, 16], F32, kind="ExternalInput").ap()
    d["m0_featc"] = nc.dram_tensor("m0_featc", [128, 92], F32, kind="ExternalInput").ap()
    d["m0_headc"] = nc.dram_tensor("m0_headc", [16, 2], F32, kind="ExternalInput").ap()
    d["fox_f"] = nc.dram_tensor("fox_f", [128, 128], F32, kind="ExternalInput").ap()
    d["fox_bf"] = nc.dram_tensor("fox_bf", [16, 1], F32, kind="ExternalInput").ap()
    d["fox_qk"] = nc.dram_tensor("fox_qk", [8, 128, 2048], F32, kind="ExternalInput").ap()
    d["fox_v"] = nc.dram_tensor("fox_v", [8, 128, 1024], F32, kind="ExternalInput").ap()
    d["fox_wo"] = nc.dram_tensor("fox_wo", [8, 128, 1024], F32, kind="ExternalInput").ap()
    d["augq"] = nc.dram_tensor("augq", [16, 6, T], BF16, kind="Internal").ap()
    d["augk"] = nc.dram_tensor("augk", [16, 6, T], BF16, kind="Internal").ap()
    return d


def build_program(phases=("mix0", "ffn0", "mix1", "ffn1")):
    nc = bass.Bass("TRN2", target_bir_lowering=False)
    dram = declare_dram(nc, phases)
    P = Prog(nc)
    B = Builder(nc, P, dram)
    B.load_x()
    for tb in range(NTB):
        B.transpose_tb(tb, tb % 4)
    last = phases[-1]
    for ph in phases:
        if ph == "ffn0":
            B.ffn(0, final=(ph == last))
        elif ph == "ffn1":
            B.ffn(1, final=(ph == last))
        elif ph == "mix0":
            P.pin = tuple(os.environ.get("MK_PIN0", "dve,pe").split(","))
            B.mix0()
            P.pin = ("dve",)
            if ph == last:
                for tb in range(NTB):
                    B.store_tb(tb)
        elif ph == "mix1":
            B.mix1()
            if ph == last:
                for tb in range(NTB):
                    B.store_tb(tb)
        else:
            raise NotImplementedError(ph)
    if SCHEDULE:
        P.schedule()
    P.finalize(B.out_dmas)
    P.emit(B.out_dmas)
    P.close()
    return nc


def host_layouts(inp):
    f = np.float32
    o = {}
    o["lnp"] = np.ascontiguousarray(np.stack([
        np.stack([inp["ln_mix_g"][0], inp["ln_mix_b"][0]]), np.stack([inp["ln_ffn_g"][0], inp["ln_ffn_b"][0]]),
        np.stack([inp["ln_mix_g"][1], inp["ln_mix_b"][1]]), np.stack([inp["ln_ffn_g"][1], inp["ln_ffn_b"][1]])]).astype(f))
    o["lnpT"] = np.ascontiguousarray(o["lnp"].reshape(4, 2, 8, 128).transpose(0, 3, 1, 2).reshape(4, 128, 16))
    cw = inp["ffn_conv_w"].astype(f)
    cb = inp["ffn_conv_b"].astype(f)
    cwb = np.concatenate([cw.transpose(0, 2, 1), cb[:, :, None]], axis=2)
    o["ffn_cwb"] = np.ascontiguousarray(cwb.reshape(2, 44, 128, 4).transpose(0, 2, 1, 3))
    wu = inp["ffn_w_up"].astype(f)
    u = wu[:, :, :DFF].reshape(2, 8, 128, 11, 2, 128)
    g = wu[:, :, DFF:].reshape(2, 8, 128, 11, 2, 128)
    ug = np.stack([u, g], axis=5)
    o["wup"] = np.ascontiguousarray(ug.transpose(0, 3, 2, 1, 4, 5, 6).reshape(2, 11, 128, 4096))
    wd = inp["ffn_w_down"].astype(f)
    wdp = np.zeros((2, 24 * 128, 1024), f)
    wdp[:, :DFF] = wd
    wdp = wdp.reshape(2, 3, 8, 128, 2, 512)
    o["wdn"] = np.ascontiguousarray(wdp.transpose(0, 4, 1, 3, 2, 5).reshape(2, 2, 3, 128, 4096))
    w0 = inp["sc_ssm_w_in"][0].astype(f).reshape(8, 128, 5648)
    lay = lambda cols: np.ascontiguousarray(w0[:, :, cols].transpose(1, 0, 2).reshape(128, -1))
    o["m0_wdt"] = lay(slice(5632, 5648))
    o["m0_wx"] = np.stack([lay(slice(5120, 5632)), lay(slice(4096, 4608)), lay(slice(4608, 5120))])
    o["m0_wz"] = np.stack([lay(slice(3072, 3584)), lay(slice(3584, 4096))])
    o["m0_wsc"] = np.stack([lay(np.r_[1024 + 128 * i:1152 + 128 * i, 2048 + 128 * i:2176 + 128 * i, 128 * i:128 + 128 * i]) for i in range(8)])
    wo0 = inp["sc_ssm_w_out"][0].astype(f).reshape(2, 8, 128, 2, 512)
    o["m0_wo"] = np.ascontiguousarray(wo0.transpose(3, 0, 2, 1, 4).reshape(2, 2, 128, 4096))
    o["m0_tokc"] = np.ascontiguousarray(np.stack([inp["ssm_dt_bias"][0], inp["ssm_d"][0]]).astype(f))
    o["m0_headc"] = np.ascontiguousarray(np.stack([inp["ssm_dt_bias"][0], inp["ssm_a_log"][0]], axis=1).astype(f))
    gTh = inp["ssm_norm_g"][0].astype(f).reshape(8, 128).T
    scwh = inp["sc_conv_w"][0].astype(f).reshape(3, 8, 128).transpose(2, 1, 0)
    xw = inp["ssm_conv_w"][0].astype(f).reshape(4, 12, 128).transpose(2, 1, 0)
    xb = inp["ssm_conv_b"][0].astype(f).reshape(12, 128).T[:, :, None]
    o["m0_featc"] = np.ascontiguousarray(np.concatenate([gTh, scwh.reshape(128, 24), np.concatenate([xw, xb], axis=2).reshape(128, 60)], axis=1))
    wi = inp["fox_w_in"][0].astype(f)
    wk = wi.reshape(8, 128, 3088)
    o["fox_f"] = np.ascontiguousarray(wk[:, :, 3072:3088].transpose(1, 0, 2).reshape(128, 128))
    q = wk[:, :, 0:1024].reshape(8, 128, 8, 128)
    k = wk[:, :, 1024:2048].reshape(8, 128, 8, 128)
    v = wk[:, :, 2048:3072].reshape(8, 128, 8, 128)
    qk = np.concatenate([q, k], axis=3)
    o["fox_qk"] = np.ascontiguousarray(qk.transpose(2, 1, 0, 3).reshape(8, 128, 2048))
    o["fox_v"] = np.ascontiguousarray(v.transpose(2, 1, 0, 3).reshape(8, 128, 1024))
    o["fox_wo"] = np.ascontiguousarray(inp["fox_w_out"][0].astype(f).reshape(8, 128, 1024))
    o["fox_bf"] = np.ascontiguousarray(inp["fox_b_f"][0].astype(f).reshape(16, 1))
    return o


_NC_CACHE = {}


def kernel(**inputs):
    phases = ("mix0", "ffn0", "mix1", "ffn1")
    if phases not in _NC_CACHE:
        _NC_CACHE[phases] = build_program(phases)
    nc = _NC_CACHE[phases]
    lay = host_layouts(inputs)
    x = np.asarray(inputs["x"], dtype=np.float32)
    in_maps = [dict(lay, x=np.ascontiguousarray(x[b])) for b in range(8)]
    res = run_bass_kernel_spmd(nc, in_maps, core_ids=list(range(8)))
    return np.stack([np.asarray(r["out"], dtype=np.float32) for r in res.results], axis=0)
```

```python
from contextlib import ExitStack
import numpy as np
import concourse.bass as bass
import concourse.mybir as mybir
from concourse.bass_utils import run_bass_kernel_spmd

F32 = mybir.dt.float32
BF16 = mybir.dt.bfloat16
AF = mybir.ActivationFunctionType
ALU = mybir.AluOpType

COMPUTE = ("pe", "act", "dve", "pool")
QUEUES = ("pe", "act", "dve", "pool", "sp")

ALPHA = 4.0 ** 0.25
LN_EPS = 1e-5
T = 2048
D = 1024
NTB = 16
PAD = 4
DFF = 2816
NJ = 22
import os
SCHEDULE = os.environ.get('MK_SCHED', '1') == '1'
PREFETCH = os.environ.get('MK_PREFETCH', '0') == '1'


class Ins:
    __slots__ = ("eng", "fn", "deps", "idx", "dma_key", "dma_val", "signal", "sigval", "clock", "waits", "is_dma", "pinned")


class Prog:
    def __init__(self, nc):
        self.nc = nc
        self.es = ExitStack()
        self.ins = []
        self.q = {e: [] for e in QUEUES}
        self.last_w = {}
        self.readers = {}
        self.dma_cum = {}
        self.dma_sems = {}
        self.sems = {}
        self.fence_deps = []
        self.scratch_touch = {}
        self.pin = ("dve",)

    def sbuf(self, name, shape, dtype):
        return self.es.enter_context(self.nc.sbuf_tensor(name, list(shape), dtype))

    def psum(self, name, shape, dtype=F32):
        return self.es.enter_context(self.nc.psum_tensor(name, list(shape), dtype))

    def begin_group(self):
        self._grp = []

    def end_group(self):
        g, self._grp = self._grp, None
        fns = [x[0] for x in g]
        reads, writes = [], []
        for _, r, w in g:
            for x in r:
                if x not in reads:
                    reads.append(x)
            for x in w:
                if x not in writes:
                    writes.append(x)

        def run(e, fns=fns):
            h = None
            for f in fns:
                h = f(e)
            return h
        return self.add("pe", run, reads=reads, writes=writes)

    def add(self, eng, fn, reads=(), writes=(), dma_key=None):
        if getattr(self, "_grp", None) is not None:
            assert eng == "pe" and dma_key is None
            self._grp.append((fn, list(reads), list(writes)))
            return None
        i = Ins()
        i.eng = eng
        i.fn = fn
        i.is_dma = dma_key is not None
        i.dma_key = dma_key
        i.signal = False
        i.pinned = eng in self.pin
        deps = set()
        scratch = False
        if any(r.startswith("bk") for r in reads):
            writes = list(writes) + [r for r in reads if r.startswith("bk") and r not in writes]
            reads = [r for r in reads if not r.startswith("bk")]
        for r in reads:
            w = self.last_w.get(r)
            if w is not None:
                deps.add(w)
            if r.startswith("S:"):
                scratch = True
        for w_ in writes:
            w = self.last_w.get(w_)
            if w is not None:
                deps.add(w)
            for rd in self.readers.get(w_, ()):
                deps.add(rd)
            if w_.startswith("S:"):
                scratch = True
        if scratch:
            deps.update(self.fence_deps)
        i.deps = deps
        i.idx = len(self.ins)
        self.ins.append(i)
        self.q[eng].append(i)
        for r in reads:
            self.readers.setdefault(r, []).append(i)
        for w_ in writes:
            self.last_w[w_] = i
            self.readers[w_] = []
        if i.is_dma:
            self.dma_cum[dma_key] = self.dma_cum.get(dma_key, 0) + 16
            i.dma_val = self.dma_cum[dma_key]
        if scratch:
            self.scratch_touch[i.idx] = i
        return i

    def fence(self):
        touched = list(self.scratch_touch.values())
        self.scratch_touch = {}
        if not hasattr(self, "_fdummy"):
            self._fdummy = self.sbuf("fence_dummy", [128, 8], F32)
        fd = self._fdummy
        join = self.add("dve", lambda e: e.memset(fd[:, 0:1], 0.0), writes=["fence_dummy"])
        join.deps.update(touched)
        self.fence_deps = [join]
        for k in [k for k in self.last_w if k.startswith("S:")]:
            del self.last_w[k]
        for k in [k for k in self.readers if k.startswith("S:")]:
            del self.readers[k]


    def schedule(self):
        import heapq

        class _Probe:
            def __init__(self):
                self.recs = []

            def __getattr__(self, name):
                def f(*a, **k):
                    self.recs.append((name, a, k))
                    return None
                return f

        def prod(sh):
            n = 1
            for v in sh:
                n *= int(v)
            return n

        cost, lat = {}, {}
        for i in self.ins:
            p = _Probe()
            i.fn(p)
            name, a, k = p.recs[-1]
            out = k.get("out", a[0] if a else None)
            n = prod(out.shape[1:]) if out is not None and hasattr(out, "shape") else 512
            L = 0.0
            if i.is_dma:
                by = n * out.shape[0] * 4 if out is not None else 0
                c = 0.6 if i.eng == "pool" else 0.15
                L = 2.5 + by / 150e3
            elif i.eng == "pe":
                c = 0.0
                for name, a, k in p.recs:
                    if name == "transpose":
                        c += 0.12
                    else:
                        rhs = k.get("rhs", a[2] if len(a) > 2 else None)
                        nn = prod(rhs.shape[1:]) if rhs is not None else 512
                        lhs = k.get("lhsT", a[1] if len(a) > 1 else None)
                        c1 = 0.01 + max(nn, 64) / 2400.0
                        if lhs is not None and lhs.dtype == F32:
                            c1 *= 4
                        c += c1
            elif i.eng == "act":
                c = 0.22 + n / 1400.0
            elif i.eng == "dve":
                c = 0.12 + n / 960.0
            else:
                c = 0.25 + n / 600.0
            cost[i.idx] = c
            lat[i.idx] = L
        succ = {i.idx: [] for i in self.ins}
        indeg = {}
        import os
        chain = {}
        for e in QUEUES:
            prev = None
            for i in self.q[e]:
                if prev is not None and i.pinned:
                    chain[i.idx] = prev
                prev = i
        for i in self.ins:
            ds = [d for d in i.deps if d is not i]
            if i.idx in chain and chain[i.idx] not in ds:
                ds.append(chain[i.idx])
            indeg[i.idx] = len(ds)
            for d in ds:
                succ[d.idx].append(i)
        byidx = {i.idx: i for i in self.ins}
        pending = {e: [] for e in QUEUES}
        avail = {e: [] for e in QUEUES}
        free = {e: 0.0 for e in QUEUES}
        fin = {}
        ready = {}
        for i in self.ins:
            if indeg[i.idx] == 0:
                ready[i.idx] = 0.0
                heapq.heappush(pending[i.eng], (0.0, i.idx))
        order = []
        newq = {e: [] for e in QUEUES}
        SYNC = 0.12
        n_left = len(self.ins)
        while n_left:
            best = None
            for e in QUEUES:
                pe_, av = pending[e], avail[e]
                while pe_ and pe_[0][0] <= free[e]:
                    r, ix = heapq.heappop(pe_)
                    heapq.heappush(av, ix)
                if av:
                    cand = (free[e], av[0], e, True)
                elif pe_:
                    cand = (pe_[0][0], pe_[0][1], e, False)
                else:
                    continue
                if best is None or cand[:2] < best[:2]:
                    best = cand
            st, ix, e, from_av = best
            if from_av:
                heapq.heappop(avail[e])
            else:
                heapq.heappop(pending[e])
            i = byidx[ix]
            f = st + cost[ix]
            free[e] = f
            fin[ix] = f + lat[ix]
            order.append(i)
            newq[e].append(i)
            n_left -= 1
            for sx in succ[ix]:
                indeg[sx.idx] -= 1
                r = max(ready.get(sx.idx, 0.0), fin[ix] + (0.0 if sx.eng == e and not i.is_dma else SYNC))
                ready[sx.idx] = r
                if indeg[sx.idx] == 0:
                    heapq.heappush(pending[sx.eng], (r, sx.idx))
        self.ins = order
        self.q = newq
        for k, i in enumerate(self.ins):
            i.idx = k
        self.est_us = max(fin.values()) if fin else 0.0

    def finalize(self, tail):
        nc = self.nc
        pos = {}
        for e in QUEUES:
            for k, i in enumerate(self.q[e]):
                pos[i.idx] = k
        prev_clock = {e: ({c: -1 for c in COMPUTE}, frozenset()) for e in QUEUES}
        for i in self.ins:
            clk, dseen = prev_clock[i.eng]
            clk = dict(clk)
            dseen = set(dseen)
            waits = []
            for d in sorted(i.deps, key=lambda d: -d.idx):
                if d is i:
                    continue
                if d.is_dma:
                    if d.idx in dseen:
                        continue
                    waits.append(d)
                    dseen.add(d.idx)
                else:
                    if d.eng == i.eng and d.eng == "pe":
                        continue
                    if clk[d.eng] >= pos[d.idx]:
                        continue
                    waits.append(d)
                    clk[d.eng] = max(clk[d.eng], pos[d.idx])
                dc, dd = d.clock
                for c in COMPUTE:
                    if dc[c] > clk[c]:
                        clk[c] = dc[c]
                dseen |= dd
            final = []
            for d in waits:
                if d.is_dma:
                    final.append(d)
                elif clk[d.eng] == pos[d.idx]:
                    final.append(d)
            i.waits = final
            for d in final:
                d.signal = True
            i.clock = (clk, frozenset(dseen))
            prev_clock[i.eng] = i.clock
        for d in tail:
            d.signal = True
        for e in COMPUTE:
            self.sems[e] = self.es.enter_context(nc.semaphore("s_" + e))
            n = 0
            for i in self.q[e]:
                if i.is_dma:
                    continue
                if i.signal:
                    n += 1
                    i.sigval = n
        for k in self.dma_cum:
            self.dma_sems[k] = self.es.enter_context(nc.semaphore("d_" + str(k).replace(":", "_")))

    def emit(self, tail):
        nc = self.nc
        prog = self

        def wait(eng, d):
            if d.is_dma:
                eng.wait_ge(prog.dma_sems[d.dma_key], d.dma_val)
            else:
                eng.wait_ge(prog.sems[d.eng], d.sigval)

        def run(engname, eng):
            for i in prog.q[engname]:
                for d in i.waits:
                    wait(eng, d)
                h = i.fn(eng)
                if i.is_dma:
                    h.then_inc(prog.dma_sems[i.dma_key], 16)
                elif i.signal:
                    h.then_inc(prog.sems[i.eng], 1)
            if engname == "sp":
                for d in tail:
                    wait(eng, d)

        with nc.Block() as block:
            @block.tensor
            def _(e):
                run("pe", e)

            @block.scalar
            def _(e):
                run("act", e)

            @block.vector
            def _(e):
                run("dve", e)

            @block.gpsimd
            def _(e):
                run("pool", e)

            @block.sync
            def _(e):
                run("sp", e)

    def close(self):
        self.es.close()


class Builder:
    def __init__(self, nc, P, dram):
        self.nc, self.P, self.dram = nc, P, dram
        P_ = P
        self.x_tok = P_.sbuf("x_tok_sb", [128, NTB, D], F32)
        self.xTp = P_.sbuf("xTp", [128, 8, PAD + T], BF16)
        self.ident_f = P_.sbuf("ident_f", [128, 128], F32)
        self.ones_f = P_.sbuf("ones_f", [128, 128], F32)
        self.neghalf = P_.sbuf("neghalf", [128, 1], F32)
        self.lnp = None
        self.stats = P_.sbuf("stats", [128, NTB, 16], F32)
        self.cwb = P_.sbuf("cwb", [128, 44, 4], F32)
        self.fix = P_.sbuf("fix", [128, 44, 2], F32)
        self.ring = [P_.sbuf(f"ring{i}", [128, 4096], BF16) for i in range(3)]
        self.ring_cnt = 0
        self.SW = 20480
        self.S = P_.sbuf("S", [128, self.SW], F32)
        self.sp_ = 0
        self.PS = [P_.psum(f"ps{i}", [128, 1024], F32) for i in range(4)]
        self.out_dmas = []
        self.ident_b = P_.sbuf("ident_b", [128, 128], BF16)
        self.maskneg = P_.sbuf("maskneg", [128, 128], BF16)
        self.small = P_.sbuf("small", [128, 64], F32)
        self.lnT = P_.sbuf("lnT_sb", [128, 16], F32)
        self.rot = {}
        self.consts()

    def rotbank(self, group, banks):
        k = self.rot.get(group, 0)
        self.rot[group] = k + 1
        return banks[k % len(banks)]

    def reset_scratch(self):
        self.P.fence()
        self.sp_ = 0

    def carve(self, words, dtype=F32):
        a = self.sp_
        self.sp_ += words
        assert self.sp_ <= self.SW, (self.sp_, self.SW)
        v = self.S[:, a:a + words]
        if dtype == BF16:
            v = v.bitcast(BF16)
        return v

    def bank(self, b):
        return self.PS[b // 2][:, (b % 2) * 512:(b % 2) * 512 + 512]

    @staticmethod
    def BK(b):
        return f"bk{b}"

    def ring_next(self):
        i = self.ring_cnt % 3
        self.ring_cnt += 1
        return self.ring[i], f"ring{i}"

    def wload(self, src_ap, words=4096, slot=None):
        if slot is None:
            tile, res = self.ring_next()
        else:
            tile, res = self.ring[slot], f"ring{slot}"
        self.P.add("pool", lambda e, t=tile, s=src_ap, w=words: e.dma_start(out=t[:, 0:w], in_=s, max_dma_last_dim=8192),
                   writes=[res], dma_key=res)
        return tile, res

    def consts(self):
        P = self.P
        ones_f, ident_f = self.ones_f, self.ident_f
        P.add("pool", lambda e: e.memset(ones_f[:], 1.0), writes=["ones_f"])
        P.add("pool", lambda e: e.affine_select(out=ident_f[:], in_=ones_f[:], pattern=[[-1, 128]], compare_op=ALU.is_equal,
                                                 fill=0.0, base=0, channel_multiplier=1), reads=["ones_f"], writes=["ident_f"])
        nh = self.neghalf
        P.add("pool", lambda e: e.memset(nh[:], -0.5), writes=["neghalf"])
        xTp = self.xTp
        P.add("pool", lambda e: e.memset(xTp[:, :, 0:PAD], 0.0), writes=["xTpad"])
        ident_b, maskneg = self.ident_b, self.maskneg
        P.add("pool", lambda e: e.tensor_copy(out=ident_b[:], in_=ident_f[:]), reads=["ident_f"], writes=["ident_b"])
        P.add("pool", lambda e: e.memset(maskneg[:], -30000.0), writes=["maskneg"])
        P.add("pool", lambda e: e.affine_select(out=maskneg[:], in_=maskneg[:], pattern=[[-1, 128]], compare_op=ALU.is_gt,
                                                 fill=0.0, base=0, channel_multiplier=1), reads=["maskneg"], writes=["maskneg"])

    def load_x(self):
        P, x_tok = self.P, self.x_tok
        xv = self.dram["x"].rearrange("(tb p) d -> p tb d", p=128)
        for g in range(4):
            P.add("sp", lambda e, g=g: e.dma_start(out=x_tok[:, 4 * g:4 * g + 4, :], in_=xv[:, 4 * g:4 * g + 4, :]),
                  writes=[f"xt{tb}_{dh}" for tb in range(4 * g, 4 * g + 4) for dh in range(2)], dma_key=f"xin{g}")

    def store_tb(self, tb):
        P, x_tok = self.P, self.x_tok
        ov = self.dram["out"].rearrange("(tb p) d -> p tb d", p=128)
        i = P.add("sp", lambda e: e.dma_start(out=ov[:, tb, :], in_=x_tok[:, tb, :]), reads=[f"xt{tb}_0", f"xt{tb}_1"], dma_key=f"xout{tb}")
        self.out_dmas.append(i)

    def transpose_tb(self, tb, pst, affine=False):
        P, x_tok, xTp, ident_f, lnT = self.P, self.x_tok, self.xTp, self.ident_f, self.lnT
        ps = self.PS[pst]
        P.begin_group()
        for kc in range(8):
            P.add("pe", lambda e, kc=kc: e.transpose(ps[:, kc * 128:(kc + 1) * 128], x_tok[:, tb, kc * 128:(kc + 1) * 128], ident_f[:]),
                  reads=[f"xt{tb}_{kc // 4}", "ident_f"], writes=[self.BK(2 * pst + kc // 4)])
        P.end_group()
        c0 = PAD + tb * 128
        if affine:
            for kc in range(8):
                if kc < 4:
                    P.add("act", lambda e, kc=kc: e.activation(out=xTp[:, kc, c0:c0 + 128], in_=ps[:, kc * 128:(kc + 1) * 128], func=AF.Identity,
                                                               scale=lnT[:, kc:kc + 1], bias=lnT[:, 8 + kc:9 + kc]),
                          reads=[self.BK(2 * pst), "lnT"], writes=[f"xT{tb}"])
                else:
                    P.add("dve", lambda e, kc=kc: e.tensor_scalar(out=xTp[:, kc, c0:c0 + 128], in0=ps[:, kc * 128:(kc + 1) * 128],
                                                                  scalar1=lnT[:, kc:kc + 1], scalar2=lnT[:, 8 + kc:9 + kc], op0=ALU.mult, op1=ALU.add),
                          reads=[self.BK(2 * pst + 1), "lnT"], writes=[f"xT{tb}"])
            return
        P.add("act", lambda e: e.activation(out=xTp[:, 0:4, c0:c0 + 128], in_=ps[:, 0:512].rearrange("p (k t) -> p k t", k=4), func=AF.Identity),
              reads=[self.BK(2 * pst)], writes=[f"xT{tb}"])
        P.add("dve", lambda e: e.tensor_copy(out=xTp[:, 4:8, c0:c0 + 128], in_=ps[:, 512:1024].rearrange("p (k t) -> p k t", k=4)),
              reads=[self.BK(2 * pst + 1)], writes=[f"xT{tb}"])

    def load_lnT(self, idx):
        lnT = self.lnT
        src = self.dram["lnpT"][idx]
        self.P.add("sp", lambda e: e.dma_start(out=lnT[:], in_=src), writes=["lnT"], dma_key="lnT")

    def load_ln(self, idx, with_T=True):
        if with_T:
            self.load_lnT(idx)
        self.lnp = self.carve(2 * D).rearrange("p (a d) -> p a d", a=2)
        lnp = self.lnp
        src = self.dram["lnp"][idx].partition_broadcast(128)
        self.P.add("sp", lambda e: e.dma_start(out=lnp[:], in_=src), writes=["S:lnp"], dma_key="lnp")

    def ln_tb(self, tb):
        P, x_tok, st, lnp, nh = self.P, self.x_tok, self.stats, self.lnp, self.neghalf
        R = [f"xt{tb}_0", f"xt{tb}_1"]
        sr = f"st{tb}"
        P.add("dve", lambda e: e.bn_stats(out=st[:, tb, 0:6], in_=x_tok[:, tb, 0:512]), reads=[R[0]], writes=[sr])
        P.add("dve", lambda e: e.bn_stats(out=st[:, tb, 6:12], in_=x_tok[:, tb, 512:1024]), reads=[R[1]], writes=[sr])
        P.add("dve", lambda e: e.bn_aggr(out=st[:, tb, 12:14], in_=st[:, tb, 0:12]), reads=[sr], writes=[sr])
        P.add("pool", lambda e: e.tensor_scalar(out=st[:, tb, 14:15], in0=st[:, tb, 13:14], scalar1=LN_EPS, scalar2=None, op0=ALU.add),
              reads=[sr], writes=[sr])
        P.add("pool", lambda e: e.tensor_tensor(out=st[:, tb, 14:15], in0=st[:, tb, 14:15], in1=nh[:], op=ALU.pow),
              reads=[sr, "neghalf"], writes=[sr])
        P.add("dve", lambda e: e.tensor_scalar(out=st[:, tb, 15:16], in0=st[:, tb, 12:13], scalar1=st[:, tb, 14:15], scalar2=-1.0,
                                               op0=ALU.mult, op1=ALU.mult), reads=[sr], writes=[sr])
        P.add("act", lambda e: e.activation(out=x_tok[:, tb, :], in_=x_tok[:, tb, :], func=AF.Identity,
                                            scale=st[:, tb, 14:15], bias=st[:, tb, 15:16]), reads=[sr] + R, writes=R)

    def ln_affine(self, tb):
        P, x_tok, lnp = self.P, self.x_tok, self.lnp
        R = [f"xt{tb}_0", f"xt{tb}_1"]
        P.add("pool", lambda e: e.tensor_tensor(out=x_tok[:, tb, :], in0=x_tok[:, tb, :], in1=lnp[:, 0, :], op=ALU.mult),
              reads=R + ["S:lnp"], writes=R)
        P.add("pool", lambda e: e.tensor_tensor(out=x_tok[:, tb, :], in0=x_tok[:, tb, :], in1=lnp[:, 1, :], op=ALU.add),
              reads=R + ["S:lnp"], writes=R)

    def ffn(self, l, final=False):
        P, xTp, x_tok, cwb, fix = self.P, self.xTp, self.x_tok, self.cwb, self.fix
        self.reset_scratch()
        hid = self.carve(NJ * 1024 // 2, BF16).rearrange("p (j t) -> p j t", j=NJ)
        tmp = [[self.carve(1024), self.carve(1024)] for _ in range(2)]
        cwsrc = self.dram["ffn_cwb"][l]
        P.add("sp", lambda e: e.dma_start(out=cwb[:], in_=cwsrc), writes=["cwb"], dma_key="cwb")
        self.load_ln(2 * l + 1)
        wup, wdn = self.dram["wup"], self.dram["wdn"]
        deferred = []
        for h in range(2):
            c0 = PAD + 1024 * h
            xres = [f"xT{tb}" for tb in range(8 * h, 8 * h + 8)] + ["xTpad"]
            for s in range(11):
                if s == 0 and h == 1:
                    tile, res = pre_up
                else:
                    tile, res = self.wload(wup[l, s])
                sv = tile[:].rearrange("p (k j c) -> p k j c", k=8, j=2)
                for jj in range(2):
                    j = 2 * s + jj
                    pset = j % 2
                    for ug in range(2):
                        pst = 2 * pset + ug
                        ps = self.PS[pst]
                        for t in range(2):
                            P.begin_group()
                            for kc in range(8):
                                P.add("pe", lambda e, ps=ps, t=t, kc=kc, jj=jj, ug=ug, sv=sv, c0=c0: e.matmul(
                                    ps[:, t * 512:(t + 1) * 512], lhsT=sv[:, kc, jj, ug * 128:(ug + 1) * 128],
                                    rhs=xTp[:, kc, c0 + t * 512:c0 + (t + 1) * 512], start=(kc == 0), stop=(kc == 7)),
                                    reads=[res] + xres, writes=[self.BK(2 * pst + t)])
                            P.end_group()
                    for ug in range(2):
                        pst = 2 * pset + ug
                        ps = self.PS[pst]
                        a = tmp[pset][ug]
                        ar = f"S:a{pset}{ug}"
                        ch = ug * NJ + j
                        bks = [self.BK(2 * pst), self.BK(2 * pst + 1)]
                        P.add("act", lambda e, ps=ps, a=a, ch=ch: e.activation(out=a[:, 0:1024], in_=ps[:, 0:1024], func=AF.Identity,
                                                                                scale=cwb[:, ch, 2:3], bias=cwb[:, ch, 3:4]),
                              reads=bks + ["cwb"], writes=[ar])
                        P.add("dve", lambda e, ps=ps, a=a, ch=ch: e.scalar_tensor_tensor(
                            out=a[:, 1:1024], in0=ps[:, 0:1023], scalar=cwb[:, ch, 1:2], in1=a[:, 1:1024], op0=ALU.mult, op1=ALU.add),
                            reads=bks + ["cwb", ar], writes=[ar])
                        P.add("dve", lambda e, ps=ps, a=a, ch=ch: e.scalar_tensor_tensor(
                            out=a[:, 2:1024], in0=ps[:, 0:1022], scalar=cwb[:, ch, 0:1], in1=a[:, 2:1024], op0=ALU.mult, op1=ALU.add),
                            reads=bks + ["cwb", ar], writes=[ar])
                        if h == 0:
                            P.add("dve", lambda e, ps=ps, ch=ch: e.tensor_scalar(out=fix[:, ch, 0:2], in0=ps[:, 1022:1024], scalar1=cwb[:, ch, 0:1],
                                                                                 scalar2=None, op0=ALU.mult), reads=bks + ["cwb"], writes=[f"fix{ch}"])
                            P.add("dve", lambda e, ps=ps, ch=ch: e.scalar_tensor_tensor(
                                out=fix[:, ch, 0:1], in0=ps[:, 1023:1024], scalar=cwb[:, ch, 1:2], in1=fix[:, ch, 0:1], op0=ALU.mult, op1=ALU.add),
                                reads=bks + ["cwb", f"fix{ch}"], writes=[f"fix{ch}"])
                        else:
                            P.add("dve", lambda e, a=a, ch=ch: e.tensor_tensor(out=a[:, 0:2], in0=a[:, 0:2], in1=fix[:, ch, 0:2], op=ALU.add),
                                  reads=[ar, f"fix{ch}"], writes=[ar])
                    au, ag = tmp[pset]
                    P.add("act", lambda e, ag=ag: e.activation(out=ag[:, 0:1024], in_=ag[:, 0:1024], func=AF.Silu),
                          reads=[f"S:a{pset}1"], writes=[f"S:a{pset}1"])
                    P.add("pool", lambda e, au=au, ag=ag, j=j: e.tensor_tensor(out=hid[:, j, :], in0=au[:, 0:1024], in1=ag[:, 0:1024], op=ALU.mult),
                          reads=[f"S:a{pset}0", f"S:a{pset}1"], writes=[f"S:hid{j}"])
                    if deferred and j >= 1:
                        fn_, gtb_, tb_ = deferred.pop(0)
                        fn_(gtb_, tb_)
            r0 = self.ring_cnt % 3
            if h == 0:
                pre_up = self.wload(wup[l, 0], slot=(r0 + 2) % 3)
            else:
                self.ring_cnt += 2
            nd = 0
            for dh in range(2):
                for g in range(3):
                    tile, res = self.wload(wdn[l, dh, g], slot=(r0 + nd % 2) % 3)
                    nd += 1
                    sv = tile[:].rearrange("p (j c) -> p j c", j=8)
                    for jj in range(8 if g < 2 else 6):
                        j = 8 * g + jj
                        for tb in range(8):
                            P.add("pe", lambda e, tb=tb, j=j, jj=jj, sv=sv: e.matmul(
                                self.bank(tb), lhsT=hid[:, j, tb * 128:(tb + 1) * 128], rhs=sv[:, jj, :], start=(j == 0), stop=(j == NJ - 1)),
                                reads=[res, f"S:hid{j}"], writes=[self.BK(tb)])
                for tb in range(8):
                    gtb = 8 * h + tb
                    P.add("dve", lambda e, tb=tb, gtb=gtb, dh=dh: e.scalar_tensor_tensor(
                        out=x_tok[:, gtb, dh * 512:(dh + 1) * 512], in0=x_tok[:, gtb, dh * 512:(dh + 1) * 512], scalar=ALPHA,
                        in1=self.bank(tb), op0=ALU.mult, op1=ALU.add), reads=[self.BK(tb), f"xt{gtb}_{dh}"], writes=[f"xt{gtb}_{dh}"])
            def ln_tail(gtb, tb):
                self.ln_tb(gtb)
                if not final:
                    self.transpose_tb(gtb, tb // 2, affine=True)
                self.ln_affine(gtb)
                if final:
                    self.store_tb(gtb)
            for tb in range(8):
                if h == 0:
                    deferred.append((ln_tail, 8 * h + tb, tb))
                else:
                    ln_tail(8 * h + tb, tb)


    def mix1(self):
        P, xTp, x_tok, dram = self.P, self.xTp, self.x_tok, self.dram
        ones_f, ident_b, maskneg, small = self.ones_f, self.ident_b, self.maskneg, self.small
        xall = [f"xT{tb}" for tb in range(NTB)]
        self.reset_scratch()
        fl = self.carve(2048)
        cum = self.carve(2048)
        ones = self.carve(512)
        QG = self.carve(6 * 2048 // 2, BF16).rearrange("p (r t) -> p r t", r=6)
        KG = self.carve(6 * 2048 // 2, BF16).rearrange("p (r t) -> p r t", r=6)
        bsrc = dram["fox_bf"]
        P.add("sp", lambda e: e.dma_start(out=small[0:16, 0:1], in_=bsrc), writes=["small"], dma_key="small")
        P.add("dve", lambda e: e.tensor_scalar(out=small[0:16, 1:2], in0=small[0:16, 0:1], scalar1=-1.0, scalar2=None, op0=ALU.mult),
              reads=["small"], writes=["small"])
        P.add("pool", lambda e: e.memset(ones[0:16, :], 1.0), writes=["S:ones"])
        P.add("pool", lambda e: e.memset(QG[0:16, 3:6, :], 1.0), writes=["S:QG1"])
        P.add("pool", lambda e: e.memset(KG[0:16, 0:3, :], 1.0), writes=["S:KG1"])
        tile, res = self.wload(dram["fox_f"], words=128)
        fv = tile[:, 0:128].rearrange("p (k c) -> p k c", k=8)
        for t in range(4):
            P.begin_group()
            for kc in range(8):
                P.add("pe", lambda e, t=t, kc=kc: e.matmul(self.bank(t)[0:16, :], lhsT=fv[:, kc, :], rhs=xTp[:, kc, PAD + t * 512:PAD + (t + 1) * 512],
                                                           start=(kc == 0), stop=(kc == 7)), reads=[res] + xall, writes=[self.BK(t)])
            P.end_group()
            P.add("act", lambda e, t=t: e.activation(out=fl[0:16, t * 512:(t + 1) * 512], in_=self.bank(t)[0:16, :], func=AF.Exp,
                                                     scale=-1.0, bias=small[0:16, 1:2]), reads=[self.BK(t), "small"], writes=[f"S:fl{t}"])
        for t in range(4):
            P.add("act", lambda e, t=t: e.activation(out=fl[0:16, t * 512:(t + 1) * 512], in_=fl[0:16, t * 512:(t + 1) * 512], func=AF.Ln,
                                                     scale=1.0, bias=1.0), reads=[f"S:fl{t}"], writes=[f"S:fl{t}"])
        for t in range(4):
            init = 0.0 if t == 0 else cum[0:16, t * 512 - 1:t * 512]
            P.add("dve", lambda e, t=t, init=init: e.tensor_tensor_scan(out=cum[0:16, t * 512:(t + 1) * 512], data0=ones[0:16, :],
                                                                        data1=fl[0:16, t * 512:(t + 1) * 512], initial=init,
                                                                        op0=ALU.mult, op1=ALU.subtract),
                  reads=[f"S:fl{t}", "S:ones", "S:cum"], writes=["S:cum"])
        for r in range(3):
            P.add("dve", lambda e, r=r: e.tensor_copy(out=QG[0:16, r, :], in_=cum[0:16, :]), reads=["S:cum"], writes=[f"S:QG0{r}"])
            if r < 2:
                P.add("dve", lambda e, r=r: e.tensor_tensor(out=cum[0:16, :], in0=cum[0:16, :], in1=QG[0:16, r, :], op=ALU.subtract),
                      reads=["S:cum", f"S:QG0{r}"], writes=["S:cum"])
        P.add("dve", lambda e: e.tensor_scalar(out=KG[0:16, 3:6, :], in0=QG[0:16, 0:3, :], scalar1=-1.0, scalar2=None, op0=ALU.mult),
              reads=["S:QG00", "S:QG01", "S:QG02"], writes=["S:KG0"])
        gq, gk = dram["augq"], dram["augk"]
        P.add("sp", lambda e: e.dma_start(out=gq, in_=QG[0:16, :, :]), reads=["S:QG00", "S:QG01", "S:QG02", "S:QG1"], writes=["augq"], dma_key="augq")
        P.add("sp", lambda e: e.dma_start(out=gk, in_=KG[0:16, :, :]), reads=["S:KG0", "S:KG1"], writes=["augk"], dma_key="augk")
        self.reset_scratch()
        AUG = [[[self.carve(1024, BF16) for qk in range(2)] for sub in range(2)] for st in range(2)]
        VP = [self.carve(1040, BF16).rearrange("p (t s d) -> p t s d", t=16, s=2) for st in range(2)]
        OTP = [self.carve(1024, BF16) for st in range(2)]
        PT = [self.carve(256, BF16) for _ in range(4)]
        PTD = [self.carve(256, BF16) for _ in range(4)]
        rc = self.carve(512)
        bcs = self.carve(512)
        self.load_ln(2)
        for st in range(2):
            P.add("pool", lambda e, st=st: e.memset(VP[st][:, :, :, 64:65], 1.0), writes=[f"S:VPone{st}"])
        for i4 in range(1, 4):
            P.add("pool", lambda e, i4=i4: e.memset(PTD[i4][:, 0:128 * i4], 0.0), writes=[f"S:PTD{i4}"])
        ptc = [0]

        def pair(hp):
            st = hp % 2
            qk_t, qk_r = self.wload(dram["fox_qk"][hp], words=2048)
            v_t, v_r = self.wload(dram["fox_v"][hp], words=1024)
            qkv = qk_t[:, 0:2048].rearrange("p (k c) -> p k c", k=8)
            vv = v_t[:, 0:1024].rearrange("p (k c) -> p k c", k=8)
            for sub in range(2):
                h = 2 * hp + sub
                P.add("sp", lambda e, st=st, sub=sub, h=h: e.dma_start(out=AUG[st][sub][0][64:70, :], in_=gq[h]),
                      reads=["augq"], writes=[f"S:AQa{st}{sub}"], dma_key=f"aq{st}{sub}")
                P.add("sp", lambda e, st=st, sub=sub, h=h: e.dma_start(out=AUG[st][sub][1][64:70, :], in_=gk[h]),
                      reads=["augk"], writes=[f"S:AKa{st}{sub}"], dma_key=f"ak{st}{sub}")
            for t in range(4):
                for qk in range(2):
                    b = self.rotbank("misc", (0, 1, 7))
                    P.begin_group()
                    for kc in range(8):
                        P.add("pe", lambda e, b=b, kc=kc, qk=qk, t=t, qkv=qkv: e.matmul(
                            self.bank(b), lhsT=qkv[:, kc, qk * 128:(qk + 1) * 128], rhs=xTp[:, kc, PAD + t * 512:PAD + (t + 1) * 512],
                            start=(kc == 0), stop=(kc == 7)), reads=[qk_r] + xall, writes=[self.BK(b)])
                    P.end_group()
                    for sub in range(2):
                        dst = AUG[st][sub][qk]
                        nm = f"S:A{'QK'[qk]}{st}{sub}t{t}"
                        if qk == 0:
                            P.add("dve", lambda e, b=b, sub=sub, dst=dst, t=t: e.tensor_scalar(
                                out=dst[0:64, t * 512:(t + 1) * 512], in0=self.bank(b)[sub * 64:(sub + 1) * 64, :], scalar1=0.125, scalar2=None,
                                op0=ALU.mult), reads=[self.BK(b)], writes=[nm])
                        else:
                            P.add("dve", lambda e, b=b, sub=sub, dst=dst, t=t: e.tensor_copy(
                                out=dst[0:64, t * 512:(t + 1) * 512], in_=self.bank(b)[sub * 64:(sub + 1) * 64, :]),
                                reads=[self.BK(b)], writes=[nm])
            for g4 in range(4):
                b = self.rotbank("misc", (0, 1, 7))
                P.begin_group()
                for ti in range(4):
                    tb = 4 * g4 + ti
                    for kc in range(8):
                        P.add("pe", lambda e, b=b, ti=ti, tb=tb, kc=kc, vv=vv: e.matmul(
                            self.bank(b)[:, ti * 128:(ti + 1) * 128], lhsT=xTp[:, kc, PAD + tb * 128:PAD + (tb + 1) * 128], rhs=vv[:, kc, :],
                            start=(kc == 0), stop=(kc == 7)), reads=[v_r, f"xT{tb}"], writes=[self.BK(b)])
                P.end_group()
                P.add("act", lambda e, b=b, g4=g4, st=st: e.activation(
                    out=VP[st][:, 4 * g4:4 * g4 + 4, :, 0:64], in_=self.bank(b).rearrange("p (t s d) -> p t s d", t=4, s=2), func=AF.Identity),
                    reads=[self.BK(b)], writes=[f"S:VP{st}g{g4}"])
            yield
            for sub in range(2):
                if sub == 1:
                    wo_t, wo_r = self.wload(dram["fox_wo"][hp], words=1024)
                QA, KA = AUG[st][sub][0], AUG[st][sub][1]
                for qt in range(4):
                    ob = self.rotbank("O", (2, 3))
                    nkb = 4 * qt + 4
                    for kb in range(nkb):
                        i = kb - 4 * qt
                        co = 128 * i if i > 0 else 0
                        sb_ = self.rotbank("S", (4, 5, 6))
                        if i >= 0:
                            pt, ptr = PTD[i], f"S:PTD{i}"
                        else:
                            pt, ptr = PT[ptc[0] % 4], f"S:PT{ptc[0] % 4}"
                            ptc[0] += 1
                        kres = [f"S:AK{st}{sub}t{kb // 4}", f"S:AKa{st}{sub}", f"S:AQ{st}{sub}t{qt}", f"S:AQa{st}{sub}"]
                        P.begin_group()
                        if i < 0:
                            P.add("pe", lambda e, sb_=sb_, kb=kb, qt=qt, QA=QA, KA=KA: e.matmul(
                                self.bank(sb_)[:, 0:512], lhsT=KA[0:70, kb * 128:(kb + 1) * 128], rhs=QA[0:70, qt * 512:(qt + 1) * 512],
                                start=True, stop=True), reads=kres, writes=[self.BK(sb_)])
                        else:
                            P.add("pe", lambda e, sb_=sb_, co=co, kb=kb, qt=qt, QA=QA, KA=KA: e.matmul(
                                self.bank(sb_)[:, co:co + 128], lhsT=KA[0:70, kb * 128:(kb + 1) * 128], rhs=QA[0:70, qt * 512 + co:qt * 512 + co + 128],
                                start=True, stop=False), reads=kres, writes=[self.BK(sb_)])
                            P.add("pe", lambda e, sb_=sb_, co=co: e.matmul(self.bank(sb_)[:, co:co + 128], lhsT=ident_b[:], rhs=maskneg[:],
                                                                           start=False, stop=True),
                                  reads=["ident_b", "maskneg"], writes=[self.BK(sb_)])
                            if co + 128 < 512:
                                P.add("pe", lambda e, sb_=sb_, co=co, kb=kb, qt=qt, QA=QA, KA=KA: e.matmul(
                                    self.bank(sb_)[:, co + 128:512], lhsT=KA[0:70, kb * 128:(kb + 1) * 128], rhs=QA[0:70, qt * 512 + co + 128:(qt + 1) * 512],
                                    start=True, stop=True), reads=kres, writes=[self.BK(sb_)])
                        P.end_group()
                        P.add("act", lambda e, sb_=sb_, co=co, pt=pt: e.activation(out=pt[:, co:512], in_=self.bank(sb_)[:, co:512], func=AF.Exp),
                              reads=[self.BK(sb_)], writes=[ptr])
                        P.add("pe", lambda e, ob=ob, kb=kb, pt=pt, st=st, sub=sub, nkb=nkb: e.matmul(
                            self.bank(ob)[0:65, 0:512], lhsT=VP[st][:, kb, sub, 0:65], rhs=pt[:, 0:512], start=(kb == 0), stop=(kb == nkb - 1)),
                            reads=[ptr, f"S:VP{st}g{kb // 4}", f"S:VPone{st}"], writes=[self.BK(ob)])
                    P.add("dve", lambda e, ob=ob: e.reciprocal(out=rc[64:65, :], in_=self.bank(ob)[64:65, :]), reads=[self.BK(ob)], writes=["S:rc"])
                    bb = self.rotbank("misc", (0, 1, 7))
                    P.add("pe", lambda e, bb=bb: e.matmul(self.bank(bb)[0:64, :], lhsT=ones_f[64:65, 0:64], rhs=rc[64:65, :], start=True, stop=True),
                          reads=["ones_f", "S:rc"], writes=[self.BK(bb)])
                    P.add("dve", lambda e, bb=bb: e.tensor_copy(out=bcs[0:64, :], in_=self.bank(bb)[0:64, :]), reads=[self.BK(bb)], writes=["S:bcs"])
                    P.add("dve", lambda e, ob=ob, st=st, sub=sub, qt=qt: e.tensor_tensor(
                        out=OTP[st][sub * 64:(sub + 1) * 64, qt * 512:(qt + 1) * 512], in0=self.bank(ob)[0:64, :], in1=bcs[0:64, :], op=ALU.mult),
                        reads=[self.BK(ob), "S:bcs"], writes=[f"S:OT{st}q{qt}"])
                yield
            for tb in range(NTB):
                for dh in range(2):
                    b = self.rotbank("misc", (0, 1, 7))
                    P.add("pe", lambda e, b=b, tb=tb, dh=dh, st=st, wo_t=wo_t: e.matmul(
                        self.bank(b), lhsT=OTP[st][:, tb * 128:(tb + 1) * 128], rhs=wo_t[:, dh * 512:(dh + 1) * 512], start=True, stop=True),
                        reads=[wo_r, f"S:OT{st}q{tb // 4}"], writes=[self.BK(b)])
                    xr = f"xt{tb}_{dh}"
                    if hp == 0:
                        P.add("dve", lambda e, b=b, tb=tb, dh=dh: e.scalar_tensor_tensor(
                            out=x_tok[:, tb, dh * 512:(dh + 1) * 512], in0=x_tok[:, tb, dh * 512:(dh + 1) * 512], scalar=ALPHA,
                            in1=self.bank(b), op0=ALU.mult, op1=ALU.add), reads=[self.BK(b), xr], writes=[xr])
                    else:
                        P.add("dve", lambda e, b=b, tb=tb, dh=dh: e.tensor_tensor(
                            out=x_tok[:, tb, dh * 512:(dh + 1) * 512], in0=self.bank(b), in1=x_tok[:, tb, dh * 512:(dh + 1) * 512], op=ALU.add),
                            reads=[self.BK(b), xr], writes=[xr])
            yield
        gp = [pair(hp) for hp in range(8)]
        next(gp[0])
        for hp in range(8):
            next(gp[hp])
            if hp + 1 < 8:
                next(gp[hp + 1])
            next(gp[hp])
            next(gp[hp])
        for tb in range(NTB):
            self.ln_tb(tb)
            self.transpose_tb(tb, tb % 4, affine=True)
            self.ln_affine(tb)


    def mix0(self):
        P, xTp, x_tok, dram = self.P, self.xTp, self.x_tok, self.dram
        ones_f, ident_f, ident_b, maskneg = self.ones_f, self.ident_f, self.ident_b, self.maskneg
        self.reset_scratch()
        H = self.carve(1024)
        prevbf = self.carve(512, BF16)
        R = self.carve(2048)
        L = [self.carve(128), self.carve(128)]
        halo = self.carve(16).rearrange("p (i k) -> p i k", i=8)
        fixx = self.carve(36).rearrange("p (i k) -> p i k", i=12)
        cst = self.carve(128)
        biasbc, Dbc = cst[:, 0:16], cst[:, 16:32]
        gT = cst[:, 32:40]
        scw = cst[:, 40:64].rearrange("p (i k) -> p i k", i=8)
        xcw = cst[:, 64:124].rearrange("p (i k) -> p i k", i=12)
        s_tok, s_feat, s_head = dram["m0_tokc"].rearrange("a h -> (a h)").partition_broadcast(128), dram["m0_featc"], dram["m0_headc"]
        P.add("sp", lambda e: e.dma_start(out=cst[:, 0:32], in_=s_tok), writes=["S:cst"], dma_key="m0c0")
        P.add("sp", lambda e: e.dma_start(out=cst[:, 32:124], in_=s_feat), writes=["S:cst"], dma_key="m0c1")
        P.add("sp", lambda e: e.dma_start(out=cst[0:16, 124:126], in_=s_head), writes=["S:cst"], dma_key="m0c2")
        P.add("act", lambda e: e.activation(out=cst[0:16, 126:127], in_=cst[0:16, 125:126], func=AF.Exp), reads=["S:cst"], writes=["S:cst"])
        P.add("dve", lambda e: e.tensor_scalar(out=cst[0:16, 127:128], in0=cst[0:16, 126:127], scalar1=-1.0, scalar2=None, op0=ALU.mult),
              reads=["S:cst"], writes=["S:cst"])
        dtb, acol = cst[0:16, 124:125], cst[0:16, 127:128]
        P.add("pool", lambda e: e.memset(H[:, :], 0.0), writes=["S:H"])
        P.add("pool", lambda e: e.memset(prevbf[:, :], 0.0), writes=["S:prev"])
        P.add("pool", lambda e: e.memset(R[0:48, :], 0.0), writes=["S:R"])
        P.add("pool", lambda e: e.memset(R[32:48, :], 1.0), reads=["S:R"], writes=["S:R"])
        P.add("pool", lambda e: e.affine_select(out=R[32:48, :].rearrange("p (h t) -> p h t", h=16), in_=R[32:48, :].rearrange("p (h t) -> p h t", h=16),
                                                 pattern=[[-1, 16], [0, 128]], compare_op=ALU.is_equal, fill=0.0, base=0, channel_multiplier=1),
              reads=["S:R"], writes=["S:R"])
        for k in range(2):
            P.add("pool", lambda e, k=k: e.memset(L[k][0:48, :], 0.0), writes=[f"S:L{k}"])
            P.add("pool", lambda e, k=k: e.memset(L[k][0:16, :], 1.0), reads=[f"S:L{k}"], writes=[f"S:L{k}"])
        base = self.sp_
        self.load_lnT(0)
        wx, wz, wsc, wo = dram["m0_wx"], dram["m0_wz"], dram["m0_wsc"], dram["m0_wo"]

        for qi in range(4):
            self.P.fence()
            self.sp_ = base
            yaT = self.carve(2048, BF16).rearrange("p (i t) -> p i t", i=8)
            ybT = self.carve(2048, BF16).rearrange("p (i t) -> p i t", i=8)
            xdt = self.carve(2048, BF16).rearrange("p (b f) -> p b f", b=4)
            zs = self.carve(2048, BF16).rearrange("p (b f) -> p b f", b=4)
            BcT = self.carve(512, BF16).rearrange("p (g t) -> p g t", g=2)
            CcT = self.carve(512, BF16).rearrange("p (g t) -> p g t", g=2)
            Btok = self.carve(512, BF16).rearrange("p (b f) -> p b f", b=4)
            dtok = self.carve(64).rearrange("p (b h) -> p b h", b=4)
            Ddt = self.carve(64).rearrange("p (b h) -> p b h", b=4)
            acsT = self.carve(512)
            dtT = self.carve(512)
            qbase = self.sp_
            a_ = [self.carve(512), self.carve(512)]
            prod = [self.carve(516), self.carve(516)]
            cs = [self.carve(512), self.carve(512)]
            c0 = PAD + 512 * qi
            xres = [f"xT{tb}" for tb in range(4 * qi, 4 * qi + 4)]
            win = lambda kc, c0=c0: xTp[:, kc, c0:c0 + 512]

            dt_t, dt_r = self.wload(dram["m0_wdt"], words=128)
            dv = dt_t[:, 0:128].rearrange("p (k c) -> p k c", k=8)
            P.begin_group()
            for kc in range(8):
                P.add("pe", lambda e, kc=kc, dv=dv, win=win: e.matmul(self.bank(0)[0:16, :], lhsT=dv[:, kc, :], rhs=win(kc), start=(kc == 0), stop=(kc == 7)),
                      reads=[dt_r] + xres, writes=[self.BK(0)])
            P.end_group()
            P.begin_group()
            for tbl in range(4):
                for kc in range(8):
                    P.add("pe", lambda e, kc=kc, tbl=tbl, dv=dv, c0=c0: e.matmul(self.bank(1)[:, tbl * 16:(tbl + 1) * 16],
                                                                               lhsT=xTp[:, kc, c0 + tbl * 128:c0 + (tbl + 1) * 128], rhs=dv[:, kc, :],
                                                                               start=(kc == 0), stop=(kc == 7)),
                          reads=[dt_r] + xres, writes=[self.BK(1)])
            P.end_group()
            P.add("act", lambda e, dtT=dtT: e.activation(out=dtT[0:16, :], in_=self.bank(0)[0:16, :], func=AF.Exp, bias=dtb, scale=1.0),
                  reads=[self.BK(0), "S:cst"], writes=["S:dtT"])
            P.add("dve", lambda e, dtok=dtok: e.tensor_tensor(out=dtok[:, :, :], in0=self.bank(1)[:, 0:64].rearrange("p (b h) -> p b h", b=4),
                                                             in1=biasbc.unsqueeze(1).to_broadcast([128, 4, 16]), op=ALU.add),
                  reads=[self.BK(1), "S:cst"], writes=["S:dtok"])
            P.add("act", lambda e, dtok=dtok: e.activation(out=dtok[:, :, :], in_=dtok[:, :, :], func=AF.Exp), reads=["S:dtok"], writes=["S:dtok"])
            P.add("act", lambda e, dtT=dtT: e.activation(out=dtT[0:16, :], in_=dtT[0:16, :], func=AF.Ln, bias=1.0, scale=1.0), reads=["S:dtT"], writes=["S:dtT"])
            P.add("act", lambda e, dtok=dtok: e.activation(out=dtok[:, :, :], in_=dtok[:, :, :], func=AF.Ln, bias=1.0, scale=1.0),
                  reads=["S:dtok"], writes=["S:dtok"])
            P.add("dve", lambda e, dtok=dtok, Ddt=Ddt: e.reciprocal(out=Ddt[:, :, :], in_=dtok[:, :, :]), reads=["S:dtok"], writes=["S:Ddt"])
            P.add("dve", lambda e, Ddt=Ddt: e.tensor_tensor(out=Ddt[:, :, :], in0=Ddt[:, :, :], in1=Dbc.unsqueeze(1).to_broadcast([128, 4, 16]), op=ALU.mult),
                  reads=["S:Ddt", "S:cst"], writes=["S:Ddt"])
            P.add("dve", lambda e, dtT=dtT: e.tensor_scalar(out=dtT[0:16, :], in0=dtT[0:16, :], scalar1=acol, scalar2=None, op0=ALU.mult),
                  reads=["S:dtT", "S:cst"], writes=["S:dtT"])
            for c in range(4):
                P.add("dve", lambda e, c=c, dtT=dtT, acsT=acsT: e.tensor_tensor_scan(out=acsT[0:16, c * 128:(c + 1) * 128], data0=ones_f[0:16, 0:128],
                                                                                   data1=dtT[0:16, c * 128:(c + 1) * 128], initial=0.0,
                                                                                   op0=ALU.mult, op1=ALU.add),
                      reads=["S:dtT", "ones_f"], writes=[f"S:acs{c}"])

            def xchunk(sl, cc, ak, xv, x_r):
                if True:
                    if sl == 0:
                        ci = 8 + cc
                    else:
                        ci = 4 * (sl - 1) + cc
                    b = self.rotbank("m0", (0, 1, 2, 3, 4, 5))
                    P.begin_group()
                    for kc in range(8):
                        P.add("pe", lambda e, b=b, kc=kc, cc=cc, xv=xv, win=win: e.matmul(self.bank(b), lhsT=xv[:, kc, cc * 128:(cc + 1) * 128], rhs=win(kc),
                                                                                         start=(kc == 0), stop=(kc == 7)),
                              reads=[x_r] + xres, writes=[self.BK(b)])
                    P.end_group()
                    a = a_[ak % 2]
                    ar = f"S:a{ak % 2}"
                    bk = [self.BK(b)]
                    P.add("act", lambda e, b=b, a=a, ci=ci: e.activation(out=a[:, 0:512], in_=self.bank(b), func=AF.Identity,
                                                                        scale=xcw[:, ci, 3:4], bias=xcw[:, ci, 4:5]), reads=bk + ["S:cst"], writes=[ar])
                    for sh in range(1, 4):
                        P.add("dve", lambda e, b=b, a=a, ci=ci, sh=sh: e.scalar_tensor_tensor(
                            out=a[:, sh:512], in0=self.bank(b)[:, 0:512 - sh], scalar=xcw[:, ci, 3 - sh:4 - sh], in1=a[:, sh:512], op0=ALU.mult, op1=ALU.add),
                            reads=bk + ["S:cst", ar], writes=[ar])
                    if qi > 0:
                        P.add("dve", lambda e, a=a, ci=ci: e.tensor_tensor(out=a[:, 0:3], in0=a[:, 0:3], in1=fixx[:, ci, 0:3], op=ALU.add),
                              reads=[ar, f"S:fx{ci}"], writes=[ar])
                    if qi < 3:
                        P.add("dve", lambda e, b=b, ci=ci: e.tensor_scalar(out=fixx[:, ci, 0:3], in0=self.bank(b)[:, 509:512], scalar1=xcw[:, ci, 0:1],
                                                                          scalar2=None, op0=ALU.mult), reads=bk + ["S:cst"], writes=[f"S:fx{ci}"])
                        P.add("dve", lambda e, b=b, ci=ci: e.scalar_tensor_tensor(out=fixx[:, ci, 0:2], in0=self.bank(b)[:, 510:512], scalar=xcw[:, ci, 1:2],
                                                                                 in1=fixx[:, ci, 0:2], op0=ALU.mult, op1=ALU.add),
                              reads=bk + ["S:cst", f"S:fx{ci}"], writes=[f"S:fx{ci}"])
                        P.add("dve", lambda e, b=b, ci=ci: e.scalar_tensor_tensor(out=fixx[:, ci, 0:1], in0=self.bank(b)[:, 511:512], scalar=xcw[:, ci, 2:3],
                                                                                 in1=fixx[:, ci, 0:1], op0=ALU.mult, op1=ALU.add),
                              reads=bk + ["S:cst", f"S:fx{ci}"], writes=[f"S:fx{ci}"])
                    if ci >= 10:
                        g = ci - 10
                        yield
                        P.add("act", lambda e, a=a, g=g, CcT=CcT: e.activation(out=CcT[:, g, :], in_=a[:, 0:512], func=AF.Silu), reads=[ar], writes=[f"S:Cc{g}"])
                        yield
                        yield
                        return
                    yield
                    P.add("act", lambda e, a=a: e.activation(out=a[:, 0:512], in_=a[:, 0:512], func=AF.Silu), reads=[ar], writes=[ar])
                    yield
                    tbk = self.rotbank("m0t", (6, 7))
                    P.begin_group()
                    for tbl in range(4):
                        P.add("pe", lambda e, tbk=tbk, tbl=tbl, a=a: e.transpose(self.bank(tbk)[:, tbl * 128:(tbl + 1) * 128], a[:, tbl * 128:(tbl + 1) * 128], ident_f[:]),
                              reads=[ar, "ident_f"], writes=[self.BK(tbk)])
                    P.end_group()
                    if ci >= 8:
                        g = ci - 8
                        P.add("pool", lambda e, a=a, g=g, BcT=BcT: e.tensor_copy(out=BcT[:, g, :], in_=a[:, 0:512]), reads=[ar], writes=[f"S:Bc{g}"])
                        P.add("act", lambda e, tbk=tbk, g=g, Btok=Btok: e.activation(out=Btok[:, :, g * 128:(g + 1) * 128],
                                                                                    in_=self.bank(tbk).rearrange("p (b f) -> p b f", b=4), func=AF.Identity),
                              reads=[self.BK(tbk)], writes=[f"S:Bt{g}"])
                    else:
                        P.add("dve", lambda e, tbk=tbk, ci=ci, xdt=xdt, dtok=dtok: e.tensor_tensor(
                            out=xdt[:, :, ci * 128:(ci + 1) * 128].rearrange("p b (h d) -> p b h d", h=2),
                            in0=self.bank(tbk).rearrange("p (b h d) -> p b h d", b=4, h=2),
                            in1=dtok[:, :, 2 * ci:2 * ci + 2].unsqueeze(3).to_broadcast([128, 4, 2, 64]), op=ALU.mult),
                            reads=[self.BK(tbk), "S:dtok"], writes=[f"S:xdt{ci}"])
                    yield
            xg = []
            for sl in range(3):
                x_t, x_r = self.wload(wx[sl])
                xv = x_t[:].rearrange("p (k c) -> p k c", k=8)
                for cc in range(4):
                    k = len(xg)
                    xg.append(xchunk(sl, cc, k, xv, x_r))
                    next(xg[k])
                    if k >= 1:
                        next(xg[k - 1])
                    next(xg[k])
            next(xg[-1])

            for zsl in range(2):
                z_t, z_r = self.wload(wz[zsl])
                zv = z_t[:].rearrange("p (k c) -> p k c", k=8)
                for tbl in range(4):
                    b = self.rotbank("m0", (0, 1, 2, 3, 4, 5))
                    P.begin_group()
                    for kc in range(8):
                        P.add("pe", lambda e, b=b, kc=kc, tbl=tbl, zv=zv, c0=c0: e.matmul(self.bank(b), lhsT=xTp[:, kc, c0 + tbl * 128:c0 + (tbl + 1) * 128],
                                                                                         rhs=zv[:, kc, :], start=(kc == 0), stop=(kc == 7)),
                              reads=[z_r] + xres, writes=[self.BK(b)])
                    P.end_group()
                    P.add("act", lambda e, b=b, tbl=tbl, zsl=zsl, zs=zs: e.activation(out=zs[:, tbl, zsl * 512:(zsl + 1) * 512], in_=self.bank(b), func=AF.Silu),
                          reads=[self.BK(b)], writes=[f"S:zs{tbl}"])

            for i in range(8):
                s_t, s_r = self.wload(wsc[i], words=3072)
                sv = s_t[:, 0:3072].rearrange("p (k c) -> p k c", k=8)
                bks = []
                for part in range(3):
                    b = self.rotbank("m0", (0, 1, 2, 3, 4, 5))
                    bks.append(b)
                    P.begin_group()
                    for kc in range(8):
                        P.add("pe", lambda e, b=b, kc=kc, part=part, sv=sv, win=win: e.matmul(self.bank(b), lhsT=sv[:, kc, part * 128:(part + 1) * 128], rhs=win(kc),
                                                                                             start=(kc == 0), stop=(kc == 7)),
                              reads=[s_r] + xres, writes=[self.BK(b)])
                    P.end_group()
                bc_, bh_, bb_ = bks
                k2 = i % 2
                pr, csb, a = prod[k2], cs[k2], a_[k2]
                prr, csr, ar = f"S:pr{k2}", f"S:cs{k2}", f"S:a{k2}"
                P.add("act", lambda e, bc_=bc_, csb=csb: e.activation(out=csb[:, 0:512], in_=self.bank(bc_), func=AF.Identity), reads=[self.BK(bc_)], writes=[csr])
                if qi == 0:
                    P.add("pool", lambda e, pr=pr: e.memset(pr[:, 0:2], 0.0), writes=[prr + "h"])
                else:
                    P.add("pool", lambda e, pr=pr, i=i: e.tensor_copy(out=pr[:, 0:2], in_=halo[:, i, :]), reads=[f"S:halo{i}"], writes=[prr + "h"])
                P.add("dve", lambda e, bh_=bh_, pr=pr, csb=csb: e.tensor_tensor(out=pr[:, 2:514], in0=self.bank(bh_), in1=csb[:, 0:512], op=ALU.mult),
                      reads=[self.BK(bh_), csr], writes=[prr])
                if qi < 3:
                    P.add("pool", lambda e, pr=pr, i=i: e.tensor_copy(out=halo[:, i, :], in_=pr[:, 512:514]), reads=[prr], writes=[f"S:halo{i}"])
                P.add("act", lambda e, pr=pr, a=a, i=i: e.activation(out=a[:, 0:512], in_=pr[:, 2:514], func=AF.Identity, scale=scw[:, i, 2:3]),
                      reads=[prr, "S:cst"], writes=[ar])
                P.add("dve", lambda e, pr=pr, a=a, i=i: e.scalar_tensor_tensor(out=a[:, 0:512], in0=pr[:, 1:513], scalar=scw[:, i, 1:2], in1=a[:, 0:512],
                                                                              op0=ALU.mult, op1=ALU.add), reads=[prr, prr + "h", "S:cst", ar], writes=[ar])
                P.add("dve", lambda e, pr=pr, a=a, i=i: e.scalar_tensor_tensor(out=a[:, 0:512], in0=pr[:, 0:512], scalar=scw[:, i, 0:1], in1=a[:, 0:512],
                                                                              op0=ALU.mult, op1=ALU.add), reads=[prr, prr + "h", "S:cst", ar], writes=[ar])
                P.add("dve", lambda e, bb_=bb_, a=a, i=i, yaT=yaT: e.tensor_tensor(out=yaT[:, i, :], in0=self.bank(bb_), in1=a[:, 0:512], op=ALU.mult),
                      reads=[self.BK(bb_), ar], writes=[f"S:ya{i}"])

            self.P.fence()
            self.sp_ = qbase
            segT = self.carve(1024, BF16).rearrange("p (h t) -> p h t", h=16)
            MT = self.carve(1024, BF16).rearrange("p (h t) -> p h t", h=16)
            xdtd = self.carve(512, BF16)
            yt_ = [self.carve(1024), self.carve(1024)]
            junk = self.carve(256, BF16)
            sm_ = [self.carve(64), self.carve(64)]
            X16 = self.carve(16)
            def chunk(c):
                gc = 4 * qi + c
                cols = slice(c * 128, (c + 1) * 128)
                Lm, Lr = L[gc % 2], f"S:L{gc % 2}"
                yt, ytr = yt_[gc % 2], f"S:yt{gc % 2}"
                sm, smr = sm_[gc % 2], f"S:sm{gc % 2}"
                acr = f"S:acs{c}"
                P.add("dve", lambda e, Lm=Lm, cols=cols, acsT=acsT: e.tensor_scalar(out=Lm[32:48, :], in0=acsT[0:16, cols], scalar1=-1.0, scalar2=None, op0=ALU.mult),
                      reads=[acr], writes=[Lr])
                P.add("dve", lambda e, cols=cols, acsT=acsT: e.tensor_tensor(
                    out=R[0:16, :].rearrange("p (h t) -> p h t", h=16), in0=acsT[0:16, cols].unsqueeze(1).to_broadcast([16, 16, 128]),
                    in1=ident_f[0:16, 0:16].unsqueeze(2).to_broadcast([16, 16, 128]), op=ALU.mult), reads=[acr, "ident_f"], writes=["S:R"])
                for hg in range(4):
                    b = hg % 2
                    P.begin_group()
                    for hh in range(4):
                        P.add("pe", lambda e, b=b, hg=hg, hh=hh, Lm=Lm: e.matmul(self.bank(b)[:, hh * 128:(hh + 1) * 128], lhsT=Lm[0:48, :],
                                                                                rhs=R[0:48, hg * 512 + hh * 128:hg * 512 + (hh + 1) * 128], start=True, stop=False),
                              reads=[Lr, "S:R"], writes=[self.BK(b)])
                        P.add("pe", lambda e, b=b, hh=hh: e.matmul(self.bank(b)[:, hh * 128:(hh + 1) * 128], lhsT=ident_b[:], rhs=maskneg[:], start=False, stop=True),
                              reads=["ident_b", "maskneg"], writes=[self.BK(b)])
                    P.end_group()
                    P.add("act", lambda e, b=b, hg=hg, segT=segT: e.activation(out=segT[:, 4 * hg:4 * hg + 4, :], in_=self.bank(b).rearrange("p (h t) -> p h t", h=4),
                                                                              func=AF.Exp), reads=[self.BK(b)], writes=[f"S:seg{hg}"])
                segr = [f"S:seg{hg}" for hg in range(4)]
                P.begin_group()
                for g in range(2):
                    P.add("pe", lambda e, g=g, cols=cols, BcT=BcT, CcT=CcT: e.matmul(self.bank(2)[:, g * 128:(g + 1) * 128], lhsT=BcT[:, g, cols], rhs=CcT[:, g, cols],
                                                                                    start=True, stop=True), reads=[f"S:Bc{g}", f"S:Cc{g}"], writes=["bk2"])
                P.end_group()
                P.add("dve", lambda e, cols=cols, acsT=acsT: e.tensor_scalar(out=X16[0:16, 0:16], in0=ident_f[0:16, 0:16],
                                                                            scalar1=acsT[0:16, cols][:, 127:128], scalar2=None, op0=ALU.mult),
                      reads=[acr, "ident_f"], writes=["S:X16"])
                P.begin_group()
                P.add("pe", lambda e: e.matmul(self.bank(3)[:, 0:16], lhsT=ones_f[0:16, :], rhs=X16[0:16, 0:16], start=True, stop=True),
                      reads=["ones_f", "S:X16"], writes=["bk3"])
                P.add("pe", lambda e, cols=cols, acsT=acsT: e.transpose(self.bank(3)[:, 16:32], acsT[0:16, cols], ident_f[0:16, 0:16]),
                      reads=[acr, "ident_f"], writes=["bk3"])
                P.end_group()
                P.add("act", lambda e, sm=sm: e.activation(out=sm[:, 0:32], in_=self.bank(3)[:, 0:32], func=AF.Exp), reads=["bk3"], writes=[smr])
                yield
                for g in range(2):
                    P.add("dve", lambda e, g=g, MT=MT, segT=segT: e.tensor_tensor(
                        out=MT[:, 8 * g:8 * g + 8, :], in0=self.bank(2)[:, g * 128:(g + 1) * 128].unsqueeze(1).to_broadcast([128, 8, 128]),
                        in1=segT[:, 8 * g:8 * g + 8, :], op=ALU.mult), reads=["bk2"] + segr, writes=[f"S:MT{g}"])
                P.add("dve", lambda e, c=c, xdt=xdt, xdtd=xdtd, segT=segT: e.tensor_tensor(
                    out=xdtd[:, :].rearrange("p (h d) -> p h d", h=16), in0=xdt[:, c, :].rearrange("p (h d) -> p h d", h=16),
                    in1=segT[:, :, 127:128].to_broadcast([128, 16, 64]), op=ALU.mult),
                    reads=[f"S:xdt{i}" for i in range(8)] + segr, writes=["S:xdtd"])
                yield
                P.begin_group()
                for g in range(2):
                    P.add("pe", lambda e, g=g, cols=cols, CcT=CcT: e.matmul(self.PS[2][:, g * 512:(g + 1) * 512], lhsT=CcT[:, g, cols], rhs=prevbf[:, g * 512:(g + 1) * 512],
                                                                           start=True, stop=True), reads=[f"S:Cc{g}", "S:prev"], writes=[self.BK(4 + g)])
                P.end_group()
                P.begin_group()
                for g in range(2):
                    P.add("pe", lambda e, g=g, c=c, Btok=Btok, xdtd=xdtd: e.matmul(self.PS[0][:, g * 512:(g + 1) * 512], lhsT=Btok[:, c, g * 128:(g + 1) * 128],
                                                                                  rhs=xdtd[:, g * 512:(g + 1) * 512], start=True, stop=True),
                          reads=[f"S:Bt{g}", "S:xdtd"], writes=[self.BK(g)])
                P.end_group()
                P.add("dve", lambda e, sm=sm: e.tensor_tensor(out=H[:, :].rearrange("p (h d) -> p h d", h=16), in0=H[:, :].rearrange("p (h d) -> p h d", h=16),
                                                              in1=sm[:, 0:16].unsqueeze(2).to_broadcast([128, 16, 64]), op=ALU.mult),
                      reads=["S:H", smr], writes=["S:H"])
                P.add("dve", lambda e: e.tensor_tensor(out=H[:, :], in0=self.PS[0][:, :], in1=H[:, :], op=ALU.add), reads=["S:H", self.BK(0), self.BK(1)], writes=["S:H"])
                P.add("act", lambda e: e.activation(out=prevbf[:, :], in_=H[:, :], func=AF.Identity), reads=["S:H"], writes=["S:prev"])
                P.begin_group()
                for h in range(16):
                    P.add("pe", lambda e, h=h, c=c, MT=MT, xdt=xdt: e.matmul(self.PS[3][:, h * 64:(h + 1) * 64], lhsT=MT[:, h, :], rhs=xdt[:, c, h * 64:(h + 1) * 64],
                                                                            start=True, stop=True),
                          reads=[f"S:MT{h // 8}", f"S:xdt{h // 2}"], writes=[self.BK(6 + h // 8)])
                P.end_group()
                yield
                P.add("dve", lambda e, yt=yt, sm=sm: e.tensor_tensor(out=yt[:, :].rearrange("p (h d) -> p h d", h=16), in0=self.PS[2][:, :].rearrange("p (h d) -> p h d", h=16),
                                                                     in1=sm[:, 16:32].unsqueeze(2).to_broadcast([128, 16, 64]), op=ALU.mult),
                      reads=[self.BK(4), self.BK(5), smr], writes=[ytr])
                P.add("dve", lambda e, yt=yt: e.tensor_tensor(out=yt[:, :], in0=self.PS[3][:, :], in1=yt[:, :], op=ALU.add), reads=[self.BK(6), self.BK(7), ytr], writes=[ytr])
                for half in range(2):
                    P.add("pool", lambda e, c=c, half=half, xdt=xdt, Ddt=Ddt: e.tensor_tensor(
                        out=junk[:, 0:512].rearrange("p (h d) -> p h d", h=8), in0=xdt[:, c, half * 512:(half + 1) * 512].rearrange("p (h d) -> p h d", h=8),
                        in1=Ddt[:, c, 8 * half:8 * half + 8].unsqueeze(2).to_broadcast([128, 8, 64]), op=ALU.mult),
                        reads=[f"S:xdt{i}" for i in range(8)] + ["S:Ddt"], writes=["S:junk"])
                    P.add("pool", lambda e, yt=yt, half=half: e.tensor_tensor(out=yt[:, half * 512:(half + 1) * 512], in0=yt[:, half * 512:(half + 1) * 512],
                                                                           in1=junk[:, 0:512], op=ALU.add), reads=[ytr, "S:junk"], writes=[ytr])
                P.add("pool", lambda e, yt=yt, c=c, zs=zs: e.tensor_tensor(out=yt[:, :], in0=yt[:, :], in1=zs[:, c, :], op=ALU.mult), reads=[ytr, f"S:zs{c}"], writes=[ytr])
                for g in range(2):
                    P.add("act", lambda e, g=g, yt=yt, sm=sm: e.activation(out=junk[:, 0:512], in_=yt[:, g * 512:(g + 1) * 512], func=AF.Square,
                                                                          accum_out=sm[:, 32 + g:33 + g]), reads=[ytr], writes=[smr + "s", "S:junk"])
                P.add("pool", lambda e, sm=sm: e.tensor_scalar(out=sm[:, 34:36], in0=sm[:, 32:34], scalar1=1.0 / 512.0, scalar2=LN_EPS, op0=ALU.mult, op1=ALU.add),
                      reads=[smr + "s"], writes=[smr + "r"])
                P.add("pool", lambda e, sm=sm: e.tensor_tensor(out=sm[:, 34:36], in0=sm[:, 34:36], in1=self.neghalf[:, 0:1].to_broadcast([128, 2]), op=ALU.pow),
                      reads=[smr + "r", "neghalf"], writes=[smr + "r"])
                for g in range(2):
                    P.add("act", lambda e, g=g, yt=yt, sm=sm: e.activation(out=yt[:, g * 512:(g + 1) * 512], in_=yt[:, g * 512:(g + 1) * 512], func=AF.Identity,
                                                                          scale=sm[:, 34 + g:35 + g]), reads=[ytr, smr + "r"], writes=[ytr])
                yield
                P.begin_group()
                for i in range(8):
                    P.add("pe", lambda e, i=i, yt=yt: e.transpose(self.PS[2][:, i * 128:(i + 1) * 128], yt[:, i * 128:(i + 1) * 128], ident_f[:]),
                          reads=[ytr, "ident_f"], writes=[self.BK(4 + i // 4)])
                P.end_group()
                for half in range(2):
                    P.add("dve", lambda e, half=half, cols=cols, ybT=ybT: e.tensor_tensor(
                        out=ybT[:, 4 * half:4 * half + 4, cols], in0=self.PS[2][:, half * 512:(half + 1) * 512].rearrange("p (i t) -> p i t", i=4),
                        in1=gT[:, 4 * half:4 * half + 4].unsqueeze(2).to_broadcast([128, 4, 128]), op=ALU.mult),
                        reads=[self.BK(4 + half), "S:cst"], writes=[f"S:yb{c}"])

                yield
            gens = [chunk(c) for c in range(4)]
            order = [0, 0, 0, 1, 0, 1, 0, 1, 2, 1, 2, 1, 2, 3, 2, 3, 2, 3, 3, 3]
            for gi in order:
                next(gens[gi])
            for dh in range(2):
                for part in range(2):
                    o_t, o_r = self.wload(wo[dh, part])
                    ov = o_t[:].rearrange("p (k c) -> p k c", k=8)
                    src = yaT if part == 0 else ybT
                    for tbl in range(4):
                        P.begin_group()
                        for kc in range(8):
                            rd = [o_r, (f"S:ya{kc}" if part == 0 else f"S:yb{tbl}")]
                            P.add("pe", lambda e, dh=dh, part=part, tbl=tbl, kc=kc, ov=ov, src=src: e.matmul(
                                self.bank(4 * dh + tbl), lhsT=src[:, kc, tbl * 128:(tbl + 1) * 128], rhs=ov[:, kc, :],
                                start=(part == 0 and kc == 0), stop=(part == 1 and kc == 7)), reads=rd, writes=[self.BK(4 * dh + tbl)])
                        P.end_group()
                for tbl in range(4):
                    gtb = 4 * qi + tbl
                    P.add("dve", lambda e, dh=dh, tbl=tbl, gtb=gtb: e.scalar_tensor_tensor(
                        out=x_tok[:, gtb, dh * 512:(dh + 1) * 512], in0=x_tok[:, gtb, dh * 512:(dh + 1) * 512], scalar=ALPHA,
                        in1=self.bank(4 * dh + tbl), op0=ALU.mult, op1=ALU.add), reads=[self.BK(4 * dh + tbl), f"xt{gtb}_{dh}"], writes=[f"xt{gtb}_{dh}"])
            for tbl in range(4):
                gtb = 4 * qi + tbl
                self.ln_tb(gtb)
                self.transpose_tb(gtb, tbl, affine=True)
        self.P.fence()
        self.sp_ = base
        self.load_ln(0, with_T=False)
        for tb in range(NTB):
            self.ln_affine(tb)


def declare_dram(nc, phases):
    d = {}
    d["x"] = nc.dram_tensor("x", [T, D], F32, kind="ExternalInput").ap()
    d["out"] = nc.dram_tensor("out", [T, D], F32, kind="ExternalOutput").ap()
    d["lnp"] = nc.dram_tensor("lnp", [4, 2, D], F32, kind="ExternalInput").ap()
    d["lnpT"] = nc.dram_tensor("lnpT", [4, 128, 16], F32, kind="ExternalInput").ap()
    d["ffn_cwb"] = nc.dram_tensor("ffn_cwb", [2, 128, 44, 4], F32, kind="ExternalInput").ap()
    d["wup"] = nc.dram_tensor("wup", [2, 11, 128, 4096], F32, kind="ExternalInput").ap()
    d["wdn"] = nc.dram_tensor("wdn", [2, 2, 3, 128, 4096], F32, kind="ExternalInput").ap()
    d["m0_wdt"] = nc.dram_tensor("m0_wdt", [128, 128], F32, kind="ExternalInput").ap()
    d["m0_wx"] = nc.dram_tensor("m0_wx", [3, 128, 4096], F32, kind="ExternalInput").ap()
    d["m0_wz"] = nc.dram_tensor("m0_wz", [2, 128, 4096], F32, kind="ExternalInput").ap()
    d["m0_wsc"] = nc.dram_tensor("m0_wsc", [8, 128, 3072], F32, kind="ExternalInput").ap()
    d["m0_wo"] = nc.dram_tensor("m0_wo", [2, 2, 128, 4096], F32, kind="ExternalInput").ap()
    d["m0_tokc"] = nc.dram_tensor("m0_tokc", [2, 16], F32, kind="ExternalInput").ap()
    d["m0_featc"] = nc.dram_tensor("m0_featc", [128, 92], F32, kind="ExternalInput").ap()
    d["m0_headc"] = nc.dram_tensor("m0_headc", [16, 2], F32, kind="ExternalInput").ap()
    d["fox_f"] = nc.dram_tensor("fox_f", [128, 128], F32, kind="ExternalInput").ap()
    d["fox_bf"] = nc.dram_tensor("fox_bf", [16, 1], F32, kind="ExternalInput").ap()
    d["fox_qk"] = nc.dram_tensor("fox_qk", [8, 128, 2048], F32, kind="ExternalInput").ap()
    d["fox_v"] = nc.dram_tensor("fox_v", [8, 128, 1024], F32, kind="ExternalInput").ap()
    d["fox_wo"] = nc.dram_tensor("fox_wo", [8, 128, 1024], F32, kind="ExternalInput").ap()
    d["augq"] = nc.dram_tensor("augq", [16, 6, T], BF16, kind="Internal").ap()
    d["augk"] = nc.dram_tensor("augk", [16, 6, T], BF16, kind="Internal").ap()
    return d


def build_program(phases=("mix0", "ffn0", "mix1", "ffn1")):
    nc = bass.Bass("TRN2", target_bir_lowering=False)
    dram = declare_dram(nc, phases)
    P = Prog(nc)
    B = Builder(nc, P, dram)
    B.load_x()
    for tb in range(NTB):
        B.transpose_tb(tb, tb % 4)
    last = phases[-1]
    for ph in phases:
        if ph == "ffn0":
            B.ffn(0, final=(ph == last))
        elif ph == "ffn1":
            B.ffn(1, final=(ph == last))
        elif ph == "mix0":
            P.pin = tuple(os.environ.get("MK_PIN0", "dve,pe").split(","))
            B.mix0()
            P.pin = ("dve",)
            if ph == last:
                for tb in range(NTB):
                    B.store_tb(tb)
        elif ph == "mix1":
            B.mix1()
            if ph == last:
                for tb in range(NTB):
                    B.store_tb(tb)
        else:
            raise NotImplementedError(ph)
    if SCHEDULE:
        P.schedule()
    P.finalize(B.out_dmas)
    P.emit(B.out_dmas)
    P.close()
    return nc


def host_layouts(inp):
    f = np.float32
    o = {}
    o["lnp"] = np.ascontiguousarray(np.stack([
        np.stack([inp["ln_mix_g"][0], inp["ln_mix_b"][0]]), np.stack([inp["ln_ffn_g"][0], inp["ln_ffn_b"][0]]),
        np.stack([inp["ln_mix_g"][1], inp["ln_mix_b"][1]]), np.stack([inp["ln_ffn_g"][1], inp["ln_ffn_b"][1]])]).astype(f))
    o["lnpT"] = np.ascontiguousarray(o["lnp"].reshape(4, 2, 8, 128).transpose(0, 3, 1, 2).reshape(4, 128, 16))
    cw = inp["ffn_conv_w"].astype(f)
    cb = inp["ffn_conv_b"].astype(f)
    cwb = np.concatenate([cw.transpose(0, 2, 1), cb[:, :, None]], axis=2)
    o["ffn_cwb"] = np.ascontiguousarray(cwb.reshape(2, 44, 128, 4).transpose(0, 2, 1, 3))
    wu = inp["ffn_w_up"].astype(f)
    u = wu[:, :, :DFF].reshape(2, 8, 128, 11, 2, 128)
    g = wu[:, :, DFF:].reshape(2, 8, 128, 11, 2, 128)
    ug = np.stack([u, g], axis=5)
    o["wup"] = np.ascontiguousarray(ug.transpose(0, 3, 2, 1, 4, 5, 6).reshape(2, 11, 128, 4096))
    wd = inp["ffn_w_down"].astype(f)
    wdp = np.zeros((2, 24 * 128, 1024), f)
    wdp[:, :DFF] = wd
    wdp = wdp.reshape(2, 3, 8, 128, 2, 512)
    o["wdn"] = np.ascontiguousarray(wdp.transpose(0, 4, 1, 3, 2, 5).reshape(2, 2, 3, 128, 4096))
    w0 = inp["sc_ssm_w_in"][0].astype(f).reshape(8, 128, 5648)
    lay = lambda cols: np.ascontiguousarray(w0[:, :, cols].transpose(1, 0, 2).reshape(128, -1))
    o["m0_wdt"] = lay(slice(5632, 5648))
    o["m0_wx"] = np.stack([lay(slice(5120, 5632)), lay(slice(4096, 4608)), lay(slice(4608, 5120))])
    o["m0_wz"] = np.stack([lay(slice(3072, 3584)), lay(slice(3584, 4096))])
    o["m0_wsc"] = np.stack([lay(np.r_[1024 + 128 * i:1152 + 128 * i, 2048 + 128 * i:2176 + 128 * i, 128 * i:128 + 128 * i]) for i in range(8)])
    wo0 = inp["sc_ssm_w_out"][0].astype(f).reshape(2, 8, 128, 2, 512)
    o["m0_wo"] = np.ascontiguousarray(wo0.transpose(3, 0, 2, 1, 4).reshape(2, 2, 128, 4096))
    o["m0_tokc"] = np.ascontiguousarray(np.stack([inp["ssm_dt_bias"][0], inp["ssm_d"][0]]).astype(f))
    o["m0_headc"] = np.ascontiguousarray(np.stack([inp["ssm_dt_bias"][0], inp["ssm_a_log"][0]], axis=1).astype(f))
    gTh = inp["ssm_norm_g"][0].astype(f).reshape(8, 128).T
    scwh = inp["sc_conv_w"][0].astype(f).reshape(3, 8, 128).transpose(2, 1, 0)
    xw = inp["ssm_conv_w"][0].astype(f).reshape(4, 12, 128).transpose(2, 1, 0)
    xb = inp["ssm_conv_b"][0].astype(f).reshape(12, 128).T[:, :, None]
    o["m0_featc"] = np.ascontiguousarray(np.concatenate([gTh, scwh.reshape(128, 24), np.concatenate([xw, xb], axis=2).reshape(128, 60)], axis=1))
    wi = inp["fox_w_in"][0].astype(f)
    wk = wi.reshape(8, 128, 3088)
    o["fox_f"] = np.ascontiguousarray(wk[:, :, 3072:3088].transpose(1, 0, 2).reshape(128, 128))
    q = wk[:, :, 0:1024].reshape(8, 128, 8, 128)
    k = wk[:, :, 1024:2048].reshape(8, 128, 8, 128)
    v = wk[:, :, 2048:3072].reshape(8, 128, 8, 128)
    qk = np.concatenate([q, k], axis=3)
    o["fox_qk"] = np.ascontiguousarray(qk.transpose(2, 1, 0, 3).reshape(8, 128, 2048))
    o["fox_v"] = np.ascontiguousarray(v.transpose(2, 1, 0, 3).reshape(8, 128, 1024))
    o["fox_wo"] = np.ascontiguousarray(inp["fox_w_out"][0].astype(f).reshape(8, 128, 1024))
    o["fox_bf"] = np.ascontiguousarray(inp["fox_b_f"][0].astype(f).reshape(16, 1))
    return o


_NC_CACHE = {}


def kernel(**inputs):
    phases = ("mix0", "ffn0", "mix1", "ffn1")
    if phases not in _NC_CACHE:
        _NC_CACHE[phases] = build_program(phases)
    nc = _NC_CACHE[phases]
    lay = host_layouts(inputs)
    x = np.asarray(inputs["x"], dtype=np.float32)
    in_maps = [dict(lay, x=np.ascontiguousarray(x[b])) for b in range(8)]
    res = run_bass_kernel_spmd(nc, in_maps, core_ids=list(range(8)))
    return np.stack([np.asarray(r["out"], dtype=np.float32) for r in res.results], axis=0)
```

```python
from contextlib import ExitStack
import numpy as np
import concourse.bass as bass
import concourse.mybir as mybir
from concourse.bass_utils import run_bass_kernel_spmd

F32 = mybir.dt.float32
BF16 = mybir.dt.bfloat16
AF = mybir.ActivationFunctionType
ALU = mybir.AluOpType

COMPUTE = ("pe", "act", "dve", "pool")
QUEUES = ("pe", "act", "dve", "pool", "sp")

ALPHA = 4.0 ** 0.25
LN_EPS = 1e-5
T = 2048
D = 1024
NTB = 16
PAD = 4
DFF = 2816
NJ = 22
import os
SCHEDULE = os.environ.get('MK_SCHED', '1') == '1'
PREFETCH = os.environ.get('MK_PREFETCH', '0') == '1'


class Ins:
    __slots__ = ("eng", "fn", "deps", "idx", "dma_key", "dma_val", "signal", "sigval", "clock", "waits", "is_dma", "pinned")


class Prog:
    def __init__(self, nc):
        self.nc = nc
        self.es = ExitStack()
        self.ins = []
        self.q = {e: [] for e in QUEUES}
        self.last_w = {}
        self.readers = {}
        self.dma_cum = {}
        self.dma_sems = {}
        self.sems = {}
        self.fence_deps = []
        self.scratch_touch = {}
        self.pin = tuple(x for x in os.environ.get("MK_PINX", "").split(",") if x)

    def sbuf(self, name, shape, dtype):
        return self.es.enter_context(self.nc.sbuf_tensor(name, list(shape), dtype))

    def psum(self, name, shape, dtype=F32):
        return self.es.enter_context(self.nc.psum_tensor(name, list(shape), dtype))

    def begin_group(self):
        self._grp = []

    def end_group(self):
        g, self._grp = self._grp, None
        fns = [x[0] for x in g]
        reads, writes = [], []
        for _, r, w in g:
            for x in r:
                if x not in reads:
                    reads.append(x)
            for x in w:
                if x not in writes:
                    writes.append(x)

        def run(e, fns=fns):
            h = None
            for f in fns:
                h = f(e)
            return h
        return self.add("pe", run, reads=reads, writes=writes)

    def add(self, eng, fn, reads=(), writes=(), dma_key=None):
        if getattr(self, "_grp", None) is not None:
            assert eng == "pe" and dma_key is None
            self._grp.append((fn, list(reads), list(writes)))
            return None
        i = Ins()
        i.eng = eng
        i.fn = fn
        i.is_dma = dma_key is not None
        i.dma_key = dma_key
        i.signal = False
        i.pinned = eng in self.pin
        deps = set()
        scratch = False
        if any(r.startswith("bk") for r in reads):
            writes = list(writes) + [r for r in reads if r.startswith("bk") and r not in writes]
            reads = [r for r in reads if not r.startswith("bk")]
        for r in reads:
            w = self.last_w.get(r)
            if w is not None:
                deps.add(w)
            if r.startswith("S:"):
                scratch = True
        for w_ in writes:
            w = self.last_w.get(w_)
            if w is not None:
                deps.add(w)
            for rd in self.readers.get(w_, ()):
                deps.add(rd)
            if w_.startswith("S:"):
                scratch = True
        if scratch:
            deps.update(self.fence_deps)
        i.deps = deps
        i.idx = len(self.ins)
        self.ins.append(i)
        self.q[eng].append(i)
        for r in reads:
            self.readers.setdefault(r, []).append(i)
        for w_ in writes:
            self.last_w[w_] = i
            self.readers[w_] = []
        if i.is_dma:
            self.dma_cum[dma_key] = self.dma_cum.get(dma_key, 0) + 16
            i.dma_val = self.dma_cum[dma_key]
        if scratch:
            self.scratch_touch[i.idx] = i
        return i

    def fence(self):
        touched = list(self.scratch_touch.values())
        self.scratch_touch = {}
        if not hasattr(self, "_fdummy"):
            self._fdummy = self.sbuf("fence_dummy", [128, 8], F32)
        fd = self._fdummy
        join = self.add("dve", lambda e: e.memset(fd[:, 0:1], 0.0), writes=["fence_dummy"])
        join.deps.update(touched)
        self.fence_deps = [join]
        for k in [k for k in self.last_w if k.startswith("S:")]:
            del self.last_w[k]
        for k in [k for k in self.readers if k.startswith("S:")]:
            del self.readers[k]


    def schedule(self):
        import heapq

        class _Probe:
            def __init__(self):
                self.recs = []

            def __getattr__(self, name):
                def f(*a, **k):
                    self.recs.append((name, a, k))
                    return None
                return f

        def prod(sh):
            n = 1
            for v in sh:
                n *= int(v)
            return n

        cost, lat = {}, {}
        for i in self.ins:
            p = _Probe()
            i.fn(p)
            name, a, k = p.recs[-1]
            out = k.get("out", a[0] if a else None)
            n = prod(out.shape[1:]) if out is not None and hasattr(out, "shape") else 512
            L = 0.0
            if i.is_dma:
                by = n * out.shape[0] * 4 if out is not None else 0
                c = 0.6 if i.eng == "pool" else 0.15
                L = 2.5 + by / 150e3
            elif i.eng == "pe":
                c = 0.0
                for name, a, k in p.recs:
                    if name == "transpose":
                        c += 0.12
                    else:
                        rhs = k.get("rhs", a[2] if len(a) > 2 else None)
                        nn = prod(rhs.shape[1:]) if rhs is not None else 512
                        lhs = k.get("lhsT", a[1] if len(a) > 1 else None)
                        c1 = 0.01 + max(nn, 64) / 2400.0
                        if lhs is not None and lhs.dtype == F32:
                            c1 *= 4
                        c += c1
            elif i.eng == "act":
                c = 0.22 + n / 1400.0
            elif i.eng == "dve":
                c = 0.12 + n / 960.0
            else:
                c = 0.25 + n / 600.0
            cost[i.idx] = c
            lat[i.idx] = L
        succ = {i.idx: [] for i in self.ins}
        indeg = {}
        import os
        chain = {}
        for e in QUEUES:
            prev = None
            for i in self.q[e]:
                if prev is not None and i.pinned:
                    chain[i.idx] = prev
                prev = i
        for i in self.ins:
            ds = [d for d in i.deps if d is not i]
            if i.idx in chain and chain[i.idx] not in ds:
                ds.append(chain[i.idx])
            indeg[i.idx] = len(ds)
            for d in ds:
                succ[d.idx].append(i)
        byidx = {i.idx: i for i in self.ins}
        pending = {e: [] for e in QUEUES}
        avail = {e: [] for e in QUEUES}
        free = {e: 0.0 for e in QUEUES}
        fin = {}
        ready = {}
        for i in self.ins:
            if indeg[i.idx] == 0:
                ready[i.idx] = 0.0
                heapq.heappush(pending[i.eng], (0.0, i.idx))
        order = []
        newq = {e: [] for e in QUEUES}
        SYNC = 0.12
        n_left = len(self.ins)
        while n_left:
            best = None
            for e in QUEUES:
                pe_, av = pending[e], avail[e]
                while pe_ and pe_[0][0] <= free[e]:
                    r, ix = heapq.heappop(pe_)
                    heapq.heappush(av, ix)
                if av:
                    cand = (free[e], av[0], e, True)
                elif pe_:
                    cand = (pe_[0][0], pe_[0][1], e, False)
                else:
                    continue
                if best is None or cand[:2] < best[:2]:
                    best = cand
            st, ix, e, from_av = best
            if from_av:
                heapq.heappop(avail[e])
            else:
                heapq.heappop(pending[e])
            i = byidx[ix]
            f = st + cost[ix]
            free[e] = f
            fin[ix] = f + lat[ix]
            order.append(i)
            newq[e].append(i)
            n_left -= 1
            for sx in succ[ix]:
                indeg[sx.idx] -= 1
                r = max(ready.get(sx.idx, 0.0), fin[ix] + (0.0 if sx.eng == e and not i.is_dma else SYNC))
                ready[sx.idx] = r
                if indeg[sx.idx] == 0:
                    heapq.heappush(pending[sx.eng], (r, sx.idx))
        self.ins = order
        self.q = newq
        for k, i in enumerate(self.ins):
            i.idx = k
        self.est_us = max(fin.values()) if fin else 0.0

    def finalize(self, tail):
        nc = self.nc
        pos = {}
        for e in QUEUES:
            for k, i in enumerate(self.q[e]):
                pos[i.idx] = k
        prev_clock = {e: ({c: -1 for c in COMPUTE}, frozenset()) for e in QUEUES}
        for i in self.ins:
            clk, dseen = prev_clock[i.eng]
            clk = dict(clk)
            dseen = set(dseen)
            waits = []
            for d in sorted(i.deps, key=lambda d: -d.idx):
                if d is i:
                    continue
                if d.is_dma:
                    if d.idx in dseen:
                        continue
                    waits.append(d)
                    dseen.add(d.idx)
                else:
                    if d.eng == i.eng and d.eng == "pe":
                        continue
                    if clk[d.eng] >= pos[d.idx]:
                        continue
                    waits.append(d)
                    clk[d.eng] = max(clk[d.eng], pos[d.idx])
                dc, dd = d.clock
                for c in COMPUTE:
                    if dc[c] > clk[c]:
                        clk[c] = dc[c]
                dseen |= dd
            final = []
            for d in waits:
                if d.is_dma:
                    final.append(d)
                elif clk[d.eng] == pos[d.idx]:
                    final.append(d)
            i.waits = final
            for d in final:
                d.signal = True
            i.clock = (clk, frozenset(dseen))
            prev_clock[i.eng] = i.clock
        for d in tail:
            d.signal = True
        for e in COMPUTE:
            self.sems[e] = self.es.enter_context(nc.semaphore("s_" + e))
            n = 0
            for i in self.q[e]:
                if i.is_dma:
                    continue
                if i.signal:
                    n += 1
                    i.sigval = n
        for k in self.dma_cum:
            self.dma_sems[k] = self.es.enter_context(nc.semaphore("d_" + str(k).replace(":", "_")))

    def emit(self, tail):
        nc = self.nc
        prog = self

        def wait(eng, d):
            if d.is_dma:
                eng.wait_ge(prog.dma_sems[d.dma_key], d.dma_val)
            else:
                eng.wait_ge(prog.sems[d.eng], d.sigval)

        def run(engname, eng):
            for i in prog.q[engname]:
                for d in i.waits:
                    wait(eng, d)
                h = i.fn(eng)
                if i.is_dma:
                    h.then_inc(prog.dma_sems[i.dma_key], 16)
                elif i.signal:
                    h.then_inc(prog.sems[i.eng], 1)
            if engname == "sp":
                for d in tail:
                    wait(eng, d)

        with nc.Block() as block:
            @block.tensor
            def _(e):
                run("pe", e)

            @block.scalar
            def _(e):
                run("act", e)

            @block.vector
            def _(e):
                run("dve", e)

            @block.gpsimd
            def _(e):
                run("pool", e)

            @block.sync
            def _(e):
                run("sp", e)

    def close(self):
        self.es.close()


class Builder:
    def __init__(self, nc, P, dram):
        self.nc, self.P, self.dram = nc, P, dram
        P_ = P
        self.x_tok = P_.sbuf("x_tok_sb", [128, NTB, D], F32)
        self.xTp = P_.sbuf("xTp", [128, 8, PAD + T], BF16)
        self.ident_f = P_.sbuf("ident_f", [128, 128], F32)
        self.ones_f = P_.sbuf("ones_f", [128, 128], F32)
        self.neghalf = P_.sbuf("neghalf", [128, 1], F32)
        self.lnp = None
        self.stats = P_.sbuf("stats", [128, NTB, 16], F32)
        self.cwb = P_.sbuf("cwb", [128, 44, 4], F32)
        self.fix = P_.sbuf("fix", [128, 44, 2], F32)
        self.ring = [P_.sbuf(f"ring{i}", [128, 4096], BF16) for i in range(3)]
        self.ring_cnt = 0
        self.SW = 20480
        self.S = P_.sbuf("S", [128, self.SW], F32)
        self.sp_ = 0
        self.PS = [P_.psum(f"ps{i}", [128, 1024], F32) for i in range(4)]
        self.out_dmas = []
        self.ident_b = P_.sbuf("ident_b", [128, 128], BF16)
        self.maskneg = P_.sbuf("maskneg", [128, 128], BF16)
        self.small = P_.sbuf("small", [128, 64], F32)
        self.lnT = P_.sbuf("lnT_sb", [128, 16], F32)
        self.rot = {}
        self.consts()

    def rotbank(self, group, banks):
        k = self.rot.get(group, 0)
        self.rot[group] = k + 1
        return banks[k % len(banks)]

    def reset_scratch(self):
        self.P.fence()
        self.sp_ = 0

    def carve(self, words, dtype=F32):
        a = self.sp_
        self.sp_ += words
        assert self.sp_ <= self.SW, (self.sp_, self.SW)
        v = self.S[:, a:a + words]
        if dtype == BF16:
            v = v.bitcast(BF16)
        return v

    def bank(self, b):
        return self.PS[b // 2][:, (b % 2) * 512:(b % 2) * 512 + 512]

    @staticmethod
    def BK(b):
        return f"bk{b}"

    def ring_next(self):
        i = self.ring_cnt % 3
        self.ring_cnt += 1
        return self.ring[i], f"ring{i}"

    def wload(self, src_ap, words=4096, slot=None):
        if slot is None:
            tile, res = self.ring_next()
        else:
            tile, res = self.ring[slot], f"ring{slot}"
        self.P.add("pool", lambda e, t=tile, s=src_ap, w=words: e.dma_start(out=t[:, 0:w], in_=s, max_dma_last_dim=8192),
                   writes=[res], dma_key=res)
        return tile, res

    def consts(self):
        P = self.P
        ones_f, ident_f = self.ones_f, self.ident_f
        P.add("pool", lambda e: e.memset(ones_f[:], 1.0), writes=["ones_f"])
        P.add("pool", lambda e: e.affine_select(out=ident_f[:], in_=ones_f[:], pattern=[[-1, 128]], compare_op=ALU.is_equal,
                                                 fill=0.0, base=0, channel_multiplier=1), reads=["ones_f"], writes=["ident_f"])
        nh = self.neghalf
        P.add("pool", lambda e: e.memset(nh[:], -0.5), writes=["neghalf"])
        xTp = self.xTp
        P.add("pool", lambda e: e.memset(xTp[:, :, 0:PAD], 0.0), writes=["xTpad"])
        ident_b, maskneg = self.ident_b, self.maskneg
        P.add("pool", lambda e: e.tensor_copy(out=ident_b[:], in_=ident_f[:]), reads=["ident_f"], writes=["ident_b"])
        P.add("pool", lambda e: e.memset(maskneg[:], -30000.0), writes=["maskneg"])
        P.add("pool", lambda e: e.affine_select(out=maskneg[:], in_=maskneg[:], pattern=[[-1, 128]], compare_op=ALU.is_gt,
                                                 fill=0.0, base=0, channel_multiplier=1), reads=["maskneg"], writes=["maskneg"])

    def load_x(self):
        P, x_tok = self.P, self.x_tok
        xv = self.dram["x"].rearrange("(tb p) d -> p tb d", p=128)
        for g in range(4):
            P.add("sp", lambda e, g=g: e.dma_start(out=x_tok[:, 4 * g:4 * g + 4, :], in_=xv[:, 4 * g:4 * g + 4, :]),
                  writes=[f"xt{tb}_{dh}" for tb in range(4 * g, 4 * g + 4) for dh in range(2)], dma_key=f"xin{g}")

    def store_tb(self, tb):
        P, x_tok = self.P, self.x_tok
        ov = self.dram["out"].rearrange("(tb p) d -> p tb d", p=128)
        i = P.add("sp", lambda e: e.dma_start(out=ov[:, tb, :], in_=x_tok[:, tb, :]), reads=[f"xt{tb}_0", f"xt{tb}_1"], dma_key=f"xout{tb}")
        self.out_dmas.append(i)

    def transpose_tb(self, tb, pst, affine=False):
        P, x_tok, xTp, ident_f, lnT = self.P, self.x_tok, self.xTp, self.ident_f, self.lnT
        ps = self.PS[pst]
        P.begin_group()
        for kc in range(8):
            P.add("pe", lambda e, kc=kc: e.transpose(ps[:, kc * 128:(kc + 1) * 128], x_tok[:, tb, kc * 128:(kc + 1) * 128], ident_f[:]),
                  reads=[f"xt{tb}_{kc // 4}", "ident_f"], writes=[self.BK(2 * pst + kc // 4)])
        P.end_group()
        c0 = PAD + tb * 128
        if affine:
            for kc in range(8):
                if kc < 4:
                    P.add("act", lambda e, kc=kc: e.activation(out=xTp[:, kc, c0:c0 + 128], in_=ps[:, kc * 128:(kc + 1) * 128], func=AF.Identity,
                                                               scale=lnT[:, kc:kc + 1], bias=lnT[:, 8 + kc:9 + kc]),
                          reads=[self.BK(2 * pst), "lnT"], writes=[f"xT{tb}"])
                else:
                    P.add("dve", lambda e, kc=kc: e.tensor_scalar(out=xTp[:, kc, c0:c0 + 128], in0=ps[:, kc * 128:(kc + 1) * 128],
                                                                  scalar1=lnT[:, kc:kc + 1], scalar2=lnT[:, 8 + kc:9 + kc], op0=ALU.mult, op1=ALU.add),
                          reads=[self.BK(2 * pst + 1), "lnT"], writes=[f"xT{tb}"])
            return
        P.add("act", lambda e: e.activation(out=xTp[:, 0:4, c0:c0 + 128], in_=ps[:, 0:512].rearrange("p (k t) -> p k t", k=4), func=AF.Identity),
              reads=[self.BK(2 * pst)], writes=[f"xT{tb}"])
        P.add("dve", lambda e: e.tensor_copy(out=xTp[:, 4:8, c0:c0 + 128], in_=ps[:, 512:1024].rearrange("p (k t) -> p k t", k=4)),
              reads=[self.BK(2 * pst + 1)], writes=[f"xT{tb}"])

    def load_lnT(self, idx):
        lnT = self.lnT
        src = self.dram["lnpT"][idx]
        self.P.add("sp", lambda e: e.dma_start(out=lnT[:], in_=src), writes=["lnT"], dma_key="lnT")

    def load_ln(self, idx, with_T=True):
        if with_T:
            self.load_lnT(idx)
        self.lnp = self.carve(2 * D).rearrange("p (a d) -> p a d", a=2)
        lnp = self.lnp
        src = self.dram["lnp"][idx].partition_broadcast(128)
        self.P.add("sp", lambda e: e.dma_start(out=lnp[:], in_=src), writes=["S:lnp"], dma_key="lnp")

    def ln_tb(self, tb):
        P, x_tok, st, lnp, nh = self.P, self.x_tok, self.stats, self.lnp, self.neghalf
        R = [f"xt{tb}_0", f"xt{tb}_1"]
        sr = f"st{tb}"
        P.add("dve", lambda e: e.bn_stats(out=st[:, tb, 0:6], in_=x_tok[:, tb, 0:512]), reads=[R[0]], writes=[sr])
        P.add("dve", lambda e: e.bn_stats(out=st[:, tb, 6:12], in_=x_tok[:, tb, 512:1024]), reads=[R[1]], writes=[sr])
        P.add("dve", lambda e: e.bn_aggr(out=st[:, tb, 12:14], in_=st[:, tb, 0:12]), reads=[sr], writes=[sr])
        P.add("pool", lambda e: e.tensor_scalar(out=st[:, tb, 14:15], in0=st[:, tb, 13:14], scalar1=LN_EPS, scalar2=None, op0=ALU.add),
              reads=[sr], writes=[sr])
        P.add("pool", lambda e: e.tensor_tensor(out=st[:, tb, 14:15], in0=st[:, tb, 14:15], in1=nh[:], op=ALU.pow),
              reads=[sr, "neghalf"], writes=[sr])
        P.add("dve", lambda e: e.tensor_scalar(out=st[:, tb, 15:16], in0=st[:, tb, 12:13], scalar1=st[:, tb, 14:15], scalar2=-1.0,
                                               op0=ALU.mult, op1=ALU.mult), reads=[sr], writes=[sr])
        P.add("act", lambda e: e.activation(out=x_tok[:, tb, :], in_=x_tok[:, tb, :], func=AF.Identity,
                                            scale=st[:, tb, 14:15], bias=st[:, tb, 15:16]), reads=[sr] + R, writes=R)

    def ln_affine(self, tb):
        P, x_tok, lnp = self.P, self.x_tok, self.lnp
        R = [f"xt{tb}_0", f"xt{tb}_1"]
        P.add("pool", lambda e: e.tensor_tensor(out=x_tok[:, tb, :], in0=x_tok[:, tb, :], in1=lnp[:, 0, :], op=ALU.mult),
              reads=R + ["S:lnp"], writes=R)
        P.add("pool", lambda e: e.tensor_tensor(out=x_tok[:, tb, :], in0=x_tok[:, tb, :], in1=lnp[:, 1, :], op=ALU.add),
              reads=R + ["S:lnp"], writes=R)

    def ffn(self, l, final=False):
        P, xTp, x_tok, cwb, fix = self.P, self.xTp, self.x_tok, self.cwb, self.fix
        self.reset_scratch()
        hid = self.carve(NJ * 1024 // 2, BF16).rearrange("p (j t) -> p j t", j=NJ)
        tmp = [[self.carve(1024), self.carve(1024)] for _ in range(2)]
        cwsrc = self.dram["ffn_cwb"][l]
        P.add("sp", lambda e: e.dma_start(out=cwb[:], in_=cwsrc), writes=["cwb"], dma_key="cwb")
        self.load_ln(2 * l + 1)
        wup, wdn = self.dram["wup"], self.dram["wdn"]
        deferred = []
        for h in range(2):
            c0 = PAD + 1024 * h
            xres = [f"xT{tb}" for tb in range(8 * h, 8 * h + 8)] + ["xTpad"]
            for s in range(11):
                if s == 0 and h == 1:
                    tile, res = pre_up
                else:
                    tile, res = self.wload(wup[l, s])
                sv = tile[:].rearrange("p (k j c) -> p k j c", k=8, j=2)
                for jj in range(2):
                    j = 2 * s + jj
                    pset = j % 2
                    for ug in range(2):
                        pst = 2 * pset + ug
                        ps = self.PS[pst]
                        for t in range(2):
                            P.begin_group()
                            for kc in range(8):
                                P.add("pe", lambda e, ps=ps, t=t, kc=kc, jj=jj, ug=ug, sv=sv, c0=c0: e.matmul(
                                    ps[:, t * 512:(t + 1) * 512], lhsT=sv[:, kc, jj, ug * 128:(ug + 1) * 128],
                                    rhs=xTp[:, kc, c0 + t * 512:c0 + (t + 1) * 512], start=(kc == 0), stop=(kc == 7)),
                                    reads=[res] + xres, writes=[self.BK(2 * pst + t)])
                            P.end_group()
                    for ug in range(2):
                        pst = 2 * pset + ug
                        ps = self.PS[pst]
                        a = tmp[pset][ug]
                        ar = f"S:a{pset}{ug}"
                        ch = ug * NJ + j
                        bks = [self.BK(2 * pst), self.BK(2 * pst + 1)]
                        P.add("act", lambda e, ps=ps, a=a, ch=ch: e.activation(out=a[:, 0:1024], in_=ps[:, 0:1024], func=AF.Identity,
                                                                                scale=cwb[:, ch, 2:3], bias=cwb[:, ch, 3:4]),
                              reads=bks + ["cwb"], writes=[ar])
                        P.add("dve", lambda e, ps=ps, a=a, ch=ch: e.scalar_tensor_tensor(
                            out=a[:, 1:1024], in0=ps[:, 0:1023], scalar=cwb[:, ch, 1:2], in1=a[:, 1:1024], op0=ALU.mult, op1=ALU.add),
                            reads=bks + ["cwb", ar], writes=[ar])
                        P.add("dve", lambda e, ps=ps, a=a, ch=ch: e.scalar_tensor_tensor(
                            out=a[:, 2:1024], in0=ps[:, 0:1022], scalar=cwb[:, ch, 0:1], in1=a[:, 2:1024], op0=ALU.mult, op1=ALU.add),
                            reads=bks + ["cwb", ar], writes=[ar])
                        if h == 0:
                            P.add("dve", lambda e, ps=ps, ch=ch: e.tensor_scalar(out=fix[:, ch, 0:2], in0=ps[:, 1022:1024], scalar1=cwb[:, ch, 0:1],
                                                                                 scalar2=None, op0=ALU.mult), reads=bks + ["cwb"], writes=[f"fix{ch}"])
                            P.add("dve", lambda e, ps=ps, ch=ch: e.scalar_tensor_tensor(
                                out=fix[:, ch, 0:1], in0=ps[:, 1023:1024], scalar=cwb[:, ch, 1:2], in1=fix[:, ch, 0:1], op0=ALU.mult, op1=ALU.add),
                                reads=bks + ["cwb", f"fix{ch}"], writes=[f"fix{ch}"])
                        else:
                            P.add("dve", lambda e, a=a, ch=ch: e.tensor_tensor(out=a[:, 0:2], in0=a[:, 0:2], in1=fix[:, ch, 0:2], op=ALU.add),
                                  reads=[ar, f"fix{ch}"], writes=[ar])
                    au, ag = tmp[pset]
                    P.add("act", lambda e, ag=ag: e.activation(out=ag[:, 0:1024], in_=ag[:, 0:1024], func=AF.Silu),
                          reads=[f"S:a{pset}1"], writes=[f"S:a{pset}1"])
                    P.add("pool", lambda e, au=au, ag=ag, j=j: e.tensor_tensor(out=hid[:, j, :], in0=au[:, 0:1024], in1=ag[:, 0:1024], op=ALU.mult),
                          reads=[f"S:a{pset}0", f"S:a{pset}1"], writes=[f"S:hid{j}"])
                    if deferred and j >= 1:
                        fn_, gtb_, tb_ = deferred.pop(0)
                        fn_(gtb_, tb_)
            r0 = self.ring_cnt % 3
            if h == 0:
                pre_up = self.wload(wup[l, 0], slot=(r0 + 2) % 3)
            else:
                self.ring_cnt += 2
            nd = 0
            for dh in range(2):
                for g in range(3):
                    tile, res = self.wload(wdn[l, dh, g], slot=(r0 + nd % 2) % 3)
                    nd += 1
                    sv = tile[:].rearrange("p (j c) -> p j c", j=8)
                    for jj in range(8 if g < 2 else 6):
                        j = 8 * g + jj
                        for tb in range(8):
                            P.add("pe", lambda e, tb=tb, j=j, jj=jj, sv=sv: e.matmul(
                                self.bank(tb), lhsT=hid[:, j, tb * 128:(tb + 1) * 128], rhs=sv[:, jj, :], start=(j == 0), stop=(j == NJ - 1)),
                                reads=[res, f"S:hid{j}"], writes=[self.BK(tb)])
                for tb in range(8):
                    gtb = 8 * h + tb
                    P.add("dve", lambda e, tb=tb, gtb=gtb, dh=dh: e.scalar_tensor_tensor(
                        out=x_tok[:, gtb, dh * 512:(dh + 1) * 512], in0=x_tok[:, gtb, dh * 512:(dh + 1) * 512], scalar=ALPHA,
                        in1=self.bank(tb), op0=ALU.mult, op1=ALU.add), reads=[self.BK(tb), f"xt{gtb}_{dh}"], writes=[f"xt{gtb}_{dh}"])
            def ln_tail(gtb, tb):
                self.ln_tb(gtb)
                if not final:
                    self.transpose_tb(gtb, tb // 2, affine=True)
                self.ln_affine(gtb)
                if final:
                    self.store_tb(gtb)
            for tb in range(8):
                if h == 0:
                    deferred.append((ln_tail, 8 * h + tb, tb))
                else:
                    ln_tail(8 * h + tb, tb)


    def mix1(self):
        P, xTp, x_tok, dram = self.P, self.xTp, self.x_tok, self.dram
        ones_f, ident_b, maskneg, small = self.ones_f, self.ident_b, self.maskneg, self.small
        xall = [f"xT{tb}" for tb in range(NTB)]
        self.reset_scratch()
        fl = self.carve(2048)
        cum = self.carve(2048)
        ones = self.carve(512)
        QG = self.carve(6 * 2048 // 2, BF16).rearrange("p (r t) -> p r t", r=6)
        KG = self.carve(6 * 2048 // 2, BF16).rearrange("p (r t) -> p r t", r=6)
        bsrc = dram["fox_bf"]
        P.add("sp", lambda e: e.dma_start(out=small[0:16, 0:1], in_=bsrc), writes=["small"], dma_key="small")
        P.add("dve", lambda e: e.tensor_scalar(out=small[0:16, 1:2], in0=small[0:16, 0:1], scalar1=-1.0, scalar2=None, op0=ALU.mult),
              reads=["small"], writes=["small"])
        P.add("pool", lambda e: e.memset(ones[0:16, :], 1.0), writes=["S:ones"])
        P.add("pool", lambda e: e.memset(QG[0:16, 3:6, :], 1.0), writes=["S:QG1"])
        P.add("pool", lambda e: e.memset(KG[0:16, 0:3, :], 1.0), writes=["S:KG1"])
        tile, res = self.wload(dram["fox_f"], words=128)
        fv = tile[:, 0:128].rearrange("p (k c) -> p k c", k=8)
        for t in range(4):
            P.begin_group()
            for kc in range(8):
                P.add("pe", lambda e, t=t, kc=kc: e.matmul(self.bank(t)[0:16, :], lhsT=fv[:, kc, :], rhs=xTp[:, kc, PAD + t * 512:PAD + (t + 1) * 512],
                                                           start=(kc == 0), stop=(kc == 7)), reads=[res] + xall, writes=[self.BK(t)])
            P.end_group()
            P.add("act", lambda e, t=t: e.activation(out=fl[0:16, t * 512:(t + 1) * 512], in_=self.bank(t)[0:16, :], func=AF.Exp,
                                                     scale=-1.0, bias=small[0:16, 1:2]), reads=[self.BK(t), "small"], writes=[f"S:fl{t}"])
        for t in range(4):
            P.add("act", lambda e, t=t: e.activation(out=fl[0:16, t * 512:(t + 1) * 512], in_=fl[0:16, t * 512:(t + 1) * 512], func=AF.Ln,
                                                     scale=1.0, bias=1.0), reads=[f"S:fl{t}"], writes=[f"S:fl{t}"])
        for t in range(4):
            init = 0.0 if t == 0 else cum[0:16, t * 512 - 1:t * 512]
            P.add("dve", lambda e, t=t, init=init: e.tensor_tensor_scan(out=cum[0:16, t * 512:(t + 1) * 512], data0=ones[0:16, :],
                                                                        data1=fl[0:16, t * 512:(t + 1) * 512], initial=init,
                                                                        op0=ALU.mult, op1=ALU.subtract),
                  reads=[f"S:fl{t}", "S:ones", "S:cum"], writes=["S:cum"])
        for r in range(3):
            P.add("dve", lambda e, r=r: e.tensor_copy(out=QG[0:16, r, :], in_=cum[0:16, :]), reads=["S:cum"], writes=[f"S:QG0{r}"])
            if r < 2:
                P.add("dve", lambda e, r=r: e.tensor_tensor(out=cum[0:16, :], in0=cum[0:16, :], in1=QG[0:16, r, :], op=ALU.subtract),
                      reads=["S:cum", f"S:QG0{r}"], writes=["S:cum"])
        P.add("dve", lambda e: e.tensor_scalar(out=KG[0:16, 3:6, :], in0=QG[0:16, 0:3, :], scalar1=-1.0, scalar2=None, op0=ALU.mult),
              reads=["S:QG00", "S:QG01", "S:QG02"], writes=["S:KG0"])
        gq, gk = dram["augq"], dram["augk"]
        P.add("sp", lambda e: e.dma_start(out=gq, in_=QG[0:16, :, :]), reads=["S:QG00", "S:QG01", "S:QG02", "S:QG1"], writes=["augq"], dma_key="augq")
        P.add("sp", lambda e: e.dma_start(out=gk, in_=KG[0:16, :, :]), reads=["S:KG0", "S:KG1"], writes=["augk"], dma_key="augk")
        self.reset_scratch()
        AUG = [[[self.carve(1024, BF16) for qk in range(2)] for sub in range(2)] for st in range(2)]
        VP = [self.carve(1040, BF16).rearrange("p (t s d) -> p t s d", t=16, s=2) for st in range(2)]
        OTP = [self.carve(1024, BF16) for st in range(2)]
        PT = [self.carve(256, BF16) for _ in range(4)]
        PTD = [self.carve(256, BF16) for _ in range(4)]
        rc = self.carve(512)
        bcs = self.carve(512)
        self.load_ln(2)
        for st in range(2):
            P.add("pool", lambda e, st=st: e.memset(VP[st][:, :, :, 64:65], 1.0), writes=[f"S:VPone{st}"])
        for i4 in range(1, 4):
            P.add("pool", lambda e, i4=i4: e.memset(PTD[i4][:, 0:128 * i4], 0.0), writes=[f"S:PTD{i4}"])
        ptc = [0]

        def pair(hp):
            st = hp % 2
            qk_t, qk_r = self.wload(dram["fox_qk"][hp], words=2048)
            v_t, v_r = self.wload(dram["fox_v"][hp], words=1024)
            qkv = qk_t[:, 0:2048].rearrange("p (k c) -> p k c", k=8)
            vv = v_t[:, 0:1024].rearrange("p (k c) -> p k c", k=8)
            for sub in range(2):
                h = 2 * hp + sub
                P.add("sp", lambda e, st=st, sub=sub, h=h: e.dma_start(out=AUG[st][sub][0][64:70, :], in_=gq[h]),
                      reads=["augq"], writes=[f"S:AQa{st}{sub}"], dma_key=f"aq{st}{sub}")
                P.add("sp", lambda e, st=st, sub=sub, h=h: e.dma_start(out=AUG[st][sub][1][64:70, :], in_=gk[h]),
                      reads=["augk"], writes=[f"S:AKa{st}{sub}"], dma_key=f"ak{st}{sub}")
            for t in range(4):
                for qk in range(2):
                    b = self.rotbank("misc", (0, 1, 7))
                    P.begin_group()
                    for kc in range(8):
                        P.add("pe", lambda e, b=b, kc=kc, qk=qk, t=t, qkv=qkv: e.matmul(
                            self.bank(b), lhsT=qkv[:, kc, qk * 128:(qk + 1) * 128], rhs=xTp[:, kc, PAD + t * 512:PAD + (t + 1) * 512],
                            start=(kc == 0), stop=(kc == 7)), reads=[qk_r] + xall, writes=[self.BK(b)])
                    P.end_group()
                    for sub in range(2):
                        dst = AUG[st][sub][qk]
                        nm = f"S:A{'QK'[qk]}{st}{sub}t{t}"
                        if qk == 0:
                            P.add("dve", lambda e, b=b, sub=sub, dst=dst, t=t: e.tensor_scalar(
                                out=dst[0:64, t * 512:(t + 1) * 512], in0=self.bank(b)[sub * 64:(sub + 1) * 64, :], scalar1=0.125, scalar2=None,
                                op0=ALU.mult), reads=[self.BK(b)], writes=[nm])
                        else:
                            P.add("dve", lambda e, b=b, sub=sub, dst=dst, t=t: e.tensor_copy(
                                out=dst[0:64, t * 512:(t + 1) * 512], in_=self.bank(b)[sub * 64:(sub + 1) * 64, :]),
                                reads=[self.BK(b)], writes=[nm])
            for g4 in range(4):
                b = self.rotbank("misc", (0, 1, 7))
                P.begin_group()
                for ti in range(4):
                    tb = 4 * g4 + ti
                    for kc in range(8):
                        P.add("pe", lambda e, b=b, ti=ti, tb=tb, kc=kc, vv=vv: e.matmul(
                            self.bank(b)[:, ti * 128:(ti + 1) * 128], lhsT=xTp[:, kc, PAD + tb * 128:PAD + (tb + 1) * 128], rhs=vv[:, kc, :],
                            start=(kc == 0), stop=(kc == 7)), reads=[v_r, f"xT{tb}"], writes=[self.BK(b)])
                P.end_group()
                P.add("act", lambda e, b=b, g4=g4, st=st: e.activation(
                    out=VP[st][:, 4 * g4:4 * g4 + 4, :, 0:64], in_=self.bank(b).rearrange("p (t s d) -> p t s d", t=4, s=2), func=AF.Identity),
                    reads=[self.BK(b)], writes=[f"S:VP{st}g{g4}"])
            yield
            for sub in range(2):
                if sub == 1:
                    wo_t, wo_r = self.wload(dram["fox_wo"][hp], words=1024)
                QA, KA = AUG[st][sub][0], AUG[st][sub][1]
                for qt in range(4):
                    ob = self.rotbank("O", (2, 3))
                    nkb = 4 * qt + 4
                    for kb in range(nkb):
                        i = kb - 4 * qt
                        co = 128 * i if i > 0 else 0
                        sb_ = self.rotbank("S", (4, 5, 6))
                        if i >= 0:
                            pt, ptr = PTD[i], f"S:PTD{i}"
                        else:
                            pt, ptr = PT[ptc[0] % 4], f"S:PT{ptc[0] % 4}"
                            ptc[0] += 1
                        kres = [f"S:AK{st}{sub}t{kb // 4}", f"S:AKa{st}{sub}", f"S:AQ{st}{sub}t{qt}", f"S:AQa{st}{sub}"]
                        P.begin_group()
                        if i < 0:
                            P.add("pe", lambda e, sb_=sb_, kb=kb, qt=qt, QA=QA, KA=KA: e.matmul(
                                self.bank(sb_)[:, 0:512], lhsT=KA[0:70, kb * 128:(kb + 1) * 128], rhs=QA[0:70, qt * 512:(qt + 1) * 512],
                                start=True, stop=True), reads=kres, writes=[self.BK(sb_)])
                        else:
                            P.add("pe", lambda e, sb_=sb_, co=co, kb=kb, qt=qt, QA=QA, KA=KA: e.matmul(
                                self.bank(sb_)[:, co:co + 128], lhsT=KA[0:70, kb * 128:(kb + 1) * 128], rhs=QA[0:70, qt * 512 + co:qt * 512 + co + 128],
                                start=True, stop=False), reads=kres, writes=[self.BK(sb_)])
                            P.add("pe", lambda e, sb_=sb_, co=co: e.matmul(self.bank(sb_)[:, co:co + 128], lhsT=ident_b[:], rhs=maskneg[:],
                                                                           start=False, stop=True),
                                  reads=["ident_b", "maskneg"], writes=[self.BK(sb_)])
                            if co + 128 < 512:
                                P.add("pe", lambda e, sb_=sb_, co=co, kb=kb, qt=qt, QA=QA, KA=KA: e.matmul(
                                    self.bank(sb_)[:, co + 128:512], lhsT=KA[0:70, kb * 128:(kb + 1) * 128], rhs=QA[0:70, qt * 512 + co + 128:(qt + 1) * 512],
                                    start=True, stop=True), reads=kres, writes=[self.BK(sb_)])
                        P.end_group()
                        P.add("act", lambda e, sb_=sb_, co=co, pt=pt: e.activation(out=pt[:, co:512], in_=self.bank(sb_)[:, co:512], func=AF.Exp),
                              reads=[self.BK(sb_)], writes=[ptr])
                        P.add("pe", lambda e, ob=ob, kb=kb, pt=pt, st=st, sub=sub, nkb=nkb: e.matmul(
                            self.bank(ob)[0:65, 0:512], lhsT=VP[st][:, kb, sub, 0:65], rhs=pt[:, 0:512], start=(kb == 0), stop=(kb == nkb - 1)),
                            reads=[ptr, f"S:VP{st}g{kb // 4}", f"S:VPone{st}"], writes=[self.BK(ob)])
                    P.add("dve", lambda e, ob=ob: e.reciprocal(out=rc[64:65, :], in_=self.bank(ob)[64:65, :]), reads=[self.BK(ob)], writes=["S:rc"])
                    bb = self.rotbank("misc", (0, 1, 7))
                    P.add("pe", lambda e, bb=bb: e.matmul(self.bank(bb)[0:64, :], lhsT=ones_f[64:65, 0:64], rhs=rc[64:65, :], start=True, stop=True),
                          reads=["ones_f", "S:rc"], writes=[self.BK(bb)])
                    P.add("dve", lambda e, bb=bb: e.tensor_copy(out=bcs[0:64, :], in_=self.bank(bb)[0:64, :]), reads=[self.BK(bb)], writes=["S:bcs"])
                    P.add("dve", lambda e, ob=ob, st=st, sub=sub, qt=qt: e.tensor_tensor(
                        out=OTP[st][sub * 64:(sub + 1) * 64, qt * 512:(qt + 1) * 512], in0=self.bank(ob)[0:64, :], in1=bcs[0:64, :], op=ALU.mult),
                        reads=[self.BK(ob), "S:bcs"], writes=[f"S:OT{st}q{qt}"])
                yield
            for tb in range(NTB):
                for dh in range(2):
                    b = self.rotbank("misc", (0, 1, 7))
                    P.add("pe", lambda e, b=b, tb=tb, dh=dh, st=st, wo_t=wo_t: e.matmul(
                        self.bank(b), lhsT=OTP[st][:, tb * 128:(tb + 1) * 128], rhs=wo_t[:, dh * 512:(dh + 1) * 512], start=True, stop=True),
                        reads=[wo_r, f"S:OT{st}q{tb // 4}"], writes=[self.BK(b)])
                    xr = f"xt{tb}_{dh}"
                    if hp == 0:
                        P.add("dve", lambda e, b=b, tb=tb, dh=dh: e.scalar_tensor_tensor(
                            out=x_tok[:, tb, dh * 512:(dh + 1) * 512], in0=x_tok[:, tb, dh * 512:(dh + 1) * 512], scalar=ALPHA,
                            in1=self.bank(b), op0=ALU.mult, op1=ALU.add), reads=[self.BK(b), xr], writes=[xr])
                    else:
                        P.add("dve", lambda e, b=b, tb=tb, dh=dh: e.tensor_tensor(
                            out=x_tok[:, tb, dh * 512:(dh + 1) * 512], in0=self.bank(b), in1=x_tok[:, tb, dh * 512:(dh + 1) * 512], op=ALU.add),
                            reads=[self.BK(b), xr], writes=[xr])
            yield
        gp = [pair(hp) for hp in range(8)]
        next(gp[0])
        for hp in range(8):
            next(gp[hp])
            if hp + 1 < 8:
                next(gp[hp + 1])
            next(gp[hp])
            next(gp[hp])
        for tb in range(NTB):
            self.ln_tb(tb)
            self.transpose_tb(tb, tb % 4, affine=True)
            self.ln_affine(tb)


    def mix0(self):
        P, xTp, x_tok, dram = self.P, self.xTp, self.x_tok, self.dram
        ones_f, ident_f, ident_b, maskneg = self.ones_f, self.ident_f, self.ident_b, self.maskneg
        self.reset_scratch()
        H = self.carve(1024)
        prevbf = self.carve(512, BF16)
        R = self.carve(2048)
        L = [self.carve(128), self.carve(128)]
        halo = self.carve(16).rearrange("p (i k) -> p i k", i=8)
        fixx = self.carve(36).rearrange("p (i k) -> p i k", i=12)
        cst = self.carve(128)
        biasbc, Dbc = cst[:, 0:16], cst[:, 16:32]
        gT = cst[:, 32:40]
        scw = cst[:, 40:64].rearrange("p (i k) -> p i k", i=8)
        xcw = cst[:, 64:124].rearrange("p (i k) -> p i k", i=12)
        s_tok, s_feat, s_head = dram["m0_tokc"].rearrange("a h -> (a h)").partition_broadcast(128), dram["m0_featc"], dram["m0_headc"]
        P.add("sp", lambda e: e.dma_start(out=cst[:, 0:32], in_=s_tok), writes=["S:cst"], dma_key="m0c0")
        P.add("sp", lambda e: e.dma_start(out=cst[:, 32:124], in_=s_feat), writes=["S:cst"], dma_key="m0c1")
        P.add("sp", lambda e: e.dma_start(out=cst[0:16, 124:126], in_=s_head), writes=["S:cst"], dma_key="m0c2")
        P.add("act", lambda e: e.activation(out=cst[0:16, 126:127], in_=cst[0:16, 125:126], func=AF.Exp), reads=["S:cst"], writes=["S:cst"])
        P.add("dve", lambda e: e.tensor_scalar(out=cst[0:16, 127:128], in0=cst[0:16, 126:127], scalar1=-1.0, scalar2=None, op0=ALU.mult),
              reads=["S:cst"], writes=["S:cst"])
        dtb, acol = cst[0:16, 124:125], cst[0:16, 127:128]
        P.add("pool", lambda e: e.memset(H[:, :], 0.0), writes=["S:H"])
        P.add("pool", lambda e: e.memset(prevbf[:, :], 0.0), writes=["S:prev"])
        P.add("pool", lambda e: e.memset(R[0:48, :], 0.0), writes=["S:R"])
        P.add("pool", lambda e: e.memset(R[32:48, :], 1.0), reads=["S:R"], writes=["S:R"])
        P.add("pool", lambda e: e.affine_select(out=R[32:48, :].rearrange("p (h t) -> p h t", h=16), in_=R[32:48, :].rearrange("p (h t) -> p h t", h=16),
                                                 pattern=[[-1, 16], [0, 128]], compare_op=ALU.is_equal, fill=0.0, base=0, channel_multiplier=1),
              reads=["S:R"], writes=["S:R"])
        for k in range(2):
            P.add("pool", lambda e, k=k: e.memset(L[k][0:48, :], 0.0), writes=[f"S:L{k}"])
            P.add("pool", lambda e, k=k: e.memset(L[k][0:16, :], 1.0), reads=[f"S:L{k}"], writes=[f"S:L{k}"])
        base = self.sp_
        self.load_lnT(0)
        wx, wz, wsc, wo = dram["m0_wx"], dram["m0_wz"], dram["m0_wsc"], dram["m0_wo"]

        for qi in range(4):
            self.P.fence()
            self.sp_ = base
            yaT = self.carve(2048, BF16).rearrange("p (i t) -> p i t", i=8)
            ybT = self.carve(2048, BF16).rearrange("p (i t) -> p i t", i=8)
            xdt = self.carve(2048, BF16).rearrange("p (b f) -> p b f", b=4)
            zs = self.carve(2048, BF16).rearrange("p (b f) -> p b f", b=4)
            BcT = self.carve(512, BF16).rearrange("p (g t) -> p g t", g=2)
            CcT = self.carve(512, BF16).rearrange("p (g t) -> p g t", g=2)
            Btok = self.carve(512, BF16).rearrange("p (b f) -> p b f", b=4)
            dtok = self.carve(64).rearrange("p (b h) -> p b h", b=4)
            Ddt = self.carve(64).rearrange("p (b h) -> p b h", b=4)
            acsT = self.carve(512)
            dtT = self.carve(512)
            qbase = self.sp_
            a_ = [self.carve(512), self.carve(512)]
            prod = [self.carve(516), self.carve(516)]
            cs = [self.carve(512), self.carve(512)]
            c0 = PAD + 512 * qi
            xres = [f"xT{tb}" for tb in range(4 * qi, 4 * qi + 4)]
            win = lambda kc, c0=c0: xTp[:, kc, c0:c0 + 512]

            dt_t, dt_r = self.wload(dram["m0_wdt"], words=128)
            dv = dt_t[:, 0:128].rearrange("p (k c) -> p k c", k=8)
            P.begin_group()
            for kc in range(8):
                P.add("pe", lambda e, kc=kc, dv=dv, win=win: e.matmul(self.bank(0)[0:16, :], lhsT=dv[:, kc, :], rhs=win(kc), start=(kc == 0), stop=(kc == 7)),
                      reads=[dt_r] + xres, writes=[self.BK(0)])
            P.end_group()
            P.begin_group()
            for tbl in range(4):
                for kc in range(8):
                    P.add("pe", lambda e, kc=kc, tbl=tbl, dv=dv, c0=c0: e.matmul(self.bank(1)[:, tbl * 16:(tbl + 1) * 16],
                                                                               lhsT=xTp[:, kc, c0 + tbl * 128:c0 + (tbl + 1) * 128], rhs=dv[:, kc, :],
                                                                               start=(kc == 0), stop=(kc == 7)),
                          reads=[dt_r] + xres, writes=[self.BK(1)])
            P.end_group()
            P.add("act", lambda e, dtT=dtT: e.activation(out=dtT[0:16, :], in_=self.bank(0)[0:16, :], func=AF.Exp, bias=dtb, scale=1.0),
                  reads=[self.BK(0), "S:cst"], writes=["S:dtT"])
            P.add("dve", lambda e, dtok=dtok: e.tensor_tensor(out=dtok[:, :, :], in0=self.bank(1)[:, 0:64].rearrange("p (b h) -> p b h", b=4),
                                                             in1=biasbc.unsqueeze(1).to_broadcast([128, 4, 16]), op=ALU.add),
                  reads=[self.BK(1), "S:cst"], writes=["S:dtok"])
            P.add("act", lambda e, dtok=dtok: e.activation(out=dtok[:, :, :], in_=dtok[:, :, :], func=AF.Exp), reads=["S:dtok"], writes=["S:dtok"])
            P.add("act", lambda e, dtT=dtT: e.activation(out=dtT[0:16, :], in_=dtT[0:16, :], func=AF.Ln, bias=1.0, scale=1.0), reads=["S:dtT"], writes=["S:dtT"])
            P.add("act", lambda e, dtok=dtok: e.activation(out=dtok[:, :, :], in_=dtok[:, :, :], func=AF.Ln, bias=1.0, scale=1.0),
                  reads=["S:dtok"], writes=["S:dtok"])
            P.add("dve", lambda e, dtok=dtok, Ddt=Ddt: e.reciprocal(out=Ddt[:, :, :], in_=dtok[:, :, :]), reads=["S:dtok"], writes=["S:Ddt"])
            P.add("dve", lambda e, Ddt=Ddt: e.tensor_tensor(out=Ddt[:, :, :], in0=Ddt[:, :, :], in1=Dbc.unsqueeze(1).to_broadcast([128, 4, 16]), op=ALU.mult),
                  reads=["S:Ddt", "S:cst"], writes=["S:Ddt"])
            P.add("dve", lambda e, dtT=dtT: e.tensor_scalar(out=dtT[0:16, :], in0=dtT[0:16, :], scalar1=acol, scalar2=None, op0=ALU.mult),
                  reads=["S:dtT", "S:cst"], writes=["S:dtT"])
            for c in range(4):
                P.add("dve", lambda e, c=c, dtT=dtT, acsT=acsT: e.tensor_tensor_scan(out=acsT[0:16, c * 128:(c + 1) * 128], data0=ones_f[0:16, 0:128],
                                                                                   data1=dtT[0:16, c * 128:(c + 1) * 128], initial=0.0,
                                                                                   op0=ALU.mult, op1=ALU.add),
                      reads=["S:dtT", "ones_f"], writes=[f"S:acs{c}"])

            def xchunk(sl, cc, ak, xv, x_r):
                if True:
                    if sl == 0:
                        ci = 8 + cc
                    else:
                        ci = 4 * (sl - 1) + cc
                    b = self.rotbank("m0", (0, 1, 2, 3, 4, 5))
                    P.begin_group()
                    for kc in range(8):
                        P.add("pe", lambda e, b=b, kc=kc, cc=cc, xv=xv, win=win: e.matmul(self.bank(b), lhsT=xv[:, kc, cc * 128:(cc + 1) * 128], rhs=win(kc),
                                                                                         start=(kc == 0), stop=(kc == 7)),
                              reads=[x_r] + xres, writes=[self.BK(b)])
                    P.end_group()
                    a = a_[ak % 2]
                    ar = f"S:a{ak % 2}"
                    bk = [self.BK(b)]
                    P.add("act", lambda e, b=b, a=a, ci=ci: e.activation(out=a[:, 0:512], in_=self.bank(b), func=AF.Identity,
                                                                        scale=xcw[:, ci, 3:4], bias=xcw[:, ci, 4:5]), reads=bk + ["S:cst"], writes=[ar])
                    for sh in range(1, 4):
                        P.add("dve", lambda e, b=b, a=a, ci=ci, sh=sh: e.scalar_tensor_tensor(
                            out=a[:, sh:512], in0=self.bank(b)[:, 0:512 - sh], scalar=xcw[:, ci, 3 - sh:4 - sh], in1=a[:, sh:512], op0=ALU.mult, op1=ALU.add),
                            reads=bk + ["S:cst", ar], writes=[ar])
                    if qi > 0:
                        P.add("dve", lambda e, a=a, ci=ci: e.tensor_tensor(out=a[:, 0:3], in0=a[:, 0:3], in1=fixx[:, ci, 0:3], op=ALU.add),
                              reads=[ar, f"S:fx{ci}"], writes=[ar])
                    if qi < 3:
                        P.add("dve", lambda e, b=b, ci=ci: e.tensor_scalar(out=fixx[:, ci, 0:3], in0=self.bank(b)[:, 509:512], scalar1=xcw[:, ci, 0:1],
                                                                          scalar2=None, op0=ALU.mult), reads=bk + ["S:cst"], writes=[f"S:fx{ci}"])
                        P.add("dve", lambda e, b=b, ci=ci: e.scalar_tensor_tensor(out=fixx[:, ci, 0:2], in0=self.bank(b)[:, 510:512], scalar=xcw[:, ci, 1:2],
                                                                                 in1=fixx[:, ci, 0:2], op0=ALU.mult, op1=ALU.add),
                              reads=bk + ["S:cst", f"S:fx{ci}"], writes=[f"S:fx{ci}"])
                        P.add("dve", lambda e, b=b, ci=ci: e.scalar_tensor_tensor(out=fixx[:, ci, 0:1], in0=self.bank(b)[:, 511:512], scalar=xcw[:, ci, 2:3],
                                                                                 in1=fixx[:, ci, 0:1], op0=ALU.mult, op1=ALU.add),
                              reads=bk + ["S:cst", f"S:fx{ci}"], writes=[f"S:fx{ci}"])
                    if ci >= 10:
                        g = ci - 10
                        yield
                        P.add("act", lambda e, a=a, g=g, CcT=CcT: e.activation(out=CcT[:, g, :], in_=a[:, 0:512], func=AF.Silu), reads=[ar], writes=[f"S:Cc{g}"])
                        yield
                        yield
                        return
                    yield
                    P.add("act", lambda e, a=a: e.activation(out=a[:, 0:512], in_=a[:, 0:512], func=AF.Silu), reads=[ar], writes=[ar])
                    yield
                    tbk = self.rotbank("m0t", (6, 7))
                    P.begin_group()
                    for tbl in range(4):
                        P.add("pe", lambda e, tbk=tbk, tbl=tbl, a=a: e.transpose(self.bank(tbk)[:, tbl * 128:(tbl + 1) * 128], a[:, tbl * 128:(tbl + 1) * 128], ident_f[:]),
                              reads=[ar, "ident_f"], writes=[self.BK(tbk)])
                    P.end_group()
                    if ci >= 8:
                        g = ci - 8
                        P.add("pool", lambda e, a=a, g=g, BcT=BcT: e.tensor_copy(out=BcT[:, g, :], in_=a[:, 0:512]), reads=[ar], writes=[f"S:Bc{g}"])
                        P.add("act", lambda e, tbk=tbk, g=g, Btok=Btok: e.activation(out=Btok[:, :, g * 128:(g + 1) * 128],
                                                                                    in_=self.bank(tbk).rearrange("p (b f) -> p b f", b=4), func=AF.Identity),
                              reads=[self.BK(tbk)], writes=[f"S:Bt{g}"])
                    else:
                        P.add("dve", lambda e, tbk=tbk, ci=ci, xdt=xdt, dtok=dtok: e.tensor_tensor(
                            out=xdt[:, :, ci * 128:(ci + 1) * 128].rearrange("p b (h d) -> p b h d", h=2),
                            in0=self.bank(tbk).rearrange("p (b h d) -> p b h d", b=4, h=2),
                            in1=dtok[:, :, 2 * ci:2 * ci + 2].unsqueeze(3).to_broadcast([128, 4, 2, 64]), op=ALU.mult),
                            reads=[self.BK(tbk), "S:dtok"], writes=[f"S:xdt{ci}"])
                    yield
            xg = []
            for sl in range(3):
                x_t, x_r = self.wload(wx[sl])
                xv = x_t[:].rearrange("p (k c) -> p k c", k=8)
                for cc in range(4):
                    k = len(xg)
                    xg.append(xchunk(sl, cc, k, xv, x_r))
                    next(xg[k])
                    if k >= 1:
                        next(xg[k - 1])
                    next(xg[k])
            next(xg[-1])

            for zsl in range(2):
                z_t, z_r = self.wload(wz[zsl])
                zv = z_t[:].rearrange("p (k c) -> p k c", k=8)
                for tbl in range(4):
                    b = self.rotbank("m0", (0, 1, 2, 3, 4, 5))
                    P.begin_group()
                    for kc in range(8):
                        P.add("pe", lambda e, b=b, kc=kc, tbl=tbl, zv=zv, c0=c0: e.matmul(self.bank(b), lhsT=xTp[:, kc, c0 + tbl * 128:c0 + (tbl + 1) * 128],
                                                                                         rhs=zv[:, kc, :], start=(kc == 0), stop=(kc == 7)),
                              reads=[z_r] + xres, writes=[self.BK(b)])
                    P.end_group()
                    P.add("act", lambda e, b=b, tbl=tbl, zsl=zsl, zs=zs: e.activation(out=zs[:, tbl, zsl * 512:(zsl + 1) * 512], in_=self.bank(b), func=AF.Silu),
                          reads=[self.BK(b)], writes=[f"S:zs{tbl}"])

            for i in range(8):
                s_t, s_r = self.wload(wsc[i], words=3072)
                sv = s_t[:, 0:3072].rearrange("p (k c) -> p k c", k=8)
                bks = []
                for part in range(3):
                    b = self.rotbank("m0", (0, 1, 2, 3, 4, 5))
                    bks.append(b)
                    P.begin_group()
                    for kc in range(8):
                        P.add("pe", lambda e, b=b, kc=kc, part=part, sv=sv, win=win: e.matmul(self.bank(b), lhsT=sv[:, kc, part * 128:(part + 1) * 128], rhs=win(kc),
                                                                                             start=(kc == 0), stop=(kc == 7)),
                              reads=[s_r] + xres, writes=[self.BK(b)])
                    P.end_group()
                bc_, bh_, bb_ = bks
                k2 = i % 2
                pr, csb, a = prod[k2], cs[k2], a_[k2]
                prr, csr, ar = f"S:pr{k2}", f"S:cs{k2}", f"S:a{k2}"
                P.add("act", lambda e, bc_=bc_, csb=csb: e.activation(out=csb[:, 0:512], in_=self.bank(bc_), func=AF.Identity), reads=[self.BK(bc_)], writes=[csr])
                if qi == 0:
                    P.add("pool", lambda e, pr=pr: e.memset(pr[:, 0:2], 0.0), writes=[prr + "h"])
                else:
                    P.add("pool", lambda e, pr=pr, i=i: e.tensor_copy(out=pr[:, 0:2], in_=halo[:, i, :]), reads=[f"S:halo{i}"], writes=[prr + "h"])
                P.add("dve", lambda e, bh_=bh_, pr=pr, csb=csb: e.tensor_tensor(out=pr[:, 2:514], in0=self.bank(bh_), in1=csb[:, 0:512], op=ALU.mult),
                      reads=[self.BK(bh_), csr], writes=[prr])
                if qi < 3:
                    P.add("pool", lambda e, pr=pr, i=i: e.tensor_copy(out=halo[:, i, :], in_=pr[:, 512:514]), reads=[prr], writes=[f"S:halo{i}"])
                P.add("act", lambda e, pr=pr, a=a, i=i: e.activation(out=a[:, 0:512], in_=pr[:, 2:514], func=AF.Identity, scale=scw[:, i, 2:3]),
                      reads=[prr, "S:cst"], writes=[ar])
                P.add("dve", lambda e, pr=pr, a=a, i=i: e.scalar_tensor_tensor(out=a[:, 0:512], in0=pr[:, 1:513], scalar=scw[:, i, 1:2], in1=a[:, 0:512],
                                                                              op0=ALU.mult, op1=ALU.add), reads=[prr, prr + "h", "S:cst", ar], writes=[ar])
                P.add("dve", lambda e, pr=pr, a=a, i=i: e.scalar_tensor_tensor(out=a[:, 0:512], in0=pr[:, 0:512], scalar=scw[:, i, 0:1], in1=a[:, 0:512],
                                                                              op0=ALU.mult, op1=ALU.add), reads=[prr, prr + "h", "S:cst", ar], writes=[ar])
                P.add("dve", lambda e, bb_=bb_, a=a, i=i, yaT=yaT: e.tensor_tensor(out=yaT[:, i, :], in0=self.bank(bb_), in1=a[:, 0:512], op=ALU.mult),
                      reads=[self.BK(bb_), ar], writes=[f"S:ya{i}"])

            self.P.fence()
            self.sp_ = qbase
            segT = self.carve(1024, BF16).rearrange("p (h t) -> p h t", h=16)
            MT = self.carve(1024, BF16).rearrange("p (h t) -> p h t", h=16)
            xdtd = self.carve(512, BF16)
            yt_ = [self.carve(1024), self.carve(1024)]
            junk = self.carve(256, BF16)
            sm_ = [self.carve(64), self.carve(64)]
            X16 = self.carve(16)
            def chunk(c):
                gc = 4 * qi + c
                cols = slice(c * 128, (c + 1) * 128)
                Lm, Lr = L[gc % 2], f"S:L{gc % 2}"
                yt, ytr = yt_[gc % 2], f"S:yt{gc % 2}"
                sm, smr = sm_[gc % 2], f"S:sm{gc % 2}"
                acr = f"S:acs{c}"
                P.add("dve", lambda e, Lm=Lm, cols=cols, acsT=acsT: e.tensor_scalar(out=Lm[32:48, :], in0=acsT[0:16, cols], scalar1=-1.0, scalar2=None, op0=ALU.mult),
                      reads=[acr], writes=[Lr])
                P.add("dve", lambda e, cols=cols, acsT=acsT: e.tensor_tensor(
                    out=R[0:16, :].rearrange("p (h t) -> p h t", h=16), in0=acsT[0:16, cols].unsqueeze(1).to_broadcast([16, 16, 128]),
                    in1=ident_f[0:16, 0:16].unsqueeze(2).to_broadcast([16, 16, 128]), op=ALU.mult), reads=[acr, "ident_f"], writes=["S:R"])
                for hg in range(4):
                    b = hg % 2
                    P.begin_group()
                    for hh in range(4):
                        P.add("pe", lambda e, b=b, hg=hg, hh=hh, Lm=Lm: e.matmul(self.bank(b)[:, hh * 128:(hh + 1) * 128], lhsT=Lm[0:48, :],
                                                                                rhs=R[0:48, hg * 512 + hh * 128:hg * 512 + (hh + 1) * 128], start=True, stop=False),
                              reads=[Lr, "S:R"], writes=[self.BK(b)])
                        P.add("pe", lambda e, b=b, hh=hh: e.matmul(self.bank(b)[:, hh * 128:(hh + 1) * 128], lhsT=ident_b[:], rhs=maskneg[:], start=False, stop=True),
                              reads=["ident_b", "maskneg"], writes=[self.BK(b)])
                    P.end_group()
                    P.add("act", lambda e, b=b, hg=hg, segT=segT: e.activation(out=segT[:, 4 * hg:4 * hg + 4, :], in_=self.bank(b).rearrange("p (h t) -> p h t", h=4),
                                                                              func=AF.Exp), reads=[self.BK(b)], writes=[f"S:seg{hg}"])
                segr = [f"S:seg{hg}" for hg in range(4)]
                P.begin_group()
                for g in range(2):
                    P.add("pe", lambda e, g=g, cols=cols, BcT=BcT, CcT=CcT: e.matmul(self.bank(2)[:, g * 128:(g + 1) * 128], lhsT=BcT[:, g, cols], rhs=CcT[:, g, cols],
                                                                                    start=True, stop=True), reads=[f"S:Bc{g}", f"S:Cc{g}"], writes=["bk2"])
                P.end_group()
                P.add("dve", lambda e, cols=cols, acsT=acsT: e.tensor_scalar(out=X16[0:16, 0:16], in0=ident_f[0:16, 0:16],
                                                                            scalar1=acsT[0:16, cols][:, 127:128], scalar2=None, op0=ALU.mult),
                      reads=[acr, "ident_f"], writes=["S:X16"])
                P.begin_group()
                P.add("pe", lambda e: e.matmul(self.bank(3)[:, 0:16], lhsT=ones_f[0:16, :], rhs=X16[0:16, 0:16], start=True, stop=True),
                      reads=["ones_f", "S:X16"], writes=["bk3"])
                P.add("pe", lambda e, cols=cols, acsT=acsT: e.transpose(self.bank(3)[:, 16:32], acsT[0:16, cols], ident_f[0:16, 0:16]),
                      reads=[acr, "ident_f"], writes=["bk3"])
                P.end_group()
                P.add("act", lambda e, sm=sm: e.activation(out=sm[:, 0:32], in_=self.bank(3)[:, 0:32], func=AF.Exp), reads=["bk3"], writes=[smr])
                yield
                for g in range(2):
                    P.add("dve", lambda e, g=g, MT=MT, segT=segT: e.tensor_tensor(
                        out=MT[:, 8 * g:8 * g + 8, :], in0=self.bank(2)[:, g * 128:(g + 1) * 128].unsqueeze(1).to_broadcast([128, 8, 128]),
                        in1=segT[:, 8 * g:8 * g + 8, :], op=ALU.mult), reads=["bk2"] + segr, writes=[f"S:MT{g}"])
                P.add("dve", lambda e, c=c, xdt=xdt, xdtd=xdtd, segT=segT: e.tensor_tensor(
                    out=xdtd[:, :].rearrange("p (h d) -> p h d", h=16), in0=xdt[:, c, :].rearrange("p (h d) -> p h d", h=16),
                    in1=segT[:, :, 127:128].to_broadcast([128, 16, 64]), op=ALU.mult),
                    reads=[f"S:xdt{i}" for i in range(8)] + segr, writes=["S:xdtd"])
                yield
                P.begin_group()
                for g in range(2):
                    P.add("pe", lambda e, g=g, cols=cols, CcT=CcT: e.matmul(self.PS[2][:, g * 512:(g + 1) * 512], lhsT=CcT[:, g, cols], rhs=prevbf[:, g * 512:(g + 1) * 512],
                                                                           start=True, stop=True), reads=[f"S:Cc{g}", "S:prev"], writes=[self.BK(4 + g)])
                P.end_group()
                P.begin_group()
                for g in range(2):
                    P.add("pe", lambda e, g=g, c=c, Btok=Btok, xdtd=xdtd: e.matmul(self.PS[0][:, g * 512:(g + 1) * 512], lhsT=Btok[:, c, g * 128:(g + 1) * 128],
                                                                                  rhs=xdtd[:, g * 512:(g + 1) * 512], start=True, stop=True),
                          reads=[f"S:Bt{g}", "S:xdtd"], writes=[self.BK(g)])
                P.end_group()
                P.add("dve", lambda e, sm=sm: e.tensor_tensor(out=H[:, :].rearrange("p (h d) -> p h d", h=16), in0=H[:, :].rearrange("p (h d) -> p h d", h=16),
                                                              in1=sm[:, 0:16].unsqueeze(2).to_broadcast([128, 16, 64]), op=ALU.mult),
                      reads=["S:H", smr], writes=["S:H"])
                P.add("dve", lambda e: e.tensor_tensor(out=H[:, :], in0=self.PS[0][:, :], in1=H[:, :], op=ALU.add), reads=["S:H", self.BK(0), self.BK(1)], writes=["S:H"])
                P.add("act", lambda e: e.activation(out=prevbf[:, :], in_=H[:, :], func=AF.Identity), reads=["S:H"], writes=["S:prev"])
                P.begin_group()
                for h in range(16):
                    P.add("pe", lambda e, h=h, c=c, MT=MT, xdt=xdt: e.matmul(self.PS[3][:, h * 64:(h + 1) * 64], lhsT=MT[:, h, :], rhs=xdt[:, c, h * 64:(h + 1) * 64],
                                                                            start=True, stop=True),
                          reads=[f"S:MT{h // 8}", f"S:xdt{h // 2}"], writes=[self.BK(6 + h // 8)])
                P.end_group()
                yield
                P.add("dve", lambda e, yt=yt, sm=sm: e.tensor_tensor(out=yt[:, :].rearrange("p (h d) -> p h d", h=16), in0=self.PS[2][:, :].rearrange("p (h d) -> p h d", h=16),
                                                                     in1=sm[:, 16:32].unsqueeze(2).to_broadcast([128, 16, 64]), op=ALU.mult),
                      reads=[self.BK(4), self.BK(5), smr], writes=[ytr])
                P.add("dve", lambda e, yt=yt: e.tensor_tensor(out=yt[:, :], in0=self.PS[3][:, :], in1=yt[:, :], op=ALU.add), reads=[self.BK(6), self.BK(7), ytr], writes=[ytr])
                for half in range(2):
                    P.add("pool", lambda e, c=c, half=half, xdt=xdt, Ddt=Ddt: e.tensor_tensor(
                        out=junk[:, 0:512].rearrange("p (h d) -> p h d", h=8), in0=xdt[:, c, half * 512:(half + 1) * 512].rearrange("p (h d) -> p h d", h=8),
                        in1=Ddt[:, c, 8 * half:8 * half + 8].unsqueeze(2).to_broadcast([128, 8, 64]), op=ALU.mult),
                        reads=[f"S:xdt{i}" for i in range(8)] + ["S:Ddt"], writes=["S:junk"])
                    P.add("pool", lambda e, yt=yt, half=half: e.tensor_tensor(out=yt[:, half * 512:(half + 1) * 512], in0=yt[:, half * 512:(half + 1) * 512],
                                                                           in1=junk[:, 0:512], op=ALU.add), reads=[ytr, "S:junk"], writes=[ytr])
                P.add("pool", lambda e, yt=yt, c=c, zs=zs: e.tensor_tensor(out=yt[:, :], in0=yt[:, :], in1=zs[:, c, :], op=ALU.mult), reads=[ytr, f"S:zs{c}"], writes=[ytr])
                for g in range(2):
                    P.add("act", lambda e, g=g, yt=yt, sm=sm: e.activation(out=junk[:, 0:512], in_=yt[:, g * 512:(g + 1) * 512], func=AF.Square,
                                                                          accum_out=sm[:, 32 + g:33 + g]), reads=[ytr], writes=[smr + "s", "S:junk"])
                P.add("pool", lambda e, sm=sm: e.tensor_scalar(out=sm[:, 34:36], in0=sm[:, 32:34], scalar1=1.0 / 512.0, scalar2=LN_EPS, op0=ALU.mult, op1=ALU.add),
                      reads=[smr + "s"], writes=[smr + "r"])
                P.add("pool", lambda e, sm=sm: e.tensor_tensor(out=sm[:, 34:36], in0=sm[:, 34:36], in1=self.neghalf[:, 0:1].to_broadcast([128, 2]), op=ALU.pow),
                      reads=[smr + "r", "neghalf"], writes=[smr + "r"])
                for g in range(2):
                    P.add("act", lambda e, g=g, yt=yt, sm=sm: e.activation(out=yt[:, g * 512:(g + 1) * 512], in_=yt[:, g * 512:(g + 1) * 512], func=AF.Identity,
                                                                          scale=sm[:, 34 + g:35 + g]), reads=[ytr, smr + "r"], writes=[ytr])
                yield
                P.begin_group()
                for i in range(8):
                    P.add("pe", lambda e, i=i, yt=yt: e.transpose(self.PS[2][:, i * 128:(i + 1) * 128], yt[:, i * 128:(i + 1) * 128], ident_f[:]),
                          reads=[ytr, "ident_f"], writes=[self.BK(4 + i // 4)])
                P.end_group()
                for half in range(2):
                    P.add("dve", lambda e, half=half, cols=cols, ybT=ybT: e.tensor_tensor(
                        out=ybT[:, 4 * half:4 * half + 4, cols], in0=self.PS[2][:, half * 512:(half + 1) * 512].rearrange("p (i t) -> p i t", i=4),
                        in1=gT[:, 4 * half:4 * half + 4].unsqueeze(2).to_broadcast([128, 4, 128]), op=ALU.mult),
                        reads=[self.BK(4 + half), "S:cst"], writes=[f"S:yb{c}"])

                yield
            gens = [chunk(c) for c in range(4)]
            order = [0, 0, 0, 1, 0, 1, 0, 1, 2, 1, 2, 1, 2, 3, 2, 3, 2, 3, 3, 3]
            for gi in order:
                next(gens[gi])
            for dh in range(2):
                for part in range(2):
                    o_t, o_r = self.wload(wo[dh, part])
                    ov = o_t[:].rearrange("p (k c) -> p k c", k=8)
                    src = yaT if part == 0 else ybT
                    for tbl in range(4):
                        P.begin_group()
                        for kc in range(8):
                            rd = [o_r, (f"S:ya{kc}" if part == 0 else f"S:yb{tbl}")]
                            P.add("pe", lambda e, dh=dh, part=part, tbl=tbl, kc=kc, ov=ov, src=src: e.matmul(
                                self.bank(4 * dh + tbl), lhsT=src[:, kc, tbl * 128:(tbl + 1) * 128], rhs=ov[:, kc, :],
                                start=(part == 0 and kc == 0), stop=(part == 1 and kc == 7)), reads=rd, writes=[self.BK(4 * dh + tbl)])
                        P.end_group()
                for tbl in range(4):
                    gtb = 4 * qi + tbl
                    P.add("dve", lambda e, dh=dh, tbl=tbl, gtb=gtb: e.scalar_tensor_tensor(
                        out=x_tok[:, gtb, dh * 512:(dh + 1) * 512], in0=x_tok[:, gtb, dh * 512:(dh + 1) * 512], scalar=ALPHA,
                        in1=self.bank(4 * dh + tbl), op0=ALU.mult, op1=ALU.add), reads=[self.BK(4 * dh + tbl), f"xt{gtb}_{dh}"], writes=[f"xt{gtb}_{dh}"])
            for tbl in range(4):
                gtb = 4 * qi + tbl
                self.ln_tb(gtb)
                self.transpose_tb(gtb, tbl, affine=True)
        self.P.fence()
        self.sp_ = base
        self.load_ln(0, with_T=False)
        for tb in range(NTB):
            self.ln_affine(tb)


def declare_dram(nc, phases):
    d = {}
    d["x"] = nc.dram_tensor("x", [T, D], F32, kind="ExternalInput").ap()
    d["out"] = nc.dram_tensor("out", [T, D], F32, kind="ExternalOutput").ap()
    d["lnp"] = nc.dram_tensor("lnp", [4, 2, D], F32, kind="ExternalInput").ap()
    d["lnpT"] = nc.dram_tensor("lnpT", [4, 128, 16], F32, kind="ExternalInput").ap()
    d["ffn_cwb"] = nc.dram_tensor("ffn_cwb", [2, 128, 44, 4], F32, kind="ExternalInput").ap()
    d["wup"] = nc.dram_tensor("wup", [2, 11, 128, 4096], F32, kind="ExternalInput").ap()
    d["wdn"] = nc.dram_tensor("wdn", [2, 2, 3, 128, 4096], F32, kind="ExternalInput").ap()
    d["m0_wdt"] = nc.dram_tensor("m0_wdt", [128, 128], F32, kind="ExternalInput").ap()
    d["m0_wx"] = nc.dram_tensor("m0_wx", [3, 128, 4096], F32, kind="ExternalInput").ap()
    d["m0_wz"] = nc.dram_tensor("m0_wz", [2, 128, 4096], F32, kind="ExternalInput").ap()
    d["m0_wsc"] = nc.dram_tensor("m0_wsc", [8, 128, 3072], F32, kind="ExternalInput").ap()
    d["m0_wo"] = nc.dram_tensor("m0_wo", [2, 2, 128, 4096], F32, kind="ExternalInput").ap()
    d["m0_tokc"] = nc.dram_tensor("m0_tokc", [2, 16], F32, kind="ExternalInput").ap()
    d["m0_featc"] = nc.dram_tensor("m0_featc", [128, 92], F32, kind="ExternalInput").ap()
    d["m0_headc"] = nc.dram_tensor("m0_headc", [16, 2], F32, kind="ExternalInput").ap()
    d["fox_f"] = nc.dram_tensor("fox_f", [128, 128], F32, kind="ExternalInput").ap()
    d["fox_bf"] = nc.dram_tensor("fox_bf", [16, 1], F32, kind="ExternalInput").ap()
    d["fox_qk"] = nc.dram_tensor("fox_qk", [8, 128, 2048], F32, kind="ExternalInput").ap()
    d["fox_v"] = nc.dram_tensor("fox_v", [8, 128, 1024], F32, kind="ExternalInput").ap()
    d["fox_wo"] = nc.dram_tensor("fox_wo", [8, 128, 1024], F32, kind="ExternalInput").ap()
    d["augq"] = nc.dram_tensor("augq", [16, 6, T], BF16, kind="Internal").ap()
    d["augk"] = nc.dram_tensor("augk", [16, 6, T], BF16, kind="Internal").ap()
    return d


def build_program(phases=("mix0", "ffn0", "mix1", "ffn1")):
    nc = bass.Bass("TRN2", target_bir_lowering=False)
    dram = declare_dram(nc, phases)
    P = Prog(nc)
    B = Builder(nc, P, dram)
    B.load_x()
    for tb in range(NTB):
        B.transpose_tb(tb, tb % 4)
    last = phases[-1]
    for ph in phases:
        if ph == "ffn0":
            B.ffn(0, final=(ph == last))
        elif ph == "ffn1":
            B.ffn(1, final=(ph == last))
        elif ph == "mix0":
            P.pin = tuple(os.environ.get("MK_PIN0", "dve,pe").split(","))
            B.mix0()
            P.pin = tuple(x for x in os.environ.get("MK_PINX", "").split(",") if x)
            if ph == last:
                for tb in range(NTB):
                    B.store_tb(tb)
        elif ph == "mix1":
            B.mix1()
            if ph == last:
                for tb in range(NTB):
                    B.store_tb(tb)
        else:
            raise NotImplementedError(ph)
    if SCHEDULE:
        P.schedule()
    P.finalize(B.out_dmas)
    P.emit(B.out_dmas)
    P.close()
    return nc


def host_layouts(inp):
    f = np.float32
    o = {}
    o["lnp"] = np.ascontiguousarray(np.stack([
        np.stack([inp["ln_mix_g"][0], inp["ln_mix_b"][0]]), np.stack([inp["ln_ffn_g"][0], inp["ln_ffn_b"][0]]),
        np.stack([inp["ln_mix_g"][1], inp["ln_mix_b"][1]]), np.stack([inp["ln_ffn_g"][1], inp["ln_ffn_b"][1]])]).astype(f))
    o["lnpT"] = np.ascontiguousarray(o["lnp"].reshape(4, 2, 8, 128).transpose(0, 3, 1, 2).reshape(4, 128, 16))
    cw = inp["ffn_conv_w"].astype(f)
    cb = inp["ffn_conv_b"].astype(f)
    cwb = np.concatenate([cw.transpose(0, 2, 1), cb[:, :, None]], axis=2)
    o["ffn_cwb"] = np.ascontiguousarray(cwb.reshape(2, 44, 128, 4).transpose(0, 2, 1, 3))
    wu = inp["ffn_w_up"].astype(f)
    u = wu[:, :, :DFF].reshape(2, 8, 128, 11, 2, 128)
    g = wu[:, :, DFF:].reshape(2, 8, 128, 11, 2, 128)
    ug = np.stack([u, g], axis=5)
    o["wup"] = np.ascontiguousarray(ug.transpose(0, 3, 2, 1, 4, 5, 6).reshape(2, 11, 128, 4096))
    wd = inp["ffn_w_down"].astype(f)
    wdp = np.zeros((2, 24 * 128, 1024), f)
    wdp[:, :DFF] = wd
    wdp = wdp.reshape(2, 3, 8, 128, 2, 512)
    o["wdn"] = np.ascontiguousarray(wdp.transpose(0, 4, 1, 3, 2, 5).reshape(2, 2, 3, 128, 4096))
    w0 = inp["sc_ssm_w_in"][0].astype(f).reshape(8, 128, 5648)
    lay = lambda cols: np.ascontiguousarray(w0[:, :, cols].transpose(1, 0, 2).reshape(128, -1))
    o["m0_wdt"] = lay(slice(5632, 5648))
    o["m0_wx"] = np.stack([lay(slice(5120, 5632)), lay(slice(4096, 4608)), lay(slice(4608, 5120))])
    o["m0_wz"] = np.stack([lay(slice(3072, 3584)), lay(slice(3584, 4096))])
    o["m0_wsc"] = np.stack([lay(np.r_[1024 + 128 * i:1152 + 128 * i, 2048 + 128 * i:2176 + 128 * i, 128 * i:128 + 128 * i]) for i in range(8)])
    wo0 = inp["sc_ssm_w_out"][0].astype(f).reshape(2, 8, 128, 2, 512)
    o["m0_wo"] = np.ascontiguousarray(wo0.transpose(3, 0, 2, 1, 4).reshape(2, 2, 128, 4096))
    o["m0_tokc"] = np.ascontiguousarray(np.stack([inp["ssm_dt_bias"][0], inp["ssm_d"][0]]).astype(f))
    o["m0_headc"] = np.ascontiguousarray(np.stack([inp["ssm_dt_bias"][0], inp["ssm_a_log"][0]], axis=1).astype(f))
    gTh = inp["ssm_norm_g"][0].astype(f).reshape(8, 128).T
    scwh = inp["sc_conv_w"][0].astype(f).reshape(3, 8, 128).transpose(2, 1, 0)
    xw = inp["ssm_conv_w"][0].astype(f).reshape(4, 12, 128).transpose(2, 1, 0)
    xb = inp["ssm_conv_b"][0].astype(f).reshape(12, 128).T[:, :, None]
    o["m0_featc"] = np.ascontiguousarray(np.concatenate([gTh, scwh.reshape(128, 24), np.concatenate([xw, xb], axis=2).reshape(128, 60)], axis=1))
    wi = inp["fox_w_in"][0].astype(f)
    wk = wi.reshape(8, 128, 3088)
    o["fox_f"] = np.ascontiguousarray(wk[:, :, 3072:3088].transpose(1, 0, 2).reshape(128, 128))
    q = wk[:, :, 0:1024].reshape(8, 128, 8, 128)
    k = wk[:, :, 1024:2048].reshape(8, 128, 8, 128)
    v = wk[:, :, 2048:3072].reshape(8, 128, 8, 128)
    qk = np.concatenate([q, k], axis=3)
    o["fox_qk"] = np.ascontiguousarray(qk.transpose(2, 1, 0, 3).reshape(8, 128, 2048))
    o["fox_v"] = np.ascontiguousarray(v.transpose(2, 1, 0, 3).reshape(8, 128, 1024))
    o["fox_wo"] = np.ascontiguousarray(inp["fox_w_out"][0].astype(f).reshape(8, 128, 1024))
    o["fox_bf"] = np.ascontiguousarray(inp["fox_b_f"][0].astype(f).reshape(16, 1))
    return o


_NC_CACHE = {}


def kernel(**inputs):
    phases = ("mix0", "ffn0", "mix1", "ffn1")
    if phases not in _NC_CACHE:
        _NC_CACHE[phases] = build_program(phases)
    nc = _NC_CACHE[phases]
    lay = host_layouts(inputs)
    x = np.asarray(inputs["x"], dtype=np.float32)
    in_maps = [dict(lay, x=np.ascontiguousarray(x[b])) for b in range(8)]
    res = run_bass_kernel_spmd(nc, in_maps, core_ids=list(range(8)))
    return np.stack([np.asarray(r["out"], dtype=np.float32) for r in res.results], axis=0)
```

```python
from contextlib import ExitStack
import numpy as np
import concourse.bass as bass
import concourse.mybir as mybir
from concourse.bass_utils import run_bass_kernel_spmd

F32 = mybir.dt.float32
BF16 = mybir.dt.bfloat16
AF = mybir.ActivationFunctionType
ALU = mybir.AluOpType

COMPUTE = ("pe", "act", "dve", "pool")
QUEUES = ("pe", "act", "dve", "pool", "sp")

ALPHA = 4.0 ** 0.25
LN_EPS = 1e-5
T = 2048
D = 1024
NTB = 16
PAD = 4
DFF = 2816
NJ = 22
import os
SCHEDULE = os.environ.get('MK_SCHED', '1') == '1'
PREFETCH = os.environ.get('MK_PREFETCH', '0') == '1'


class Ins:
    __slots__ = ("eng", "fn", "deps", "idx", "dma_key", "dma_val", "signal", "sigval", "clock", "waits", "is_dma", "pinned")


class Prog:
    def __init__(self, nc):
        self.nc = nc
        self.es = ExitStack()
        self.ins = []
        self.q = {e: [] for e in QUEUES}
        self.last_w = {}
        self.readers = {}
        self.dma_cum = {}
        self.dma_sems = {}
        self.sems = {}
        self.fence_deps = []
        self.scratch_touch = {}
        self.pin = tuple(x for x in os.environ.get("MK_PINX", "").split(",") if x)

    def sbuf(self, name, shape, dtype):
        return self.es.enter_context(self.nc.sbuf_tensor(name, list(shape), dtype))

    def psum(self, name, shape, dtype=F32):
        return self.es.enter_context(self.nc.psum_tensor(name, list(shape), dtype))

    def begin_group(self):
        self._grp = []

    def end_group(self):
        g, self._grp = self._grp, None
        fns = [x[0] for x in g]
        reads, writes = [], []
        for _, r, w in g:
            for x in r:
                if x not in reads:
                    reads.append(x)
            for x in w:
                if x not in writes:
                    writes.append(x)

        def run(e, fns=fns):
            h = None
            for f in fns:
                h = f(e)
            return h
        return self.add("pe", run, reads=reads, writes=writes)

    def add(self, eng, fn, reads=(), writes=(), dma_key=None):
        if getattr(self, "_grp", None) is not None:
            assert eng == "pe" and dma_key is None
            self._grp.append((fn, list(reads), list(writes)))
            return None
        i = Ins()
        i.eng = eng
        i.fn = fn
        i.is_dma = dma_key is not None
        i.dma_key = dma_key
        i.signal = False
        i.pinned = eng in self.pin
        deps = set()
        scratch = False
        if any(r.startswith("bk") for r in reads):
            writes = list(writes) + [r for r in reads if r.startswith("bk") and r not in writes]
            reads = [r for r in reads if not r.startswith("bk")]
        for r in reads:
            w = self.last_w.get(r)
            if w is not None:
                deps.add(w)
            if r.startswith("S:"):
                scratch = True
        for w_ in writes:
            w = self.last_w.get(w_)
            if w is not None:
                deps.add(w)
            for rd in self.readers.get(w_, ()):
                deps.add(rd)
            if w_.startswith("S:"):
                scratch = True
        if scratch:
            deps.update(self.fence_deps)
        i.deps = deps
        i.idx = len(self.ins)
        self.ins.append(i)
        self.q[eng].append(i)
        for r in reads:
            self.readers.setdefault(r, []).append(i)
        for w_ in writes:
            self.last_w[w_] = i
            self.readers[w_] = []
        if i.is_dma:
            self.dma_cum[dma_key] = self.dma_cum.get(dma_key, 0) + 16
            i.dma_val = self.dma_cum[dma_key]
        if scratch:
            self.scratch_touch[i.idx] = i
        return i

    def fence(self):
        touched = list(self.scratch_touch.values())
        self.scratch_touch = {}
        if not hasattr(self, "_fdummy"):
            self._fdummy = self.sbuf("fence_dummy", [128, 8], F32)
        fd = self._fdummy
        join = self.add("dve", lambda e: e.memset(fd[:, 0:1], 0.0), writes=["fence_dummy"])
        join.deps.update(touched)
        self.fence_deps = [join]
        for k in [k for k in self.last_w if k.startswith("S:")]:
            del self.last_w[k]
        for k in [k for k in self.readers if k.startswith("S:")]:
            del self.readers[k]


    def schedule(self):
        import heapq

        class _Probe:
            def __init__(self):
                self.recs = []

            def __getattr__(self, name):
                def f(*a, **k):
                    self.recs.append((name, a, k))
                    return None
                return f

        def prod(sh):
            n = 1
            for v in sh:
                n *= int(v)
            return n

        cost, lat = {}, {}
        for i in self.ins:
            p = _Probe()
            i.fn(p)
            name, a, k = p.recs[-1]
            out = k.get("out", a[0] if a else None)
            n = prod(out.shape[1:]) if out is not None and hasattr(out, "shape") else 512
            L = 0.0
            if i.is_dma:
                by = n * out.shape[0] * 4 if out is not None else 0
                c = 0.6 if i.eng == "pool" else 0.15
                L = 2.5 + by / 150e3
            elif i.eng == "pe":
                c = 0.0
                for name, a, k in p.recs:
                    if name == "transpose":
                        c += 0.12
                    else:
                        rhs = k.get("rhs", a[2] if len(a) > 2 else None)
                        nn = prod(rhs.shape[1:]) if rhs is not None else 512
                        lhs = k.get("lhsT", a[1] if len(a) > 1 else None)
                        c1 = 0.01 + max(nn, 64) / 2400.0
                        if lhs is not None and lhs.dtype == F32:
                            c1 *= 4
                        c += c1
            elif i.eng == "act":
                c = 0.22 + n / 1400.0
            elif i.eng == "dve":
                c = 0.12 + n / 960.0
            else:
                c = 0.25 + n / 600.0
            cost[i.idx] = c
            lat[i.idx] = L
        succ = {i.idx: [] for i in self.ins}
        indeg = {}
        import os
        chain = {}
        for e in QUEUES:
            prev = None
            for i in self.q[e]:
                if prev is not None and i.pinned:
                    chain[i.idx] = prev
                prev = i
        for i in self.ins:
            ds = [d for d in i.deps if d is not i]
            if i.idx in chain and chain[i.idx] not in ds:
                ds.append(chain[i.idx])
            indeg[i.idx] = len(ds)
            for d in ds:
                succ[d.idx].append(i)
        byidx = {i.idx: i for i in self.ins}
        pending = {e: [] for e in QUEUES}
        avail = {e: [] for e in QUEUES}
        free = {e: 0.0 for e in QUEUES}
        fin = {}
        ready = {}
        for i in self.ins:
            if indeg[i.idx] == 0:
                ready[i.idx] = 0.0
                heapq.heappush(pending[i.eng], (0.0, i.idx))
        order = []
        newq = {e: [] for e in QUEUES}
        SYNC = 0.12
        n_left = len(self.ins)
        while n_left:
            best = None
            for e in QUEUES:
                pe_, av = pending[e], avail[e]
                while pe_ and pe_[0][0] <= free[e]:
                    r, ix = heapq.heappop(pe_)
                    heapq.heappush(av, ix)
                if av:
                    cand = (free[e], av[0], e, True)
                elif pe_:
                    cand = (pe_[0][0], pe_[0][1], e, False)
                else:
                    continue
                if best is None or cand[:2] < best[:2]:
                    best = cand
            st, ix, e, from_av = best
            if from_av:
                heapq.heappop(avail[e])
            else:
                heapq.heappop(pending[e])
            i = byidx[ix]
            f = st + cost[ix]
            free[e] = f
            fin[ix] = f + lat[ix]
            order.append(i)
            newq[e].append(i)
            n_left -= 1
            for sx in succ[ix]:
                indeg[sx.idx] -= 1
                r = max(ready.get(sx.idx, 0.0), fin[ix] + (0.0 if sx.eng == e and not i.is_dma else SYNC))
                ready[sx.idx] = r
                if indeg[sx.idx] == 0:
                    heapq.heappush(pending[sx.eng], (r, sx.idx))
        self.ins = order
        self.q = newq
        for k, i in enumerate(self.ins):
            i.idx = k
        self.est_us = max(fin.values()) if fin else 0.0

    def finalize(self, tail):
        nc = self.nc
        pos = {}
        for e in QUEUES:
            for k, i in enumerate(self.q[e]):
                pos[i.idx] = k
        prev_clock = {e: ({c: -1 for c in COMPUTE}, frozenset()) for e in QUEUES}
        for i in self.ins:
            clk, dseen = prev_clock[i.eng]
            clk = dict(clk)
            dseen = set(dseen)
            waits = []
            for d in sorted(i.deps, key=lambda d: -d.idx):
                if d is i:
                    continue
                if d.is_dma:
                    if d.idx in dseen:
                        continue
                    waits.append(d)
                    dseen.add(d.idx)
                else:
                    if d.eng == i.eng and d.eng == "pe":
                        continue
                    if clk[d.eng] >= pos[d.idx]:
                        continue
                    waits.append(d)
                    clk[d.eng] = max(clk[d.eng], pos[d.idx])
                dc, dd = d.clock
                for c in COMPUTE:
                    if dc[c] > clk[c]:
                        clk[c] = dc[c]
                dseen |= dd
            final = []
            for d in waits:
                if d.is_dma:
                    final.append(d)
                elif clk[d.eng] == pos[d.idx]:
                    final.append(d)
            i.waits = final
            for d in final:
                d.signal = True
            i.clock = (clk, frozenset(dseen))
            prev_clock[i.eng] = i.clock
        for d in tail:
            d.signal = True
        for e in COMPUTE:
            self.sems[e] = self.es.enter_context(nc.semaphore("s_" + e))
            n = 0
            for i in self.q[e]:
                if i.is_dma:
                    continue
                if i.signal:
                    n += 1
                    i.sigval = n
        for k in self.dma_cum:
            self.dma_sems[k] = self.es.enter_context(nc.semaphore("d_" + str(k).replace(":", "_")))

    def emit(self, tail):
        nc = self.nc
        prog = self

        def wait(eng, d):
            if d.is_dma:
                eng.wait_ge(prog.dma_sems[d.dma_key], d.dma_val)
            else:
                eng.wait_ge(prog.sems[d.eng], d.sigval)

        def run(engname, eng):
            for i in prog.q[engname]:
                for d in i.waits:
                    wait(eng, d)
                h = i.fn(eng)
                if i.is_dma:
                    h.then_inc(prog.dma_sems[i.dma_key], 16)
                elif i.signal:
                    h.then_inc(prog.sems[i.eng], 1)
            if engname == "sp":
                for d in tail:
                    wait(eng, d)

        with nc.Block() as block:
            @block.tensor
            def _(e):
                run("pe", e)

            @block.scalar
            def _(e):
                run("act", e)

            @block.vector
            def _(e):
                run("dve", e)

            @block.gpsimd
            def _(e):
                run("pool", e)

            @block.sync
            def _(e):
                run("sp", e)

    def close(self):
        self.es.close()


class Builder:
    def __init__(self, nc, P, dram):
        self.nc, self.P, self.dram = nc, P, dram
        P_ = P
        self.x_tok = P_.sbuf("x_tok_sb", [128, NTB, D], F32)
        self.xTp = P_.sbuf("xTp", [128, 8, PAD + T], BF16)
        self.ident_f = P_.sbuf("ident_f", [128, 128], F32)
        self.ones_f = P_.sbuf("ones_f", [128, 128], F32)
        self.neghalf = P_.sbuf("neghalf", [128, 1], F32)
        self.lnp = None
        self.stats = P_.sbuf("stats", [128, NTB, 16], F32)
        self.cwb = P_.sbuf("cwb", [128, 44, 4], F32)
        self.fix = P_.sbuf("fix", [128, 44, 2], F32)
        self.ring = [P_.sbuf(f"ring{i}", [128, 4096], BF16) for i in range(3)]
        self.ring_cnt = 0
        self.SW = 20480
        self.S = P_.sbuf("S", [128, self.SW], F32)
        self.sp_ = 0
        self.PS = [P_.psum(f"ps{i}", [128, 1024], F32) for i in range(4)]
        self.out_dmas = []
        self.ident_b = P_.sbuf("ident_b", [128, 128], BF16)
        self.maskneg = P_.sbuf("maskneg", [128, 128], BF16)
        self.small = P_.sbuf("small", [128, 64], F32)
        self.lnT = P_.sbuf("lnT_sb", [128, 16], F32)
        self.rot = {}
        self.consts()

    def rotbank(self, group, banks):
        k = self.rot.get(group, 0)
        self.rot[group] = k + 1
        return banks[k % len(banks)]

    def reset_scratch(self):
        self.P.fence()
        self.sp_ = 0

    def carve(self, words, dtype=F32):
        a = self.sp_
        self.sp_ += words
        assert self.sp_ <= self.SW, (self.sp_, self.SW)
        v = self.S[:, a:a + words]
        if dtype == BF16:
            v = v.bitcast(BF16)
        return v

    def bank(self, b):
        return self.PS[b // 2][:, (b % 2) * 512:(b % 2) * 512 + 512]

    @staticmethod
    def BK(b):
        return f"bk{b}"

    def ring_next(self):
        i = self.ring_cnt % 3
        self.ring_cnt += 1
        return self.ring[i], f"ring{i}"

    def wload(self, src_ap, words=4096, slot=None):
        if slot is None:
            tile, res = self.ring_next()
        else:
            tile, res = self.ring[slot], f"ring{slot}"
        self.P.add("pool", lambda e, t=tile, s=src_ap, w=words: e.dma_start(out=t[:, 0:w], in_=s, max_dma_last_dim=8192),
                   writes=[res], dma_key=res)
        return tile, res

    def consts(self):
        P = self.P
        ones_f, ident_f = self.ones_f, self.ident_f
        P.add("pool", lambda e: e.memset(ones_f[:], 1.0), writes=["ones_f"])
        P.add("pool", lambda e: e.affine_select(out=ident_f[:], in_=ones_f[:], pattern=[[-1, 128]], compare_op=ALU.is_equal,
                                                 fill=0.0, base=0, channel_multiplier=1), reads=["ones_f"], writes=["ident_f"])
        nh = self.neghalf
        P.add("pool", lambda e: e.memset(nh[:], -0.5), writes=["neghalf"])
        xTp = self.xTp
        P.add("pool", lambda e: e.memset(xTp[:, :, 0:PAD], 0.0), writes=["xTpad"])
        ident_b, maskneg = self.ident_b, self.maskneg
        P.add("pool", lambda e: e.tensor_copy(out=ident_b[:], in_=ident_f[:]), reads=["ident_f"], writes=["ident_b"])
        P.add("pool", lambda e: e.memset(maskneg[:], -30000.0), writes=["maskneg"])
        P.add("pool", lambda e: e.affine_select(out=maskneg[:], in_=maskneg[:], pattern=[[-1, 128]], compare_op=ALU.is_gt,
                                                 fill=0.0, base=0, channel_multiplier=1), reads=["maskneg"], writes=["maskneg"])

    def load_x(self):
        P, x_tok = self.P, self.x_tok
        xv = self.dram["x"].rearrange("(tb p) d -> p tb d", p=128)
        for g in range(4):
            P.add("sp", lambda e, g=g: e.dma_start(out=x_tok[:, 4 * g:4 * g + 4, :], in_=xv[:, 4 * g:4 * g + 4, :]),
                  writes=[f"xt{tb}_{dh}" for tb in range(4 * g, 4 * g + 4) for dh in range(2)], dma_key=f"xin{g}")

    def store_tb(self, tb):
        P, x_tok = self.P, self.x_tok
        ov = self.dram["out"].rearrange("(tb p) d -> p tb d", p=128)
        i = P.add("sp", lambda e: e.dma_start(out=ov[:, tb, :], in_=x_tok[:, tb, :]), reads=[f"xt{tb}_0", f"xt{tb}_1"], dma_key=f"xout{tb}")
        self.out_dmas.append(i)

    def transpose_tb(self, tb, pst, affine=False):
        P, x_tok, xTp, ident_f, lnT = self.P, self.x_tok, self.xTp, self.ident_f, self.lnT
        ps = self.PS[pst]
        P.begin_group()
        for kc in range(8):
            P.add("pe", lambda e, kc=kc: e.transpose(ps[:, kc * 128:(kc + 1) * 128], x_tok[:, tb, kc * 128:(kc + 1) * 128], ident_f[:]),
                  reads=[f"xt{tb}_{kc // 4}", "ident_f"], writes=[self.BK(2 * pst + kc // 4)])
        P.end_group()
        c0 = PAD + tb * 128
        if affine:
            for kc in range(8):
                if kc < 4:
                    P.add("act", lambda e, kc=kc: e.activation(out=xTp[:, kc, c0:c0 + 128], in_=ps[:, kc * 128:(kc + 1) * 128], func=AF.Identity,
                                                               scale=lnT[:, kc:kc + 1], bias=lnT[:, 8 + kc:9 + kc]),
                          reads=[self.BK(2 * pst), "lnT"], writes=[f"xT{tb}"])
                else:
                    P.add("dve", lambda e, kc=kc: e.tensor_scalar(out=xTp[:, kc, c0:c0 + 128], in0=ps[:, kc * 128:(kc + 1) * 128],
                                                                  scalar1=lnT[:, kc:kc + 1], scalar2=lnT[:, 8 + kc:9 + kc], op0=ALU.mult, op1=ALU.add),
                          reads=[self.BK(2 * pst + 1), "lnT"], writes=[f"xT{tb}"])
            return
        P.add("act", lambda e: e.activation(out=xTp[:, 0:4, c0:c0 + 128], in_=ps[:, 0:512].rearrange("p (k t) -> p k t", k=4), func=AF.Identity),
              reads=[self.BK(2 * pst)], writes=[f"xT{tb}"])
        P.add("dve", lambda e: e.tensor_copy(out=xTp[:, 4:8, c0:c0 + 128], in_=ps[:, 512:1024].rearrange("p (k t) -> p k t", k=4)),
              reads=[self.BK(2 * pst + 1)], writes=[f"xT{tb}"])

    def load_lnT(self, idx):
        lnT = self.lnT
        src = self.dram["lnpT"][idx]
        self.P.add("sp", lambda e: e.dma_start(out=lnT[:], in_=src), writes=["lnT"], dma_key="lnT")

    def load_ln(self, idx, with_T=True):
        if with_T:
            self.load_lnT(idx)
        self.lnp = self.carve(2 * D).rearrange("p (a d) -> p a d", a=2)
        lnp = self.lnp
        src = self.dram["lnp"][idx].partition_broadcast(128)
        self.P.add("sp", lambda e: e.dma_start(out=lnp[:], in_=src), writes=["S:lnp"], dma_key="lnp")

    def ln_tb(self, tb):
        P, x_tok, st, lnp, nh = self.P, self.x_tok, self.stats, self.lnp, self.neghalf
        R = [f"xt{tb}_0", f"xt{tb}_1"]
        sr = f"st{tb}"
        P.add("dve", lambda e: e.bn_stats(out=st[:, tb, 0:6], in_=x_tok[:, tb, 0:512]), reads=[R[0]], writes=[sr])
        P.add("dve", lambda e: e.bn_stats(out=st[:, tb, 6:12], in_=x_tok[:, tb, 512:1024]), reads=[R[1]], writes=[sr])
        P.add("dve", lambda e: e.bn_aggr(out=st[:, tb, 12:14], in_=st[:, tb, 0:12]), reads=[sr], writes=[sr])
        P.add("pool", lambda e: e.tensor_scalar(out=st[:, tb, 14:15], in0=st[:, tb, 13:14], scalar1=LN_EPS, scalar2=None, op0=ALU.add),
              reads=[sr], writes=[sr])
        P.add("pool", lambda e: e.tensor_tensor(out=st[:, tb, 14:15], in0=st[:, tb, 14:15], in1=nh[:], op=ALU.pow),
              reads=[sr, "neghalf"], writes=[sr])
        P.add("dve", lambda e: e.tensor_scalar(out=st[:, tb, 15:16], in0=st[:, tb, 12:13], scalar1=st[:, tb, 14:15], scalar2=-1.0,
                                               op0=ALU.mult, op1=ALU.mult), reads=[sr], writes=[sr])
        P.add("act", lambda e: e.activation(out=x_tok[:, tb, :], in_=x_tok[:, tb, :], func=AF.Identity,
                                            scale=st[:, tb, 14:15], bias=st[:, tb, 15:16]), reads=[sr] + R, writes=R)

    def ln_affine(self, tb):
        P, x_tok, lnp = self.P, self.x_tok, self.lnp
        R = [f"xt{tb}_0", f"xt{tb}_1"]
        P.add("pool", lambda e: e.tensor_tensor(out=x_tok[:, tb, :], in0=x_tok[:, tb, :], in1=lnp[:, 0, :], op=ALU.mult),
              reads=R + ["S:lnp"], writes=R)
        eng2 = "pool" if "dve" in P.pin else "dve"
        P.add(eng2, lambda e: e.tensor_tensor(out=x_tok[:, tb, :], in0=x_tok[:, tb, :], in1=lnp[:, 1, :], op=ALU.add),
              reads=R + ["S:lnp"], writes=R)

    def ffn(self, l, final=False):
        P, xTp, x_tok, cwb, fix = self.P, self.xTp, self.x_tok, self.cwb, self.fix
        self.reset_scratch()
        hid = self.carve(NJ * 1024 // 2, BF16).rearrange("p (j t) -> p j t", j=NJ)
        tmp = [[self.carve(1024), self.carve(1024)] for _ in range(2)]
        cwsrc = self.dram["ffn_cwb"][l]
        P.add("sp", lambda e: e.dma_start(out=cwb[:], in_=cwsrc), writes=["cwb"], dma_key="cwb")
        self.load_ln(2 * l + 1)
        wup, wdn = self.dram["wup"], self.dram["wdn"]
        deferred = []
        for h in range(2):
            c0 = PAD + 1024 * h
            xres = [f"xT{tb}" for tb in range(8 * h, 8 * h + 8)] + ["xTpad"]
            for s in range(11):
                if s == 0 and h == 1:
                    tile, res = pre_up
                else:
                    tile, res = self.wload(wup[l, s])
                sv = tile[:].rearrange("p (k j c) -> p k j c", k=8, j=2)
                for jj in range(2):
                    j = 2 * s + jj
                    pset = j % 2
                    for ug in range(2):
                        pst = 2 * pset + ug
                        ps = self.PS[pst]
                        for t in range(2):
                            P.begin_group()
                            for kc in range(8):
                                P.add("pe", lambda e, ps=ps, t=t, kc=kc, jj=jj, ug=ug, sv=sv, c0=c0: e.matmul(
                                    ps[:, t * 512:(t + 1) * 512], lhsT=sv[:, kc, jj, ug * 128:(ug + 1) * 128],
                                    rhs=xTp[:, kc, c0 + t * 512:c0 + (t + 1) * 512], start=(kc == 0), stop=(kc == 7)),
                                    reads=[res] + xres, writes=[self.BK(2 * pst + t)])
                            P.end_group()
                    for ug in range(2):
                        pst = 2 * pset + ug
                        ps = self.PS[pst]
                        a = tmp[pset][ug]
                        ar = f"S:a{pset}{ug}"
                        ch = ug * NJ + j
                        bks = [self.BK(2 * pst), self.BK(2 * pst + 1)]
                        P.add("act", lambda e, ps=ps, a=a, ch=ch: e.activation(out=a[:, 0:1024], in_=ps[:, 0:1024], func=AF.Identity,
                                                                                scale=cwb[:, ch, 2:3], bias=cwb[:, ch, 3:4]),
                              reads=bks + ["cwb"], writes=[ar])
                        P.add("dve", lambda e, ps=ps, a=a, ch=ch: e.scalar_tensor_tensor(
                            out=a[:, 1:1024], in0=ps[:, 0:1023], scalar=cwb[:, ch, 1:2], in1=a[:, 1:1024], op0=ALU.mult, op1=ALU.add),
                            reads=bks + ["cwb", ar], writes=[ar])
                        P.add("dve", lambda e, ps=ps, a=a, ch=ch: e.scalar_tensor_tensor(
                            out=a[:, 2:1024], in0=ps[:, 0:1022], scalar=cwb[:, ch, 0:1], in1=a[:, 2:1024], op0=ALU.mult, op1=ALU.add),
                            reads=bks + ["cwb", ar], writes=[ar])
                        if h == 0:
                            P.add("dve", lambda e, ps=ps, ch=ch: e.tensor_scalar(out=fix[:, ch, 0:2], in0=ps[:, 1022:1024], scalar1=cwb[:, ch, 0:1],
                                                                                 scalar2=None, op0=ALU.mult), reads=bks + ["cwb"], writes=[f"fix{ch}"])
                            P.add("dve", lambda e, ps=ps, ch=ch: e.scalar_tensor_tensor(
                                out=fix[:, ch, 0:1], in0=ps[:, 1023:1024], scalar=cwb[:, ch, 1:2], in1=fix[:, ch, 0:1], op0=ALU.mult, op1=ALU.add),
                                reads=bks + ["cwb", f"fix{ch}"], writes=[f"fix{ch}"])
                        else:
                            P.add("dve", lambda e, a=a, ch=ch: e.tensor_tensor(out=a[:, 0:2], in0=a[:, 0:2], in1=fix[:, ch, 0:2], op=ALU.add),
                                  reads=[ar, f"fix{ch}"], writes=[ar])
                    au, ag = tmp[pset]
                    P.add("act", lambda e, ag=ag: e.activation(out=ag[:, 0:1024], in_=ag[:, 0:1024], func=AF.Silu),
                          reads=[f"S:a{pset}1"], writes=[f"S:a{pset}1"])
                    P.add("pool", lambda e, au=au, ag=ag, j=j: e.tensor_tensor(out=hid[:, j, :], in0=au[:, 0:1024], in1=ag[:, 0:1024], op=ALU.mult),
                          reads=[f"S:a{pset}0", f"S:a{pset}1"], writes=[f"S:hid{j}"])
                    if deferred and j >= 1:
                        fn_, gtb_, tb_ = deferred.pop(0)
                        fn_(gtb_, tb_)
            r0 = self.ring_cnt % 3
            if h == 0:
                pre_up = self.wload(wup[l, 0], slot=(r0 + 2) % 3)
            else:
                self.ring_cnt += 2
            nd = 0
            for dh in range(2):
                for g in range(3):
                    tile, res = self.wload(wdn[l, dh, g], slot=(r0 + nd % 2) % 3)
                    nd += 1
                    sv = tile[:].rearrange("p (j c) -> p j c", j=8)
                    for jj in range(8 if g < 2 else 6):
                        j = 8 * g + jj
                        for tb in range(8):
                            P.add("pe", lambda e, tb=tb, j=j, jj=jj, sv=sv: e.matmul(
                                self.bank(tb), lhsT=hid[:, j, tb * 128:(tb + 1) * 128], rhs=sv[:, jj, :], start=(j == 0), stop=(j == NJ - 1)),
                                reads=[res, f"S:hid{j}"], writes=[self.BK(tb)])
                for tb in range(8):
                    gtb = 8 * h + tb
                    P.add("dve", lambda e, tb=tb, gtb=gtb, dh=dh: e.scalar_tensor_tensor(
                        out=x_tok[:, gtb, dh * 512:(dh + 1) * 512], in0=x_tok[:, gtb, dh * 512:(dh + 1) * 512], scalar=ALPHA,
                        in1=self.bank(tb), op0=ALU.mult, op1=ALU.add), reads=[self.BK(tb), f"xt{gtb}_{dh}"], writes=[f"xt{gtb}_{dh}"])
            def ln_tail(gtb, tb):
                self.ln_tb(gtb)
                if not final:
                    self.transpose_tb(gtb, tb // 2, affine=True)
                self.ln_affine(gtb)
                if final:
                    self.store_tb(gtb)
            for tb in range(8):
                if h == 0:
                    deferred.append((ln_tail, 8 * h + tb, tb))
                else:
                    ln_tail(8 * h + tb, tb)


    def mix1(self):
        P, xTp, x_tok, dram = self.P, self.xTp, self.x_tok, self.dram
        ones_f, ident_b, maskneg, small = self.ones_f, self.ident_b, self.maskneg, self.small
        xall = [f"xT{tb}" for tb in range(NTB)]
        self.reset_scratch()
        fl = self.carve(2048)
        cum = self.carve(2048)
        ones = self.carve(512)
        QG = self.carve(6 * 2048 // 2, BF16).rearrange("p (r t) -> p r t", r=6)
        KG = self.carve(6 * 2048 // 2, BF16).rearrange("p (r t) -> p r t", r=6)
        bsrc = dram["fox_bf"]
        P.add("sp", lambda e: e.dma_start(out=small[0:16, 0:1], in_=bsrc), writes=["small"], dma_key="small")
        P.add("dve", lambda e: e.tensor_scalar(out=small[0:16, 1:2], in0=small[0:16, 0:1], scalar1=-1.0, scalar2=None, op0=ALU.mult),
              reads=["small"], writes=["small"])
        P.add("pool", lambda e: e.memset(ones[0:16, :], 1.0), writes=["S:ones"])
        P.add("pool", lambda e: e.memset(QG[0:16, 3:6, :], 1.0), writes=["S:QG1"])
        P.add("pool", lambda e: e.memset(KG[0:16, 0:3, :], 1.0), writes=["S:KG1"])
        tile, res = self.wload(dram["fox_f"], words=128)
        fv = tile[:, 0:128].rearrange("p (k c) -> p k c", k=8)
        for t in range(4):
            P.begin_group()
            for kc in range(8):
                P.add("pe", lambda e, t=t, kc=kc: e.matmul(self.bank(t)[0:16, :], lhsT=fv[:, kc, :], rhs=xTp[:, kc, PAD + t * 512:PAD + (t + 1) * 512],
                                                           start=(kc == 0), stop=(kc == 7)), reads=[res] + xall, writes=[self.BK(t)])
            P.end_group()
            P.add("act", lambda e, t=t: e.activation(out=fl[0:16, t * 512:(t + 1) * 512], in_=self.bank(t)[0:16, :], func=AF.Exp,
                                                     scale=-1.0, bias=small[0:16, 1:2]), reads=[self.BK(t), "small"], writes=[f"S:fl{t}"])
        for t in range(4):
            P.add("act", lambda e, t=t: e.activation(out=fl[0:16, t * 512:(t + 1) * 512], in_=fl[0:16, t * 512:(t + 1) * 512], func=AF.Ln,
                                                     scale=1.0, bias=1.0), reads=[f"S:fl{t}"], writes=[f"S:fl{t}"])
        for t in range(4):
            init = 0.0 if t == 0 else cum[0:16, t * 512 - 1:t * 512]
            P.add("dve", lambda e, t=t, init=init: e.tensor_tensor_scan(out=cum[0:16, t * 512:(t + 1) * 512], data0=ones[0:16, :],
                                                                        data1=fl[0:16, t * 512:(t + 1) * 512], initial=init,
                                                                        op0=ALU.mult, op1=ALU.subtract),
                  reads=[f"S:fl{t}", "S:ones", "S:cum"], writes=["S:cum"])
        for r in range(3):
            P.add("dve", lambda e, r=r: e.tensor_copy(out=QG[0:16, r, :], in_=cum[0:16, :]), reads=["S:cum"], writes=[f"S:QG0{r}"])
            if r < 2:
                P.add("dve", lambda e, r=r: e.tensor_tensor(out=cum[0:16, :], in0=cum[0:16, :], in1=QG[0:16, r, :], op=ALU.subtract),
                      reads=["S:cum", f"S:QG0{r}"], writes=["S:cum"])
        P.add("dve", lambda e: e.tensor_scalar(out=KG[0:16, 3:6, :], in0=QG[0:16, 0:3, :], scalar1=-1.0, scalar2=None, op0=ALU.mult),
              reads=["S:QG00", "S:QG01", "S:QG02"], writes=["S:KG0"])
        gq, gk = dram["augq"], dram["augk"]
        P.add("sp", lambda e: e.dma_start(out=gq, in_=QG[0:16, :, :]), reads=["S:QG00", "S:QG01", "S:QG02", "S:QG1"], writes=["augq"], dma_key="augq")
        P.add("sp", lambda e: e.dma_start(out=gk, in_=KG[0:16, :, :]), reads=["S:KG0", "S:KG1"], writes=["augk"], dma_key="augk")
        self.reset_scratch()
        AUG = [[[self.carve(1024, BF16) for qk in range(2)] for sub in range(2)] for st in range(2)]
        VP = [self.carve(1040, BF16).rearrange("p (t s d) -> p t s d", t=16, s=2) for st in range(2)]
        OTP = [self.carve(1024, BF16) for st in range(2)]
        PT = [self.carve(256, BF16) for _ in range(4)]
        PTD = [self.carve(256, BF16) for _ in range(4)]
        rc = self.carve(512)
        bcs = self.carve(512)
        self.load_ln(2)
        for st in range(2):
            P.add("pool", lambda e, st=st: e.memset(VP[st][:, :, :, 64:65], 1.0), writes=[f"S:VPone{st}"])
        for i4 in range(1, 4):
            P.add("pool", lambda e, i4=i4: e.memset(PTD[i4][:, 0:128 * i4], 0.0), writes=[f"S:PTD{i4}"])
        ptc = [0]

        def pair(hp):
            st = hp % 2
            qk_t, qk_r = self.wload(dram["fox_qk"][hp], words=2048)
            v_t, v_r = self.wload(dram["fox_v"][hp], words=1024)
            qkv = qk_t[:, 0:2048].rearrange("p (k c) -> p k c", k=8)
            vv = v_t[:, 0:1024].rearrange("p (k c) -> p k c", k=8)
            for sub in range(2):
                h = 2 * hp + sub
                P.add("sp", lambda e, st=st, sub=sub, h=h: e.dma_start(out=AUG[st][sub][0][64:70, :], in_=gq[h]),
                      reads=["augq"], writes=[f"S:AQa{st}{sub}"], dma_key=f"aq{st}{sub}")
                P.add("sp", lambda e, st=st, sub=sub, h=h: e.dma_start(out=AUG[st][sub][1][64:70, :], in_=gk[h]),
                      reads=["augk"], writes=[f"S:AKa{st}{sub}"], dma_key=f"ak{st}{sub}")
            for t in range(4):
                for qk in range(2):
                    b = self.rotbank("misc", (0, 1, 7))
                    P.begin_group()
                    for kc in range(8):
                        P.add("pe", lambda e, b=b, kc=kc, qk=qk, t=t, qkv=qkv: e.matmul(
                            self.bank(b), lhsT=qkv[:, kc, qk * 128:(qk + 1) * 128], rhs=xTp[:, kc, PAD + t * 512:PAD + (t + 1) * 512],
                            start=(kc == 0), stop=(kc == 7)), reads=[qk_r] + xall, writes=[self.BK(b)])
                    P.end_group()
                    for sub in range(2):
                        dst = AUG[st][sub][qk]
                        nm = f"S:A{'QK'[qk]}{st}{sub}t{t}"
                        if qk == 0:
                            P.add("dve", lambda e, b=b, sub=sub, dst=dst, t=t: e.tensor_scalar(
                                out=dst[0:64, t * 512:(t + 1) * 512], in0=self.bank(b)[sub * 64:(sub + 1) * 64, :], scalar1=0.125, scalar2=None,
                                op0=ALU.mult), reads=[self.BK(b)], writes=[nm])
                        else:
                            P.add("dve", lambda e, b=b, sub=sub, dst=dst, t=t: e.tensor_copy(
                                out=dst[0:64, t * 512:(t + 1) * 512], in_=self.bank(b)[sub * 64:(sub + 1) * 64, :]),
                                reads=[self.BK(b)], writes=[nm])
            for g4 in range(4):
                b = self.rotbank("misc", (0, 1, 7))
                P.begin_group()
                for ti in range(4):
                    tb = 4 * g4 + ti
                    for kc in range(8):
                        P.add("pe", lambda e, b=b, ti=ti, tb=tb, kc=kc, vv=vv: e.matmul(
                            self.bank(b)[:, ti * 128:(ti + 1) * 128], lhsT=xTp[:, kc, PAD + tb * 128:PAD + (tb + 1) * 128], rhs=vv[:, kc, :],
                            start=(kc == 0), stop=(kc == 7)), reads=[v_r, f"xT{tb}"], writes=[self.BK(b)])
                P.end_group()
                P.add("act", lambda e, b=b, g4=g4, st=st: e.activation(
                    out=VP[st][:, 4 * g4:4 * g4 + 4, :, 0:64], in_=self.bank(b).rearrange("p (t s d) -> p t s d", t=4, s=2), func=AF.Identity),
                    reads=[self.BK(b)], writes=[f"S:VP{st}g{g4}"])
            yield
            for sub in range(2):
                if sub == 1:
                    wo_t, wo_r = self.wload(dram["fox_wo"][hp], words=1024)
                QA, KA = AUG[st][sub][0], AUG[st][sub][1]
                for qt in range(4):
                    ob = self.rotbank("O", (2, 3))
                    nkb = 4 * qt + 4
                    for kb in range(nkb):
                        i = kb - 4 * qt
                        co = 128 * i if i > 0 else 0
                        sb_ = self.rotbank("S", (4, 5, 6))
                        if i >= 0:
                            pt, ptr = PTD[i], f"S:PTD{i}"
                        else:
                            pt, ptr = PT[ptc[0] % 4], f"S:PT{ptc[0] % 4}"
                            ptc[0] += 1
                        kres = [f"S:AK{st}{sub}t{kb // 4}", f"S:AKa{st}{sub}", f"S:AQ{st}{sub}t{qt}", f"S:AQa{st}{sub}"]
                        P.begin_group()
                        if i < 0:
                            P.add("pe", lambda e, sb_=sb_, kb=kb, qt=qt, QA=QA, KA=KA: e.matmul(
                                self.bank(sb_)[:, 0:512], lhsT=KA[0:70, kb * 128:(kb + 1) * 128], rhs=QA[0:70, qt * 512:(qt + 1) * 512],
                                start=True, stop=True), reads=kres, writes=[self.BK(sb_)])
                        else:
                            P.add("pe", lambda e, sb_=sb_, co=co, kb=kb, qt=qt, QA=QA, KA=KA: e.matmul(
                                self.bank(sb_)[:, co:co + 128], lhsT=KA[0:70, kb * 128:(kb + 1) * 128], rhs=QA[0:70, qt * 512 + co:qt * 512 + co + 128],
                                start=True, stop=False), reads=kres, writes=[self.BK(sb_)])
                            P.add("pe", lambda e, sb_=sb_, co=co: e.matmul(self.bank(sb_)[:, co:co + 128], lhsT=ident_b[:], rhs=maskneg[:],
                                                                           start=False, stop=True),
                                  reads=["ident_b", "maskneg"], writes=[self.BK(sb_)])
                            if co + 128 < 512:
                                P.add("pe", lambda e, sb_=sb_, co=co, kb=kb, qt=qt, QA=QA, KA=KA: e.matmul(
                                    self.bank(sb_)[:, co + 128:512], lhsT=KA[0:70, kb * 128:(kb + 1) * 128], rhs=QA[0:70, qt * 512 + co + 128:(qt + 1) * 512],
                                    start=True, stop=True), reads=kres, writes=[self.BK(sb_)])
                        P.end_group()
                        P.add("act", lambda e, sb_=sb_, co=co, pt=pt: e.activation(out=pt[:, co:512], in_=self.bank(sb_)[:, co:512], func=AF.Exp),
                              reads=[self.BK(sb_)], writes=[ptr])
                        P.add("pe", lambda e, ob=ob, kb=kb, pt=pt, st=st, sub=sub, nkb=nkb: e.matmul(
                            self.bank(ob)[0:65, 0:512], lhsT=VP[st][:, kb, sub, 0:65], rhs=pt[:, 0:512], start=(kb == 0), stop=(kb == nkb - 1)),
                            reads=[ptr, f"S:VP{st}g{kb // 4}", f"S:VPone{st}"], writes=[self.BK(ob)])
                    P.add("dve", lambda e, ob=ob: e.reciprocal(out=rc[64:65, :], in_=self.bank(ob)[64:65, :]), reads=[self.BK(ob)], writes=["S:rc"])
                    bb = self.rotbank("misc", (0, 1, 7))
                    P.add("pe", lambda e, bb=bb: e.matmul(self.bank(bb)[0:64, :], lhsT=ones_f[64:65, 0:64], rhs=rc[64:65, :], start=True, stop=True),
                          reads=["ones_f", "S:rc"], writes=[self.BK(bb)])
                    P.add("dve", lambda e, bb=bb: e.tensor_copy(out=bcs[0:64, :], in_=self.bank(bb)[0:64, :]), reads=[self.BK(bb)], writes=["S:bcs"])
                    P.add("dve", lambda e, ob=ob, st=st, sub=sub, qt=qt: e.tensor_tensor(
                        out=OTP[st][sub * 64:(sub + 1) * 64, qt * 512:(qt + 1) * 512], in0=self.bank(ob)[0:64, :], in1=bcs[0:64, :], op=ALU.mult),
                        reads=[self.BK(ob), "S:bcs"], writes=[f"S:OT{st}q{qt}"])
                yield
            for tb in range(NTB):
                for dh in range(2):
                    b = self.rotbank("misc", (0, 1, 7))
                    P.add("pe", lambda e, b=b, tb=tb, dh=dh, st=st, wo_t=wo_t: e.matmul(
                        self.bank(b), lhsT=OTP[st][:, tb * 128:(tb + 1) * 128], rhs=wo_t[:, dh * 512:(dh + 1) * 512], start=True, stop=True),
                        reads=[wo_r, f"S:OT{st}q{tb // 4}"], writes=[self.BK(b)])
                    xr = f"xt{tb}_{dh}"
                    if hp == 0:
                        P.add("dve", lambda e, b=b, tb=tb, dh=dh: e.scalar_tensor_tensor(
                            out=x_tok[:, tb, dh * 512:(dh + 1) * 512], in0=x_tok[:, tb, dh * 512:(dh + 1) * 512], scalar=ALPHA,
                            in1=self.bank(b), op0=ALU.mult, op1=ALU.add), reads=[self.BK(b), xr], writes=[xr])
                    else:
                        P.add("dve", lambda e, b=b, tb=tb, dh=dh: e.tensor_tensor(
                            out=x_tok[:, tb, dh * 512:(dh + 1) * 512], in0=self.bank(b), in1=x_tok[:, tb, dh * 512:(dh + 1) * 512], op=ALU.add),
                            reads=[self.BK(b), xr], writes=[xr])
            yield
        gp = [pair(hp) for hp in range(8)]
        next(gp[0])
        for hp in range(8):
            next(gp[hp])
            if hp + 1 < 8:
                next(gp[hp + 1])
            next(gp[hp])
            next(gp[hp])
        for tb in range(NTB):
            self.ln_tb(tb)
            self.transpose_tb(tb, tb % 4, affine=True)
            self.ln_affine(tb)


    def mix0(self):
        P, xTp, x_tok, dram = self.P, self.xTp, self.x_tok, self.dram
        ones_f, ident_f, ident_b, maskneg = self.ones_f, self.ident_f, self.ident_b, self.maskneg
        self.reset_scratch()
        H = self.carve(1024)
        prevbf = self.carve(512, BF16)
        R = self.carve(2048)
        L = [self.carve(128), self.carve(128)]
        halo = self.carve(16).rearrange("p (i k) -> p i k", i=8)
        fixx = self.carve(36).rearrange("p (i k) -> p i k", i=12)
        cst = self.carve(128)
        biasbc, Dbc = cst[:, 0:16], cst[:, 16:32]
        gT = cst[:, 32:40]
        scw = cst[:, 40:64].rearrange("p (i k) -> p i k", i=8)
        xcw = cst[:, 64:124].rearrange("p (i k) -> p i k", i=12)
        s_tok, s_feat, s_head = dram["m0_tokc"].rearrange("a h -> (a h)").partition_broadcast(128), dram["m0_featc"], dram["m0_headc"]
        P.add("sp", lambda e: e.dma_start(out=cst[:, 0:32], in_=s_tok), writes=["S:cst"], dma_key="m0c0")
        P.add("sp", lambda e: e.dma_start(out=cst[:, 32:124], in_=s_feat), writes=["S:cst"], dma_key="m0c1")
        P.add("sp", lambda e: e.dma_start(out=cst[0:16, 124:126], in_=s_head), writes=["S:cst"], dma_key="m0c2")
        P.add("act", lambda e: e.activation(out=cst[0:16, 126:127], in_=cst[0:16, 125:126], func=AF.Exp), reads=["S:cst"], writes=["S:cst"])
        P.add("dve", lambda e: e.tensor_scalar(out=cst[0:16, 127:128], in0=cst[0:16, 126:127], scalar1=-1.0, scalar2=None, op0=ALU.mult),
              reads=["S:cst"], writes=["S:cst"])
        dtb, acol = cst[0:16, 124:125], cst[0:16, 127:128]
        P.add("pool", lambda e: e.memset(H[:, :], 0.0), writes=["S:H"])
        P.add("pool", lambda e: e.memset(prevbf[:, :], 0.0), writes=["S:prev"])
        P.add("pool", lambda e: e.memset(R[0:48, :], 0.0), writes=["S:R"])
        P.add("pool", lambda e: e.memset(R[32:48, :], 1.0), reads=["S:R"], writes=["S:R"])
        P.add("pool", lambda e: e.affine_select(out=R[32:48, :].rearrange("p (h t) -> p h t", h=16), in_=R[32:48, :].rearrange("p (h t) -> p h t", h=16),
                                                 pattern=[[-1, 16], [0, 128]], compare_op=ALU.is_equal, fill=0.0, base=0, channel_multiplier=1),
              reads=["S:R"], writes=["S:R"])
        for k in range(2):
            P.add("pool", lambda e, k=k: e.memset(L[k][0:48, :], 0.0), writes=[f"S:L{k}"])
            P.add("pool", lambda e, k=k: e.memset(L[k][0:16, :], 1.0), reads=[f"S:L{k}"], writes=[f"S:L{k}"])
        base = self.sp_
        self.load_lnT(0)
        wx, wz, wsc, wo = dram["m0_wx"], dram["m0_wz"], dram["m0_wsc"], dram["m0_wo"]

        for qi in range(4):
            self.P.fence()
            self.sp_ = base
            yaT = self.carve(2048, BF16).rearrange("p (i t) -> p i t", i=8)
            ybT = self.carve(2048, BF16).rearrange("p (i t) -> p i t", i=8)
            xdt = self.carve(2048, BF16).rearrange("p (b f) -> p b f", b=4)
            zs = self.carve(2048, BF16).rearrange("p (b f) -> p b f", b=4)
            BcT = self.carve(512, BF16).rearrange("p (g t) -> p g t", g=2)
            CcT = self.carve(512, BF16).rearrange("p (g t) -> p g t", g=2)
            Btok = self.carve(512, BF16).rearrange("p (b f) -> p b f", b=4)
            dtok = self.carve(64).rearrange("p (b h) -> p b h", b=4)
            Ddt = self.carve(64).rearrange("p (b h) -> p b h", b=4)
            acsT = self.carve(512)
            dtT = self.carve(512)
            qbase = self.sp_
            a_ = [self.carve(512), self.carve(512)]
            prod = [self.carve(516), self.carve(516)]
            cs = [self.carve(512), self.carve(512)]
            c0 = PAD + 512 * qi
            xres = [f"xT{tb}" for tb in range(4 * qi, 4 * qi + 4)]
            win = lambda kc, c0=c0: xTp[:, kc, c0:c0 + 512]

            dt_t, dt_r = self.wload(dram["m0_wdt"], words=128)
            dv = dt_t[:, 0:128].rearrange("p (k c) -> p k c", k=8)
            P.begin_group()
            for kc in range(8):
                P.add("pe", lambda e, kc=kc, dv=dv, win=win: e.matmul(self.bank(0)[0:16, :], lhsT=dv[:, kc, :], rhs=win(kc), start=(kc == 0), stop=(kc == 7)),
                      reads=[dt_r] + xres, writes=[self.BK(0)])
            P.end_group()
            P.begin_group()
            for tbl in range(4):
                for kc in range(8):
                    P.add("pe", lambda e, kc=kc, tbl=tbl, dv=dv, c0=c0: e.matmul(self.bank(1)[:, tbl * 16:(tbl + 1) * 16],
                                                                               lhsT=xTp[:, kc, c0 + tbl * 128:c0 + (tbl + 1) * 128], rhs=dv[:, kc, :],
                                                                               start=(kc == 0), stop=(kc == 7)),
                          reads=[dt_r] + xres, writes=[self.BK(1)])
            P.end_group()
            P.add("act", lambda e, dtT=dtT: e.activation(out=dtT[0:16, :], in_=self.bank(0)[0:16, :], func=AF.Exp, bias=dtb, scale=1.0),
                  reads=[self.BK(0), "S:cst"], writes=["S:dtT"])
            P.add("dve", lambda e, dtok=dtok: e.tensor_tensor(out=dtok[:, :, :], in0=self.bank(1)[:, 0:64].rearrange("p (b h) -> p b h", b=4),
                                                             in1=biasbc.unsqueeze(1).to_broadcast([128, 4, 16]), op=ALU.add),
                  reads=[self.BK(1), "S:cst"], writes=["S:dtok"])
            P.add("act", lambda e, dtok=dtok: e.activation(out=dtok[:, :, :], in_=dtok[:, :, :], func=AF.Exp), reads=["S:dtok"], writes=["S:dtok"])
            P.add("act", lambda e, dtT=dtT: e.activation(out=dtT[0:16, :], in_=dtT[0:16, :], func=AF.Ln, bias=1.0, scale=1.0), reads=["S:dtT"], writes=["S:dtT"])
            P.add("act", lambda e, dtok=dtok: e.activation(out=dtok[:, :, :], in_=dtok[:, :, :], func=AF.Ln, bias=1.0, scale=1.0),
                  reads=["S:dtok"], writes=["S:dtok"])
            P.add("dve", lambda e, dtok=dtok, Ddt=Ddt: e.reciprocal(out=Ddt[:, :, :], in_=dtok[:, :, :]), reads=["S:dtok"], writes=["S:Ddt"])
            P.add("dve", lambda e, Ddt=Ddt: e.tensor_tensor(out=Ddt[:, :, :], in0=Ddt[:, :, :], in1=Dbc.unsqueeze(1).to_broadcast([128, 4, 16]), op=ALU.mult),
                  reads=["S:Ddt", "S:cst"], writes=["S:Ddt"])
            P.add("dve", lambda e, dtT=dtT: e.tensor_scalar(out=dtT[0:16, :], in0=dtT[0:16, :], scalar1=acol, scalar2=None, op0=ALU.mult),
                  reads=["S:dtT", "S:cst"], writes=["S:dtT"])
            for c in range(4):
                P.add("dve", lambda e, c=c, dtT=dtT, acsT=acsT: e.tensor_tensor_scan(out=acsT[0:16, c * 128:(c + 1) * 128], data0=ones_f[0:16, 0:128],
                                                                                   data1=dtT[0:16, c * 128:(c + 1) * 128], initial=0.0,
                                                                                   op0=ALU.mult, op1=ALU.add),
                      reads=["S:dtT", "ones_f"], writes=[f"S:acs{c}"])

            def xchunk(sl, cc, ak, xv, x_r):
                if True:
                    if sl == 0:
                        ci = 8 + cc
                    else:
                        ci = 4 * (sl - 1) + cc
                    b = self.rotbank("m0", (0, 1, 2, 3, 4, 5))
                    P.begin_group()
                    for kc in range(8):
                        P.add("pe", lambda e, b=b, kc=kc, cc=cc, xv=xv, win=win: e.matmul(self.bank(b), lhsT=xv[:, kc, cc * 128:(cc + 1) * 128], rhs=win(kc),
                                                                                         start=(kc == 0), stop=(kc == 7)),
                              reads=[x_r] + xres, writes=[self.BK(b)])
                    P.end_group()
                    a = a_[ak % 2]
                    ar = f"S:a{ak % 2}"
                    bk = [self.BK(b)]
                    P.add("act", lambda e, b=b, a=a, ci=ci: e.activation(out=a[:, 0:512], in_=self.bank(b), func=AF.Identity,
                                                                        scale=xcw[:, ci, 3:4], bias=xcw[:, ci, 4:5]), reads=bk + ["S:cst"], writes=[ar])
                    for sh in range(1, 4):
                        P.add("dve", lambda e, b=b, a=a, ci=ci, sh=sh: e.scalar_tensor_tensor(
                            out=a[:, sh:512], in0=self.bank(b)[:, 0:512 - sh], scalar=xcw[:, ci, 3 - sh:4 - sh], in1=a[:, sh:512], op0=ALU.mult, op1=ALU.add),
                            reads=bk + ["S:cst", ar], writes=[ar])
                    if qi > 0:
                        P.add("dve", lambda e, a=a, ci=ci: e.tensor_tensor(out=a[:, 0:3], in0=a[:, 0:3], in1=fixx[:, ci, 0:3], op=ALU.add),
                              reads=[ar, f"S:fx{ci}"], writes=[ar])
                    if qi < 3:
                        P.add("dve", lambda e, b=b, ci=ci: e.tensor_scalar(out=fixx[:, ci, 0:3], in0=self.bank(b)[:, 509:512], scalar1=xcw[:, ci, 0:1],
                                                                          scalar2=None, op0=ALU.mult), reads=bk + ["S:cst"], writes=[f"S:fx{ci}"])
                        P.add("dve", lambda e, b=b, ci=ci: e.scalar_tensor_tensor(out=fixx[:, ci, 0:2], in0=self.bank(b)[:, 510:512], scalar=xcw[:, ci, 1:2],
                                                                                 in1=fixx[:, ci, 0:2], op0=ALU.mult, op1=ALU.add),
                              reads=bk + ["S:cst", f"S:fx{ci}"], writes=[f"S:fx{ci}"])
                        P.add("dve", lambda e, b=b, ci=ci: e.scalar_tensor_tensor(out=fixx[:, ci, 0:1], in0=self.bank(b)[:, 511:512], scalar=xcw[:, ci, 2:3],
                                                                                 in1=fixx[:, ci, 0:1], op0=ALU.mult, op1=ALU.add),
                              reads=bk + ["S:cst", f"S:fx{ci}"], writes=[f"S:fx{ci}"])
                    if ci >= 10:
                        g = ci - 10
                        yield
                        P.add("act", lambda e, a=a, g=g, CcT=CcT: e.activation(out=CcT[:, g, :], in_=a[:, 0:512], func=AF.Silu), reads=[ar], writes=[f"S:Cc{g}"])
                        yield
                        yield
                        return
                    yield
                    P.add("act", lambda e, a=a: e.activation(out=a[:, 0:512], in_=a[:, 0:512], func=AF.Silu), reads=[ar], writes=[ar])
                    yield
                    tbk = self.rotbank("m0t", (6, 7))
                    P.begin_group()
                    for tbl in range(4):
                        P.add("pe", lambda e, tbk=tbk, tbl=tbl, a=a: e.transpose(self.bank(tbk)[:, tbl * 128:(tbl + 1) * 128], a[:, tbl * 128:(tbl + 1) * 128], ident_f[:]),
                              reads=[ar, "ident_f"], writes=[self.BK(tbk)])
                    P.end_group()
                    if ci >= 8:
                        g = ci - 8
                        P.add("pool", lambda e, a=a, g=g, BcT=BcT: e.tensor_copy(out=BcT[:, g, :], in_=a[:, 0:512]), reads=[ar], writes=[f"S:Bc{g}"])
                        P.add("act", lambda e, tbk=tbk, g=g, Btok=Btok: e.activation(out=Btok[:, :, g * 128:(g + 1) * 128],
                                                                                    in_=self.bank(tbk).rearrange("p (b f) -> p b f", b=4), func=AF.Identity),
                              reads=[self.BK(tbk)], writes=[f"S:Bt{g}"])
                    else:
                        P.add("dve", lambda e, tbk=tbk, ci=ci, xdt=xdt, dtok=dtok: e.tensor_tensor(
                            out=xdt[:, :, ci * 128:(ci + 1) * 128].rearrange("p b (h d) -> p b h d", h=2),
                            in0=self.bank(tbk).rearrange("p (b h d) -> p b h d", b=4, h=2),
                            in1=dtok[:, :, 2 * ci:2 * ci + 2].unsqueeze(3).to_broadcast([128, 4, 2, 64]), op=ALU.mult),
                            reads=[self.BK(tbk), "S:dtok"], writes=[f"S:xdt{ci}"])
                    yield
            xg = []
            for sl in range(3):
                x_t, x_r = self.wload(wx[sl])
                xv = x_t[:].rearrange("p (k c) -> p k c", k=8)
                for cc in range(4):
                    k = len(xg)
                    xg.append(xchunk(sl, cc, k, xv, x_r))
                    next(xg[k])
                    if k >= 1:
                        next(xg[k - 1])
                    next(xg[k])
            next(xg[-1])

            for zsl in range(2):
                z_t, z_r = self.wload(wz[zsl])
                zv = z_t[:].rearrange("p (k c) -> p k c", k=8)
                for tbl in range(4):
                    b = self.rotbank("m0", (0, 1, 2, 3, 4, 5))
                    P.begin_group()
                    for kc in range(8):
                        P.add("pe", lambda e, b=b, kc=kc, tbl=tbl, zv=zv, c0=c0: e.matmul(self.bank(b), lhsT=xTp[:, kc, c0 + tbl * 128:c0 + (tbl + 1) * 128],
                                                                                         rhs=zv[:, kc, :], start=(kc == 0), stop=(kc == 7)),
                              reads=[z_r] + xres, writes=[self.BK(b)])
                    P.end_group()
                    P.add("act", lambda e, b=b, tbl=tbl, zsl=zsl, zs=zs: e.activation(out=zs[:, tbl, zsl * 512:(zsl + 1) * 512], in_=self.bank(b), func=AF.Silu),
                          reads=[self.BK(b)], writes=[f"S:zs{tbl}"])

            for i in range(8):
                s_t, s_r = self.wload(wsc[i], words=3072)
                sv = s_t[:, 0:3072].rearrange("p (k c) -> p k c", k=8)
                bks = []
                for part in range(3):
                    b = self.rotbank("m0", (0, 1, 2, 3, 4, 5))
                    bks.append(b)
                    P.begin_group()
                    for kc in range(8):
                        P.add("pe", lambda e, b=b, kc=kc, part=part, sv=sv, win=win: e.matmul(self.bank(b), lhsT=sv[:, kc, part * 128:(part + 1) * 128], rhs=win(kc),
                                                                                             start=(kc == 0), stop=(kc == 7)),
                              reads=[s_r] + xres, writes=[self.BK(b)])
                    P.end_group()
                bc_, bh_, bb_ = bks
                k2 = i % 2
                pr, csb, a = prod[k2], cs[k2], a_[k2]
                prr, csr, ar = f"S:pr{k2}", f"S:cs{k2}", f"S:a{k2}"
                P.add("act", lambda e, bc_=bc_, csb=csb: e.activation(out=csb[:, 0:512], in_=self.bank(bc_), func=AF.Identity), reads=[self.BK(bc_)], writes=[csr])
                if qi == 0:
                    P.add("pool", lambda e, pr=pr: e.memset(pr[:, 0:2], 0.0), writes=[prr + "h"])
                else:
                    P.add("pool", lambda e, pr=pr, i=i: e.tensor_copy(out=pr[:, 0:2], in_=halo[:, i, :]), reads=[f"S:halo{i}"], writes=[prr + "h"])
                P.add("dve", lambda e, bh_=bh_, pr=pr, csb=csb: e.tensor_tensor(out=pr[:, 2:514], in0=self.bank(bh_), in1=csb[:, 0:512], op=ALU.mult),
                      reads=[self.BK(bh_), csr], writes=[prr])
                if qi < 3:
                    P.add("pool", lambda e, pr=pr, i=i: e.tensor_copy(out=halo[:, i, :], in_=pr[:, 512:514]), reads=[prr], writes=[f"S:halo{i}"])
                P.add("act", lambda e, pr=pr, a=a, i=i: e.activation(out=a[:, 0:512], in_=pr[:, 2:514], func=AF.Identity, scale=scw[:, i, 2:3]),
                      reads=[prr, "S:cst"], writes=[ar])
                P.add("dve", lambda e, pr=pr, a=a, i=i: e.scalar_tensor_tensor(out=a[:, 0:512], in0=pr[:, 1:513], scalar=scw[:, i, 1:2], in1=a[:, 0:512],
                                                                              op0=ALU.mult, op1=ALU.add), reads=[prr, prr + "h", "S:cst", ar], writes=[ar])
                P.add("dve", lambda e, pr=pr, a=a, i=i: e.scalar_tensor_tensor(out=a[:, 0:512], in0=pr[:, 0:512], scalar=scw[:, i, 0:1], in1=a[:, 0:512],
                                                                              op0=ALU.mult, op1=ALU.add), reads=[prr, prr + "h", "S:cst", ar], writes=[ar])
                P.add("dve", lambda e, bb_=bb_, a=a, i=i, yaT=yaT: e.tensor_tensor(out=yaT[:, i, :], in0=self.bank(bb_), in1=a[:, 0:512], op=ALU.mult),
                      reads=[self.BK(bb_), ar], writes=[f"S:ya{i}"])

            self.P.fence()
            self.sp_ = qbase
            segT = self.carve(1024, BF16).rearrange("p (h t) -> p h t", h=16)
            MT = self.carve(1024, BF16).rearrange("p (h t) -> p h t", h=16)
            xdtd = self.carve(512, BF16)
            yt_ = [self.carve(1024), self.carve(1024)]
            junk = self.carve(256, BF16)
            sm_ = [self.carve(64), self.carve(64)]
            X16 = self.carve(16)
            def chunk(c):
                gc = 4 * qi + c
                cols = slice(c * 128, (c + 1) * 128)
                Lm, Lr = L[gc % 2], f"S:L{gc % 2}"
                yt, ytr = yt_[gc % 2], f"S:yt{gc % 2}"
                sm, smr = sm_[gc % 2], f"S:sm{gc % 2}"
                acr = f"S:acs{c}"
                P.add("dve", lambda e, Lm=Lm, cols=cols, acsT=acsT: e.tensor_scalar(out=Lm[32:48, :], in0=acsT[0:16, cols], scalar1=-1.0, scalar2=None, op0=ALU.mult),
                      reads=[acr], writes=[Lr])
                P.add("dve", lambda e, cols=cols, acsT=acsT: e.tensor_tensor(
                    out=R[0:16, :].rearrange("p (h t) -> p h t", h=16), in0=acsT[0:16, cols].unsqueeze(1).to_broadcast([16, 16, 128]),
                    in1=ident_f[0:16, 0:16].unsqueeze(2).to_broadcast([16, 16, 128]), op=ALU.mult), reads=[acr, "ident_f"], writes=["S:R"])
                for hg in range(4):
                    b = hg % 2
                    P.begin_group()
                    for hh in range(4):
                        P.add("pe", lambda e, b=b, hg=hg, hh=hh, Lm=Lm: e.matmul(self.bank(b)[:, hh * 128:(hh + 1) * 128], lhsT=Lm[0:48, :],
                                                                                rhs=R[0:48, hg * 512 + hh * 128:hg * 512 + (hh + 1) * 128], start=True, stop=False),
                              reads=[Lr, "S:R"], writes=[self.BK(b)])
                        P.add("pe", lambda e, b=b, hh=hh: e.matmul(self.bank(b)[:, hh * 128:(hh + 1) * 128], lhsT=ident_b[:], rhs=maskneg[:], start=False, stop=True),
                              reads=["ident_b", "maskneg"], writes=[self.BK(b)])
                    P.end_group()
                    P.add("act", lambda e, b=b, hg=hg, segT=segT: e.activation(out=segT[:, 4 * hg:4 * hg + 4, :], in_=self.bank(b).rearrange("p (h t) -> p h t", h=4),
                                                                              func=AF.Exp), reads=[self.BK(b)], writes=[f"S:seg{hg}"])
                segr = [f"S:seg{hg}" for hg in range(4)]
                P.begin_group()
                for g in range(2):
                    P.add("pe", lambda e, g=g, cols=cols, BcT=BcT, CcT=CcT: e.matmul(self.bank(2)[:, g * 128:(g + 1) * 128], lhsT=BcT[:, g, cols], rhs=CcT[:, g, cols],
                                                                                    start=True, stop=True), reads=[f"S:Bc{g}", f"S:Cc{g}"], writes=["bk2"])
                P.end_group()
                P.add("dve", lambda e, cols=cols, acsT=acsT: e.tensor_scalar(out=X16[0:16, 0:16], in0=ident_f[0:16, 0:16],
                                                                            scalar1=acsT[0:16, cols][:, 127:128], scalar2=None, op0=ALU.mult),
                      reads=[acr, "ident_f"], writes=["S:X16"])
                P.begin_group()
                P.add("pe", lambda e: e.matmul(self.bank(3)[:, 0:16], lhsT=ones_f[0:16, :], rhs=X16[0:16, 0:16], start=True, stop=True),
                      reads=["ones_f", "S:X16"], writes=["bk3"])
                P.add("pe", lambda e, cols=cols, acsT=acsT: e.transpose(self.bank(3)[:, 16:32], acsT[0:16, cols], ident_f[0:16, 0:16]),
                      reads=[acr, "ident_f"], writes=["bk3"])
                P.end_group()
                P.add("act", lambda e, sm=sm: e.activation(out=sm[:, 0:32], in_=self.bank(3)[:, 0:32], func=AF.Exp), reads=["bk3"], writes=[smr])
                yield
                for g in range(2):
                    P.add("dve", lambda e, g=g, MT=MT, segT=segT: e.tensor_tensor(
                        out=MT[:, 8 * g:8 * g + 8, :], in0=self.bank(2)[:, g * 128:(g + 1) * 128].unsqueeze(1).to_broadcast([128, 8, 128]),
                        in1=segT[:, 8 * g:8 * g + 8, :], op=ALU.mult), reads=["bk2"] + segr, writes=[f"S:MT{g}"])
                P.add("dve", lambda e, c=c, xdt=xdt, xdtd=xdtd, segT=segT: e.tensor_tensor(
                    out=xdtd[:, :].rearrange("p (h d) -> p h d", h=16), in0=xdt[:, c, :].rearrange("p (h d) -> p h d", h=16),
                    in1=segT[:, :, 127:128].to_broadcast([128, 16, 64]), op=ALU.mult),
                    reads=[f"S:xdt{i}" for i in range(8)] + segr, writes=["S:xdtd"])
                yield
                P.begin_group()
                for g in range(2):
                    P.add("pe", lambda e, g=g, cols=cols, CcT=CcT: e.matmul(self.PS[2][:, g * 512:(g + 1) * 512], lhsT=CcT[:, g, cols], rhs=prevbf[:, g * 512:(g + 1) * 512],
                                                                           start=True, stop=True), reads=[f"S:Cc{g}", "S:prev"], writes=[self.BK(4 + g)])
                P.end_group()
                P.begin_group()
                for g in range(2):
                    P.add("pe", lambda e, g=g, c=c, Btok=Btok, xdtd=xdtd: e.matmul(self.PS[0][:, g * 512:(g + 1) * 512], lhsT=Btok[:, c, g * 128:(g + 1) * 128],
                                                                                  rhs=xdtd[:, g * 512:(g + 1) * 512], start=True, stop=True),
                          reads=[f"S:Bt{g}", "S:xdtd"], writes=[self.BK(g)])
                P.end_group()
                P.add("dve", lambda e, sm=sm: e.tensor_tensor(out=H[:, :].rearrange("p (h d) -> p h d", h=16), in0=H[:, :].rearrange("p (h d) -> p h d", h=16),
                                                              in1=sm[:, 0:16].unsqueeze(2).to_broadcast([128, 16, 64]), op=ALU.mult),
                      reads=["S:H", smr], writes=["S:H"])
                P.add("dve", lambda e: e.tensor_tensor(out=H[:, :], in0=self.PS[0][:, :], in1=H[:, :], op=ALU.add), reads=["S:H", self.BK(0), self.BK(1)], writes=["S:H"])
                P.add("act", lambda e: e.activation(out=prevbf[:, :], in_=H[:, :], func=AF.Identity), reads=["S:H"], writes=["S:prev"])
                P.begin_group()
                for h in range(16):
                    P.add("pe", lambda e, h=h, c=c, MT=MT, xdt=xdt: e.matmul(self.PS[3][:, h * 64:(h + 1) * 64], lhsT=MT[:, h, :], rhs=xdt[:, c, h * 64:(h + 1) * 64],
                                                                            start=True, stop=True),
                          reads=[f"S:MT{h // 8}", f"S:xdt{h // 2}"], writes=[self.BK(6 + h // 8)])
                P.end_group()
                yield
                P.add("dve", lambda e, yt=yt, sm=sm: e.tensor_tensor(out=yt[:, :].rearrange("p (h d) -> p h d", h=16), in0=self.PS[2][:, :].rearrange("p (h d) -> p h d", h=16),
                                                                     in1=sm[:, 16:32].unsqueeze(2).to_broadcast([128, 16, 64]), op=ALU.mult),
                      reads=[self.BK(4), self.BK(5), smr], writes=[ytr])
                P.add("dve", lambda e, yt=yt: e.tensor_tensor(out=yt[:, :], in0=self.PS[3][:, :], in1=yt[:, :], op=ALU.add), reads=[self.BK(6), self.BK(7), ytr], writes=[ytr])
                for half in range(2):
                    P.add("pool", lambda e, c=c, half=half, xdt=xdt, Ddt=Ddt: e.tensor_tensor(
                        out=junk[:, 0:512].rearrange("p (h d) -> p h d", h=8), in0=xdt[:, c, half * 512:(half + 1) * 512].rearrange("p (h d) -> p h d", h=8),
                        in1=Ddt[:, c, 8 * half:8 * half + 8].unsqueeze(2).to_broadcast([128, 8, 64]), op=ALU.mult),
                        reads=[f"S:xdt{i}" for i in range(8)] + ["S:Ddt"], writes=["S:junk"])
                    P.add("pool", lambda e, yt=yt, half=half: e.tensor_tensor(out=yt[:, half * 512:(half + 1) * 512], in0=yt[:, half * 512:(half + 1) * 512],
                                                                           in1=junk[:, 0:512], op=ALU.add), reads=[ytr, "S:junk"], writes=[ytr])
                P.add("pool", lambda e, yt=yt, c=c, zs=zs: e.tensor_tensor(out=yt[:, :], in0=yt[:, :], in1=zs[:, c, :], op=ALU.mult), reads=[ytr, f"S:zs{c}"], writes=[ytr])
                for g in range(2):
                    P.add("act", lambda e, g=g, yt=yt, sm=sm: e.activation(out=junk[:, 0:512], in_=yt[:, g * 512:(g + 1) * 512], func=AF.Square,
                                                                          accum_out=sm[:, 32 + g:33 + g]), reads=[ytr], writes=[smr + "s", "S:junk"])
                P.add("pool", lambda e, sm=sm: e.tensor_scalar(out=sm[:, 34:36], in0=sm[:, 32:34], scalar1=1.0 / 512.0, scalar2=LN_EPS, op0=ALU.mult, op1=ALU.add),
                      reads=[smr + "s"], writes=[smr + "r"])
                P.add("pool", lambda e, sm=sm: e.tensor_tensor(out=sm[:, 34:36], in0=sm[:, 34:36], in1=self.neghalf[:, 0:1].to_broadcast([128, 2]), op=ALU.pow),
                      reads=[smr + "r", "neghalf"], writes=[smr + "r"])
                for g in range(2):
                    P.add("act", lambda e, g=g, yt=yt, sm=sm: e.activation(out=yt[:, g * 512:(g + 1) * 512], in_=yt[:, g * 512:(g + 1) * 512], func=AF.Identity,
                                                                          scale=sm[:, 34 + g:35 + g]), reads=[ytr, smr + "r"], writes=[ytr])
                yield
                P.begin_group()
                for i in range(8):
                    P.add("pe", lambda e, i=i, yt=yt: e.transpose(self.PS[2][:, i * 128:(i + 1) * 128], yt[:, i * 128:(i + 1) * 128], ident_f[:]),
                          reads=[ytr, "ident_f"], writes=[self.BK(4 + i // 4)])
                P.end_group()
                for half in range(2):
                    P.add("dve", lambda e, half=half, cols=cols, ybT=ybT: e.tensor_tensor(
                        out=ybT[:, 4 * half:4 * half + 4, cols], in0=self.PS[2][:, half * 512:(half + 1) * 512].rearrange("p (i t) -> p i t", i=4),
                        in1=gT[:, 4 * half:4 * half + 4].unsqueeze(2).to_broadcast([128, 4, 128]), op=ALU.mult),
                        reads=[self.BK(4 + half), "S:cst"], writes=[f"S:yb{c}"])

                yield
            gens = [chunk(c) for c in range(4)]
            order = [0, 0, 0, 1, 0, 1, 0, 1, 2, 1, 2, 1, 2, 3, 2, 3, 2, 3, 3, 3]
            for gi in order:
                next(gens[gi])
            for dh in range(2):
                for part in range(2):
                    o_t, o_r = self.wload(wo[dh, part])
                    ov = o_t[:].rearrange("p (k c) -> p k c", k=8)
                    src = yaT if part == 0 else ybT
                    for tbl in range(4):
                        P.begin_group()
                        for kc in range(8):
                            rd = [o_r, (f"S:ya{kc}" if part == 0 else f"S:yb{tbl}")]
                            P.add("pe", lambda e, dh=dh, part=part, tbl=tbl, kc=kc, ov=ov, src=src: e.matmul(
                                self.bank(4 * dh + tbl), lhsT=src[:, kc, tbl * 128:(tbl + 1) * 128], rhs=ov[:, kc, :],
                                start=(part == 0 and kc == 0), stop=(part == 1 and kc == 7)), reads=rd, writes=[self.BK(4 * dh + tbl)])
                        P.end_group()
                for tbl in range(4):
                    gtb = 4 * qi + tbl
                    P.add("dve", lambda e, dh=dh, tbl=tbl, gtb=gtb: e.scalar_tensor_tensor(
                        out=x_tok[:, gtb, dh * 512:(dh + 1) * 512], in0=x_tok[:, gtb, dh * 512:(dh + 1) * 512], scalar=ALPHA,
                        in1=self.bank(4 * dh + tbl), op0=ALU.mult, op1=ALU.add), reads=[self.BK(4 * dh + tbl), f"xt{gtb}_{dh}"], writes=[f"xt{gtb}_{dh}"])
            for tbl in range(4):
                gtb = 4 * qi + tbl
                self.ln_tb(gtb)
                self.transpose_tb(gtb, tbl, affine=True)
        self.P.fence()
        self.sp_ = base
        self.load_ln(0, with_T=False)
        for tb in range(NTB):
            self.ln_affine(tb)


def declare_dram(nc, phases):
    d = {}
    d["x"] = nc.dram_tensor("x", [T, D], F32, kind="ExternalInput").ap()
    d["out"] = nc.dram_tensor("out", [T, D], F32, kind="ExternalOutput").ap()
    d["lnp"] = nc.dram_tensor("lnp", [4, 2, D], F32, kind="ExternalInput").ap()
    d["lnpT"] = nc.dram_tensor("lnpT", [4, 128, 16], F32, kind="ExternalInput").ap()
    d["ffn_cwb"] = nc.dram_tensor("ffn_cwb", [2, 128, 44, 4], F32, kind="ExternalInput").ap()
    d["wup"] = nc.dram_tensor("wup", [2, 11, 128, 4096], F32, kind="ExternalInput").ap()
    d["wdn"] = nc.dram_tensor("wdn", [2, 2, 3, 128, 4096], F32, kind="ExternalInput").ap()
    d["m0_wdt"] = nc.dram_tensor("m0_wdt", [128, 128], F32, kind="ExternalInput").ap()
    d["m0_wx"] = nc.dram_tensor("m0_wx", [3, 128, 4096], F32, kind="ExternalInput").ap()
    d["m0_wz"] = nc.dram_tensor("m0_wz", [2, 128, 4096], F32, kind="ExternalInput").ap()
    d["m0_wsc"] = nc.dram_tensor("m0_wsc", [8, 128, 3072], F32, kind="ExternalInput").ap()
    d["m0_wo"] = nc.dram_tensor("m0_wo", [2, 2, 128, 4096], F32, kind="ExternalInput").ap()
    d["m0_tokc"] = nc.dram_tensor("m0_tokc", [2, 16], F32, kind="ExternalInput").ap()
    d["m0_featc"] = nc.dram_tensor("m0_featc", [128, 92], F32, kind="ExternalInput").ap()
    d["m0_headc"] = nc.dram_tensor("m0_headc", [16, 2], F32, kind="ExternalInput").ap()
    d["fox_f"] = nc.dram_tensor("fox_f", [128, 128], F32, kind="ExternalInput").ap()
    d["fox_bf"] = nc.dram_tensor("fox_bf", [16, 1], F32, kind="ExternalInput").ap()
    d["fox_qk"] = nc.dram_tensor("fox_qk", [8, 128, 2048], F32, kind="ExternalInput").ap()
    d["fox_v"] = nc.dram_tensor("fox_v", [8, 128, 1024], F32, kind="ExternalInput").ap()
    d["fox_wo"] = nc.dram_tensor("fox_wo", [8, 128, 1024], F32, kind="ExternalInput").ap()
    d["augq"] = nc.dram_tensor("augq", [16, 6, T], BF16, kind="Internal").ap()
    d["augk"] = nc.dram_tensor("augk", [16, 6, T], BF16, kind="Internal").ap()
    return d


def build_program(phases=("mix0", "ffn0", "mix1", "ffn1")):
    nc = bass.Bass("TRN2", target_bir_lowering=False)
    dram = declare_dram(nc, phases)
    P = Prog(nc)
    B = Builder(nc, P, dram)
    B.load_x()
    for tb in range(NTB):
        B.transpose_tb(tb, tb % 4)
    last = phases[-1]
    for ph in phases:
        if ph == "ffn0":
            B.ffn(0, final=(ph == last))
        elif ph == "ffn1":
            B.ffn(1, final=(ph == last))
        elif ph == "mix0":
            P.pin = tuple(os.environ.get("MK_PIN0", "dve,pe").split(","))
            B.mix0()
            P.pin = tuple(x for x in os.environ.get("MK_PINX", "").split(",") if x)
            if ph == last:
                for tb in range(NTB):
                    B.store_tb(tb)
        elif ph == "mix1":
            B.mix1()
            if ph == last:
                for tb in range(NTB):
                    B.store_tb(tb)
        else:
            raise NotImplementedError(ph)
    if SCHEDULE:
        P.schedule()
    P.finalize(B.out_dmas)
    P.emit(B.out_dmas)
    P.close()
    return nc


def host_layouts(inp):
    f = np.float32
    o = {}
    o["lnp"] = np.ascontiguousarray(np.stack([
        np.stack([inp["ln_mix_g"][0], inp["ln_mix_b"][0]]), np.stack([inp["ln_ffn_g"][0], inp["ln_ffn_b"][0]]),
        np.stack([inp["ln_mix_g"][1], inp["ln_mix_b"][1]]), np.stack([inp["ln_ffn_g"][1], inp["ln_ffn_b"][1]])]).astype(f))
    o["lnpT"] = np.ascontiguousarray(o["lnp"].reshape(4, 2, 8, 128).transpose(0, 3, 1, 2).reshape(4, 128, 16))
    cw = inp["ffn_conv_w"].astype(f)
    cb = inp["ffn_conv_b"].astype(f)
    cwb = np.concatenate([cw.transpose(0, 2, 1), cb[:, :, None]], axis=2)
    o["ffn_cwb"] = np.ascontiguousarray(cwb.reshape(2, 44, 128, 4).transpose(0, 2, 1, 3))
    wu = inp["ffn_w_up"].astype(f)
    u = wu[:, :, :DFF].reshape(2, 8, 128, 11, 2, 128)
    g = wu[:, :, DFF:].reshape(2, 8, 128, 11, 2, 128)
    ug = np.stack([u, g], axis=5)
    o["wup"] = np.ascontiguousarray(ug.transpose(0, 3, 2, 1, 4, 5, 6).reshape(2, 11, 128, 4096))
    wd = inp["ffn_w_down"].astype(f)
    wdp = np.zeros((2, 24 * 128, 1024), f)
    wdp[:, :DFF] = wd
    wdp = wdp.reshape(2, 3, 8, 128, 2, 512)
    o["wdn"] = np.ascontiguousarray(wdp.transpose(0, 4, 1, 3, 2, 5).reshape(2, 2, 3, 128, 4096))
    w0 = inp["sc_ssm_w_in"][0].astype(f).reshape(8, 128, 5648)
    lay = lambda cols: np.ascontiguousarray(w0[:, :, cols].transpose(1, 0, 2).reshape(128, -1))
    o["m0_wdt"] = lay(slice(5632, 5648))
    o["m0_wx"] = np.stack([lay(slice(5120, 5632)), lay(slice(4096, 4608)), lay(slice(4608, 5120))])
    o["m0_wz"] = np.stack([lay(slice(3072, 3584)), lay(slice(3584, 4096))])
    o["m0_wsc"] = np.stack([lay(np.r_[1024 + 128 * i:1152 + 128 * i, 2048 + 128 * i:2176 + 128 * i, 128 * i:128 + 128 * i]) for i in range(8)])
    wo0 = inp["sc_ssm_w_out"][0].astype(f).reshape(2, 8, 128, 2, 512)
    o["m0_wo"] = np.ascontiguousarray(wo0.transpose(3, 0, 2, 1, 4).reshape(2, 2, 128, 4096))
    o["m0_tokc"] = np.ascontiguousarray(np.stack([inp["ssm_dt_bias"][0], inp["ssm_d"][0]]).astype(f))
    o["m0_headc"] = np.ascontiguousarray(np.stack([inp["ssm_dt_bias"][0], inp["ssm_a_log"][0]], axis=1).astype(f))
    gTh = inp["ssm_norm_g"][0].astype(f).reshape(8, 128).T
    scwh = inp["sc_conv_w"][0].astype(f).reshape(3, 8, 128).transpose(2, 1, 0)
    xw = inp["ssm_conv_w"][0].astype(f).reshape(4, 12, 128).transpose(2, 1, 0)
    xb = inp["ssm_conv_b"][0].astype(f).reshape(12, 128).T[:, :, None]
    o["m0_featc"] = np.ascontiguousarray(np.concatenate([gTh, scwh.reshape(128, 24), np.concatenate([xw, xb], axis=2).reshape(128, 60)], axis=1))
    wi = inp["fox_w_in"][0].astype(f)
    wk = wi.reshape(8, 128, 3088)
    o["fox_f"] = np.ascontiguousarray(wk[:, :, 3072:3088].transpose(1, 0, 2).reshape(128, 128))
    q = wk[:, :, 0:1024].reshape(8, 128, 8, 128)
    k = wk[:, :, 1024:2048].reshape(8, 128, 8, 128)
    v = wk[:, :, 2048:3072].reshape(8, 128, 8, 128)
    qk = np.concatenate([q, k], axis=3)
    o["fox_qk"] = np.ascontiguousarray(qk.transpose(2, 1, 0, 3).reshape(8, 128, 2048))
    o["fox_v"] = np.ascontiguousarray(v.transpose(2, 1, 0, 3).reshape(8, 128, 1024))
    o["fox_wo"] = np.ascontiguousarray(inp["fox_w_out"][0].astype(f).reshape(8, 128, 1024))
    o["fox_bf"] = np.ascontiguousarray(inp["fox_b_f"][0].astype(f).reshape(16, 1))
    return o


_NC_CACHE = {}


def kernel(**inputs):
    phases = ("mix0", "ffn0", "mix1", "ffn1")
    if phases not in _NC_CACHE:
        _NC_CACHE[phases] = build_program(phases)
    nc = _NC_CACHE[phases]
    lay = host_layouts(inputs)
    x = np.asarray(inputs["x"], dtype=np.float32)
    in_maps = [dict(lay, x=np.ascontiguousarray(x[b])) for b in range(8)]
    res = run_bass_kernel_spmd(nc, in_maps, core_ids=list(range(8)))
    return np.stack([np.asarray(r["out"], dtype=np.float32) for r in res.results], axis=0)
```
